# Optimizing a Trainium2 kernel written in Bass

```python
import jax
import jax.numpy as jnp
from jax import lax
import numpy as np

D_MODEL = 1024
BATCH = 4
SEQ = 4096
DEPTH = 1

MEM_LEN = 256
SSM_GROUP = 16
SSM_WIDTH = 768
SSM_GROUPS = SSM_WIDTH // SSM_GROUP
SSM_STATE = 64
SSM_DT_MIN = 0.001
SSM_DT_MAX = 0.1
ATT_HEAD_DIM = 64
ATT_HEADS_PER_GROUP = 4
DILATION_PATTERN = ((128, 1), (512, 4), (2048, 16))
ATT_GROUPS = len(DILATION_PATTERN)
ATT_HEADS = ATT_GROUPS * ATT_HEADS_PER_GROUP
ATT_WIDTH = ATT_HEADS * ATT_HEAD_DIM
ATT_MERGED = ATT_HEADS_PER_GROUP * ATT_HEAD_DIM
ATT_SCALE = ATT_HEAD_DIM ** -0.5
ROT_DIM = ATT_HEAD_DIM // 4
ROPE_THETA = 500000.0
XATT_HEADS = 4
XATT_HEAD_DIM = D_MODEL // XATT_HEADS
XATT_SCALE = XATT_HEAD_DIM ** -0.5
D_FF = 4 * D_MODEL
DEEPNORM_ALPHA = (2 * DEPTH) ** 0.25
DEEPNORM_BETA = (8 * DEPTH) ** -0.25
LN_EPS = 1e-5
NEG_INF = -1e30
OFF_U = 0
OFF_Q = OFF_U + SSM_WIDTH
OFF_K = OFF_Q + ATT_WIDTH
OFF_V = OFF_K + ATT_WIDTH
OFF_GS = OFF_V + ATT_WIDTH
OFF_GA = OFF_GS + D_MODEL
IN_COLS = OFF_GA + D_MODEL

kernel_name = 'hybrid_s5_dilated_attn_block'


def layer_norm(x, g, b):
    xf = x.astype(jnp.float32)
    mu = jnp.mean(xf, axis=-1, keepdims=True)
    var = jnp.mean(jnp.square(xf - mu), axis=-1, keepdims=True)
    y = (xf - mu) * lax.rsqrt(var + LN_EPS) * g.astype(jnp.float32) + b.astype(jnp.float32)
    return y.astype(x.dtype)


def rope_partial(t, cos, sin):
    half = ROT_DIM // 2
    rot = t[..., :ROT_DIM].astype(jnp.float32)
    x1, x2 = rot[..., :half], rot[..., half:]
    c = cos[:, :, None, :]
    s = sin[:, :, None, :]
    rot = jnp.concatenate([x1 * c - x2 * s, x2 * c + x1 * s], axis=-1).astype(t.dtype)
    return jnp.concatenate([rot, t[..., ROT_DIM:]], axis=-1)


def s5_ssm(u, log_dt, a_re, a_im, b_re, b_im, c_re, c_im, d):
    f32 = jnp.float32
    bsz, s, _ = u.shape
    uf = u.astype(f32)
    ug = uf.reshape(bsz, s, SSM_GROUPS, SSM_GROUP)
    a_re = a_re.astype(f32)
    a_im = a_im.astype(f32)
    dt = jnp.exp(log_dt.astype(f32))[:, None]
    mag = jnp.exp(a_re * dt)
    ab_re = mag * jnp.cos(a_im * dt)
    ab_im = mag * jnp.sin(a_im * dt)
    den = jnp.square(a_re) + jnp.square(a_im)
    nr = ab_re - 1.0
    f_re = (nr * a_re + ab_im * a_im) / den
    f_im = (ab_im * a_re - nr * a_im) / den
    b_re = b_re.astype(f32)
    b_im = b_im.astype(f32)
    bb_re = f_re[..., None] * b_re - f_im[..., None] * b_im
    bb_im = f_re[..., None] * b_im + f_im[..., None] * b_re
    w_re = jnp.einsum('bsgc,gnc->bsgn', ug, bb_re)
    w_im = jnp.einsum('bsgc,gnc->bsgn', ug, bb_im)
    ar = jnp.broadcast_to(ab_re, w_re.shape)
    ai = jnp.broadcast_to(ab_im, w_im.shape)

    def combine(e1, e2):
        a1r, a1i, b1r, b1i = e1
        a2r, a2i, b2r, b2i = e2
        return (a2r * a1r - a2i * a1i,
                a2r * a1i + a2i * a1r,
                a2r * b1r - a2i * b1i + b2r,
                a2r * b1i + a2i * b1r + b2i)

    _, _, h_re, h_im = lax.associative_scan(combine, (ar, ai, w_re, w_im), axis=1)
    y = (jnp.einsum('bsgn,gcn->bsgc', h_re, c_re.astype(f32))
         - jnp.einsum('bsgn,gcn->bsgc', h_im, c_im.astype(f32)))
    y = y.reshape(bsz, s, SSM_WIDTH) + d.astype(f32) * uf
    return y.astype(u.dtype)


def dilated_window_attention(q, k, v, window, dilation):
    bsz, s, h, dh = q.shape
    span = window // dilation
    blk = span
    unit = blk * dilation
    length = -(-s // unit) * unit
    n_blk = length // unit
    pad = length - s

    def arrange(t):
        t = jnp.pad(t, ((0, 0), (0, pad), (0, 0), (0, 0)))
        t = t.reshape(bsz, length // dilation, dilation, h, dh)
        t = t.transpose(0, 2, 1, 3, 4)
        return t.reshape(bsz, dilation, n_blk, blk, h, dh)

    def with_prev(t):
        prev = jnp.pad(t, ((0, 0), (0, 0), (1, 0), (0, 0), (0, 0), (0, 0)))[:, :, :-1]
        return jnp.concatenate([prev, t], axis=3)

    qb = arrange(q)
    kw = with_prev(arrange(k))
    vw = with_prev(arrange(v))
    scores = jnp.einsum('brnqhd,brnkhd->brnhqk', qb, kw).astype(jnp.float32) * ATT_SCALE
    qi = jnp.arange(blk)[:, None]
    ki = jnp.arange(2 * blk)[None, :]
    steps = qi + blk - ki
    band = (steps >= 0) & (steps <= span)
    has_prev = (jnp.arange(n_blk) > 0)[:, None, None]
    valid = band[None] & (has_prev | (ki >= blk)[None])
    scores = jnp.where(valid[None, None, :, None], scores, NEG_INF)
    m = jnp.max(scores, axis=-1, keepdims=True)
    p = jnp.exp(scores - m)
    den = jnp.sum(p, axis=-1, keepdims=True)
    lse = (m + jnp.log(den))[..., 0]
    out = jnp.einsum('brnhqk,brnkhd->brnhqd', p, vw.astype(jnp.float32)) / den
    out = out.transpose(0, 1, 2, 4, 3, 5).reshape(bsz, dilation, length // dilation, h, dh)
    out = out.transpose(0, 2, 1, 3, 4).reshape(bsz, length, h, dh)[:, :s]
    lse = lse.transpose(0, 1, 2, 4, 3).reshape(bsz, dilation, length // dilation, h)
    lse = lse.transpose(0, 2, 1, 3).reshape(bsz, length, h)[:, :s]
    return out, lse


def memory_cross_attention(h, mem, w_xq, w_xkv, w_xo):
    bsz, s, _ = h.shape
    q = (h @ w_xq).reshape(bsz, s, XATT_HEADS, XATT_HEAD_DIM)
    kv = mem @ w_xkv
    k = kv[..., :D_MODEL].reshape(bsz, -1, XATT_HEADS, XATT_HEAD_DIM)
    v = kv[..., D_MODEL:].reshape(bsz, -1, XATT_HEADS, XATT_HEAD_DIM)
    scores = jnp.einsum('bshd,bmhd->bhsm', q, k).astype(jnp.float32) * XATT_SCALE
    p = jax.nn.softmax(scores, axis=-1)
    o = jnp.einsum('bhsm,bmhd->bshd', p, v.astype(jnp.float32)).astype(h.dtype)
    return o.reshape(bsz, s, D_MODEL) @ w_xo


def setup_inputs(seed: int = 0) -> dict:
    key = jax.random.key(seed)
    ks = jax.random.split(key, 40)
    f32 = jnp.float32
    L, D, G, N, C = DEPTH, D_MODEL, SSM_GROUPS, SSM_STATE, SSM_GROUP

    def nrm(k, shape, scale):
        return jax.random.normal(k, shape, f32) * scale

    def gain(k, shape):
        return 1.0 + nrm(k, shape, 0.05)

    n_idx = jnp.arange(N, dtype=f32)
    inp = {
        'x': nrm(ks[0], (BATCH, SEQ, D), 1.0),
        'mem': nrm(ks[1], (BATCH, MEM_LEN, D), 1.0),
        'positions': jnp.broadcast_to(jnp.arange(SEQ, dtype=jnp.int32)[None, :], (BATCH, SEQ)),
        'ln_in_g': gain(ks[2], (D,)),
        'ln_in_b': nrm(ks[3], (D,), 0.02),
        'w_in': nrm(ks[4], (L, D, IN_COLS), D ** -0.5),
        'b_in': nrm(ks[5], (L, IN_COLS), 0.02),
        'ssm_log_dt': jax.random.uniform(ks[6], (L, G), f32, np.log(SSM_DT_MIN), np.log(SSM_DT_MAX)),
        'ssm_a_re': -0.5 + nrm(ks[7], (L, G, N), 0.01),
        'ssm_a_im': jnp.pi * n_idx + nrm(ks[8], (L, G, N), 0.01),
        'ssm_b_re': nrm(ks[9], (L, G, N, C), (0.5 / C) ** 0.5),
        'ssm_b_im': nrm(ks[10], (L, G, N, C), (0.5 / C) ** 0.5),
        'ssm_c_re': nrm(ks[11], (L, G, C, N), (0.5 / N) ** 0.5),
        'ssm_c_im': nrm(ks[12], (L, G, C, N), (0.5 / N) ** 0.5),
        'ssm_d': nrm(ks[13], (L, SSM_WIDTH), 1.0),
        'w_glu': nrm(ks[14], (L, SSM_WIDTH, 2 * D), SSM_WIDTH ** -0.5),
        'b_glu': nrm(ks[15], (L, 2 * D), 0.02),
        'w_att_up': nrm(ks[16], (L, ATT_MERGED, D), ATT_MERGED ** -0.5),
        'w_mix_out': nrm(ks[17], (L, D, D), DEEPNORM_BETA * D ** -0.5),
        'b_mix_out': nrm(ks[18], (L, D), 0.02),
        'ln1_g': gain(ks[19], (L, D)),
        'ln1_b': nrm(ks[20], (L, D), 0.02),
        'w_xq': nrm(ks[21], (L, D, D), D ** -0.5),
        'w_xkv': nrm(ks[22], (L, D, 2 * D), D ** -0.5),
        'w_xo': nrm(ks[23], (L, D, D), DEEPNORM_BETA * D ** -0.5),
        'ln2_g': gain(ks[24], (L, D)),
        'ln2_b': nrm(ks[25], (L, D), 0.02),
        'w_ff1': nrm(ks[26], (L, D, D_FF), D ** -0.5),
        'b_ff1': nrm(ks[27], (L, D_FF), 0.02),
        'w_ff2': nrm(ks[28], (L, D_FF, D), DEEPNORM_BETA * D_FF ** -0.5),
        'b_ff2': nrm(ks[29], (L, D), 0.02),
        'ln3_g': gain(ks[30], (L, D)),
        'ln3_b': nrm(ks[31], (L, D), 0.02),
    }
    return inp


def reference(x, mem, positions, ln_in_g, ln_in_b, w_in, b_in, ssm_log_dt, ssm_a_re, ssm_a_im,
              ssm_b_re, ssm_b_im, ssm_c_re, ssm_c_im, ssm_d, w_glu, b_glu, w_att_up, w_mix_out,
              b_mix_out, ln1_g, ln1_b, w_xq, w_xkv, w_xo, ln2_g, ln2_b, w_ff1, b_ff1, w_ff2, b_ff2,
              ln3_g, ln3_b):
    bsz, s, _ = x.shape
    inv_freq = ROPE_THETA ** (-jnp.arange(0, ROT_DIM, 2, dtype=jnp.float32) / ROT_DIM)
    ang = positions.astype(jnp.float32)[..., None] * inv_freq
    cos, sin = jnp.cos(ang), jnp.sin(ang)

    h = layer_norm(x, ln_in_g, ln_in_b)
    for l in range(DEPTH):
        proj = h @ w_in[l] + b_in[l]
        u = proj[..., OFF_U:OFF_U + SSM_WIDTH]
        q = proj[..., OFF_Q:OFF_Q + ATT_WIDTH].reshape(bsz, s, ATT_HEADS, ATT_HEAD_DIM)
        k = proj[..., OFF_K:OFF_K + ATT_WIDTH].reshape(bsz, s, ATT_HEADS, ATT_HEAD_DIM)
        v = proj[..., OFF_V:OFF_V + ATT_WIDTH].reshape(bsz, s, ATT_HEADS, ATT_HEAD_DIM)
        g_ssm = proj[..., OFF_GS:OFF_GS + D_MODEL]
        g_att = proj[..., OFF_GA:OFF_GA + D_MODEL]

        y = s5_ssm(u, ssm_log_dt[l], ssm_a_re[l], ssm_a_im[l], ssm_b_re[l], ssm_b_im[l],
                   ssm_c_re[l], ssm_c_im[l], ssm_d[l])
        z = jax.nn.gelu(y) @ w_glu[l] + b_glu[l]
        b_ssm = z[..., :D_MODEL] * jax.nn.sigmoid(z[..., D_MODEL:])

        q = rope_partial(q, cos, sin)
        k = rope_partial(k, cos, sin)
        outs, lses = [], []
        for gi, (win, dil) in enumerate(DILATION_PATTERN):
            sl = slice(gi * ATT_HEADS_PER_GROUP, (gi + 1) * ATT_HEADS_PER_GROUP)
            o_g, lse_g = dilated_window_attention(q[:, :, sl], k[:, :, sl], v[:, :, sl], win, dil)
            outs.append(o_g)
            lses.append(lse_g)
        wts = jax.nn.softmax(jnp.stack(lses, axis=0), axis=0)
        att = jnp.einsum('gbsh,gbshd->bshd', wts, jnp.stack(outs, axis=0)).astype(h.dtype)
        b_att = att.reshape(bsz, s, ATT_MERGED) @ w_att_up[l]

        mixed = jax.nn.sigmoid(g_ssm) * b_ssm + jax.nn.sigmoid(g_att) * b_att
        h = layer_norm(DEEPNORM_ALPHA * h + (mixed @ w_mix_out[l] + b_mix_out[l]), ln1_g[l], ln1_b[l])

        xo = memory_cross_attention(h, mem, w_xq[l], w_xkv[l], w_xo[l])
        h = layer_norm(DEEPNORM_ALPHA * h + xo, ln2_g[l], ln2_b[l])

        ff = jnp.square(jax.nn.relu(h @ w_ff1[l] + b_ff1[l])) @ w_ff2[l] + b_ff2[l]
        h = layer_norm(DEEPNORM_ALPHA * h + ff, ln3_g[l], ln3_b[l])
    return h
```

```python
import numpy as np
import ml_dtypes
import concourse.bass as bass
import concourse.mybir as mybir
from concourse.bass_utils import run_bass_kernel_spmd

F32 = mybir.dt.float32
BF16 = mybir.dt.bfloat16
I32 = mybir.dt.int32
AF = mybir.ActivationFunctionType
ALU = mybir.AluOpType
_DSZ = {F32: 4, BF16: 2, I32: 4}


class Buf:
    __slots__ = ("name", "t", "writer", "readers", "dsem", "last_dma", "shape", "dtype", "kind")

    def __init__(self, name, t, shape=None, dtype=None, kind="sbuf"):
        self.name = name
        self.kind = kind
        self.t = t
        self.writer = None
        self.readers = []
        self.dsem = None
        self.last_dma = None
        self.shape = shape
        self.dtype = dtype


class Ev:
    __slots__ = ("eng", "fn", "waits", "order", "need_inc", "semval", "is_dma", "dsem", "dval", "ndma",
                 "cost", "lat", "idx", "t_end", "done", "nsucc", "fence")

    def __init__(self, eng, fn, cost=0.3):
        self.eng = eng
        self.fn = fn
        self.waits = []
        self.order = []
        self.need_inc = False
        self.semval = None
        self.is_dma = False
        self.dsem = None
        self.dval = None
        self.ndma = 0
        self.cost = cost
        self.lat = 0.0
        self.t_end = None
        self.fence = False


class _Phase:
    def __init__(self, mk):
        self.mk = mk

    def __enter__(self):
        self.mk._phase_stack.append(self.mk._sb_off)
        return self

    def __exit__(self, *a):
        self.mk.barrier()
        self.mk._sb_off = self.mk._phase_stack.pop()
        return False


class MK:
    ENGS = ("pe", "act", "dve", "pool", "sp")

    def __init__(self, nc, n_dma_sems=72, reorder=True):
        self.nc = nc
        self.reorder = reorder
        self.segs = [[]]
        self._sb_off = 0
        self._sb_max = 0
        self._phase_stack = []
        self._uid = 0
        self.n_dma_sems = n_dma_sems
        self.n_sw = 24
        self.dsem_free = [list(range(self.n_sw, n_dma_sems)), list(range(self.n_sw))]
        self.dsem_count = [0] * n_dma_sems
        self.dsem_bufs = []
        self.psum_banks = []
        self.sb_base = (int(nc.sbuf_base) + 63) // 64 * 64
        self.sb_limit = int(nc.sbuf_top) - self.sb_base - 1024
        for i in range(8):
            t = nc.alloc_psum_tensor(f"psb{i}", [128, 512], F32)
            self.psum_banks.append(Buf(f"psb{i}", t, [128, 512], F32, "psum"))

    def dram_in(self, name, shape, dtype):
        return Buf(name, self.nc.dram_tensor(name, list(shape), dtype, kind="ExternalInput"), shape, dtype, "dram")

    def dram_out(self, name, shape, dtype):
        return Buf(name, self.nc.dram_tensor(name, list(shape), dtype, kind="ExternalOutput"), shape, dtype, "dram")

    def dram_tmp(self, name, shape, dtype):
        return Buf(name, self.nc.dram_tensor(name, list(shape), dtype, kind="Internal"), shape, dtype, "dram")

    def sbuf(self, name, shape, dtype):
        nbytes = int(np.prod(shape[1:])) * _DSZ[dtype]
        nbytes = (nbytes + 63) // 64 * 64
        off = self._sb_off
        if off + nbytes > self.sb_limit:
            raise RuntimeError(f"SBUF overflow allocating {name}: {off}+{nbytes}")
        self._sb_off = off + nbytes
        self._sb_max = max(self._sb_max, self._sb_off)
        self._uid += 1
        t = self.nc.alloc_sbuf_tensor_at(f"{name}_{self._uid}", list(shape), dtype, offset=self.sb_base + off)
        return Buf(name, t, shape, dtype)

    def bank(self, i):
        return self.psum_banks[i]

    def view(self, name, t):
        return Buf(name, t)

    def phase(self):
        return _Phase(self)

    def _deps(self, ev, eng, reads, writes):
        deps = []
        for b in reads:
            if b.writer is not None:
                deps.append((b.writer, True))
        for b in writes:
            if b.writer is not None:
                deps.append((b.writer, False))
            for r in b.readers:
                deps.append((r, False))
        rawset = set(id(d) for d, raw in deps if raw)
        seen = set()
        for d, _ in deps:
            if d is ev or id(d) in seen:
                continue
            seen.add(id(d))
            same = (not d.is_dma) and (not ev.is_dma) and d.eng == eng
            if same and (eng == "pe" or id(d) not in rawset):
                ev.order.append(d)
            else:
                ev.waits.append(d)
        for b in reads:
            b.readers.append(ev)
        for b in writes:
            b.writer = ev
            b.readers = []

    def op(self, eng, fn, reads=(), writes=(), cost=0.3):
        ev = Ev(eng, fn, cost)
        self._deps(ev, eng, reads, writes)
        self.segs[-1].append(ev)
        return ev

    def dma(self, eng, pairs, reads=(), writes=(), sync=None, nbytes=None):
        if sync is None:
            for b in list(writes) + list(reads):
                if b.kind != "dram":
                    sync = b
                    break
        if sync is None:
            sync = (list(writes) + list(reads))[0]
        sw = 1 if eng == "pool" else 0
        if sync.dsem is None:
            sync.dsem = [None, None]
            sync.last_dma = [None, None]
            self.dsem_bufs.append(sync)
        if sync.dsem[sw] is None:
            if not self.dsem_free[sw]:
                raise RuntimeError("out of DMA semaphores")
            sync.dsem[sw] = self.dsem_free[sw].pop()
        ds = sync.dsem[sw]
        ev = Ev(eng, None, 0.6 if sw else 0.08)
        ev.is_dma = True
        ev.ndma = len(pairs)
        ev.dsem = ds
        self.dsem_count[ds] += 16 * len(pairs)
        ev.dval = self.dsem_count[ds]
        ev.fn = pairs
        if nbytes is None:
            nbytes = 0
            for (o, i) in pairs:
                try:
                    nbytes += int(o.partition_size) * int(o.free_size) * 4
                except Exception:
                    nbytes += 1 << 19
        ev.lat = 2.0 + nbytes / 150e3
        for ld in sync.last_dma:
            if ld is not None:
                ev.waits.append(ld)
        sync.last_dma[sw] = ev
        self._deps(ev, eng, reads, writes)
        self.segs[-1].append(ev)
        return ev

    def barrier(self):
        self.segs.append([])
        for b in self.dsem_bufs:
            b.dsem = None
            b.last_dma = None
        self.dsem_bufs = []
        self.dsem_free = [list(range(self.n_sw, self.n_dma_sems)), list(range(self.n_sw))]

    def _schedule(self, seg):
        ENGS = self.ENGS
        if not self.reorder:
            return {e: [ev for ev in seg if ev.eng == e] for e in ENGS}
        WIN = {"pe": 320, "act": 256, "dve": 384, "pool": 384, "sp": 64}
        SEM_LAT = 0.7
        pend = {e: [ev for ev in seg if ev.eng == e] for e in ENGS}
        out = {e: [] for e in ENGS}
        free_at = {e: 0.0 for e in ENGS}
        inseg = set(id(ev) for ev in seg)
        for ev in seg:
            ev.t_end = None
        n_left = len(seg)
        while n_left:
            best = None
            for e in ENGS:
                q = pend[e]
                lim = min(WIN[e], len(q))
                for k in range(lim):
                    ev = q[k]
                    rdy = 0.0
                    ok = True
                    for d in ev.waits:
                        if id(d) in inseg:
                            if d.t_end is None:
                                ok = False
                                break
                            t = d.t_end + SEM_LAT
                            if t > rdy:
                                rdy = t
                    if ok:
                        for d in ev.order:
                            if id(d) in inseg and d.t_end is None:
                                ok = False
                                break
                    if not ok:
                        continue
                    start = rdy if rdy > free_at[e] else free_at[e]
                    key = (start, k)
                    if best is None or key < best[0]:
                        best = (key, e, k, ev, start)
                    if start <= free_at[e]:
                        break
            if best is None:
                raise RuntimeError("scheduler deadlock")
            _, e, k, ev, start = best
            pend[e].pop(k)
            out[e].append(ev)
            free_at[e] = start + ev.cost
            ev.t_end = start + ev.cost + ev.lat
            n_left -= 1
        return out

    def finish(self):
        import contextlib
        nc = self.nc
        prog = {e: [] for e in self.ENGS}
        for seg in self.segs:
            if not seg:
                continue
            sch = self._schedule(seg)
            lastc = [sch[e][-1] for e in self.ENGS if sch[e] and not all(x.is_dma for x in sch[e])]
            lastc = []
            for e in self.ENGS:
                for ev in reversed(sch[e]):
                    if not ev.is_dma:
                        lastc.append(ev)
                        break
            dmas = {}
            for ev in seg:
                if ev.is_dma and (ev.dsem not in dmas or dmas[ev.dsem].dval < ev.dval):
                    dmas[ev.dsem] = ev
            for e in self.ENGS:
                prog[e].extend(sch[e])
                nop = Ev(e, lambda eng: eng.nop())
                nop.fence = True
                nop.waits = [d for d in lastc if d.eng != e] + list(dmas.values())
                prog[e].append(nop)
        for e in self.ENGS:
            for ev in prog[e]:
                for d in ev.waits:
                    if not d.is_dma:
                        d.need_inc = True
        self.prog = prog
        with contextlib.ExitStack() as es:
            esem = {e: es.enter_context(nc.semaphore(f"s_{e}")) for e in self.ENGS}
            dsems = [es.enter_context(nc.semaphore(f"d_{i}")) for i in range(self.n_dma_sems)]
            for e in self.ENGS:
                c = 0
                for ev in prog[e]:
                    if ev.need_inc and not ev.is_dma:
                        c += 1
                        ev.semval = c
            mk = self

            def replay(ename, eng):
                seen = {}
                for ev in prog[ename]:
                    for d in ev.waits:
                        if d.is_dma:
                            key, sem, val = ("d", d.dsem), dsems[d.dsem], d.dval
                        else:
                            key, sem, val = ("e", d.eng), esem[d.eng], d.semval
                        if seen.get(key, 0) < val:
                            eng.wait_ge(sem, val)
                            seen[key] = val
                    if ev.is_dma:
                        for (o, i) in ev.fn:
                            eng.dma_start(out=o, in_=i).then_inc(dsems[ev.dsem], 16)
                    else:
                        inst = ev.fn(eng)
                        if ev.need_inc:
                            inst.then_inc(esem[ename], 1)

            with nc.Block() as block:
                @block.tensor
                def _(eng):
                    replay("pe", eng)

                @block.scalar
                def _(eng):
                    replay("act", eng)

                @block.vector
                def _(eng):
                    replay("dve", eng)

                @block.gpsimd
                def _(eng):
                    replay("pool", eng)

                @block.sync
                def _(eng):
                    replay("sp", eng)

    def stats(self):
        return {e: len(self.prog[e]) for e in self.ENGS}, self._sb_max


def apx(buf, poff, npart, off, dims):
    t = buf.t
    shp = buf.shape
    rowlen = int(np.prod(shp[1:]))
    return bass.AP(t, poff * rowlen + off, [[rowlen, npart]] + [[int(s), int(c)] for (s, c) in dims])


D_MODEL = 1024
SEQ = 4096
HALF = 2048
NT = 16
G = 48
ALPHA = 2.0 ** 0.25
EPS = 1e-5
TWO_PI = float(2 * np.pi)
DPOW = [0, 1, 2, 3, 4, 5, 6, 7, 8, 16, 24, 32, 64, 96, 128, 256, 384, 512, 1024, 1536]
ND = len(DPOW)
SCAN_IDX = [[DPOW.index(8 * e * 4 ** l) for e in (1, 2, 3)] for l in range(4)]
DILS = (1, 4, 16)
GELU_C = float(np.sqrt(2.0 / np.pi))


class K:
    def __init__(self, dbg=None):
        self.dbg = dbg or ()
        nc = bass.Bass("TRN2", target_bir_lowering=False)
        self.nc = nc
        self.mk = MK(nc)
        self.din = {}
        self.build()

    def inp(self, name, shape, dtype=F32):
        b = self.mk.dram_in(name, shape, dtype)
        self.din[name] = b
        return b

    @staticmethod
    def _fs(ap):
        try:
            return float(ap.free_size)
        except Exception:
            return 512.0

    def mm(self, out, lhsT, rhs, start, stop, reads, writes):
        self.mk.op("pe", lambda e: e.matmul(out, lhsT, rhs, start=start, stop=stop), reads=reads, writes=writes,
                   cost=(0.035 + self._fs(out) / 2400.0) * getattr(self, 'pe_scale', 1.0))

    def act(self, out, in_, func, reads, writes, bias=None, scale=None, eng="act"):
        kw = {}
        if bias is not None:
            kw["bias"] = bias
        if scale is not None:
            kw["scale"] = scale
        self.mk.op("act", lambda e: e.activation(out, in_, func, **kw), reads=reads, writes=writes, cost=0.2 + self._fs(out) / 1100.0)

    def tt(self, eng, out, a, b, op, reads, writes):
        self.mk.op(eng, lambda e: e.tensor_tensor(out, a, b, op), reads=reads, writes=writes, cost=self._ec(eng, out))

    def ts(self, eng, out, a, s1, s2, op0, op1, reads, writes):
        if op1 is None:
            self.mk.op(eng, lambda e: e.tensor_scalar(out, a, s1, None, op0=op0), reads=reads, writes=writes, cost=self._ec(eng, out))
        else:
            self.mk.op(eng, lambda e: e.tensor_scalar(out, a, s1, s2, op0=op0, op1=op1), reads=reads, writes=writes, cost=self._ec(eng, out))

    def stt(self, out, a, s, b, op0, op1, reads, writes):
        self.mk.op("dve", lambda e: e.scalar_tensor_tensor(out, a, s, b, op0=op0, op1=op1), reads=reads, writes=writes, cost=self._ec("dve", out))

    def copy(self, eng, out, in_, reads, writes):
        if eng == "act":
            self.mk.op("act", lambda e: e.copy(out, in_), reads=reads, writes=writes, cost=0.2 + self._fs(out) / 1100.0)
        else:
            self.mk.op(eng, lambda e: e.tensor_copy(out, in_), reads=reads, writes=writes, cost=self._ec(eng, out))

    def _ec(self, eng, out):
        f = self._fs(out)
        return (0.25 + f / 550.0) if eng == "pool" else (0.12 + f / 900.0)

    def load(self, eng, dst, dst_ap, src, src_ap):
        self.mk.dma(eng, [(dst_ap, src_ap)], reads=[src], writes=[dst])

    def dump(self, name, buf, ap, shape, dtype=F32):
        if name in self.dbg:
            o = self.mk.dram_out("dbg_" + name, shape, dtype)
            self.mk.dma("sp", [(o.t.ap(), ap)], reads=[buf], writes=[o])

    def wload(self, name, src, rows, col0, ncols, eng="pool"):
        kt = rows // 128
        b = self.mk.sbuf(name, [128, kt, ncols], BF16)
        srcap = src.t.ap().rearrange("(kt p) c -> p kt c", p=128)[:, :, col0:col0 + ncols]
        self.mk.dma(eng, [(b.t[:], srcap)], reads=[src], writes=[b])
        return b

    def wload_into(self, b, src, rows, col0, ncols, eng="pool"):
        srcap = src.t.ap().rearrange("(kt p) c -> p kt c", p=128)[:, :, col0:col0 + ncols]
        self.mk.dma(eng, [(b.t[:], srcap)], reads=[src], writes=[b])

    def layernorm(self, r, gB, bB, out32, outT, tcol, tmp16, psT_bank, st):
        mk = self.mk
        ident = self.ident
        mk.op("dve", lambda e: e.bn_stats(st.t[:, 0:6], r.t[:, 0:512]), reads=[r], writes=[st], cost=0.7)
        mk.op("dve", lambda e: e.bn_stats(st.t[:, 6:12], r.t[:, 512:1024]), reads=[r], writes=[st], cost=0.6)
        mk.op("dve", lambda e: e.bn_aggr(st.t[:, 12:14], st.t[:, 0:12]), reads=[st], writes=[st], cost=0.21)
        self.act(st.t[:, 14:15], st.t[:, 13:14], AF.Sqrt, [st, self.cst], [st], bias=self.cst.t[:, 0:1], scale=1.0)
        mk.op("dve", lambda e: e.reciprocal(st.t[:, 15:16], st.t[:, 14:15]), reads=[st], writes=[st], cost=0.17)
        self.ts("dve", st.t[:, 16:17], st.t[:, 12:13], st.t[:, 15:16], -1.0, ALU.mult, ALU.mult, [st], [st])
        self.act(out32.t[:], r.t[:], AF.Identity, [st, r], [out32], bias=st.t[:, 16:17], scale=st.t[:, 15:16])
        self.tt("dve", out32.t[:], out32.t[:], gB.t[:], ALU.mult, [out32, gB], [out32])
        self.tt("pool", out32.t[:], out32.t[:], bB.t[:], ALU.add, [out32, bB], [out32])
        if outT is None:
            return
        self.copy("act", tmp16.t[:], out32.t[:], [out32], [tmp16])
        psb = psT_bank.t[:].bitcast(BF16)
        for kt in range(8):
            mk.op("pe", lambda e, kt=kt: e.transpose(psb[:, kt * 128:(kt + 1) * 128], tmp16.t[:, kt * 128:(kt + 1) * 128], ident.t[:]),
                  reads=[tmp16, ident], writes=[psT_bank], cost=0.09)
        dst = apx(outT[0], 0, 128, tcol, [(outT[1], 8), (1, 128)])
        src = psb.rearrange("p (k c) -> p k c", k=8)
        self.copy("act", dst, src, [psT_bank], [outT[0]])

    def build(self):
        mk = self.mk
        inp = self.inp
        x_own = inp("x_own", [HALF, D_MODEL]); x_prev = inp("x_prev", [HALF, D_MODEL])
        flag_d = inp("flag", [128, 1]); posb = inp("posb", [128, 512], I32)
        mem = inp("mem", [256, D_MODEL])
        w_in = inp("w_in", [1024, 5120]); w_sw = inp("w_sw", [1024, 1536])
        bin_fm = inp("bin_fm", [128, 40]); bsw_fm = inp("bsw_fm", [128, 12]); bv_row = inp("bv_row", [1, 768])
        ln_g = inp("ln_g", [4, 1024]); ln_b = inp("ln_b", [4, 1024])
        s5p = inp("s5p", [128, 3 * G])
        s5B = inp("s5B", [128, 2 * G * 16])
        s5C = inp("s5C", [128, 2 * G * 16])
        s5d = inp("s5d", [128, 6])
        w_glu = inp("w_glu", [768, 2048]); bglu_fm = inp("bglu_fm", [128, 16])
        w_au = inp("w_au", [256, 1024]); w_mo = inp("w_mo", [1024, 1024]); bmo_row = inp("bmo_row", [1, 1024])
        w_xq = inp("w_xq", [1024, 1024]); w_xkv = inp("w_xkv", [1024, 2048]); w_xo = inp("w_xo", [1024, 1024])
        w_f1 = inp("w_f1", [1024, 4096]); bf1_fm = inp("bf1_fm", [128, 32]); w_f2 = inp("w_f2", [4096, 1024])
        bf2_row = inp("bf2_row", [1, 1024])
        c_bf = inp("c_bf", [128, 128 * 3 + 256 + 64], BF16)
        c_f32 = inp("c_f32", [128, 512])
        out = mk.dram_out("out", [HALF, D_MODEL], F32)
        res_d = [mk.dram_tmp(f"res{t}", [128, D_MODEL], F32) for t in range(NT)]
        tabs = {nm: mk.dram_tmp("tab" + nm, [128, G * 8], F32) for nm in ("ZA", "ZB", "QA", "QB")}
        tabs["WR"] = mk.dram_tmp("tabWR", [128, G * 24], F32)
        tabs["Cneg"] = mk.dram_tmp("tabCneg", [128, G * 16], BF16)

        cbf = mk.sbuf("cbf", [128, 128 * 3 + 256 + 64], BF16)
        cf = mk.sbuf("cf", [128, 512], F32)
        self.cst = cf
        self.cbf = cbf
        flag = mk.sbuf("flag", [128, 1], F32)
        mk.dma("sp", [(cbf.t[:], c_bf.t.ap())], reads=[c_bf], writes=[cbf])
        mk.dma("sp", [(cf.t[:], c_f32.t.ap())], reads=[c_f32], writes=[cf])
        mk.dma("sp", [(flag.t[:], flag_d.t.ap())], reads=[flag_d], writes=[flag])
        ident = mk.view("ident", cbf.t[:, 0:128]); ident.writer = cbf.writer
        self.ident = ident
        ones = cbf.t[:, 128:256]
        maskpc = cbf.t[:, 384:640]
        AT = mk.sbuf("AT", [128, 8, HALF], BF16)
        ATp = (AT, HALF)
        st = [mk.sbuf(f"st{i}", [128, 32], F32) for i in range(8)]

        def ln_params(i, gB, bB):
            mk.dma("sp", [(gB.t[:], ln_g.t.ap()[i:i + 1, :].partition_broadcast(128).rearrange("p o c -> p (o c)"))], reads=[ln_g], writes=[gB])
            mk.dma("sp", [(bB.t[:], ln_b.t.ap()[i:i + 1, :].partition_broadcast(128).rearrange("p o c -> p (o c)"))], reads=[ln_b], writes=[bB])

        with mk.phase():
            attT = mk.sbuf("attT", [128, 2, HALF], BF16)
            HTp = mk.sbuf("HTp", [128, 8, HALF], BF16)
            COS = mk.dram_tmp("cosd", [16, SEQ], BF16); SIN = mk.dram_tmp("sind", [16, SEQ], BF16)
            with mk.phase():
                self.s5_prep(s5p, s5C, tabs)
                self.rope_tables(posb, COS, SIN)
                gB = mk.sbuf("gB", [128, 1024], F32); bB = mk.sbuf("bB", [128, 1024], F32)
                ln_params(0, gB, bB)
                NB = 4
                xt = [mk.sbuf(f"xt{i}", [128, 1024], F32) for i in range(NB)]
                o32 = [mk.sbuf(f"o32{i}", [128, 1024], F32) for i in range(NB)]
                t16 = [mk.sbuf(f"t16{i}", [128, 1024], BF16) for i in range(NB)]

                def ld(t):
                    own = t >= NT
                    tt_ = t - NT if own else t
                    src = x_own if own else x_prev
                    mk.dma("sp", [(xt[t % NB].t[:], src.t.ap()[tt_ * 128:(tt_ + 1) * 128, :])], reads=[src], writes=[xt[t % NB]])

                ld(0); ld(1)
                for t in range(2 * NT):
                    own = t >= NT
                    tt_ = t - NT if own else t
                    if t + 2 < 2 * NT:
                        ld(t + 2)
                    i = t % NB
                    self.layernorm(xt[i], gB, bB, o32[i], (AT if own else HTp, HALF), tt_ * 128, t16[i], mk.bank(t % 2), st[i])
                    if own:
                        mk.dma("sp", [(res_d[tt_].t.ap(), o32[i].t[:])], reads=[o32[i]], writes=[res_d[tt_]])
            self.dump("hT", AT, AT.t[:], [128, 8, HALF], BF16)
            self.pe_scale = 1.3
            self.attention(w_in, w_sw, bin_fm, bsw_fm, bv_row, COS, SIN, flag, AT, HTp, attT, ones, maskpc)
            self.pe_scale = 1.0
            self.dump("attT", attT, attT.t[:], [128, 2, HALF], BF16)
            gyd = [mk.dram_tmp(f"gyd{j}", [128, HALF], BF16) for j in range(6)]
            self.s5(w_in, bin_fm, s5B, s5C, s5d, tabs, flag, AT, HTp, gyd)
            gy = mk.sbuf("gy", [128, 6, HALF], BF16)
            mk.dma("sp", [(gy.t[:, j, :], gyd[j].t.ap()) for j in range(6)], reads=gyd, writes=[gy])
            MIX = mk.sbuf("MIX", [128, 8, HALF], BF16)
            self.phaseE(w_in, bin_fm, w_glu, bglu_fm, w_au, AT, gy, attT, MIX)
            self.dump("MIX", MIX, MIX.t[:], [128, 8, HALF], BF16)
            self.pe_scale = 1.7
            self.proj_ln(MIX, w_mo, bmo_row, 1, AT, res_d, ln_params, st, ones, NB=4)
        self.dump("h1T", AT, AT.t[:], [128, 8, HALF], BF16)
        self.xattn(mem, w_xq, w_xkv, w_xo, AT, res_d, ln_params, st, ones)
        self.dump("h2T", AT, AT.t[:], [128, 8, HALF], BF16)
        self.pe_scale = 1.0
        self.ffn(w_f1, bf1_fm, w_f2, bf2_row, AT, res_d, ln_params, st, ones, out)
        mk.finish()

    def phaseE(self, w_in, bin_fm, w_glu, bglu_fm, w_au, AT, gy, attT, MIX):
        mk = self.mk
        with mk.phase():
            wg1 = [mk.sbuf(f"wg1{i}", [128, 6, 128], BF16) for i in range(2)]
            wg2 = [mk.sbuf(f"wg2{i}", [128, 6, 128], BF16) for i in range(2)]
            wgs = [mk.sbuf(f"wgs{i}", [128, 8, 128], BF16) for i in range(2)]
            wga = [mk.sbuf(f"wga{i}", [128, 8, 128], BF16) for i in range(2)]
            wau = [mk.sbuf(f"wau{i}", [128, 2, 128], BF16) for i in range(2)]
            binb = mk.sbuf("binb", [128, 40], F32); bglu = mk.sbuf("bglu", [128, 16], F32)
            mk.dma("sp", [(binb.t[:], bin_fm.t.ap())], reads=[bin_fm], writes=[binb])
            mk.dma("sp", [(bglu.t[:], bglu_fm.t.ap())], reads=[bglu_fm], writes=[bglu])
            tmp = [[mk.sbuf(f"e{k}{i}", [128, 512], F32) for k in range(5)] for i in range(2)]
            it = 0
            def ldw(mt):
                self.wload_into(wg1[mt % 2], w_glu, 768, mt * 128, 128)
                self.wload_into(wg2[mt % 2], w_glu, 768, 1024 + mt * 128, 128)
                self.wload_into(wgs[mt % 2], w_in, 1024, 3072 + mt * 128, 128)
                self.wload_into(wga[mt % 2], w_in, 1024, 4096 + mt * 128, 128)
                self.wload_into(wau[mt % 2], w_au, 256, mt * 128, 128)

            ldw(0)
            for mt in range(8):
                w1, w2, w3, w4, w5 = wg1[mt % 2], wg2[mt % 2], wgs[mt % 2], wga[mt % 2], wau[mt % 2]
                if mt + 1 < 8:
                    ldw(mt + 1)
                for blk in range(4):
                    bs = slice(blk * 512, (blk + 1) * 512)
                    sg2, t1, sgs, sga, t2 = tmp[it % 2]
                    it += 1
                    bset = 4 * (it % 2)
                    pz1, pz2, pgs, pga = [mk.bank(bset + i) for i in range(4)]
                    pba = pz2
                    for kt in range(6):
                        self.mm(pz1.t[:], w1.t[:, kt, :], gy.t[:, kt, bs], kt == 0, kt == 5, [w1, gy], [pz1])
                    for kt in range(6):
                        self.mm(pz2.t[:], w2.t[:, kt, :], gy.t[:, kt, bs], kt == 0, kt == 5, [w2, gy], [pz2])
                    for kt in range(8):
                        self.mm(pgs.t[:], w3.t[:, kt, :], AT.t[:, kt, bs], kt == 0, kt == 7, [w3, AT], [pgs])
                    for kt in range(8):
                        self.mm(pga.t[:], w4.t[:, kt, :], AT.t[:, kt, bs], kt == 0, kt == 7, [w4, AT], [pga])
                    self.act(sg2.t[:], pz2.t[:], AF.Sigmoid, [pz2, bglu], [sg2], bias=bglu.t[:, 8 + mt:9 + mt])
                    for kt in range(2):
                        self.mm(pba.t[:], w5.t[:, kt, :], attT.t[:, kt, bs], kt == 0, kt == 1, [w5, attT], [pba])
                    self.stt(t1.t[:], pz1.t[:], bglu.t[:, mt:mt + 1], sg2.t[:], ALU.add, ALU.mult, [pz1, bglu, sg2], [t1])
                    self.act(sgs.t[:], pgs.t[:], AF.Sigmoid, [pgs, binb], [sgs], bias=binb.t[:, 24 + mt:25 + mt])
                    self.act(sga.t[:], pga.t[:], AF.Sigmoid, [pga, binb], [sga], bias=binb.t[:, 32 + mt:33 + mt])
                    self.tt("pool", t1.t[:], t1.t[:], sgs.t[:], ALU.mult, [t1, sgs], [t1])
                    self.tt("dve", t2.t[:], pba.t[:], sga.t[:], ALU.mult, [pba, sga], [t2])
                    self.tt("pool", MIX.t[:, mt, bs], t1.t[:], t2.t[:], ALU.add, [t1, t2], [MIX])

    def proj_ln(self, X, w, brow_d, ln_idx, AT, res_d, ln_params, st, ones, NB=3):
        mk = self.mk
        with mk.phase():
            W = self.wload("W", w, 1024, 0, 1024)
            gB = mk.sbuf("gB", [128, 1024], F32); bB = mk.sbuf("bB", [128, 1024], F32)
            ln_params(ln_idx, gB, bB)
            if brow_d is not None:
                brow = mk.sbuf("brow", [1, 1024], BF16)
                mk.dma("pool", [(brow.t[:], brow_d.t.ap())], reads=[brow_d], writes=[brow])
            rt = [mk.sbuf(f"rt{i}", [128, 1024], F32) for i in range(NB)]
            o32 = [mk.sbuf(f"o32{i}", [128, 1024], F32) for i in range(NB)]
            t16 = [mk.sbuf(f"t16{i}", [128, 1024], BF16) for i in range(NB)]

            def ld(t):
                mk.dma("sp", [(rt[t % NB].t[:], res_d[t].t.ap())], reads=[res_d[t]], writes=[rt[t % NB]])

            ld(0); ld(1)
            for t in range(NT):
                i = t % NB
                ts_ = slice(t * 128, (t + 1) * 128)
                if t + 2 < NT:
                    ld(t + 2)
                for half in range(2):
                    hs = slice(half * 512, (half + 1) * 512)
                    ps = mk.bank(2 * (t % 3) + half)
                    for kt in range(8):
                        self.mm(ps.t[:], X.t[:, kt, ts_], W.t[:, kt, hs], kt == 0, (kt == 7 and brow_d is None), [X, W], [ps])
                    if brow_d is not None:
                        self.mm(ps.t[:], ones[0:1, :], brow.t[0:1, hs], False, True, [self.cbf, brow], [ps])
                    self.stt(rt[i].t[:, hs], rt[i].t[:, hs], ALPHA, ps.t[:], ALU.mult, ALU.add, [rt[i], ps], [rt[i]])
                self.layernorm(rt[i], gB, bB, o32[i], (AT, HALF), t * 128, t16[i], mk.bank(6 + t % 2), st[t % 8])
                mk.dma("sp", [(res_d[t].t.ap(), o32[i].t[:])], reads=[o32[i]], writes=[res_d[t]])

    def xattn(self, mem, w_xq, w_xkv, w_xo, AT, res_d, ln_params, st, ones):
        mk = self.mk
        ident = self.ident
        with mk.phase():
            KmT = mk.sbuf("KmT", [128, 8, 256], BF16)
            Vm = mk.sbuf("Vm", [128, 2, 1024], BF16)
            OX = mk.sbuf("OX", [128, 8, HALF], BF16)
            with mk.phase():
                memT = mk.sbuf("memT", [128, 8, 256], BF16)
                mt32 = mk.sbuf("mt32", [128, 1024], F32); mt16 = mk.sbuf("mt16", [128, 1024], BF16)
                for mtile in range(2):
                    mk.dma("sp", [(mt32.t[:], mem.t.ap()[mtile * 128:(mtile + 1) * 128, :])], reads=[mem], writes=[mt32])
                    self.copy("dve", mt16.t[:], mt32.t[:], [mt32], [mt16])
                    pst = mk.bank(0)
                    psb = pst.t[:].bitcast(BF16)
                    for kt in range(8):
                        mk.op("pe", lambda e, kt=kt, psb=psb: e.transpose(psb[:, kt * 128:(kt + 1) * 128], mt16.t[:, kt * 128:(kt + 1) * 128], ident.t[:]),
                              reads=[mt16, ident], writes=[pst])
                    self.copy("act", apx(memT, 0, 128, mtile * 128, [(256, 8), (1, 128)]), psb.rearrange("p (k c) -> p k c", k=8), [pst], [memT])
                wk = self.wload("wk", w_xkv, 1024, 0, 1024)
                for mt in range(8):
                    ps = mk.bank(1 + mt % 2)
                    for kt in range(8):
                        self.mm(ps.t[:, 0:256], wk.t[:, kt, mt * 128:(mt + 1) * 128], memT.t[:, kt, :], kt == 0, kt == 7, [wk, memT], [ps])
                    self.copy("act" if mt % 2 else "dve", KmT.t[:, mt, :], ps.t[:, 0:256], [ps], [KmT])
                wv = self.wload("wvx", w_xkv, 1024, 1024, 1024)
                for mtile in range(2):
                    for half in range(2):
                        ps = mk.bank(3 + half)
                        for kt in range(8):
                            self.mm(ps.t[:], memT.t[:, kt, mtile * 128:(mtile + 1) * 128], wv.t[:, kt, half * 512:(half + 1) * 512], kt == 0, kt == 7, [memT, wv], [ps])
                        self.copy("act" if half else "dve", Vm.t[:, mtile, half * 512:(half + 1) * 512], ps.t[:], [ps], [Vm])
            with mk.phase():
                QX = mk.sbuf("QX", [128, 8, HALF], BF16)
                wq = self.wload("wxq", w_xq, 1024, 0, 1024)
                for blk in range(4):
                    bs = slice(blk * 512, (blk + 1) * 512)
                    for mt in range(8):
                        ps = mk.bank((blk * 8 + mt) % 4)
                        for kt in range(8):
                            self.mm(ps.t[:], wq.t[:, kt, mt * 128:(mt + 1) * 128], AT.t[:, kt, bs], kt == 0, kt == 7, [wq, AT], [ps])
                        self.copy("act" if mt % 2 else "dve", QX.t[:, mt, bs], ps.t[:], [ps], [QX])
                PTx = [[mk.sbuf(f"PTx{i}{m}", [128, 512], BF16) for m in range(2)] for i in range(2)]
                rden = [mk.sbuf(f"rden{i}", [128, 512], F32) for i in range(2)]
                it = 0
                for blk in range(4):
                    bs = slice(blk * 512, (blk + 1) * 512)
                    for h in range(4):
                        i = it % 2
                        it += 1
                        for mtile in range(2):
                            pS = mk.bank(2 * (it % 2) + mtile)
                            for j in range(2):
                                self.mm(pS.t[:], KmT.t[:, 2 * h + j, mtile * 128:(mtile + 1) * 128], QX.t[:, 2 * h + j, bs], j == 0, j == 1, [KmT, QX], [pS])
                            self.act(PTx[i][mtile].t[:], pS.t[:], AF.Exp, [pS], [PTx[i][mtile]], scale=1.0 / 16.0)
                        pD = mk.bank(4 + it % 2)
                        for mtile in range(2):
                            self.mm(pD.t[:], ones, PTx[i][mtile].t[:], mtile == 0, mtile == 1, [self.cbf, PTx[i][mtile]], [pD])
                        mk.op("dve", lambda e, i=i, pD=pD: e.reciprocal(rden[i].t[:], pD.t[:]), reads=[pD], writes=[rden[i]], cost=0.7)
                        for j in range(2):
                            pO = mk.bank(6 + j)
                            for mtile in range(2):
                                self.mm(pO.t[:], Vm.t[:, mtile, (2 * h + j) * 128:(2 * h + j + 1) * 128], PTx[i][mtile].t[:], mtile == 0, mtile == 1, [Vm, PTx[i][mtile]], [pO])
                            self.tt("dve", OX.t[:, 2 * h + j, bs], pO.t[:], rden[i].t[:], ALU.mult, [pO, rden[i]], [OX])
            self.dump("OX", OX, OX.t[:], [128, 8, HALF], BF16)
            self.proj_ln(OX, w_xo, None, 2, AT, res_d, ln_params, st, ones, NB=6)

    def ffn(self, w_f1, bf1_fm, w_f2, bf2_row, AT, res_d, ln_params, st, ones, out):
        mk = self.mk
        with mk.phase():
            acc = [mk.sbuf(f"acc{t}", [128, 1024], F32) for t in range(NT)]
            bf1 = mk.sbuf("bf1", [128, 32], F32)
            brow = mk.sbuf("brow2", [1, 1024], BF16)
            mk.dma("sp", [(bf1.t[:], bf1_fm.t.ap())], reads=[bf1_fm], writes=[bf1])
            mk.dma("pool", [(brow.t[:], bf2_row.t.ap())], reads=[bf2_row], writes=[brow])
            for t in range(NT):
                mk.dma("sp", [(acc[t].t[:], res_d[t].t.ap())], reads=[res_d[t]], writes=[acc[t]])
                mk.op("act", lambda e, t=t: e.mul(acc[t].t[:], acc[t].t[:], ALPHA), reads=[acc[t]], writes=[acc[t]])
            with mk.phase():
                W1 = [mk.sbuf(f"W1{i}", [128, 8, 512], BF16) for i in range(2)]
                W2 = [mk.sbuf(f"W2{i}", [128, 4, 1024], BF16) for i in range(2)]
                hid = [mk.sbuf(f"hid{i}", [128, 4, 512], BF16) for i in range(2)]
                tf_ = [mk.sbuf(f"tf{i}", [128, 512], F32) for i in range(2)]
                hi = 0
                oi = 0
                def ldw(c):
                    self.wload_into(W1[c % 2], w_f1, 1024, c * 512, 512)
                    mk.dma("pool", [(W2[c % 2].t[:], w_f2.t.ap()[c * 512:(c + 1) * 512, :].rearrange("(kt p) c -> p kt c", p=128))], reads=[w_f2], writes=[W2[c % 2]])

                ldw(0)
                for c in range(8):
                    w1 = W1[c % 2]; w2 = W2[c % 2]
                    if c + 1 < 8:
                        ldw(c + 1)
                    for blk in range(4):
                        bs = slice(blk * 512, (blk + 1) * 512)
                        hb = hid[hi % 2]
                        hi += 1
                        for ft in range(4):
                            pH = mk.bank(ft % 2)
                            for kt in range(8):
                                self.mm(pH.t[:], w1.t[:, kt, ft * 128:(ft + 1) * 128], AT.t[:, kt, bs], kt == 0, kt == 7, [w1, AT], [pH])
                            tb = tf_[ft % 2]
                            self.act(tb.t[:], pH.t[:], AF.Relu, [pH, bf1], [tb], bias=bf1.t[:, c * 4 + ft:c * 4 + ft + 1])
                            self.tt("pool", hb.t[:, ft, :], tb.t[:], tb.t[:], ALU.mult, [tb], [hb])
                        for tl in range(4):
                            T = blk * 4 + tl
                            for half in range(2):
                                hs = slice(half * 512, (half + 1) * 512)
                                pO = mk.bank(2 + oi % 6)
                                oi += 1
                                for ft in range(4):
                                    self.mm(pO.t[:], hb.t[:, ft, tl * 128:(tl + 1) * 128], w2.t[:, ft, hs], ft == 0, (ft == 3 and c != 0), [hb, w2], [pO])
                                if c == 0:
                                    self.mm(pO.t[:], ones[0:1, :], brow.t[0:1, hs], False, True, [self.cbf, brow], [pO])
                                self.tt("dve", acc[T].t[:, hs], acc[T].t[:, hs], pO.t[:], ALU.add, [acc[T], pO], [acc[T]])
            with mk.phase():
                gB = mk.sbuf("gB", [128, 1024], F32); bB = mk.sbuf("bB", [128, 1024], F32)
                ln_params(3, gB, bB)
                o32 = [mk.sbuf(f"o32{i}", [128, 1024], F32) for i in range(6)]
                for t in range(NT):
                    i = t % 6
                    self.layernorm(acc[t], gB, bB, o32[i], None, 0, None, None, st[t % 8])
                    mk.dma("sp", [(out.t.ap()[t * 128:(t + 1) * 128, :], o32[i].t[:])], reads=[o32[i]], writes=[out])

    def sin_of(self, out, ang, tmpi, tmpf, tmpm, eng="dve", out_ap=None):
        PI = float(np.pi)
        self.ts(eng, tmpi.t[:], ang.t[:], 1.0 / TWO_PI, None, ALU.mult, None, [ang], [tmpi])
        self.copy(eng, tmpf.t[:], tmpi.t[:], [tmpi], [tmpf])
        self.ts(eng, tmpf.t[:], tmpf.t[:], -TWO_PI, None, ALU.mult, None, [tmpf], [tmpf])
        self.tt(eng, tmpf.t[:], tmpf.t[:], ang.t[:], ALU.add, [tmpf, ang], [tmpf])
        self.ts(eng, tmpm.t[:], tmpf.t[:], PI, -TWO_PI, ALU.is_gt, ALU.mult, [tmpf], [tmpm])
        self.tt(eng, tmpf.t[:], tmpf.t[:], tmpm.t[:], ALU.add, [tmpf, tmpm], [tmpf])
        self.ts(eng, tmpm.t[:], tmpf.t[:], -PI, TWO_PI, ALU.is_lt, ALU.mult, [tmpf], [tmpm])
        self.tt(eng, tmpf.t[:], tmpf.t[:], tmpm.t[:], ALU.add, [tmpf, tmpm], [tmpf])
        self.ts(eng, tmpf.t[:], tmpf.t[:], 3.14159, -3.14159, ALU.min, ALU.max, [tmpf], [tmpf])
        self.act(out.t[:] if out_ap is None else out_ap, tmpf.t[:], AF.Sin, [tmpf], [out])

    def rope_tables(self, posb, COS, SIN):
        mk = self.mk
        cf = self.cst
        CH = 512
        pi_ = mk.sbuf("posi", [128, CH], I32)
        ang = mk.sbuf("ang", [128, CH], F32); sc = mk.sbuf("sc", [128, CH], F32)
        ti = mk.sbuf("ti", [128, CH], I32); tf = mk.sbuf("tf", [128, CH], F32); tm = mk.sbuf("tm", [128, CH], F32)
        s16 = mk.sbuf("s16", [128, CH], BF16); c16 = mk.sbuf("c16", [128, CH], BF16)
        mk.dma("sp", [(pi_.t[:], posb.t.ap())], reads=[posb], writes=[pi_])
        self.copy("dve", ang.t[:], pi_.t[:], [pi_], [ang])
        self.ts("dve", ang.t[:], ang.t[:], cf.t[:, 3:4], None, ALU.mult, None, [ang, cf], [ang])
        self.sin_of(sc, ang, ti, tf, tm)
        self.ts("dve", s16.t[:], sc.t[:], cf.t[:, 4:5], None, ALU.mult, None, [sc, cf], [s16])
        self.ts("dve", ang.t[:], ang.t[:], float(np.pi / 2), None, ALU.add, None, [ang], [ang])
        self.sin_of(c16, ang, ti, tf, tm)
        mk.dma("sp", [(SIN.t.ap().rearrange("q (c j) -> q c j", c=8)[:, c, :], s16.t[16 * c:16 * c + 16, :]) for c in range(8)], reads=[s16], writes=[SIN])
        mk.dma("sp", [(COS.t.ap().rearrange("q (c j) -> q c j", c=8)[:, c, :], c16.t[16 * c:16 * c + 16, :]) for c in range(8)], reads=[c16], writes=[COS])

    def s5_prep(self, s5p, s5C, tabs):
        mk = self.mk
        cf = self.cst
        P = mk.sbuf("P", [128, 3 * G], F32)
        CRI = mk.sbuf("CRI", [128, 2 * G * 16], F32)
        for (b_, s_) in ((P, s5p), (CRI, s5C)):
            mk.dma("sp", [(b_.t[:], s_.t.ap())], reads=[s_], writes=[b_])
        n2 = G * ND
        PR = mk.sbuf("PR", [128, G, ND], F32); PI_ = mk.sbuf("PI", [128, G, ND], F32)
        ZA = mk.sbuf("ZA", [128, G, 8], F32); ZB = mk.sbuf("ZB", [128, G, 8], F32)
        QA = mk.sbuf("QA", [128, G, 8], F32); QB = mk.sbuf("QB", [128, G, 8], F32)
        WR = mk.sbuf("WR", [128, G, 12, 2], F32)
        Cneg = mk.sbuf("Cneg", [128, G * 16], BF16)
        dt = mk.sbuf("dt", [128, G], F32); lam = mk.sbuf("lam", [128, G], F32); th = mk.sbuf("th", [128, G], F32)
        ANG = mk.sbuf("ANG", [128, n2], F32); LAM = mk.sbuf("LAMb", [128, n2], F32)
        SN = mk.sbuf("SN", [128, n2], F32); CS = mk.sbuf("CS", [128, n2], F32)
        ti = mk.sbuf("ti", [128, n2], I32); tf = mk.sbuf("tf", [128, n2], F32); tm = mk.sbuf("tm", [128, n2], F32)
        self.act(dt.t[:], P.t[:, 0:G], AF.Exp, [P], [dt])
        self.tt("dve", lam.t[:], P.t[:, G:2 * G], dt.t[:], ALU.mult, [P, dt], [lam])
        self.tt("dve", th.t[:], P.t[:, 2 * G:3 * G], dt.t[:], ALU.mult, [P, dt], [th])
        dp = apx(cf, 0, 128, 16, [(0, G), (1, ND)])
        self.tt("dve", ANG.t[:].rearrange("p (g k) -> p g k", g=G), apx(th, 0, 128, 0, [(1, G), (0, ND)]), dp, ALU.mult, [th, cf], [ANG])
        self.tt("dve", LAM.t[:].rearrange("p (g k) -> p g k", g=G), apx(lam, 0, 128, 0, [(1, G), (0, ND)]), dp, ALU.mult, [lam, cf], [LAM])
        self.act(LAM.t[:], LAM.t[:], AF.Exp, [LAM], [LAM])
        self.sin_of(SN, ANG, ti, tf, tm)
        self.ts("dve", ANG.t[:], ANG.t[:], float(np.pi / 2), None, ALU.add, None, [ANG], [ANG])
        self.sin_of(CS, ANG, ti, tf, tm)
        prf = PR.t[:].rearrange("p g k -> p (g k)"); pif = PI_.t[:].rearrange("p g k -> p (g k)")
        self.tt("dve", prf, LAM.t[:], CS.t[:], ALU.mult, [LAM, CS], [PR])
        self.tt("dve", pif, LAM.t[:], SN.t[:], ALU.mult, [LAM, SN], [PI_])
        nr = mk.sbuf("nr", [128, G], F32); den = mk.sbuf("den", [128, G], F32); t1 = mk.sbuf("t1", [128, G], F32)
        fr = mk.sbuf("fr", [128, G], F32); fi = mk.sbuf("fi", [128, G], F32)
        are = P.t[:, G:2 * G]; aim = P.t[:, 2 * G:3 * G]
        pr1 = apx(PR, 0, 128, 1, [(ND, G)]); pi1 = apx(PI_, 0, 128, 1, [(ND, G)])
        self.ts("dve", nr.t[:], pr1, -1.0, None, ALU.add, None, [PR], [nr])
        self.tt("dve", den.t[:], are, are, ALU.mult, [P], [den])
        self.tt("dve", t1.t[:], aim, aim, ALU.mult, [P], [t1])
        self.tt("dve", den.t[:], den.t[:], t1.t[:], ALU.add, [den, t1], [den])
        mk.op("dve", lambda e: e.reciprocal(den.t[:], den.t[:]), reads=[den], writes=[den])
        self.tt("dve", fr.t[:], nr.t[:], are, ALU.mult, [nr, P], [fr])
        self.tt("dve", t1.t[:], pi1, aim, ALU.mult, [PI_, P], [t1])
        self.tt("dve", fr.t[:], fr.t[:], t1.t[:], ALU.add, [fr, t1], [fr])
        self.tt("dve", fr.t[:], fr.t[:], den.t[:], ALU.mult, [fr, den], [fr])
        self.tt("dve", fi.t[:], pi1, are, ALU.mult, [PI_, P], [fi])
        self.tt("dve", t1.t[:], nr.t[:], aim, ALU.mult, [nr, P], [t1])
        self.tt("dve", fi.t[:], fi.t[:], t1.t[:], ALU.subtract, [fi, t1], [fi])
        self.tt("dve", fi.t[:], fi.t[:], den.t[:], ALU.mult, [fi, den], [fi])
        ZR = mk.sbuf("ZR", [128, G, 8], F32); ZI = mk.sbuf("ZI", [128, G, 8], F32); T8 = mk.sbuf("T8", [128, G, 8], F32)
        pr8 = apx(PR, 0, 128, 0, [(ND, G), (1, 8)]); pi8 = apx(PI_, 0, 128, 0, [(ND, G), (1, 8)])
        frb = apx(fr, 0, 128, 0, [(1, G), (0, 8)]); fib = apx(fi, 0, 128, 0, [(1, G), (0, 8)])
        self.tt("dve", ZR.t[:], pr8, frb, ALU.mult, [PR, fr], [ZR])
        self.tt("dve", T8.t[:], pi8, fib, ALU.mult, [PI_, fi], [T8])
        self.tt("dve", ZR.t[:], ZR.t[:], T8.t[:], ALU.subtract, [ZR, T8], [ZR])
        self.tt("dve", ZI.t[:], pr8, fib, ALU.mult, [PR, fi], [ZI])
        self.tt("dve", T8.t[:], pi8, frb, ALU.mult, [PI_, fr], [T8])
        self.tt("dve", ZI.t[:], ZI.t[:], T8.t[:], ALU.add, [ZI, T8], [ZI])
        U_, L_ = slice(0, 64), slice(64, 128)
        self.copy("dve", ZA.t[U_], ZR.t[U_], [ZR], [ZA]); self.copy("dve", ZA.t[L_], ZI.t[L_], [ZI], [ZA])
        self.ts("dve", ZB.t[U_], ZI.t[U_], -1.0, None, ALU.mult, None, [ZI], [ZB]); self.copy("dve", ZB.t[L_], ZR.t[L_], [ZR], [ZB])
        pr18 = lambda sl: apx(PR, sl.start, 64, 1, [(ND, G), (1, 8)])
        pi18 = lambda sl: apx(PI_, sl.start, 64, 1, [(ND, G), (1, 8)])
        self.copy("dve", QA.t[U_], pr18(U_), [PR], [QA]); self.ts("dve", QA.t[L_], pi18(L_), -1.0, None, ALU.mult, None, [PI_], [QA])
        self.ts("dve", QB.t[U_], pi18(U_), -1.0, None, ALU.mult, None, [PI_], [QB]); self.ts("dve", QB.t[L_], pr18(L_), -1.0, None, ALU.mult, None, [PR], [QB])
        wro = lambda sl, h: apx(WR, sl.start, 64, h, [(24, G), (2, 12)])
        prs = lambda sl: apx(PR, sl.start, 64, 8, [(ND, G), (1, 12)])
        pis = lambda sl: apx(PI_, sl.start, 64, 8, [(ND, G), (1, 12)])
        self.copy("dve", wro(U_, 0), prs(U_), [PR], [WR]); self.copy("dve", wro(U_, 1), pis(U_), [PI_], [WR])
        self.ts("dve", wro(L_, 0), pis(L_), -1.0, None, ALU.mult, None, [PI_], [WR]); self.copy("dve", wro(L_, 1), prs(L_), [PR], [WR])
        self.copy("dve", Cneg.t[U_], CRI.t[U_, 0:G * 16], [CRI], [Cneg])
        self.ts("dve", Cneg.t[L_], CRI.t[L_, G * 16:2 * G * 16], -1.0, None, ALU.mult, None, [CRI], [Cneg])

        for nm, b_ in (("ZA", ZA), ("ZB", ZB), ("QA", QA), ("QB", QB), ("WR", WR), ("Cneg", Cneg)):
            mk.dma("sp", [(tabs[nm].t.ap(), b_.t[:])], reads=[b_], writes=[tabs[nm]])

    def s5(self, w_in, bin_fm, s5B, s5C, s5d, tabs, flag, AT, HTp, gyd):
        self.pe_scale = 1.8
        try:
            self._s5(w_in, bin_fm, s5B, s5C, s5d, tabs, flag, AT, HTp, gyd)
        finally:
            self.pe_scale = 1.0

    def _s5(self, w_in, bin_fm, s5B, s5C, s5d, tabs, flag, AT, HTp, gyd):
        mk = self.mk
        cf = self.cst
        ident = self.ident
        with mk.phase():
            BRI = mk.sbuf("BRI", [128, 2 * G * 16], F32)
            CRI = mk.sbuf("CRI", [128, 2 * G * 16], F32)
            dd = mk.sbuf("dd", [128, 6], F32)
            binb = mk.sbuf("binb", [128, 40], F32)
            for (b_, s_) in ((BRI, s5B), (CRI, s5C), (dd, s5d), (binb, bin_fm)):
                mk.dma("sp", [(b_.t[:], s_.t.ap())], reads=[s_], writes=[b_])
            ZA = mk.sbuf("ZA", [128, G, 8], F32); ZB = mk.sbuf("ZB", [128, G, 8], F32)
            QA = mk.sbuf("QA", [128, G, 8], F32); QB = mk.sbuf("QB", [128, G, 8], F32)
            WR = mk.sbuf("WR", [128, G, 12, 2], F32)
            Cneg = mk.sbuf("Cneg", [128, G * 16], BF16)
            for nm, b_ in (("ZA", ZA), ("ZB", ZB), ("QA", QA), ("QB", QB), ("WR", WR), ("Cneg", Cneg)):
                mk.dma("sp", [(b_.t[:], tabs[nm].t.ap())], reads=[tabs[nm]], writes=[b_])

            wu = [mk.sbuf(f"wu{i}", [128, 8, 128], BF16) for i in range(2)]
            u_ = [mk.sbuf(f"u{i}", [128, SEQ], BF16) for i in range(2)]
            gys = [mk.sbuf(f"gys{i}", [128, HALF], BF16) for i in range(2)]
            Gf = mk.sbuf("Gf", [128, 1024], F32); Gf2 = mk.sbuf("Gf2", [128, 1024], F32)
            Gall = mk.sbuf("Gall", [128, 8, 128], BF16)
            Pm = mk.sbuf("Pm", [128, 8, 8, 128], BF16)
            Toep_ = [mk.sbuf(f"Toep{i}", [128, 8, 128], BF16) for i in range(2)]
            tK = mk.sbuf("tK", [128, 4, 128], F32); Dg = mk.sbuf("Dg", [128, 128], F32)
            Qp = mk.sbuf("Qp", [128, 8, 8, 64], BF16)
            qa = mk.sbuf("qa", [128, 256], F32); qb = mk.sbuf("qb", [128, 256], F32)
            NS = 4
            Rot = [mk.sbuf(f"Rot{i}", [128, 12, 128], BF16) for i in range(NS)]
            Xp_ = [mk.sbuf(f"Xp{i}", [128, 256], BF16) for i in range(NS)]
            Sp_ = [[mk.sbuf(f"Sp{k}{i}", [128, 64], BF16) for i in range(2)] for k in range(NS)]
            So_ = [[mk.sbuf(f"So{k}{i}", [128, 256], BF16) for i in range(2)] for k in range(NS)]
            Hext_ = [mk.sbuf(f"Hext{i}", [128, 8, 257], BF16) for i in range(2)]
            xs = mk.sbuf("xs", [128, 256], F32); x2 = mk.sbuf("x2", [128, 256], F32); sg = mk.sbuf("sg", [128, 256], F32)
            evq = [0]

            def evac(out, in_, reads, writes, scale=None):
                evq[0] += 1
                if scale is not None or evq[0] % 2 == 0:
                    if scale is not None:
                        self.act(out, in_, AF.Copy, reads, writes, scale=scale)
                    else:
                        self.copy("act", out, in_, reads, writes)
                else:
                    self.copy("dve", out, in_, reads, writes)

            for j in range(6):
                g0 = 8 * j
                w = wu[j % 2]
                u = u_[j % 2]; Toep = Toep_[j % 2]; Hext = Hext_[j % 2]; gyj = gys[j % 2]
                self.wload_into(w, w_in, 1024, 128 * j, 128)
                for blk in range(8):
                    src = HTp if blk < 4 else AT
                    c0 = (blk % 4) * 512
                    ps = mk.bank(2 + blk % 2)
                    for kt in range(8):
                        self.mm(ps.t[:], w.t[:, kt, :], src.t[:, kt, c0:c0 + 512], kt == 0, kt == 7, [w, src], [ps])
                    self.act(apx(u, 0, 128, (blk // 4) * HALF + (blk % 4) * 64, [(256, 8), (1, 64)]), apx(ps, 0, 128, 0, [(1, 8), (8, 64)]),
                             AF.Identity, [ps, binb], [u], bias=binb.t[:, j:j + 1])
                if j == 0:
                    self.dump("u0", u, u.t[:], [128, SEQ], BF16)
                za = apx(ZA, 0, 128, g0 * 8, [(1, 8), (8, 8), (0, 16)]); zb = apx(ZB, 0, 128, g0 * 8, [(1, 8), (8, 8), (0, 16)])
                br = apx(BRI, 0, 128, g0 * 16, [(0, 8), (16, 8), (1, 16)]); bi = apx(BRI, 0, 128, G * 16 + g0 * 16, [(0, 8), (16, 8), (1, 16)])
                gf4 = Gf.t[:].rearrange("p (d g c) -> p d g c", d=8, g=8); gf24 = Gf2.t[:].rearrange("p (d g c) -> p d g c", d=8, g=8)
                self.tt("dve", gf4, za, br, ALU.mult, [ZA, BRI], [Gf])
                self.tt("pool", gf24, zb, bi, ALU.mult, [ZB, BRI], [Gf2])
                self.tt("dve", Gall.t[:].rearrange("p d c -> p (d c)"), Gf.t[:], Gf2.t[:], ALU.add, [Gf, Gf2], [Gall])
                for hb in range(2):
                    pst = mk.bank(4)
                    pstb = pst.t[:].bitcast(BF16)
                    for dd_ in range(4):
                        d = hb * 4 + dd_
                        mk.op("pe", lambda e, d=d, dd_=dd_, pstb=pstb: e.transpose(pstb[:, dd_ * 128:(dd_ + 1) * 128], Gall.t[:, d, :], ident.t[:]),
                              reads=[Gall, ident], writes=[pst])
                    for gl in range(8):
                        o = apx(Pm, 0, 128, (hb * 4 * 8 + gl) * 128, [(8 * 128, 4), (1, 128)])
                        i_ = pstb[:, 0:512].rearrange("p (d c) -> p d c", d=4)
                        if gl % 2 == 0:
                            self.ts("dve", o, i_, cf.t[:, 8 + gl:9 + gl], None, ALU.mult, None, [pst, cf], [Pm])
                        else:
                            self.act(o, i_, AF.Copy, [pst, cf], [Pm], scale=cf.t[:, 8 + gl:9 + gl])
                    psk = mk.bank(5)
                    for dd_ in range(4):
                        d = hb * 4 + dd_
                        self.mm(psk.t[:, dd_ * 128:(dd_ + 1) * 128], Gall.t[:, d, :], Cneg.t[:, g0 * 16:g0 * 16 + 128], True, True, [Gall, Cneg], [psk])
                    self.tt("dve", tK.t[:], psk.t[:].rearrange("p (d c) -> p d c", d=4), apx(cf, 0, 128, 128, [(0, 4), (1, 128)]), ALU.mult, [psk, cf], [tK])
                    if hb == 0:
                        self.ts("dve", Dg.t[:], cf.t[:, 256:384], dd.t[:, j:j + 1], None, ALU.mult, None, [cf, dd], [Dg])
                        self.tt("dve", tK.t[:, 0, :], tK.t[:, 0, :], Dg.t[:], ALU.add, [tK, Dg], [tK])
                    self.copy("dve", Toep.t[:, hb * 4:(hb + 1) * 4, :], tK.t[:], [tK], [Toep])
                mk.op("pool", lambda e: e.memset(Qp.t[:], 0.0), reads=[], writes=[Qp])
                for q in range(4):
                    o = apx(Qp, 0, 128, q * 64 + 16 * q, [(512, 8), (256, 2), (1, 16)])
                    a0 = apx(QA, 0, 128, (g0 + q) * 8, [(1, 8), (32, 2), (0, 16)]); b0 = apx(QB, 0, 128, (g0 + q) * 8, [(1, 8), (32, 2), (0, 16)])
                    c0_ = apx(CRI, 0, 128, (g0 + q) * 16, [(0, 8), (64, 2), (1, 16)]); c1_ = apx(CRI, 0, 128, G * 16 + (g0 + q) * 16, [(0, 8), (64, 2), (1, 16)])
                    v3 = lambda b_: b_.t[:].rearrange("p (s k c) -> p s k c", s=8, k=2)
                    self.tt("dve", v3(qa), a0, c0_, ALU.mult, [QA, CRI], [qa])
                    self.tt("pool", v3(qb), b0, c1_, ALU.mult, [QB, CRI], [qb])
                    self.tt("dve", o, v3(qa), v3(qb), ALU.add, [qa, qb, Qp], [Qp])
                for gl in range(8):
                    g = g0 + gl
                    gp = gl % NS
                    R = Rot[gp]
                    Xp = Xp_[gp]; Sp = Sp_[gp]; So = So_[gp]
                    self.tt("pool", R.t[:].rearrange("p e (h c) -> p (e h) c", h=2), apx(cf, 0, 128, 384, [(0, 24), (1, 64)]),
                            apx(WR, 0, 128, g * 24, [(1, 24), (0, 64)]), ALU.mult, [cf, WR], [R])
                    ps = mk.bank(2 * gp)
                    for s in range(8):
                        self.mm(ps.t[:, 0:256], Pm.t[:, 7 - s, gl, :], apx(u, 0, 128, s * 256, [(1, 256)]), s == 0, s == 7, [Pm, u], [ps])
                    self.act(Xp.t[:], ps.t[:, 0:256], AF.Copy, [ps, flag], [Xp], scale=flag.t[:, 0:1])
                    S = Xp
                    for l in range(4):
                        N = 256 // 4 ** (l + 1)
                        ps2 = mk.bank(2 * gp + 1)
                        for e in range(4):
                            lhs = ident.t[:] if e == 0 else R.t[:, l * 3 + e - 1, :]
                            self.mm(ps2.t[:, 0:N], lhs, apx(S, 0, 128, 3 - e, [(4, N)]), e == 0, e == 3, [ident, R, S], [ps2])
                        if l < 3:
                            S2 = Sp[l % 2]
                            evac(S2.t[:, 0:N], ps2.t[:, 0:N], [ps2], [S2])
                            S = S2
                        else:
                            evac(Hext.t[:, gl, 0:1], ps2.t[:, 0:1], [ps2], [Hext])
                    ps = mk.bank(2 * gp)
                    for s in range(8):
                        self.mm(ps.t[:, 0:256], Pm.t[:, 7 - s, gl, :], apx(u, 0, 128, HALF + s * 256, [(1, 256)]), s == 0, False, [Pm, u], [ps])
                    self.mm(ps.t[:, 0:1], R.t[:, 0, :], Hext.t[:, gl, 0:1], False, True, [R, Hext], [ps])
                    S = So[0]
                    evac(S.t[:], ps.t[:, 0:256], [ps], [S])
                    for l in range(4):
                        d = 4 ** l
                        ps2 = mk.bank(2 * gp + 1)
                        self.mm(ps2.t[:, 0:256], ident.t[:], S.t[:, 0:256], True, False, [ident, S], [ps2])
                        for e in range(1, 4):
                            self.mm(ps2.t[:, e * d:256], R.t[:, l * 3 + e - 1, :], S.t[:, 0:256 - e * d], False, e == 3, [R, S], [ps2])
                        if l < 3:
                            S2 = So[(l + 1) % 2]
                            evac(S2.t[:], ps2.t[:, 0:256], [ps2], [S2])
                            S = S2
                        else:
                            evac(Hext.t[:, gl, 1:257], ps2.t[:, 0:256], [ps2], [Hext])
                if j == 0:
                    self.dump("Hext", Hext, Hext.t[:], [128, 8, 257], BF16)
                for s in range(8):
                    ps = mk.bank(2 + s % 2)
                    for gl in range(8):
                        hq = gl // 4
                        self.mm(ps.t[64 * hq:64 * hq + 64, 0:256], Qp.t[:, s, gl, :], Hext.t[:, gl, 0:256], gl % 4 == 0, False, [Qp, Hext], [ps])
                    for d in range(s + 1):
                        self.mm(ps.t[:, 0:256], Toep.t[:, d, :], apx(u, 0, 128, HALF + (s - d) * 256, [(1, 256)]), False, d == s, [Toep, u], [ps])
                    self.copy("act", xs.t[:], ps.t[:, 0:256], [ps], [xs])
                    self.tt("pool", x2.t[:], xs.t[:], xs.t[:], ALU.mult, [xs], [x2])
                    self.ts("pool", x2.t[:], x2.t[:], 2 * GELU_C * 0.044715, 2 * GELU_C, ALU.mult, ALU.add, [x2], [x2])
                    self.tt("pool", x2.t[:], x2.t[:], xs.t[:], ALU.mult, [x2, xs], [x2])
                    self.act(sg.t[:], x2.t[:], AF.Sigmoid, [x2], [sg])
                    self.tt("dve", apx(gyj, 0, 128, s, [(8, 256)]), xs.t[:], sg.t[:], ALU.mult, [xs, sg], [gyj])
                mk.dma("sp", [(gyd[j].t.ap(), gyj.t[:])], reads=[gyj], writes=[gyd[j]])

    def attention(self, w_in, w_sw, bin_fm, bsw_fm, bv_row, COS, SIN, flag, AT, HTp, attT, ones, maskpc):
        mk = self.mk
        cf = self.cst
        with mk.phase():
            binb = mk.sbuf("binb", [128, 40], F32); bswb = mk.sbuf("bswb", [128, 12], F32)
            bvb = mk.sbuf("bvb", [1, 768], BF16)
            mk.dma("sp", [(binb.t[:], bin_fm.t.ap())], reads=[bin_fm], writes=[binb])
            mk.dma("sp", [(bswb.t[:], bsw_fm.t.ap())], reads=[bsw_fm], writes=[bswb])
            mk.dma("pool", [(bvb.t[:], bv_row.t.ap())], reads=[bv_row], writes=[bvb])
            accN = mk.sbuf("accN", [128, 2, HALF], F32)
            accD = mk.sbuf("accD", [128, 2, HALF], F32)
            COSd, SINd = COS, SIN
            COS = mk.sbuf("COS", [128, SEQ], BF16); SIN = mk.sbuf("SIN", [128, SEQ], BF16)
            mk.op("pool", lambda e: e.memset(COS.t[:], 1.0), reads=[], writes=[COS], cost=5.0)
            mk.op("pool", lambda e: e.memset(SIN.t[:], 0.0), reads=[], writes=[SIN], cost=5.0)
            mk.dma("sp", [(COS.t[0:16, :], COSd.t.ap()), (COS.t[64:80, :], COSd.t.ap())], reads=[COSd], writes=[COS])
            mk.dma("sp", [(SIN.t[0:16, :], SINd.t.ap()), (SIN.t[64:80, :], SINd.t.ap())], reads=[SINd], writes=[SIN])
            wq = [mk.sbuf(f"wq{i}", [128, 8, 128], BF16) for i in range(2)]
            wqs = [mk.sbuf(f"wqs{i}", [128, 8, 128], BF16) for i in range(2)]
            wv = mk.sbuf("wv", [128, 8, 256], BF16)
            t1 = [mk.sbuf(f"rt1{i}", [128, 512], F32) for i in range(2)]
            t2 = [mk.sbuf(f"rt2{i}", [128, 512], F32) for i in range(2)]
            PT = [mk.sbuf(f"PT{i}", [128, 256], BF16) for i in range(3)]
            mask0 = mk.sbuf("mask0", [128, 256], BF16)
            self.copy("dve", mask0.t[:], maskpc, [self.cbf], [mask0])
            self.ts("dve", mask0.t[:, 0:128], mask0.t[:, 0:128], flag.t[:, 0:1], None, ALU.mult, None, [mask0, flag], [mask0])
            cnt = [0]
            wc = [0]

            def proj_rope(wa, wb, src, c0, bias_a, bias_b, tok0_tab, dst, oap, a0, n, dil):
                i = cnt[0] % 2
                cnt[0] += 1
                pa = mk.bank(2 * i); pb = mk.bank(2 * i + 1)
                for kt in range(8):
                    self.mm(pa.t[:], wa.t[:, kt, :], src.t[:, kt, c0:c0 + 512], kt == 0, kt == 7, [wa, src], [pa])
                for kt in range(8):
                    self.mm(pb.t[:], wb.t[:, kt, :], src.t[:, kt, c0:c0 + 512], kt == 0, kt == 7, [wb, src], [pb])
                self.stt(t1[i].t[:], pa.t[:], bias_a, COS.t[:, tok0_tab:tok0_tab + 512], ALU.add, ALU.mult, [pa, COS, binb, bswb], [t1[i]])
                self.stt(t2[i].t[:], pb.t[:], bias_b, SIN.t[:, tok0_tab:tok0_tab + 512], ALU.add, ALU.mult, [pb, SIN, binb, bswb], [t2[i]])
                if dil == 1:
                    i0 = t1[i].t[:, a0:a0 + n]; i1 = t2[i].t[:, a0:a0 + n]
                else:
                    i0 = apx(t1[i], 0, 128, a0, [(1, dil), (dil, n // dil)])
                    i1 = apx(t2[i], 0, 128, a0, [(1, dil), (dil, n // dil)])
                self.tt("pool", oap, i0, i1, ALU.add, [t1[i], t2[i]], [dst])

            def store_ap(dst, pt, L, dil, m0, n):
                if dil == 1:
                    return dst.t[:, pt, m0:m0 + n]
                return apx(dst, 0, 128, pt * dil * L + m0, [(L, dil), (1, n // dil)])

            it = 0
            vi = 0
            qbi = [0]
            for g in range(3):
                dil = DILS[g]
                Lq = HALF // dil
                Lr = 128 + HALF // dil
                nkb = 1 + 16 // dil
                nq = 16 // dil
                with mk.phase():
                    qT = mk.sbuf(f"qT{g}", [128, 2, HALF], BF16)
                    kT = mk.sbuf(f"kT{g}", [128, 2, dil * Lr], BF16)
                    V = mk.sbuf(f"V{g}", [128, dil * nkb, 256], BF16)
                    for pt in range(2):
                        mt = 2 * g + pt
                        wa = wq[wc[0] % 2]; wb = wqs[wc[0] % 2]; wc[0] += 1
                        self.wload_into(wa, w_in, 1024, 768 + 128 * mt, 128)
                        self.wload_into(wb, w_sw, 1024, 128 * mt, 128)
                        for blk in range(4):
                            oap = store_ap(qT, pt, Lq, dil, blk * 512 // dil, 512)
                            proj_rope(wa, wb, AT, blk * 512, binb.t[:, 6 + mt:7 + mt], bswb.t[:, mt:mt + 1], HALF + blk * 512,
                                      qT, oap, 0, 512, dil)
                        wa = wq[wc[0] % 2]; wb = wqs[wc[0] % 2]; wc[0] += 1
                        self.wload_into(wa, w_in, 1024, 1536 + 128 * mt, 128)
                        self.wload_into(wb, w_sw, 1024, 768 + 128 * mt, 128)
                        for blk in range(8):
                            prev = blk < 4
                            src = HTp if prev else AT
                            if prev:
                                lo = HALF - 128 * dil
                                b0 = blk * 512
                                if b0 + 512 <= lo:
                                    continue
                                a0 = max(lo, b0) - b0
                                n = 512 - a0
                                m0 = (b0 + a0 - lo) // dil
                            else:
                                a0, n = 0, 512
                                m0 = 128 + (blk - 4) * 512 // dil
                            oap = store_ap(kT, pt, Lr, dil, m0, n)
                            proj_rope(wa, wb, src, (blk % 4) * 512, binb.t[:, 12 + mt:13 + mt], bswb.t[:, 6 + mt:7 + mt], blk * 512,
                                      kT, oap, a0, n, dil)
                    self.wload_into(wv, w_in, 1024, 2304 + 256 * g, 256)
                    for r in range(dil):
                        for b in range(nkb):
                            if b == 0:
                                src = HTp; start = HALF - 128 * dil + r
                            else:
                                src = AT; start = dil * 128 * (b - 1) + r
                            ps = mk.bank(4 + vi % 2)
                            vi += 1
                            for kt in range(8):
                                lhs = apx(src, 0, 128, kt * HALF + start, [(dil, 128)])
                                self.mm(ps.t[:, 0:256], lhs, wv.t[:, kt, :], kt == 0, False, [src, wv], [ps])
                            self.mm(ps.t[:, 0:256], ones[0:1, :], bvb.t[0:1, 256 * g:256 * g + 256], False, True, [self.cbf, bvb], [ps])
                            dst = V.t[:, r * nkb + b, :]
                            if b == 0:
                                self.act(dst, ps.t[:, 0:256], AF.Copy, [ps, flag], [V], scale=flag.t[:, 0:1])
                            elif vi % 2:
                                self.copy("act", dst, ps.t[:, 0:256], [ps], [V])
                            else:
                                self.copy("dve", dst, ps.t[:, 0:256], [ps], [V])
                    if g == 1:
                        self.dump("qT1", qT, qT.t[:], [128, 2, HALF], BF16)
                        self.dump("kT1", kT, kT.t[:], [128, 2, dil * Lr], BF16)
                        self.dump("V1", V, V.t[:], [128, dil * nkb, 256], BF16)
                    for pt in range(2):
                        for r in range(dil):
                            for qb in range(nq):
                                pN = mk.bank(4 + 2 * (qbi[0] % 2)); pD = mk.bank(5 + 2 * (qbi[0] % 2)); qbi[0] += 1
                                for hp in range(2):
                                    rows = slice(64 * hp, 64 * hp + 64)
                                    pS = mk.bank(it % 3)
                                    P_ = PT[it % 3]
                                    it += 1
                                    qap = apx(qT, 64 * hp, 64, pt * HALF + r * Lq + qb * 128, [(1, 128)])
                                    for half in range(2):
                                        kap = apx(kT, 64 * hp, 64, pt * dil * Lr + r * Lr + (qb + half) * 128, [(1, 128)])
                                        self.mm(pS.t[:, half * 128:(half + 1) * 128], kap, qap, True, True, [kT, qT], [pS])
                                    self.act(P_.t[:], pS.t[:, 0:256], AF.Exp, [pS], [P_], scale=0.125)
                                    meng = "pool" if it % 2 else "dve"
                                    if qb == 0:
                                        self.tt(meng, P_.t[:], P_.t[:], mask0.t[:], ALU.mult, [P_, mask0], [P_])
                                    else:
                                        self.tt(meng, P_.t[:], P_.t[:], maskpc, ALU.mult, [P_, self.cbf], [P_])
                                    h = 2 * pt + hp
                                    for half in range(2):
                                        vb = V.t[:, r * nkb + qb + half, h * 64:(h + 1) * 64]
                                        self.mm(pN.t[rows, 0:128], vb, P_.t[:, half * 128:(half + 1) * 128], half == 0, half == 1, [V, P_], [pN])
                                    for half in range(2):
                                        self.mm(pD.t[rows, 0:128], ones[:, 0:64], P_.t[:, half * 128:(half + 1) * 128], half == 0, half == 1, [self.cbf, P_], [pD])
                                an = apx(accN, 0, 128, pt * HALF + dil * 128 * qb + r, [(dil, 128)])
                                ad = apx(accD, 0, 128, pt * HALF + dil * 128 * qb + r, [(dil, 128)])
                                if g == 0:
                                    self.copy("dve", an, pN.t[:, 0:128], [pN], [accN])
                                    self.copy("act", ad, pD.t[:, 0:128], [pD], [accD])
                                else:
                                    self.tt("dve", an, an, pN.t[:, 0:128], ALU.add, [pN, accN], [accN])
                                    self.tt("dve", ad, ad, pD.t[:, 0:128], ALU.add, [pD, accD], [accD])
            self.dump("accN", accN, accN.t[:], [128, 2, HALF])
            self.dump("accD", accD, accD.t[:], [128, 2, HALF])
            for pt in range(2):
                mk.op("dve", lambda e, pt=pt: e.reciprocal(accD.t[:, pt, :], accD.t[:, pt, :]), reads=[accD], writes=[accD], cost=2.4)
                self.tt("dve", attT.t[:, pt, :], accN.t[:, pt, :], accD.t[:, pt, :], ALU.mult, [accN, accD], [attT])


def _consts():
    bf = ml_dtypes.bfloat16
    c_bf = np.zeros((128, 128 * 3 + 256 + 64), np.float32)
    c_bf[:, 0:128] = np.eye(128)
    c_bf[:, 128:256] = 1.0
    ik = np.arange(128)[:, None]; iq = np.arange(128)[None, :]
    c_bf[:, 384:512] = (ik >= iq)
    c_bf[:, 512:640] = (ik <= iq)
    c_f = np.zeros((128, 512), np.float32)
    p = np.arange(128)
    c_f[:, 0] = EPS
    c_f[:, 1] = p < 64
    c_f[:, 2] = p >= 64
    q = p % 64
    invf = (500000.0 ** (-(2.0 * (q % 8)) / 16.0)).astype(np.float32)
    q16 = p % 16
    c_f[:, 3] = (500000.0 ** (-(2.0 * (q16 % 8)) / 16.0)).astype(np.float32)
    c_f[:, 4] = np.where(q16 < 8, -1.0, 1.0)
    c_f[:, 8:16] = (p[:, None] // 16 == np.arange(8)[None, :])
    c_f[:, 16:16 + ND] = np.asarray(DPOW, np.float32)[None, :]
    c_f[:, 128:256] = (p[:, None] // 16 == (np.arange(128)[None, :] // 16))
    c_f[:, 256:384] = np.eye(128)
    c_f[:, 384:448] = (np.arange(64)[None, :] == (p[:, None] % 64))
    return c_bf.astype(bf), c_f


def _prep_shared(inp):
    f = lambda a: np.ascontiguousarray(np.asarray(a, dtype=np.float32))
    w_in = f(inp["w_in"][0]); b_in = f(inp["b_in"][0])
    perm = np.arange(64)
    perm[0:8] = np.arange(8, 16); perm[8:16] = np.arange(0, 8)
    colperm = (np.arange(12)[:, None] * 64 + perm[None, :]).reshape(-1)
    w_sw = np.concatenate([w_in[:, 768 + colperm], w_in[:, 1536 + colperm]], axis=1)
    b_sw = np.concatenate([b_in[768 + colperm], b_in[1536 + colperm]])
    fm = lambda v: np.ascontiguousarray(v.reshape(-1, 128).T)
    rep2 = lambda a: np.concatenate([a, a], axis=0)
    logdt = f(inp["ssm_log_dt"][0]); are = f(inp["ssm_a_re"][0]); aim = f(inp["ssm_a_im"][0])
    s5p = np.concatenate([np.broadcast_to(logdt[None, :], (128, G)), rep2(are.T), rep2(aim.T)], axis=1)
    br = f(inp["ssm_b_re"][0]).transpose(1, 0, 2).reshape(64, G * 16)
    bi = f(inp["ssm_b_im"][0]).transpose(1, 0, 2).reshape(64, G * 16)
    cr = f(inp["ssm_c_re"][0]).transpose(2, 0, 1).reshape(64, G * 16)
    ci = f(inp["ssm_c_im"][0]).transpose(2, 0, 1).reshape(64, G * 16)
    c_bf, c_f = _consts()
    sh = {
        "w_in": w_in, "w_sw": f(w_sw), "bin_fm": fm(b_in), "bsw_fm": fm(b_sw), "bv_row": f(b_in[None, 2304:3072]),
        "ln_g": f(np.stack([inp["ln_in_g"], inp["ln1_g"][0], inp["ln2_g"][0], inp["ln3_g"][0]])),
        "ln_b": f(np.stack([inp["ln_in_b"], inp["ln1_b"][0], inp["ln2_b"][0], inp["ln3_b"][0]])),
        "s5p": f(s5p), "s5B": f(np.concatenate([rep2(br), rep2(bi)], axis=1)), "s5C": f(np.concatenate([rep2(cr), rep2(ci)], axis=1)),
        "s5d": fm(f(inp["ssm_d"][0])),
        "w_glu": f(inp["w_glu"][0]), "bglu_fm": fm(f(inp["b_glu"][0])),
        "w_au": f(inp["w_att_up"][0]), "w_mo": f(inp["w_mix_out"][0]), "bmo_row": f(inp["b_mix_out"][0][None, :]),
        "w_xq": f(inp["w_xq"][0]), "w_xkv": f(inp["w_xkv"][0]), "w_xo": f(inp["w_xo"][0]),
        "w_f1": f(inp["w_ff1"][0]), "bf1_fm": fm(f(inp["b_ff1"][0])), "w_f2": f(inp["w_ff2"][0]), "bf2_row": f(inp["b_ff2"][0][None, :]),
        "c_bf": c_bf, "c_f32": c_f,
    }
    return sh


def _core_inputs(inp, sh, b, half):
    x = np.asarray(inp["x"], np.float32); pos = np.asarray(inp["positions"], np.int32)
    d = dict(sh)
    d["x_own"] = np.ascontiguousarray(x[b, half * HALF:(half + 1) * HALF])
    if half == 0:
        d["x_prev"] = np.zeros((HALF, D_MODEL), np.float32)
        pp = np.concatenate([np.zeros(HALF, np.int32), pos[b, :HALF]])
    else:
        d["x_prev"] = np.ascontiguousarray(x[b, :HALF])
        pp = pos[b]
    d["posb"] = np.ascontiguousarray(np.broadcast_to(pp.reshape(8, 1, 512), (8, 16, 512)).reshape(128, 512))
    d["flag"] = np.full((128, 1), float(half), np.float32)
    d["mem"] = np.ascontiguousarray(np.asarray(inp["mem"], np.float32)[b])
    return d


_PROG = {}


def kernel(**inputs):
    if "k" not in _PROG:
        _PROG["k"] = K()
    k = _PROG["k"]
    sh = _prep_shared(inputs)
    in_maps = [_core_inputs(inputs, sh, c // 2, c % 2) for c in range(8)]
    res = run_bass_kernel_spmd(k.nc, in_maps, core_ids=list(range(8)))
    out = np.zeros((4, SEQ, D_MODEL), np.float32)
    for c in range(8):
        out[c // 2, (c % 2) * HALF:(c % 2 + 1) * HALF] = res.results[c]["out"]
    return out
```

```python
import numpy as np
import ml_dtypes
import concourse.bass as bass
import concourse.mybir as mybir
from concourse.bass_utils import run_bass_kernel_spmd

F32 = mybir.dt.float32
BF16 = mybir.dt.bfloat16
I32 = mybir.dt.int32
AF = mybir.ActivationFunctionType
ALU = mybir.AluOpType
_DSZ = {F32: 4, BF16: 2, I32: 4}


class Buf:
    __slots__ = ("name", "t", "writer", "readers", "dsem", "last_dma", "shape", "dtype", "kind")

    def __init__(self, name, t, shape=None, dtype=None, kind="sbuf"):
        self.name = name
        self.kind = kind
        self.t = t
        self.writer = None
        self.readers = []
        self.dsem = None
        self.last_dma = None
        self.shape = shape
        self.dtype = dtype


class Ev:
    __slots__ = ("eng", "fn", "waits", "order", "need_inc", "semval", "is_dma", "dsem", "dval", "ndma",
                 "cost", "lat", "idx", "t_end", "done", "nsucc", "fence")

    def __init__(self, eng, fn, cost=0.3):
        self.eng = eng
        self.fn = fn
        self.waits = []
        self.order = []
        self.need_inc = False
        self.semval = None
        self.is_dma = False
        self.dsem = None
        self.dval = None
        self.ndma = 0
        self.cost = cost
        self.lat = 0.0
        self.t_end = None
        self.fence = False


class _Phase:
    def __init__(self, mk):
        self.mk = mk

    def __enter__(self):
        self.mk._phase_stack.append(self.mk._sb_off)
        return self

    def __exit__(self, *a):
        self.mk.barrier()
        self.mk._sb_off = self.mk._phase_stack.pop()
        return False


class MK:
    ENGS = ("pe", "act", "dve", "pool", "sp")

    def __init__(self, nc, n_dma_sems=72, reorder=True):
        self.nc = nc
        self.reorder = reorder
        self.segs = [[]]
        self._sb_off = 0
        self._sb_max = 0
        self._phase_stack = []
        self._uid = 0
        self.n_dma_sems = n_dma_sems
        self.n_sw = 24
        self.dsem_free = [list(range(self.n_sw, n_dma_sems)), list(range(self.n_sw))]
        self.dsem_count = [0] * n_dma_sems
        self.dsem_bufs = []
        self.psum_banks = []
        self.sb_base = (int(nc.sbuf_base) + 63) // 64 * 64
        self.sb_limit = int(nc.sbuf_top) - self.sb_base - 1024
        for i in range(8):
            t = nc.alloc_psum_tensor(f"psb{i}", [128, 512], F32)
            self.psum_banks.append(Buf(f"psb{i}", t, [128, 512], F32, "psum"))

    def dram_in(self, name, shape, dtype):
        return Buf(name, self.nc.dram_tensor(name, list(shape), dtype, kind="ExternalInput"), shape, dtype, "dram")

    def dram_out(self, name, shape, dtype):
        return Buf(name, self.nc.dram_tensor(name, list(shape), dtype, kind="ExternalOutput"), shape, dtype, "dram")

    def dram_tmp(self, name, shape, dtype):
        return Buf(name, self.nc.dram_tensor(name, list(shape), dtype, kind="Internal"), shape, dtype, "dram")

    def sbuf(self, name, shape, dtype):
        nbytes = int(np.prod(shape[1:])) * _DSZ[dtype]
        nbytes = (nbytes + 63) // 64 * 64
        off = self._sb_off
        if off + nbytes > self.sb_limit:
            raise RuntimeError(f"SBUF overflow allocating {name}: {off}+{nbytes}")
        self._sb_off = off + nbytes
        self._sb_max = max(self._sb_max, self._sb_off)
        self._uid += 1
        t = self.nc.alloc_sbuf_tensor_at(f"{name}_{self._uid}", list(shape), dtype, offset=self.sb_base + off)
        return Buf(name, t, shape, dtype)

    def bank(self, i):
        return self.psum_banks[i]

    def view(self, name, t):
        return Buf(name, t)

    def phase(self):
        return _Phase(self)

    def _deps(self, ev, eng, reads, writes):
        deps = []
        for b in reads:
            if b.writer is not None:
                deps.append((b.writer, True))
        for b in writes:
            if b.writer is not None:
                deps.append((b.writer, False))
            for r in b.readers:
                deps.append((r, False))
        rawset = set(id(d) for d, raw in deps if raw)
        seen = set()
        for d, _ in deps:
            if d is ev or id(d) in seen:
                continue
            seen.add(id(d))
            same = (not d.is_dma) and (not ev.is_dma) and d.eng == eng
            if same and (eng == "pe" or id(d) not in rawset):
                ev.order.append(d)
            else:
                ev.waits.append(d)
        for b in reads:
            b.readers.append(ev)
        for b in writes:
            b.writer = ev
            b.readers = []

    def op(self, eng, fn, reads=(), writes=(), cost=0.3):
        ev = Ev(eng, fn, cost)
        self._deps(ev, eng, reads, writes)
        self.segs[-1].append(ev)
        return ev

    def dma(self, eng, pairs, reads=(), writes=(), sync=None, nbytes=None):
        if sync is None:
            for b in list(writes) + list(reads):
                if b.kind != "dram":
                    sync = b
                    break
        if sync is None:
            sync = (list(writes) + list(reads))[0]
        sw = 1 if eng == "pool" else 0
        if sync.dsem is None:
            sync.dsem = [None, None]
            sync.last_dma = [None, None]
            self.dsem_bufs.append(sync)
        if sync.dsem[sw] is None:
            if not self.dsem_free[sw]:
                raise RuntimeError("out of DMA semaphores")
            sync.dsem[sw] = self.dsem_free[sw].pop()
        ds = sync.dsem[sw]
        ev = Ev(eng, None, 0.6 if sw else 0.08)
        ev.is_dma = True
        ev.ndma = len(pairs)
        ev.dsem = ds
        self.dsem_count[ds] += 16 * len(pairs)
        ev.dval = self.dsem_count[ds]
        ev.fn = pairs
        if nbytes is None:
            nbytes = 0
            for (o, i) in pairs:
                try:
                    nbytes += int(o.partition_size) * int(o.free_size) * 4
                except Exception:
                    nbytes += 1 << 19
        ev.lat = 2.0 + nbytes / 150e3
        for ld in sync.last_dma:
            if ld is not None:
                ev.waits.append(ld)
        sync.last_dma[sw] = ev
        self._deps(ev, eng, reads, writes)
        self.segs[-1].append(ev)
        return ev

    def barrier(self):
        self.segs.append([])
        for b in self.dsem_bufs:
            b.dsem = None
            b.last_dma = None
        self.dsem_bufs = []
        self.dsem_free = [list(range(self.n_sw, self.n_dma_sems)), list(range(self.n_sw))]

    def _schedule(self, seg):
        ENGS = self.ENGS
        if not self.reorder:
            return {e: [ev for ev in seg if ev.eng == e] for e in ENGS}
        WIN = {"pe": 320, "act": 256, "dve": 384, "pool": 384, "sp": 64}
        SEM_LAT = 0.7
        pend = {e: [ev for ev in seg if ev.eng == e] for e in ENGS}
        out = {e: [] for e in ENGS}
        free_at = {e: 0.0 for e in ENGS}
        inseg = set(id(ev) for ev in seg)
        for ev in seg:
            ev.t_end = None
        n_left = len(seg)
        while n_left:
            best = None
            for e in ENGS:
                q = pend[e]
                lim = min(WIN[e], len(q))
                for k in range(lim):
                    ev = q[k]
                    rdy = 0.0
                    ok = True
                    for d in ev.waits:
                        if id(d) in inseg:
                            if d.t_end is None:
                                ok = False
                                break
                            t = d.t_end + SEM_LAT
                            if t > rdy:
                                rdy = t
                    if ok:
                        for d in ev.order:
                            if id(d) in inseg and d.t_end is None:
                                ok = False
                                break
                    if not ok:
                        continue
                    start = rdy if rdy > free_at[e] else free_at[e]
                    key = (start, k)
                    if best is None or key < best[0]:
                        best = (key, e, k, ev, start)
                    if start <= free_at[e]:
                        break
            if best is None:
                raise RuntimeError("scheduler deadlock")
            _, e, k, ev, start = best
            pend[e].pop(k)
            out[e].append(ev)
            free_at[e] = start + ev.cost
            ev.t_end = start + ev.cost + ev.lat
            n_left -= 1
        return out

    def finish(self):
        import contextlib
        nc = self.nc
        prog = {e: [] for e in self.ENGS}
        for seg in self.segs:
            if not seg:
                continue
            sch = self._schedule(seg)
            lastc = [sch[e][-1] for e in self.ENGS if sch[e] and not all(x.is_dma for x in sch[e])]
            lastc = []
            for e in self.ENGS:
                for ev in reversed(sch[e]):
                    if not ev.is_dma:
                        lastc.append(ev)
                        break
            dmas = {}
            for ev in seg:
                if ev.is_dma and (ev.dsem not in dmas or dmas[ev.dsem].dval < ev.dval):
                    dmas[ev.dsem] = ev
            for e in self.ENGS:
                prog[e].extend(sch[e])
                nop = Ev(e, lambda eng: eng.nop())
                nop.fence = True
                nop.waits = [d for d in lastc if d.eng != e] + list(dmas.values())
                prog[e].append(nop)
        for e in self.ENGS:
            for ev in prog[e]:
                for d in ev.waits:
                    if not d.is_dma:
                        d.need_inc = True
        self.prog = prog
        with contextlib.ExitStack() as es:
            esem = {e: es.enter_context(nc.semaphore(f"s_{e}")) for e in self.ENGS}
            dsems = [es.enter_context(nc.semaphore(f"d_{i}")) for i in range(self.n_dma_sems)]
            for e in self.ENGS:
                c = 0
                for ev in prog[e]:
                    if ev.need_inc and not ev.is_dma:
                        c += 1
                        ev.semval = c
            mk = self

            def replay(ename, eng):
                seen = {}
                for ev in prog[ename]:
                    for d in ev.waits:
                        if d.is_dma:
                            key, sem, val = ("d", d.dsem), dsems[d.dsem], d.dval
                        else:
                            key, sem, val = ("e", d.eng), esem[d.eng], d.semval
                        if seen.get(key, 0) < val:
                            eng.wait_ge(sem, val)
                            seen[key] = val
                    if ev.is_dma:
                        for (o, i) in ev.fn:
                            eng.dma_start(out=o, in_=i).then_inc(dsems[ev.dsem], 16)
                    else:
                        inst = ev.fn(eng)
                        if ev.need_inc:
                            inst.then_inc(esem[ename], 1)

            with nc.Block() as block:
                @block.tensor
                def _(eng):
                    replay("pe", eng)

                @block.scalar
                def _(eng):
                    replay("act", eng)

                @block.vector
                def _(eng):
                    replay("dve", eng)

                @block.gpsimd
                def _(eng):
                    replay("pool", eng)

                @block.sync
                def _(eng):
                    replay("sp", eng)

    def stats(self):
        return {e: len(self.prog[e]) for e in self.ENGS}, self._sb_max


def apx(buf, poff, npart, off, dims):
    t = buf.t
    shp = buf.shape
    rowlen = int(np.prod(shp[1:]))
    return bass.AP(t, poff * rowlen + off, [[rowlen, npart]] + [[int(s), int(c)] for (s, c) in dims])


D_MODEL = 1024
SEQ = 4096
HALF = 2048
NT = 16
G = 48
ALPHA = 2.0 ** 0.25
EPS = 1e-5
TWO_PI = float(2 * np.pi)
DPOW = [0, 1, 2, 3, 4, 5, 6, 7, 8, 16, 24, 32, 64, 96, 128, 256, 384, 512, 1024, 1536]
ND = len(DPOW)
SCAN_IDX = [[DPOW.index(8 * e * 4 ** l) for e in (1, 2, 3)] for l in range(4)]
DILS = (1, 4, 16)
GELU_C = float(np.sqrt(2.0 / np.pi))


class K:
    def __init__(self, dbg=None):
        self.dbg = dbg or ()
        nc = bass.Bass("TRN2", target_bir_lowering=False)
        self.nc = nc
        self.mk = MK(nc)
        self.din = {}
        self.build()

    def inp(self, name, shape, dtype=F32):
        b = self.mk.dram_in(name, shape, dtype)
        self.din[name] = b
        return b

    @staticmethod
    def _fs(ap):
        try:
            return float(ap.free_size)
        except Exception:
            return 512.0

    def mm(self, out, lhsT, rhs, start, stop, reads, writes):
        self.mk.op("pe", lambda e: e.matmul(out, lhsT, rhs, start=start, stop=stop), reads=reads, writes=writes,
                   cost=(0.035 + self._fs(out) / 2400.0) * getattr(self, 'pe_scale', 1.0))

    def act(self, out, in_, func, reads, writes, bias=None, scale=None, eng="act"):
        kw = {}
        if bias is not None:
            kw["bias"] = bias
        if scale is not None:
            kw["scale"] = scale
        self.mk.op("act", lambda e: e.activation(out, in_, func, **kw), reads=reads, writes=writes, cost=0.2 + self._fs(out) / 1100.0)

    def tt(self, eng, out, a, b, op, reads, writes):
        self.mk.op(eng, lambda e: e.tensor_tensor(out, a, b, op), reads=reads, writes=writes, cost=self._ec(eng, out))

    def ts(self, eng, out, a, s1, s2, op0, op1, reads, writes):
        if op1 is None:
            self.mk.op(eng, lambda e: e.tensor_scalar(out, a, s1, None, op0=op0), reads=reads, writes=writes, cost=self._ec(eng, out))
        else:
            self.mk.op(eng, lambda e: e.tensor_scalar(out, a, s1, s2, op0=op0, op1=op1), reads=reads, writes=writes, cost=self._ec(eng, out))

    def stt(self, out, a, s, b, op0, op1, reads, writes):
        self.mk.op("dve", lambda e: e.scalar_tensor_tensor(out, a, s, b, op0=op0, op1=op1), reads=reads, writes=writes, cost=self._ec("dve", out))

    def copy(self, eng, out, in_, reads, writes):
        if eng == "act":
            self.mk.op("act", lambda e: e.copy(out, in_), reads=reads, writes=writes, cost=0.2 + self._fs(out) / 1100.0)
        else:
            self.mk.op(eng, lambda e: e.tensor_copy(out, in_), reads=reads, writes=writes, cost=self._ec(eng, out))

    def _ec(self, eng, out):
        f = self._fs(out)
        return (0.25 + f / 550.0) if eng == "pool" else (0.12 + f / 900.0)

    def load(self, eng, dst, dst_ap, src, src_ap):
        self.mk.dma(eng, [(dst_ap, src_ap)], reads=[src], writes=[dst])

    def dump(self, name, buf, ap, shape, dtype=F32):
        if name in self.dbg:
            o = self.mk.dram_out("dbg_" + name, shape, dtype)
            self.mk.dma("sp", [(o.t.ap(), ap)], reads=[buf], writes=[o])

    def wload(self, name, src, rows, col0, ncols, eng="pool"):
        kt = rows // 128
        b = self.mk.sbuf(name, [128, kt, ncols], BF16)
        srcap = src.t.ap().rearrange("(kt p) c -> p kt c", p=128)[:, :, col0:col0 + ncols]
        self.mk.dma(eng, [(b.t[:], srcap)], reads=[src], writes=[b])
        return b

    def wload_into(self, b, src, rows, col0, ncols, eng="pool"):
        srcap = src.t.ap().rearrange("(kt p) c -> p kt c", p=128)[:, :, col0:col0 + ncols]
        self.mk.dma(eng, [(b.t[:], srcap)], reads=[src], writes=[b])

    def layernorm(self, r, gB, bB, out32, outT, tcol, tmp16, psT_bank, st):
        mk = self.mk
        ident = self.ident
        mk.op("dve", lambda e: e.bn_stats(st.t[:, 0:6], r.t[:, 0:512]), reads=[r], writes=[st], cost=0.7)
        mk.op("dve", lambda e: e.bn_stats(st.t[:, 6:12], r.t[:, 512:1024]), reads=[r], writes=[st], cost=0.6)
        mk.op("dve", lambda e: e.bn_aggr(st.t[:, 12:14], st.t[:, 0:12]), reads=[st], writes=[st], cost=0.21)
        self.act(st.t[:, 14:15], st.t[:, 13:14], AF.Sqrt, [st, self.cst], [st], bias=self.cst.t[:, 0:1], scale=1.0)
        mk.op("dve", lambda e: e.reciprocal(st.t[:, 15:16], st.t[:, 14:15]), reads=[st], writes=[st], cost=0.17)
        self.ts("dve", st.t[:, 16:17], st.t[:, 12:13], st.t[:, 15:16], -1.0, ALU.mult, ALU.mult, [st], [st])
        self.act(out32.t[:], r.t[:], AF.Identity, [st, r], [out32], bias=st.t[:, 16:17], scale=st.t[:, 15:16])
        self.tt("dve", out32.t[:], out32.t[:], gB.t[:], ALU.mult, [out32, gB], [out32])
        self.tt("pool", out32.t[:], out32.t[:], bB.t[:], ALU.add, [out32, bB], [out32])
        if outT is None:
            return
        self.copy("act", tmp16.t[:], out32.t[:], [out32], [tmp16])
        psb = psT_bank.t[:].bitcast(BF16)
        for kt in range(8):
            mk.op("pe", lambda e, kt=kt: e.transpose(psb[:, kt * 128:(kt + 1) * 128], tmp16.t[:, kt * 128:(kt + 1) * 128], ident.t[:]),
                  reads=[tmp16, ident], writes=[psT_bank], cost=0.09)
        dst = apx(outT[0], 0, 128, tcol, [(outT[1], 8), (1, 128)])
        src = psb.rearrange("p (k c) -> p k c", k=8)
        self.copy("act", dst, src, [psT_bank], [outT[0]])

    def build(self):
        mk = self.mk
        inp = self.inp
        x_own = inp("x_own", [HALF, D_MODEL]); x_prev = inp("x_prev", [HALF, D_MODEL])
        flag_d = inp("flag", [128, 1]); posb = inp("posb", [128, 512], I32)
        mem = inp("mem", [256, D_MODEL])
        w_in = inp("w_in", [1024, 5120]); w_sw = inp("w_sw", [1024, 1536])
        bin_fm = inp("bin_fm", [128, 40]); bsw_fm = inp("bsw_fm", [128, 12]); bv_row = inp("bv_row", [1, 768])
        ln_g = inp("ln_g", [4, 1024]); ln_b = inp("ln_b", [4, 1024])
        s5p = inp("s5p", [128, 3 * G])
        s5B = inp("s5B", [128, 2 * G * 16])
        s5C = inp("s5C", [128, 2 * G * 16])
        s5d = inp("s5d", [128, 6])
        w_glu = inp("w_glu", [768, 2048]); bglu_fm = inp("bglu_fm", [128, 16])
        w_au = inp("w_au", [256, 1024]); w_mo = inp("w_mo", [1024, 1024]); bmo_row = inp("bmo_row", [1, 1024])
        w_xq = inp("w_xq", [1024, 1024]); w_xkv = inp("w_xkv", [1024, 2048]); w_xo = inp("w_xo", [1024, 1024])
        w_f1 = inp("w_f1", [1024, 4096]); bf1_fm = inp("bf1_fm", [128, 32]); w_f2 = inp("w_f2", [4096, 1024])
        bf2_row = inp("bf2_row", [1, 1024])
        c_bf = inp("c_bf", [128, 128 * 3 + 256 + 64], BF16)
        c_f32 = inp("c_f32", [128, 512])
        out = mk.dram_out("out", [HALF, D_MODEL], F32)
        res_d = [mk.dram_tmp(f"res{t}", [128, D_MODEL], F32) for t in range(NT)]
        tabs = {nm: mk.dram_tmp("tab" + nm, [128, G * 8], F32) for nm in ("ZA", "ZB", "QA", "QB")}
        tabs["WR"] = mk.dram_tmp("tabWR", [128, G * 24], F32)
        tabs["Cneg"] = mk.dram_tmp("tabCneg", [128, G * 16], BF16)

        cbf = mk.sbuf("cbf", [128, 128 * 3 + 256 + 64], BF16)
        cf = mk.sbuf("cf", [128, 512], F32)
        self.cst = cf
        self.cbf = cbf
        flag = mk.sbuf("flag", [128, 1], F32)
        mk.dma("sp", [(cbf.t[:], c_bf.t.ap())], reads=[c_bf], writes=[cbf])
        mk.dma("sp", [(cf.t[:], c_f32.t.ap())], reads=[c_f32], writes=[cf])
        mk.dma("sp", [(flag.t[:], flag_d.t.ap())], reads=[flag_d], writes=[flag])
        ident = mk.view("ident", cbf.t[:, 0:128]); ident.writer = cbf.writer
        self.ident = ident
        ones = cbf.t[:, 128:256]
        maskpc = cbf.t[:, 384:640]
        AT = mk.sbuf("AT", [128, 8, HALF], BF16)
        ATp = (AT, HALF)
        st = [mk.sbuf(f"st{i}", [128, 32], F32) for i in range(8)]

        def ln_params(i, gB, bB):
            mk.dma("sp", [(gB.t[:], ln_g.t.ap()[i:i + 1, :].partition_broadcast(128).rearrange("p o c -> p (o c)"))], reads=[ln_g], writes=[gB])
            mk.dma("sp", [(bB.t[:], ln_b.t.ap()[i:i + 1, :].partition_broadcast(128).rearrange("p o c -> p (o c)"))], reads=[ln_b], writes=[bB])

        with mk.phase():
            attT = mk.sbuf("attT", [128, 2, HALF], BF16)
            HTp = mk.sbuf("HTp", [128, 8, HALF], BF16)
            COS = mk.dram_tmp("cosd", [16, SEQ], BF16); SIN = mk.dram_tmp("sind", [16, SEQ], BF16)
            with mk.phase():
                self.s5_prep(s5p, s5C, tabs)
                self.rope_tables(posb, COS, SIN)
                gB = mk.sbuf("gB", [128, 1024], F32); bB = mk.sbuf("bB", [128, 1024], F32)
                ln_params(0, gB, bB)
                NB = 4
                xt = [mk.sbuf(f"xt{i}", [128, 1024], F32) for i in range(NB)]
                o32 = [mk.sbuf(f"o32{i}", [128, 1024], F32) for i in range(NB)]
                t16 = [mk.sbuf(f"t16{i}", [128, 1024], BF16) for i in range(NB)]

                def ld(t):
                    own = t >= NT
                    tt_ = t - NT if own else t
                    src = x_own if own else x_prev
                    mk.dma("sp", [(xt[t % NB].t[:], src.t.ap()[tt_ * 128:(tt_ + 1) * 128, :])], reads=[src], writes=[xt[t % NB]])

                ld(0); ld(1)
                for t in range(2 * NT):
                    own = t >= NT
                    tt_ = t - NT if own else t
                    if t + 2 < 2 * NT:
                        ld(t + 2)
                    i = t % NB
                    self.layernorm(xt[i], gB, bB, o32[i], (AT if own else HTp, HALF), tt_ * 128, t16[i], mk.bank(t % 2), st[i])
                    if own:
                        mk.dma("sp", [(res_d[tt_].t.ap(), o32[i].t[:])], reads=[o32[i]], writes=[res_d[tt_]])
            self.dump("hT", AT, AT.t[:], [128, 8, HALF], BF16)
            self.attention(w_in, w_sw, bin_fm, bsw_fm, bv_row, COS, SIN, flag, AT, HTp, attT, ones, maskpc)
            self.dump("attT", attT, attT.t[:], [128, 2, HALF], BF16)
            gyd = [mk.dram_tmp(f"gyd{j}", [128, HALF], BF16) for j in range(6)]
            self.s5(w_in, bin_fm, s5B, s5C, s5d, tabs, flag, AT, HTp, gyd)
            gy = mk.sbuf("gy", [128, 6, HALF], BF16)
            mk.dma("sp", [(gy.t[:, j, :], gyd[j].t.ap()) for j in range(6)], reads=gyd, writes=[gy])
            MIX = mk.sbuf("MIX", [128, 8, HALF], BF16)
            self.phaseE(w_in, bin_fm, w_glu, bglu_fm, w_au, AT, gy, attT, MIX)
            self.dump("MIX", MIX, MIX.t[:], [128, 8, HALF], BF16)
            self.proj_ln(MIX, w_mo, bmo_row, 1, AT, res_d, ln_params, st, ones, NB=4)
        self.dump("h1T", AT, AT.t[:], [128, 8, HALF], BF16)
        self.xattn(mem, w_xq, w_xkv, w_xo, AT, res_d, ln_params, st, ones)
        self.dump("h2T", AT, AT.t[:], [128, 8, HALF], BF16)
        self.ffn(w_f1, bf1_fm, w_f2, bf2_row, AT, res_d, ln_params, st, ones, out)
        mk.finish()

    def phaseE(self, w_in, bin_fm, w_glu, bglu_fm, w_au, AT, gy, attT, MIX):
        mk = self.mk
        with mk.phase():
            wg1 = [mk.sbuf(f"wg1{i}", [128, 6, 128], BF16) for i in range(2)]
            wg2 = [mk.sbuf(f"wg2{i}", [128, 6, 128], BF16) for i in range(2)]
            wgs = [mk.sbuf(f"wgs{i}", [128, 8, 128], BF16) for i in range(2)]
            wga = [mk.sbuf(f"wga{i}", [128, 8, 128], BF16) for i in range(2)]
            wau = [mk.sbuf(f"wau{i}", [128, 2, 128], BF16) for i in range(2)]
            binb = mk.sbuf("binb", [128, 40], F32); bglu = mk.sbuf("bglu", [128, 16], F32)
            mk.dma("sp", [(binb.t[:], bin_fm.t.ap())], reads=[bin_fm], writes=[binb])
            mk.dma("sp", [(bglu.t[:], bglu_fm.t.ap())], reads=[bglu_fm], writes=[bglu])
            tmp = [[mk.sbuf(f"e{k}{i}", [128, 512], F32) for k in range(5)] for i in range(2)]
            it = 0
            def ldw(mt):
                self.wload_into(wg1[mt % 2], w_glu, 768, mt * 128, 128)
                self.wload_into(wg2[mt % 2], w_glu, 768, 1024 + mt * 128, 128)
                self.wload_into(wgs[mt % 2], w_in, 1024, 3072 + mt * 128, 128)
                self.wload_into(wga[mt % 2], w_in, 1024, 4096 + mt * 128, 128)
                self.wload_into(wau[mt % 2], w_au, 256, mt * 128, 128)

            ldw(0)
            for mt in range(8):
                w1, w2, w3, w4, w5 = wg1[mt % 2], wg2[mt % 2], wgs[mt % 2], wga[mt % 2], wau[mt % 2]
                if mt + 1 < 8:
                    ldw(mt + 1)
                for blk in range(4):
                    bs = slice(blk * 512, (blk + 1) * 512)
                    sg2, t1, sgs, sga, t2 = tmp[it % 2]
                    it += 1
                    bset = 4 * (it % 2)
                    pz1, pz2, pgs, pga = [mk.bank(bset + i) for i in range(4)]
                    pba = pz2
                    for kt in range(6):
                        self.mm(pz1.t[:], w1.t[:, kt, :], gy.t[:, kt, bs], kt == 0, kt == 5, [w1, gy], [pz1])
                    for kt in range(6):
                        self.mm(pz2.t[:], w2.t[:, kt, :], gy.t[:, kt, bs], kt == 0, kt == 5, [w2, gy], [pz2])
                    for kt in range(8):
                        self.mm(pgs.t[:], w3.t[:, kt, :], AT.t[:, kt, bs], kt == 0, kt == 7, [w3, AT], [pgs])
                    for kt in range(8):
                        self.mm(pga.t[:], w4.t[:, kt, :], AT.t[:, kt, bs], kt == 0, kt == 7, [w4, AT], [pga])
                    self.act(sg2.t[:], pz2.t[:], AF.Sigmoid, [pz2, bglu], [sg2], bias=bglu.t[:, 8 + mt:9 + mt])
                    for kt in range(2):
                        self.mm(pba.t[:], w5.t[:, kt, :], attT.t[:, kt, bs], kt == 0, kt == 1, [w5, attT], [pba])
                    self.stt(t1.t[:], pz1.t[:], bglu.t[:, mt:mt + 1], sg2.t[:], ALU.add, ALU.mult, [pz1, bglu, sg2], [t1])
                    self.act(sgs.t[:], pgs.t[:], AF.Sigmoid, [pgs, binb], [sgs], bias=binb.t[:, 24 + mt:25 + mt])
                    self.act(sga.t[:], pga.t[:], AF.Sigmoid, [pga, binb], [sga], bias=binb.t[:, 32 + mt:33 + mt])
                    self.tt("pool", t1.t[:], t1.t[:], sgs.t[:], ALU.mult, [t1, sgs], [t1])
                    self.tt("dve", t2.t[:], pba.t[:], sga.t[:], ALU.mult, [pba, sga], [t2])
                    self.tt("pool", MIX.t[:, mt, bs], t1.t[:], t2.t[:], ALU.add, [t1, t2], [MIX])

    def proj_ln(self, X, w, brow_d, ln_idx, AT, res_d, ln_params, st, ones, NB=3):
        mk = self.mk
        with mk.phase():
            W = self.wload("W", w, 1024, 0, 1024)
            gB = mk.sbuf("gB", [128, 1024], F32); bB = mk.sbuf("bB", [128, 1024], F32)
            ln_params(ln_idx, gB, bB)
            if brow_d is not None:
                brow = mk.sbuf("brow", [1, 1024], BF16)
                mk.dma("pool", [(brow.t[:], brow_d.t.ap())], reads=[brow_d], writes=[brow])
            rt = [mk.sbuf(f"rt{i}", [128, 1024], F32) for i in range(NB)]
            o32 = [mk.sbuf(f"o32{i}", [128, 1024], F32) for i in range(NB)]
            t16 = [mk.sbuf(f"t16{i}", [128, 1024], BF16) for i in range(NB)]

            def ld(t):
                mk.dma("sp", [(rt[t % NB].t[:], res_d[t].t.ap())], reads=[res_d[t]], writes=[rt[t % NB]])

            ld(0); ld(1)
            for t in range(NT):
                i = t % NB
                ts_ = slice(t * 128, (t + 1) * 128)
                if t + 2 < NT:
                    ld(t + 2)
                for half in range(2):
                    hs = slice(half * 512, (half + 1) * 512)
                    ps = mk.bank(2 * (t % 3) + half)
                    for kt in range(8):
                        self.mm(ps.t[:], X.t[:, kt, ts_], W.t[:, kt, hs], kt == 0, (kt == 7 and brow_d is None), [X, W], [ps])
                    if brow_d is not None:
                        self.mm(ps.t[:], ones[0:1, :], brow.t[0:1, hs], False, True, [self.cbf, brow], [ps])
                    self.stt(rt[i].t[:, hs], rt[i].t[:, hs], ALPHA, ps.t[:], ALU.mult, ALU.add, [rt[i], ps], [rt[i]])
                self.layernorm(rt[i], gB, bB, o32[i], (AT, HALF), t * 128, t16[i], mk.bank(6 + t % 2), st[t % 8])
                mk.dma("sp", [(res_d[t].t.ap(), o32[i].t[:])], reads=[o32[i]], writes=[res_d[t]])

    def xattn(self, mem, w_xq, w_xkv, w_xo, AT, res_d, ln_params, st, ones):
        mk = self.mk
        ident = self.ident
        with mk.phase():
            KmT = mk.sbuf("KmT", [128, 8, 256], BF16)
            Vm = mk.sbuf("Vm", [128, 2, 1024], BF16)
            OX = mk.sbuf("OX", [128, 8, HALF], BF16)
            with mk.phase():
                memT = mk.sbuf("memT", [128, 8, 256], BF16)
                mt32 = mk.sbuf("mt32", [128, 1024], F32); mt16 = mk.sbuf("mt16", [128, 1024], BF16)
                for mtile in range(2):
                    mk.dma("sp", [(mt32.t[:], mem.t.ap()[mtile * 128:(mtile + 1) * 128, :])], reads=[mem], writes=[mt32])
                    self.copy("dve", mt16.t[:], mt32.t[:], [mt32], [mt16])
                    pst = mk.bank(0)
                    psb = pst.t[:].bitcast(BF16)
                    for kt in range(8):
                        mk.op("pe", lambda e, kt=kt, psb=psb: e.transpose(psb[:, kt * 128:(kt + 1) * 128], mt16.t[:, kt * 128:(kt + 1) * 128], ident.t[:]),
                              reads=[mt16, ident], writes=[pst])
                    self.copy("act", apx(memT, 0, 128, mtile * 128, [(256, 8), (1, 128)]), psb.rearrange("p (k c) -> p k c", k=8), [pst], [memT])
                wk = self.wload("wk", w_xkv, 1024, 0, 1024)
                for mt in range(8):
                    ps = mk.bank(1 + mt % 2)
                    for kt in range(8):
                        self.mm(ps.t[:, 0:256], wk.t[:, kt, mt * 128:(mt + 1) * 128], memT.t[:, kt, :], kt == 0, kt == 7, [wk, memT], [ps])
                    self.copy("act" if mt % 2 else "dve", KmT.t[:, mt, :], ps.t[:, 0:256], [ps], [KmT])
                wv = self.wload("wvx", w_xkv, 1024, 1024, 1024)
                for mtile in range(2):
                    for half in range(2):
                        ps = mk.bank(3 + half)
                        for kt in range(8):
                            self.mm(ps.t[:], memT.t[:, kt, mtile * 128:(mtile + 1) * 128], wv.t[:, kt, half * 512:(half + 1) * 512], kt == 0, kt == 7, [memT, wv], [ps])
                        self.copy("act" if half else "dve", Vm.t[:, mtile, half * 512:(half + 1) * 512], ps.t[:], [ps], [Vm])
            with mk.phase():
                QX = mk.sbuf("QX", [128, 8, HALF], BF16)
                wq = self.wload("wxq", w_xq, 1024, 0, 1024)
                for blk in range(4):
                    bs = slice(blk * 512, (blk + 1) * 512)
                    for mt in range(8):
                        ps = mk.bank((blk * 8 + mt) % 4)
                        for kt in range(8):
                            self.mm(ps.t[:], wq.t[:, kt, mt * 128:(mt + 1) * 128], AT.t[:, kt, bs], kt == 0, kt == 7, [wq, AT], [ps])
                        self.copy("act" if mt % 2 else "dve", QX.t[:, mt, bs], ps.t[:], [ps], [QX])
                PTx = [[mk.sbuf(f"PTx{i}{m}", [128, 512], BF16) for m in range(2)] for i in range(2)]
                rden = [mk.sbuf(f"rden{i}", [128, 512], F32) for i in range(2)]
                it = 0
                for blk in range(4):
                    bs = slice(blk * 512, (blk + 1) * 512)
                    for h in range(4):
                        i = it % 2
                        it += 1
                        for mtile in range(2):
                            pS = mk.bank(2 * (it % 2) + mtile)
                            for j in range(2):
                                self.mm(pS.t[:], KmT.t[:, 2 * h + j, mtile * 128:(mtile + 1) * 128], QX.t[:, 2 * h + j, bs], j == 0, j == 1, [KmT, QX], [pS])
                            self.act(PTx[i][mtile].t[:], pS.t[:], AF.Exp, [pS], [PTx[i][mtile]], scale=1.0 / 16.0)
                        pD = mk.bank(4 + it % 2)
                        for mtile in range(2):
                            self.mm(pD.t[:], ones, PTx[i][mtile].t[:], mtile == 0, mtile == 1, [self.cbf, PTx[i][mtile]], [pD])
                        mk.op("dve", lambda e, i=i, pD=pD: e.reciprocal(rden[i].t[:], pD.t[:]), reads=[pD], writes=[rden[i]], cost=0.7)
                        for j in range(2):
                            pO = mk.bank(6 + j)
                            for mtile in range(2):
                                self.mm(pO.t[:], Vm.t[:, mtile, (2 * h + j) * 128:(2 * h + j + 1) * 128], PTx[i][mtile].t[:], mtile == 0, mtile == 1, [Vm, PTx[i][mtile]], [pO])
                            self.tt("dve", OX.t[:, 2 * h + j, bs], pO.t[:], rden[i].t[:], ALU.mult, [pO, rden[i]], [OX])
            self.dump("OX", OX, OX.t[:], [128, 8, HALF], BF16)
            self.proj_ln(OX, w_xo, None, 2, AT, res_d, ln_params, st, ones, NB=6)

    def ffn(self, w_f1, bf1_fm, w_f2, bf2_row, AT, res_d, ln_params, st, ones, out):
        mk = self.mk
        with mk.phase():
            acc = [mk.sbuf(f"acc{t}", [128, 1024], F32) for t in range(NT)]
            bf1 = mk.sbuf("bf1", [128, 32], F32)
            brow = mk.sbuf("brow2", [1, 1024], BF16)
            mk.dma("sp", [(bf1.t[:], bf1_fm.t.ap())], reads=[bf1_fm], writes=[bf1])
            mk.dma("pool", [(brow.t[:], bf2_row.t.ap())], reads=[bf2_row], writes=[brow])
            for t in range(NT):
                mk.dma("sp", [(acc[t].t[:], res_d[t].t.ap())], reads=[res_d[t]], writes=[acc[t]])
                mk.op("act", lambda e, t=t: e.mul(acc[t].t[:], acc[t].t[:], ALPHA), reads=[acc[t]], writes=[acc[t]])
            with mk.phase():
                W1 = [mk.sbuf(f"W1{i}", [128, 8, 512], BF16) for i in range(2)]
                W2 = [mk.sbuf(f"W2{i}", [128, 4, 1024], BF16) for i in range(2)]
                hid = [mk.sbuf(f"hid{i}", [128, 4, 512], BF16) for i in range(2)]
                tf_ = [mk.sbuf(f"tf{i}", [128, 512], F32) for i in range(2)]
                hi = 0
                oi = 0
                def ldw(c):
                    self.wload_into(W1[c % 2], w_f1, 1024, c * 512, 512)
                    mk.dma("pool", [(W2[c % 2].t[:], w_f2.t.ap()[c * 512:(c + 1) * 512, :].rearrange("(kt p) c -> p kt c", p=128))], reads=[w_f2], writes=[W2[c % 2]])

                ldw(0)
                for c in range(8):
                    w1 = W1[c % 2]; w2 = W2[c % 2]
                    if c + 1 < 8:
                        ldw(c + 1)
                    for blk in range(4):
                        bs = slice(blk * 512, (blk + 1) * 512)
                        hb = hid[hi % 2]
                        hi += 1
                        for ft in range(4):
                            pH = mk.bank(ft % 2)
                            for kt in range(8):
                                self.mm(pH.t[:], w1.t[:, kt, ft * 128:(ft + 1) * 128], AT.t[:, kt, bs], kt == 0, kt == 7, [w1, AT], [pH])
                            tb = tf_[ft % 2]
                            self.act(tb.t[:], pH.t[:], AF.Relu, [pH, bf1], [tb], bias=bf1.t[:, c * 4 + ft:c * 4 + ft + 1])
                            self.tt("pool", hb.t[:, ft, :], tb.t[:], tb.t[:], ALU.mult, [tb], [hb])
                        for tl in range(4):
                            T = blk * 4 + tl
                            for half in range(2):
                                hs = slice(half * 512, (half + 1) * 512)
                                pO = mk.bank(2 + oi % 6)
                                oi += 1
                                for ft in range(4):
                                    self.mm(pO.t[:], hb.t[:, ft, tl * 128:(tl + 1) * 128], w2.t[:, ft, hs], ft == 0, (ft == 3 and c != 0), [hb, w2], [pO])
                                if c == 0:
                                    self.mm(pO.t[:], ones[0:1, :], brow.t[0:1, hs], False, True, [self.cbf, brow], [pO])
                                self.tt("dve", acc[T].t[:, hs], acc[T].t[:, hs], pO.t[:], ALU.add, [acc[T], pO], [acc[T]])
            with mk.phase():
                gB = mk.sbuf("gB", [128, 1024], F32); bB = mk.sbuf("bB", [128, 1024], F32)
                ln_params(3, gB, bB)
                o32 = [mk.sbuf(f"o32{i}", [128, 1024], F32) for i in range(6)]
                for t in range(NT):
                    i = t % 6
                    self.layernorm(acc[t], gB, bB, o32[i], None, 0, None, None, st[t % 8])
                    mk.dma("sp", [(out.t.ap()[t * 128:(t + 1) * 128, :], o32[i].t[:])], reads=[o32[i]], writes=[out])

    def sin_of(self, out, ang, tmpi, tmpf, tmpm, eng="dve", out_ap=None):
        PI = float(np.pi)
        self.ts(eng, tmpi.t[:], ang.t[:], 1.0 / TWO_PI, None, ALU.mult, None, [ang], [tmpi])
        self.copy(eng, tmpf.t[:], tmpi.t[:], [tmpi], [tmpf])
        self.ts(eng, tmpf.t[:], tmpf.t[:], -TWO_PI, None, ALU.mult, None, [tmpf], [tmpf])
        self.tt(eng, tmpf.t[:], tmpf.t[:], ang.t[:], ALU.add, [tmpf, ang], [tmpf])
        self.ts(eng, tmpm.t[:], tmpf.t[:], PI, -TWO_PI, ALU.is_gt, ALU.mult, [tmpf], [tmpm])
        self.tt(eng, tmpf.t[:], tmpf.t[:], tmpm.t[:], ALU.add, [tmpf, tmpm], [tmpf])
        self.ts(eng, tmpm.t[:], tmpf.t[:], -PI, TWO_PI, ALU.is_lt, ALU.mult, [tmpf], [tmpm])
        self.tt(eng, tmpf.t[:], tmpf.t[:], tmpm.t[:], ALU.add, [tmpf, tmpm], [tmpf])
        self.ts(eng, tmpf.t[:], tmpf.t[:], 3.14159, -3.14159, ALU.min, ALU.max, [tmpf], [tmpf])
        self.act(out.t[:] if out_ap is None else out_ap, tmpf.t[:], AF.Sin, [tmpf], [out])

    def rope_tables(self, posb, COS, SIN):
        mk = self.mk
        cf = self.cst
        CH = 512
        pi_ = mk.sbuf("posi", [128, CH], I32)
        ang = mk.sbuf("ang", [128, CH], F32); sc = mk.sbuf("sc", [128, CH], F32)
        ti = mk.sbuf("ti", [128, CH], I32); tf = mk.sbuf("tf", [128, CH], F32); tm = mk.sbuf("tm", [128, CH], F32)
        s16 = mk.sbuf("s16", [128, CH], BF16); c16 = mk.sbuf("c16", [128, CH], BF16)
        mk.dma("sp", [(pi_.t[:], posb.t.ap())], reads=[posb], writes=[pi_])
        self.copy("dve", ang.t[:], pi_.t[:], [pi_], [ang])
        self.ts("dve", ang.t[:], ang.t[:], cf.t[:, 3:4], None, ALU.mult, None, [ang, cf], [ang])
        self.sin_of(sc, ang, ti, tf, tm)
        self.ts("dve", s16.t[:], sc.t[:], cf.t[:, 4:5], None, ALU.mult, None, [sc, cf], [s16])
        self.ts("dve", ang.t[:], ang.t[:], float(np.pi / 2), None, ALU.add, None, [ang], [ang])
        self.sin_of(c16, ang, ti, tf, tm)
        mk.dma("sp", [(SIN.t.ap().rearrange("q (c j) -> q c j", c=8)[:, c, :], s16.t[16 * c:16 * c + 16, :]) for c in range(8)], reads=[s16], writes=[SIN])
        mk.dma("sp", [(COS.t.ap().rearrange("q (c j) -> q c j", c=8)[:, c, :], c16.t[16 * c:16 * c + 16, :]) for c in range(8)], reads=[c16], writes=[COS])

    def s5_prep(self, s5p, s5C, tabs):
        mk = self.mk
        cf = self.cst
        P = mk.sbuf("P", [128, 3 * G], F32)
        CRI = mk.sbuf("CRI", [128, 2 * G * 16], F32)
        for (b_, s_) in ((P, s5p), (CRI, s5C)):
            mk.dma("sp", [(b_.t[:], s_.t.ap())], reads=[s_], writes=[b_])
        n2 = G * ND
        PR = mk.sbuf("PR", [128, G, ND], F32); PI_ = mk.sbuf("PI", [128, G, ND], F32)
        ZA = mk.sbuf("ZA", [128, G, 8], F32); ZB = mk.sbuf("ZB", [128, G, 8], F32)
        QA = mk.sbuf("QA", [128, G, 8], F32); QB = mk.sbuf("QB", [128, G, 8], F32)
        WR = mk.sbuf("WR", [128, G, 12, 2], F32)
        Cneg = mk.sbuf("Cneg", [128, G * 16], BF16)
        dt = mk.sbuf("dt", [128, G], F32); lam = mk.sbuf("lam", [128, G], F32); th = mk.sbuf("th", [128, G], F32)
        ANG = mk.sbuf("ANG", [128, n2], F32); LAM = mk.sbuf("LAMb", [128, n2], F32)
        SN = mk.sbuf("SN", [128, n2], F32); CS = mk.sbuf("CS", [128, n2], F32)
        ti = mk.sbuf("ti", [128, n2], I32); tf = mk.sbuf("tf", [128, n2], F32); tm = mk.sbuf("tm", [128, n2], F32)
        self.act(dt.t[:], P.t[:, 0:G], AF.Exp, [P], [dt])
        self.tt("dve", lam.t[:], P.t[:, G:2 * G], dt.t[:], ALU.mult, [P, dt], [lam])
        self.tt("dve", th.t[:], P.t[:, 2 * G:3 * G], dt.t[:], ALU.mult, [P, dt], [th])
        dp = apx(cf, 0, 128, 16, [(0, G), (1, ND)])
        self.tt("dve", ANG.t[:].rearrange("p (g k) -> p g k", g=G), apx(th, 0, 128, 0, [(1, G), (0, ND)]), dp, ALU.mult, [th, cf], [ANG])
        self.tt("dve", LAM.t[:].rearrange("p (g k) -> p g k", g=G), apx(lam, 0, 128, 0, [(1, G), (0, ND)]), dp, ALU.mult, [lam, cf], [LAM])
        self.act(LAM.t[:], LAM.t[:], AF.Exp, [LAM], [LAM])
        self.sin_of(SN, ANG, ti, tf, tm)
        self.ts("dve", ANG.t[:], ANG.t[:], float(np.pi / 2), None, ALU.add, None, [ANG], [ANG])
        self.sin_of(CS, ANG, ti, tf, tm)
        prf = PR.t[:].rearrange("p g k -> p (g k)"); pif = PI_.t[:].rearrange("p g k -> p (g k)")
        self.tt("dve", prf, LAM.t[:], CS.t[:], ALU.mult, [LAM, CS], [PR])
        self.tt("dve", pif, LAM.t[:], SN.t[:], ALU.mult, [LAM, SN], [PI_])
        nr = mk.sbuf("nr", [128, G], F32); den = mk.sbuf("den", [128, G], F32); t1 = mk.sbuf("t1", [128, G], F32)
        fr = mk.sbuf("fr", [128, G], F32); fi = mk.sbuf("fi", [128, G], F32)
        are = P.t[:, G:2 * G]; aim = P.t[:, 2 * G:3 * G]
        pr1 = apx(PR, 0, 128, 1, [(ND, G)]); pi1 = apx(PI_, 0, 128, 1, [(ND, G)])
        self.ts("dve", nr.t[:], pr1, -1.0, None, ALU.add, None, [PR], [nr])
        self.tt("dve", den.t[:], are, are, ALU.mult, [P], [den])
        self.tt("dve", t1.t[:], aim, aim, ALU.mult, [P], [t1])
        self.tt("dve", den.t[:], den.t[:], t1.t[:], ALU.add, [den, t1], [den])
        mk.op("dve", lambda e: e.reciprocal(den.t[:], den.t[:]), reads=[den], writes=[den])
        self.tt("dve", fr.t[:], nr.t[:], are, ALU.mult, [nr, P], [fr])
        self.tt("dve", t1.t[:], pi1, aim, ALU.mult, [PI_, P], [t1])
        self.tt("dve", fr.t[:], fr.t[:], t1.t[:], ALU.add, [fr, t1], [fr])
        self.tt("dve", fr.t[:], fr.t[:], den.t[:], ALU.mult, [fr, den], [fr])
        self.tt("dve", fi.t[:], pi1, are, ALU.mult, [PI_, P], [fi])
        self.tt("dve", t1.t[:], nr.t[:], aim, ALU.mult, [nr, P], [t1])
        self.tt("dve", fi.t[:], fi.t[:], t1.t[:], ALU.subtract, [fi, t1], [fi])
        self.tt("dve", fi.t[:], fi.t[:], den.t[:], ALU.mult, [fi, den], [fi])
        ZR = mk.sbuf("ZR", [128, G, 8], F32); ZI = mk.sbuf("ZI", [128, G, 8], F32); T8 = mk.sbuf("T8", [128, G, 8], F32)
        pr8 = apx(PR, 0, 128, 0, [(ND, G), (1, 8)]); pi8 = apx(PI_, 0, 128, 0, [(ND, G), (1, 8)])
        frb = apx(fr, 0, 128, 0, [(1, G), (0, 8)]); fib = apx(fi, 0, 128, 0, [(1, G), (0, 8)])
        self.tt("dve", ZR.t[:], pr8, frb, ALU.mult, [PR, fr], [ZR])
        self.tt("dve", T8.t[:], pi8, fib, ALU.mult, [PI_, fi], [T8])
        self.tt("dve", ZR.t[:], ZR.t[:], T8.t[:], ALU.subtract, [ZR, T8], [ZR])
        self.tt("dve", ZI.t[:], pr8, fib, ALU.mult, [PR, fi], [ZI])
        self.tt("dve", T8.t[:], pi8, frb, ALU.mult, [PI_, fr], [T8])
        self.tt("dve", ZI.t[:], ZI.t[:], T8.t[:], ALU.add, [ZI, T8], [ZI])
        U_, L_ = slice(0, 64), slice(64, 128)
        self.copy("dve", ZA.t[U_], ZR.t[U_], [ZR], [ZA]); self.copy("dve", ZA.t[L_], ZI.t[L_], [ZI], [ZA])
        self.ts("dve", ZB.t[U_], ZI.t[U_], -1.0, None, ALU.mult, None, [ZI], [ZB]); self.copy("dve", ZB.t[L_], ZR.t[L_], [ZR], [ZB])
        pr18 = lambda sl: apx(PR, sl.start, 64, 1, [(ND, G), (1, 8)])
        pi18 = lambda sl: apx(PI_, sl.start, 64, 1, [(ND, G), (1, 8)])
        self.copy("dve", QA.t[U_], pr18(U_), [PR], [QA]); self.ts("dve", QA.t[L_], pi18(L_), -1.0, None, ALU.mult, None, [PI_], [QA])
        self.ts("dve", QB.t[U_], pi18(U_), -1.0, None, ALU.mult, None, [PI_], [QB]); self.ts("dve", QB.t[L_], pr18(L_), -1.0, None, ALU.mult, None, [PR], [QB])
        wro = lambda sl, h: apx(WR, sl.start, 64, h, [(24, G), (2, 12)])
        prs = lambda sl: apx(PR, sl.start, 64, 8, [(ND, G), (1, 12)])
        pis = lambda sl: apx(PI_, sl.start, 64, 8, [(ND, G), (1, 12)])
        self.copy("dve", wro(U_, 0), prs(U_), [PR], [WR]); self.copy("dve", wro(U_, 1), pis(U_), [PI_], [WR])
        self.ts("dve", wro(L_, 0), pis(L_), -1.0, None, ALU.mult, None, [PI_], [WR]); self.copy("dve", wro(L_, 1), prs(L_), [PR], [WR])
        self.copy("dve", Cneg.t[U_], CRI.t[U_, 0:G * 16], [CRI], [Cneg])
        self.ts("dve", Cneg.t[L_], CRI.t[L_, G * 16:2 * G * 16], -1.0, None, ALU.mult, None, [CRI], [Cneg])

        for nm, b_ in (("ZA", ZA), ("ZB", ZB), ("QA", QA), ("QB", QB), ("WR", WR), ("Cneg", Cneg)):
            mk.dma("sp", [(tabs[nm].t.ap(), b_.t[:])], reads=[b_], writes=[tabs[nm]])

    def s5(self, w_in, bin_fm, s5B, s5C, s5d, tabs, flag, AT, HTp, gyd):
        self.pe_scale = 2.3
        try:
            self._s5(w_in, bin_fm, s5B, s5C, s5d, tabs, flag, AT, HTp, gyd)
        finally:
            self.pe_scale = 1.0

    def _s5(self, w_in, bin_fm, s5B, s5C, s5d, tabs, flag, AT, HTp, gyd):
        mk = self.mk
        cf = self.cst
        ident = self.ident
        with mk.phase():
            BRI = mk.sbuf("BRI", [128, 2 * G * 16], F32)
            CRI = mk.sbuf("CRI", [128, 2 * G * 16], F32)
            dd = mk.sbuf("dd", [128, 6], F32)
            binb = mk.sbuf("binb", [128, 40], F32)
            for (b_, s_) in ((BRI, s5B), (CRI, s5C), (dd, s5d), (binb, bin_fm)):
                mk.dma("sp", [(b_.t[:], s_.t.ap())], reads=[s_], writes=[b_])
            ZA = mk.sbuf("ZA", [128, G, 8], F32); ZB = mk.sbuf("ZB", [128, G, 8], F32)
            QA = mk.sbuf("QA", [128, G, 8], F32); QB = mk.sbuf("QB", [128, G, 8], F32)
            WR = mk.sbuf("WR", [128, G, 12, 2], F32)
            Cneg = mk.sbuf("Cneg", [128, G * 16], BF16)
            for nm, b_ in (("ZA", ZA), ("ZB", ZB), ("QA", QA), ("QB", QB), ("WR", WR), ("Cneg", Cneg)):
                mk.dma("sp", [(b_.t[:], tabs[nm].t.ap())], reads=[tabs[nm]], writes=[b_])

            wu = [mk.sbuf(f"wu{i}", [128, 8, 128], BF16) for i in range(2)]
            u_ = [mk.sbuf(f"u{i}", [128, SEQ], BF16) for i in range(2)]
            gys = [mk.sbuf(f"gys{i}", [128, HALF], BF16) for i in range(2)]
            Gf = mk.sbuf("Gf", [128, 1024], F32); Gf2 = mk.sbuf("Gf2", [128, 1024], F32)
            Gall = mk.sbuf("Gall", [128, 8, 128], BF16)
            Pm = mk.sbuf("Pm", [128, 8, 8, 128], BF16)
            Toep_ = [mk.sbuf(f"Toep{i}", [128, 8, 128], BF16) for i in range(2)]
            tK = mk.sbuf("tK", [128, 4, 128], F32); Dg = mk.sbuf("Dg", [128, 128], F32)
            Qp = mk.sbuf("Qp", [128, 8, 8, 64], BF16)
            qa = mk.sbuf("qa", [128, 256], F32); qb = mk.sbuf("qb", [128, 256], F32)
            NS = 4
            Rot = [mk.sbuf(f"Rot{i}", [128, 12, 128], BF16) for i in range(NS)]
            Xp_ = [mk.sbuf(f"Xp{i}", [128, 256], BF16) for i in range(NS)]
            Sp_ = [[mk.sbuf(f"Sp{k}{i}", [128, 64], BF16) for i in range(2)] for k in range(NS)]
            So_ = [[mk.sbuf(f"So{k}{i}", [128, 256], BF16) for i in range(2)] for k in range(NS)]
            Hext_ = [mk.sbuf(f"Hext{i}", [128, 8, 257], BF16) for i in range(2)]
            xs = mk.sbuf("xs", [128, 256], F32); x2 = mk.sbuf("x2", [128, 256], F32); sg = mk.sbuf("sg", [128, 256], F32)
            evq = [0]

            def evac(out, in_, reads, writes, scale=None):
                evq[0] += 1
                if scale is not None or evq[0] % 2 == 0:
                    if scale is not None:
                        self.act(out, in_, AF.Copy, reads, writes, scale=scale)
                    else:
                        self.copy("act", out, in_, reads, writes)
                else:
                    self.copy("dve", out, in_, reads, writes)

            for j in range(6):
                g0 = 8 * j
                w = wu[j % 2]
                u = u_[j % 2]; Toep = Toep_[j % 2]; Hext = Hext_[j % 2]; gyj = gys[j % 2]
                self.wload_into(w, w_in, 1024, 128 * j, 128)
                for blk in range(8):
                    src = HTp if blk < 4 else AT
                    c0 = (blk % 4) * 512
                    ps = mk.bank(2 + blk % 2)
                    for kt in range(8):
                        self.mm(ps.t[:], w.t[:, kt, :], src.t[:, kt, c0:c0 + 512], kt == 0, kt == 7, [w, src], [ps])
                    self.act(apx(u, 0, 128, (blk // 4) * HALF + (blk % 4) * 64, [(256, 8), (1, 64)]), apx(ps, 0, 128, 0, [(1, 8), (8, 64)]),
                             AF.Identity, [ps, binb], [u], bias=binb.t[:, j:j + 1])
                if j == 0:
                    self.dump("u0", u, u.t[:], [128, SEQ], BF16)
                za = apx(ZA, 0, 128, g0 * 8, [(1, 8), (8, 8), (0, 16)]); zb = apx(ZB, 0, 128, g0 * 8, [(1, 8), (8, 8), (0, 16)])
                br = apx(BRI, 0, 128, g0 * 16, [(0, 8), (16, 8), (1, 16)]); bi = apx(BRI, 0, 128, G * 16 + g0 * 16, [(0, 8), (16, 8), (1, 16)])
                gf4 = Gf.t[:].rearrange("p (d g c) -> p d g c", d=8, g=8); gf24 = Gf2.t[:].rearrange("p (d g c) -> p d g c", d=8, g=8)
                self.tt("dve", gf4, za, br, ALU.mult, [ZA, BRI], [Gf])
                self.tt("pool", gf24, zb, bi, ALU.mult, [ZB, BRI], [Gf2])
                self.tt("dve", Gall.t[:].rearrange("p d c -> p (d c)"), Gf.t[:], Gf2.t[:], ALU.add, [Gf, Gf2], [Gall])
                for hb in range(2):
                    pst = mk.bank(4)
                    pstb = pst.t[:].bitcast(BF16)
                    for dd_ in range(4):
                        d = hb * 4 + dd_
                        mk.op("pe", lambda e, d=d, dd_=dd_, pstb=pstb: e.transpose(pstb[:, dd_ * 128:(dd_ + 1) * 128], Gall.t[:, d, :], ident.t[:]),
                              reads=[Gall, ident], writes=[pst])
                    for gl in range(8):
                        o = apx(Pm, 0, 128, (hb * 4 * 8 + gl) * 128, [(8 * 128, 4), (1, 128)])
                        i_ = pstb[:, 0:512].rearrange("p (d c) -> p d c", d=4)
                        if gl % 2 == 0:
                            self.ts("dve", o, i_, cf.t[:, 8 + gl:9 + gl], None, ALU.mult, None, [pst, cf], [Pm])
                        else:
                            self.act(o, i_, AF.Copy, [pst, cf], [Pm], scale=cf.t[:, 8 + gl:9 + gl])
                    psk = mk.bank(5)
                    for dd_ in range(4):
                        d = hb * 4 + dd_
                        self.mm(psk.t[:, dd_ * 128:(dd_ + 1) * 128], Gall.t[:, d, :], Cneg.t[:, g0 * 16:g0 * 16 + 128], True, True, [Gall, Cneg], [psk])
                    self.tt("dve", tK.t[:], psk.t[:].rearrange("p (d c) -> p d c", d=4), apx(cf, 0, 128, 128, [(0, 4), (1, 128)]), ALU.mult, [psk, cf], [tK])
                    if hb == 0:
                        self.ts("dve", Dg.t[:], cf.t[:, 256:384], dd.t[:, j:j + 1], None, ALU.mult, None, [cf, dd], [Dg])
                        self.tt("dve", tK.t[:, 0, :], tK.t[:, 0, :], Dg.t[:], ALU.add, [tK, Dg], [tK])
                    self.copy("dve", Toep.t[:, hb * 4:(hb + 1) * 4, :], tK.t[:], [tK], [Toep])
                mk.op("pool", lambda e: e.memset(Qp.t[:], 0.0), reads=[], writes=[Qp])
                for q in range(4):
                    o = apx(Qp, 0, 128, q * 64 + 16 * q, [(512, 8), (256, 2), (1, 16)])
                    a0 = apx(QA, 0, 128, (g0 + q) * 8, [(1, 8), (32, 2), (0, 16)]); b0 = apx(QB, 0, 128, (g0 + q) * 8, [(1, 8), (32, 2), (0, 16)])
                    c0_ = apx(CRI, 0, 128, (g0 + q) * 16, [(0, 8), (64, 2), (1, 16)]); c1_ = apx(CRI, 0, 128, G * 16 + (g0 + q) * 16, [(0, 8), (64, 2), (1, 16)])
                    v3 = lambda b_: b_.t[:].rearrange("p (s k c) -> p s k c", s=8, k=2)
                    self.tt("dve", v3(qa), a0, c0_, ALU.mult, [QA, CRI], [qa])
                    self.tt("pool", v3(qb), b0, c1_, ALU.mult, [QB, CRI], [qb])
                    self.tt("dve", o, v3(qa), v3(qb), ALU.add, [qa, qb, Qp], [Qp])
                for gl in range(8):
                    g = g0 + gl
                    gp = gl % NS
                    R = Rot[gp]
                    Xp = Xp_[gp]; Sp = Sp_[gp]; So = So_[gp]
                    self.tt("pool", R.t[:].rearrange("p e (h c) -> p (e h) c", h=2), apx(cf, 0, 128, 384, [(0, 24), (1, 64)]),
                            apx(WR, 0, 128, g * 24, [(1, 24), (0, 64)]), ALU.mult, [cf, WR], [R])
                    ps = mk.bank(2 * gp)
                    for s in range(8):
                        self.mm(ps.t[:, 0:256], Pm.t[:, 7 - s, gl, :], apx(u, 0, 128, s * 256, [(1, 256)]), s == 0, s == 7, [Pm, u], [ps])
                    self.act(Xp.t[:], ps.t[:, 0:256], AF.Copy, [ps, flag], [Xp], scale=flag.t[:, 0:1])
                    S = Xp
                    for l in range(4):
                        N = 256 // 4 ** (l + 1)
                        ps2 = mk.bank(2 * gp + 1)
                        for e in range(4):
                            lhs = ident.t[:] if e == 0 else R.t[:, l * 3 + e - 1, :]
                            self.mm(ps2.t[:, 0:N], lhs, apx(S, 0, 128, 3 - e, [(4, N)]), e == 0, e == 3, [ident, R, S], [ps2])
                        if l < 3:
                            S2 = Sp[l % 2]
                            evac(S2.t[:, 0:N], ps2.t[:, 0:N], [ps2], [S2])
                            S = S2
                        else:
                            evac(Hext.t[:, gl, 0:1], ps2.t[:, 0:1], [ps2], [Hext])
                    ps = mk.bank(2 * gp)
                    for s in range(8):
                        self.mm(ps.t[:, 0:256], Pm.t[:, 7 - s, gl, :], apx(u, 0, 128, HALF + s * 256, [(1, 256)]), s == 0, False, [Pm, u], [ps])
                    self.mm(ps.t[:, 0:1], R.t[:, 0, :], Hext.t[:, gl, 0:1], False, True, [R, Hext], [ps])
                    S = So[0]
                    evac(S.t[:], ps.t[:, 0:256], [ps], [S])
                    for l in range(4):
                        d = 4 ** l
                        ps2 = mk.bank(2 * gp + 1)
                        self.mm(ps2.t[:, 0:256], ident.t[:], S.t[:, 0:256], True, False, [ident, S], [ps2])
                        for e in range(1, 4):
                            self.mm(ps2.t[:, e * d:256], R.t[:, l * 3 + e - 1, :], S.t[:, 0:256 - e * d], False, e == 3, [R, S], [ps2])
                        if l < 3:
                            S2 = So[(l + 1) % 2]
                            evac(S2.t[:], ps2.t[:, 0:256], [ps2], [S2])
                            S = S2
                        else:
                            evac(Hext.t[:, gl, 1:257], ps2.t[:, 0:256], [ps2], [Hext])
                if j == 0:
                    self.dump("Hext", Hext, Hext.t[:], [128, 8, 257], BF16)
                for s in range(8):
                    ps = mk.bank(2 + s % 2)
                    for gl in range(8):
                        hq = gl // 4
                        self.mm(ps.t[64 * hq:64 * hq + 64, 0:256], Qp.t[:, s, gl, :], Hext.t[:, gl, 0:256], gl % 4 == 0, False, [Qp, Hext], [ps])
                    for d in range(s + 1):
                        self.mm(ps.t[:, 0:256], Toep.t[:, d, :], apx(u, 0, 128, HALF + (s - d) * 256, [(1, 256)]), False, d == s, [Toep, u], [ps])
                    self.copy("act", xs.t[:], ps.t[:, 0:256], [ps], [xs])
                    self.tt("pool", x2.t[:], xs.t[:], xs.t[:], ALU.mult, [xs], [x2])
                    self.ts("pool", x2.t[:], x2.t[:], 2 * GELU_C * 0.044715, 2 * GELU_C, ALU.mult, ALU.add, [x2], [x2])
                    self.tt("pool", x2.t[:], x2.t[:], xs.t[:], ALU.mult, [x2, xs], [x2])
                    self.act(sg.t[:], x2.t[:], AF.Sigmoid, [x2], [sg])
                    self.tt("dve", apx(gyj, 0, 128, s, [(8, 256)]), xs.t[:], sg.t[:], ALU.mult, [xs, sg], [gyj])
                mk.dma("sp", [(gyd[j].t.ap(), gyj.t[:])], reads=[gyj], writes=[gyd[j]])

    def attention(self, w_in, w_sw, bin_fm, bsw_fm, bv_row, COS, SIN, flag, AT, HTp, attT, ones, maskpc):
        mk = self.mk
        cf = self.cst
        with mk.phase():
            binb = mk.sbuf("binb", [128, 40], F32); bswb = mk.sbuf("bswb", [128, 12], F32)
            bvb = mk.sbuf("bvb", [1, 768], BF16)
            mk.dma("sp", [(binb.t[:], bin_fm.t.ap())], reads=[bin_fm], writes=[binb])
            mk.dma("sp", [(bswb.t[:], bsw_fm.t.ap())], reads=[bsw_fm], writes=[bswb])
            mk.dma("pool", [(bvb.t[:], bv_row.t.ap())], reads=[bv_row], writes=[bvb])
            accN = mk.sbuf("accN", [128, 2, HALF], F32)
            accD = mk.sbuf("accD", [128, 2, HALF], F32)
            COSd, SINd = COS, SIN
            COS = mk.sbuf("COS", [128, SEQ], BF16); SIN = mk.sbuf("SIN", [128, SEQ], BF16)
            mk.op("pool", lambda e: e.memset(COS.t[:], 1.0), reads=[], writes=[COS], cost=5.0)
            mk.op("pool", lambda e: e.memset(SIN.t[:], 0.0), reads=[], writes=[SIN], cost=5.0)
            mk.dma("sp", [(COS.t[0:16, :], COSd.t.ap()), (COS.t[64:80, :], COSd.t.ap())], reads=[COSd], writes=[COS])
            mk.dma("sp", [(SIN.t[0:16, :], SINd.t.ap()), (SIN.t[64:80, :], SINd.t.ap())], reads=[SINd], writes=[SIN])
            wq = [mk.sbuf(f"wq{i}", [128, 8, 128], BF16) for i in range(2)]
            wqs = [mk.sbuf(f"wqs{i}", [128, 8, 128], BF16) for i in range(2)]
            wv = mk.sbuf("wv", [128, 8, 256], BF16)
            t1 = [mk.sbuf(f"rt1{i}", [128, 512], F32) for i in range(2)]
            t2 = [mk.sbuf(f"rt2{i}", [128, 512], F32) for i in range(2)]
            PT = [mk.sbuf(f"PT{i}", [128, 256], BF16) for i in range(3)]
            mask0 = mk.sbuf("mask0", [128, 256], BF16)
            self.copy("dve", mask0.t[:], maskpc, [self.cbf], [mask0])
            self.ts("dve", mask0.t[:, 0:128], mask0.t[:, 0:128], flag.t[:, 0:1], None, ALU.mult, None, [mask0, flag], [mask0])
            cnt = [0]
            wc = [0]

            def proj_rope(wa, wb, src, c0, bias_a, bias_b, tok0_tab, dst, oap, a0, n, dil):
                i = cnt[0] % 2
                cnt[0] += 1
                pa = mk.bank(2 * i); pb = mk.bank(2 * i + 1)
                for kt in range(8):
                    self.mm(pa.t[:], wa.t[:, kt, :], src.t[:, kt, c0:c0 + 512], kt == 0, kt == 7, [wa, src], [pa])
                for kt in range(8):
                    self.mm(pb.t[:], wb.t[:, kt, :], src.t[:, kt, c0:c0 + 512], kt == 0, kt == 7, [wb, src], [pb])
                self.stt(t1[i].t[:], pa.t[:], bias_a, COS.t[:, tok0_tab:tok0_tab + 512], ALU.add, ALU.mult, [pa, COS, binb, bswb], [t1[i]])
                self.stt(t2[i].t[:], pb.t[:], bias_b, SIN.t[:, tok0_tab:tok0_tab + 512], ALU.add, ALU.mult, [pb, SIN, binb, bswb], [t2[i]])
                if dil == 1:
                    i0 = t1[i].t[:, a0:a0 + n]; i1 = t2[i].t[:, a0:a0 + n]
                else:
                    i0 = apx(t1[i], 0, 128, a0, [(1, dil), (dil, n // dil)])
                    i1 = apx(t2[i], 0, 128, a0, [(1, dil), (dil, n // dil)])
                self.tt("pool", oap, i0, i1, ALU.add, [t1[i], t2[i]], [dst])

            def store_ap(dst, pt, L, dil, m0, n):
                if dil == 1:
                    return dst.t[:, pt, m0:m0 + n]
                return apx(dst, 0, 128, pt * dil * L + m0, [(L, dil), (1, n // dil)])

            it = 0
            vi = 0
            qbi = [0]
            for g in range(3):
                dil = DILS[g]
                Lq = HALF // dil
                Lr = 128 + HALF // dil
                nkb = 1 + 16 // dil
                nq = 16 // dil
                with mk.phase():
                    qT = mk.sbuf(f"qT{g}", [128, 2, HALF], BF16)
                    kT = mk.sbuf(f"kT{g}", [128, 2, dil * Lr], BF16)
                    V = mk.sbuf(f"V{g}", [128, dil * nkb, 256], BF16)
                    for pt in range(2):
                        mt = 2 * g + pt
                        wa = wq[wc[0] % 2]; wb = wqs[wc[0] % 2]; wc[0] += 1
                        self.wload_into(wa, w_in, 1024, 768 + 128 * mt, 128)
                        self.wload_into(wb, w_sw, 1024, 128 * mt, 128)
                        for blk in range(4):
                            oap = store_ap(qT, pt, Lq, dil, blk * 512 // dil, 512)
                            proj_rope(wa, wb, AT, blk * 512, binb.t[:, 6 + mt:7 + mt], bswb.t[:, mt:mt + 1], HALF + blk * 512,
                                      qT, oap, 0, 512, dil)
                        wa = wq[wc[0] % 2]; wb = wqs[wc[0] % 2]; wc[0] += 1
                        self.wload_into(wa, w_in, 1024, 1536 + 128 * mt, 128)
                        self.wload_into(wb, w_sw, 1024, 768 + 128 * mt, 128)
                        for blk in range(8):
                            prev = blk < 4
                            src = HTp if prev else AT
                            if prev:
                                lo = HALF - 128 * dil
                                b0 = blk * 512
                                if b0 + 512 <= lo:
                                    continue
                                a0 = max(lo, b0) - b0
                                n = 512 - a0
                                m0 = (b0 + a0 - lo) // dil
                            else:
                                a0, n = 0, 512
                                m0 = 128 + (blk - 4) * 512 // dil
                            oap = store_ap(kT, pt, Lr, dil, m0, n)
                            proj_rope(wa, wb, src, (blk % 4) * 512, binb.t[:, 12 + mt:13 + mt], bswb.t[:, 6 + mt:7 + mt], blk * 512,
                                      kT, oap, a0, n, dil)
                    self.wload_into(wv, w_in, 1024, 2304 + 256 * g, 256)
                    for r in range(dil):
                        for b in range(nkb):
                            if b == 0:
                                src = HTp; start = HALF - 128 * dil + r
                            else:
                                src = AT; start = dil * 128 * (b - 1) + r
                            ps = mk.bank(4 + vi % 2)
                            vi += 1
                            for kt in range(8):
                                lhs = apx(src, 0, 128, kt * HALF + start, [(dil, 128)])
                                self.mm(ps.t[:, 0:256], lhs, wv.t[:, kt, :], kt == 0, False, [src, wv], [ps])
                            self.mm(ps.t[:, 0:256], ones[0:1, :], bvb.t[0:1, 256 * g:256 * g + 256], False, True, [self.cbf, bvb], [ps])
                            dst = V.t[:, r * nkb + b, :]
                            if b == 0:
                                self.act(dst, ps.t[:, 0:256], AF.Copy, [ps, flag], [V], scale=flag.t[:, 0:1])
                            elif vi % 2:
                                self.copy("act", dst, ps.t[:, 0:256], [ps], [V])
                            else:
                                self.copy("dve", dst, ps.t[:, 0:256], [ps], [V])
                    if g == 1:
                        self.dump("qT1", qT, qT.t[:], [128, 2, HALF], BF16)
                        self.dump("kT1", kT, kT.t[:], [128, 2, dil * Lr], BF16)
                        self.dump("V1", V, V.t[:], [128, dil * nkb, 256], BF16)
                    for pt in range(2):
                        for r in range(dil):
                            for qb in range(nq):
                                pN = mk.bank(4 + 2 * (qbi[0] % 2)); pD = mk.bank(5 + 2 * (qbi[0] % 2)); qbi[0] += 1
                                for hp in range(2):
                                    rows = slice(64 * hp, 64 * hp + 64)
                                    pS = mk.bank(it % 3)
                                    P_ = PT[it % 3]
                                    it += 1
                                    qap = apx(qT, 64 * hp, 64, pt * HALF + r * Lq + qb * 128, [(1, 128)])
                                    for half in range(2):
                                        kap = apx(kT, 64 * hp, 64, pt * dil * Lr + r * Lr + (qb + half) * 128, [(1, 128)])
                                        self.mm(pS.t[:, half * 128:(half + 1) * 128], kap, qap, True, True, [kT, qT], [pS])
                                    self.act(P_.t[:], pS.t[:, 0:256], AF.Exp, [pS], [P_], scale=0.125)
                                    meng = "pool" if it % 2 else "dve"
                                    if qb == 0:
                                        self.tt(meng, P_.t[:], P_.t[:], mask0.t[:], ALU.mult, [P_, mask0], [P_])
                                    else:
                                        self.tt(meng, P_.t[:], P_.t[:], maskpc, ALU.mult, [P_, self.cbf], [P_])
                                    h = 2 * pt + hp
                                    for half in range(2):
                                        vb = V.t[:, r * nkb + qb + half, h * 64:(h + 1) * 64]
                                        self.mm(pN.t[rows, 0:128], vb, P_.t[:, half * 128:(half + 1) * 128], half == 0, half == 1, [V, P_], [pN])
                                    for half in range(2):
                                        self.mm(pD.t[rows, 0:128], ones[:, 0:64], P_.t[:, half * 128:(half + 1) * 128], half == 0, half == 1, [self.cbf, P_], [pD])
                                an = apx(accN, 0, 128, pt * HALF + dil * 128 * qb + r, [(dil, 128)])
                                ad = apx(accD, 0, 128, pt * HALF + dil * 128 * qb + r, [(dil, 128)])
                                if g == 0:
                                    self.copy("dve", an, pN.t[:, 0:128], [pN], [accN])
                                    self.copy("act", ad, pD.t[:, 0:128], [pD], [accD])
                                else:
                                    self.tt("dve", an, an, pN.t[:, 0:128], ALU.add, [pN, accN], [accN])
                                    self.tt("dve", ad, ad, pD.t[:, 0:128], ALU.add, [pD, accD], [accD])
            self.dump("accN", accN, accN.t[:], [128, 2, HALF])
            self.dump("accD", accD, accD.t[:], [128, 2, HALF])
            for pt in range(2):
                mk.op("dve", lambda e, pt=pt: e.reciprocal(accD.t[:, pt, :], accD.t[:, pt, :]), reads=[accD], writes=[accD], cost=2.4)
                self.tt("dve", attT.t[:, pt, :], accN.t[:, pt, :], accD.t[:, pt, :], ALU.mult, [accN, accD], [attT])


def _consts():
    bf = ml_dtypes.bfloat16
    c_bf = np.zeros((128, 128 * 3 + 256 + 64), np.float32)
    c_bf[:, 0:128] = np.eye(128)
    c_bf[:, 128:256] = 1.0
    ik = np.arange(128)[:, None]; iq = np.arange(128)[None, :]
    c_bf[:, 384:512] = (ik >= iq)
    c_bf[:, 512:640] = (ik <= iq)
    c_f = np.zeros((128, 512), np.float32)
    p = np.arange(128)
    c_f[:, 0] = EPS
    c_f[:, 1] = p < 64
    c_f[:, 2] = p >= 64
    q = p % 64
    invf = (500000.0 ** (-(2.0 * (q % 8)) / 16.0)).astype(np.float32)
    q16 = p % 16
    c_f[:, 3] = (500000.0 ** (-(2.0 * (q16 % 8)) / 16.0)).astype(np.float32)
    c_f[:, 4] = np.where(q16 < 8, -1.0, 1.0)
    c_f[:, 8:16] = (p[:, None] // 16 == np.arange(8)[None, :])
    c_f[:, 16:16 + ND] = np.asarray(DPOW, np.float32)[None, :]
    c_f[:, 128:256] = (p[:, None] // 16 == (np.arange(128)[None, :] // 16))
    c_f[:, 256:384] = np.eye(128)
    c_f[:, 384:448] = (np.arange(64)[None, :] == (p[:, None] % 64))
    return c_bf.astype(bf), c_f


def _prep_shared(inp):
    f = lambda a: np.ascontiguousarray(np.asarray(a, dtype=np.float32))
    w_in = f(inp["w_in"][0]); b_in = f(inp["b_in"][0])
    perm = np.arange(64)
    perm[0:8] = np.arange(8, 16); perm[8:16] = np.arange(0, 8)
    colperm = (np.arange(12)[:, None] * 64 + perm[None, :]).reshape(-1)
    w_sw = np.concatenate([w_in[:, 768 + colperm], w_in[:, 1536 + colperm]], axis=1)
    b_sw = np.concatenate([b_in[768 + colperm], b_in[1536 + colperm]])
    fm = lambda v: np.ascontiguousarray(v.reshape(-1, 128).T)
    rep2 = lambda a: np.concatenate([a, a], axis=0)
    logdt = f(inp["ssm_log_dt"][0]); are = f(inp["ssm_a_re"][0]); aim = f(inp["ssm_a_im"][0])
    s5p = np.concatenate([np.broadcast_to(logdt[None, :], (128, G)), rep2(are.T), rep2(aim.T)], axis=1)
    br = f(inp["ssm_b_re"][0]).transpose(1, 0, 2).reshape(64, G * 16)
    bi = f(inp["ssm_b_im"][0]).transpose(1, 0, 2).reshape(64, G * 16)
    cr = f(inp["ssm_c_re"][0]).transpose(2, 0, 1).reshape(64, G * 16)
    ci = f(inp["ssm_c_im"][0]).transpose(2, 0, 1).reshape(64, G * 16)
    c_bf, c_f = _consts()
    sh = {
        "w_in": w_in, "w_sw": f(w_sw), "bin_fm": fm(b_in), "bsw_fm": fm(b_sw), "bv_row": f(b_in[None, 2304:3072]),
        "ln_g": f(np.stack([inp["ln_in_g"], inp["ln1_g"][0], inp["ln2_g"][0], inp["ln3_g"][0]])),
        "ln_b": f(np.stack([inp["ln_in_b"], inp["ln1_b"][0], inp["ln2_b"][0], inp["ln3_b"][0]])),
        "s5p": f(s5p), "s5B": f(np.concatenate([rep2(br), rep2(bi)], axis=1)), "s5C": f(np.concatenate([rep2(cr), rep2(ci)], axis=1)),
        "s5d": fm(f(inp["ssm_d"][0])),
        "w_glu": f(inp["w_glu"][0]), "bglu_fm": fm(f(inp["b_glu"][0])),
        "w_au": f(inp["w_att_up"][0]), "w_mo": f(inp["w_mix_out"][0]), "bmo_row": f(inp["b_mix_out"][0][None, :]),
        "w_xq": f(inp["w_xq"][0]), "w_xkv": f(inp["w_xkv"][0]), "w_xo": f(inp["w_xo"][0]),
        "w_f1": f(inp["w_ff1"][0]), "bf1_fm": fm(f(inp["b_ff1"][0])), "w_f2": f(inp["w_ff2"][0]), "bf2_row": f(inp["b_ff2"][0][None, :]),
        "c_bf": c_bf, "c_f32": c_f,
    }
    return sh


def _core_inputs(inp, sh, b, half):
    x = np.asarray(inp["x"], np.float32); pos = np.asarray(inp["positions"], np.int32)
    d = dict(sh)
    d["x_own"] = np.ascontiguousarray(x[b, half * HALF:(half + 1) * HALF])
    if half == 0:
        d["x_prev"] = np.zeros((HALF, D_MODEL), np.float32)
        pp = np.concatenate([np.zeros(HALF, np.int32), pos[b, :HALF]])
    else:
        d["x_prev"] = np.ascontiguousarray(x[b, :HALF])
        pp = pos[b]
    d["posb"] = np.ascontiguousarray(np.broadcast_to(pp.reshape(8, 1, 512), (8, 16, 512)).reshape(128, 512))
    d["flag"] = np.full((128, 1), float(half), np.float32)
    d["mem"] = np.ascontiguousarray(np.asarray(inp["mem"], np.float32)[b])
    return d


_PROG = {}


def kernel(**inputs):
    if "k" not in _PROG:
        _PROG["k"] = K()
    k = _PROG["k"]
    sh = _prep_shared(inputs)
    in_maps = [_core_inputs(inputs, sh, c // 2, c % 2) for c in range(8)]
    res = run_bass_kernel_spmd(k.nc, in_maps, core_ids=list(range(8)))
    out = np.zeros((4, SEQ, D_MODEL), np.float32)
    for c in range(8):
        out[c // 2, (c % 2) * HALF:(c % 2 + 1) * HALF] = res.results[c]["out"]
    return out
```

```python
import numpy as np
import ml_dtypes
import concourse.bass as bass
import concourse.mybir as mybir
from concourse.bass_utils import run_bass_kernel_spmd

F32 = mybir.dt.float32
BF16 = mybir.dt.bfloat16
I32 = mybir.dt.int32
AF = mybir.ActivationFunctionType
ALU = mybir.AluOpType
_DSZ = {F32: 4, BF16: 2, I32: 4}


class Buf:
    __slots__ = ("name", "t", "writer", "readers", "dsem", "last_dma", "shape", "dtype", "kind")

    def __init__(self, name, t, shape=None, dtype=None, kind="sbuf"):
        self.name = name
        self.kind = kind
        self.t = t
        self.writer = None
        self.readers = []
        self.dsem = None
        self.last_dma = None
        self.shape = shape
        self.dtype = dtype


class Ev:
    __slots__ = ("eng", "fn", "waits", "order", "need_inc", "semval", "is_dma", "dsem", "dval", "ndma",
                 "cost", "lat", "idx", "t_end", "done", "nsucc", "fence")

    def __init__(self, eng, fn, cost=0.3):
        self.eng = eng
        self.fn = fn
        self.waits = []
        self.order = []
        self.need_inc = False
        self.semval = None
        self.is_dma = False
        self.dsem = None
        self.dval = None
        self.ndma = 0
        self.cost = cost
        self.lat = 0.0
        self.t_end = None
        self.fence = False


class _Phase:
    def __init__(self, mk):
        self.mk = mk

    def __enter__(self):
        self.mk._phase_stack.append(self.mk._sb_off)
        return self

    def __exit__(self, *a):
        self.mk.barrier()
        self.mk._sb_off = self.mk._phase_stack.pop()
        return False


class MK:
    ENGS = ("pe", "act", "dve", "pool", "sp")

    def __init__(self, nc, n_dma_sems=72, reorder=True):
        self.nc = nc
        self.reorder = reorder
        self.segs = [[]]
        self._sb_off = 0
        self._sb_max = 0
        self._phase_stack = []
        self._uid = 0
        self.n_dma_sems = n_dma_sems
        self.n_sw = 24
        self.dsem_free = [list(range(self.n_sw, n_dma_sems)), list(range(self.n_sw))]
        self.dsem_count = [0] * n_dma_sems
        self.dsem_bufs = []
        self.psum_banks = []
        self.sb_base = (int(nc.sbuf_base) + 63) // 64 * 64
        self.sb_limit = int(nc.sbuf_top) - self.sb_base - 1024
        for i in range(8):
            t = nc.alloc_psum_tensor(f"psb{i}", [128, 512], F32)
            self.psum_banks.append(Buf(f"psb{i}", t, [128, 512], F32, "psum"))

    def dram_in(self, name, shape, dtype):
        return Buf(name, self.nc.dram_tensor(name, list(shape), dtype, kind="ExternalInput"), shape, dtype, "dram")

    def dram_out(self, name, shape, dtype):
        return Buf(name, self.nc.dram_tensor(name, list(shape), dtype, kind="ExternalOutput"), shape, dtype, "dram")

    def dram_tmp(self, name, shape, dtype):
        return Buf(name, self.nc.dram_tensor(name, list(shape), dtype, kind="Internal"), shape, dtype, "dram")

    def sbuf(self, name, shape, dtype):
        nbytes = int(np.prod(shape[1:])) * _DSZ[dtype]
        nbytes = (nbytes + 63) // 64 * 64
        off = self._sb_off
        if off + nbytes > self.sb_limit:
            raise RuntimeError(f"SBUF overflow allocating {name}: {off}+{nbytes}")
        self._sb_off = off + nbytes
        self._sb_max = max(self._sb_max, self._sb_off)
        self._uid += 1
        t = self.nc.alloc_sbuf_tensor_at(f"{name}_{self._uid}", list(shape), dtype, offset=self.sb_base + off)
        return Buf(name, t, shape, dtype)

    def bank(self, i):
        return self.psum_banks[i]

    def view(self, name, t):
        return Buf(name, t)

    def phase(self):
        return _Phase(self)

    def _deps(self, ev, eng, reads, writes):
        deps = []
        for b in reads:
            if b.writer is not None:
                deps.append((b.writer, True))
        for b in writes:
            if b.writer is not None:
                deps.append((b.writer, False))
            for r in b.readers:
                deps.append((r, False))
        rawset = set(id(d) for d, raw in deps if raw)
        seen = set()
        for d, _ in deps:
            if d is ev or id(d) in seen:
                continue
            seen.add(id(d))
            same = (not d.is_dma) and (not ev.is_dma) and d.eng == eng
            if same and (eng == "pe" or id(d) not in rawset):
                ev.order.append(d)
            else:
                ev.waits.append(d)
        for b in reads:
            b.readers.append(ev)
        for b in writes:
            b.writer = ev
            b.readers = []

    def op(self, eng, fn, reads=(), writes=(), cost=0.3):
        ev = Ev(eng, fn, cost)
        self._deps(ev, eng, reads, writes)
        self.segs[-1].append(ev)
        return ev

    def dma(self, eng, pairs, reads=(), writes=(), sync=None, nbytes=None):
        if sync is None:
            for b in list(writes) + list(reads):
                if b.kind != "dram":
                    sync = b
                    break
        if sync is None:
            sync = (list(writes) + list(reads))[0]
        sw = 1 if eng == "pool" else 0
        if sync.dsem is None:
            sync.dsem = [None, None]
            sync.last_dma = [None, None]
            self.dsem_bufs.append(sync)
        if sync.dsem[sw] is None:
            if not self.dsem_free[sw]:
                raise RuntimeError("out of DMA semaphores")
            sync.dsem[sw] = self.dsem_free[sw].pop()
        ds = sync.dsem[sw]
        ev = Ev(eng, None, 0.6 if sw else 0.08)
        ev.is_dma = True
        ev.ndma = len(pairs)
        ev.dsem = ds
        self.dsem_count[ds] += 16 * len(pairs)
        ev.dval = self.dsem_count[ds]
        ev.fn = pairs
        if nbytes is None:
            nbytes = 0
            for (o, i) in pairs:
                try:
                    nbytes += int(o.partition_size) * int(o.free_size) * 4
                except Exception:
                    nbytes += 1 << 19
        ev.lat = 2.0 + nbytes / 150e3
        for ld in sync.last_dma:
            if ld is not None:
                ev.waits.append(ld)
        sync.last_dma[sw] = ev
        self._deps(ev, eng, reads, writes)
        self.segs[-1].append(ev)
        return ev

    def barrier(self):
        self.segs.append([])
        for b in self.dsem_bufs:
            b.dsem = None
            b.last_dma = None
        self.dsem_bufs = []
        self.dsem_free = [list(range(self.n_sw, self.n_dma_sems)), list(range(self.n_sw))]

    def _schedule(self, seg):
        ENGS = self.ENGS
        if not self.reorder:
            return {e: [ev for ev in seg if ev.eng == e] for e in ENGS}
        WIN = {"pe": 320, "act": 256, "dve": 384, "pool": 384, "sp": 64}
        SEM_LAT = 0.7
        pend = {e: [ev for ev in seg if ev.eng == e] for e in ENGS}
        out = {e: [] for e in ENGS}
        free_at = {e: 0.0 for e in ENGS}
        inseg = set(id(ev) for ev in seg)
        for ev in seg:
            ev.t_end = None
        n_left = len(seg)
        while n_left:
            best = None
            for e in ENGS:
                q = pend[e]
                lim = min(WIN[e], len(q))
                for k in range(lim):
                    ev = q[k]
                    rdy = 0.0
                    ok = True
                    for d in ev.waits:
                        if id(d) in inseg:
                            if d.t_end is None:
                                ok = False
                                break
                            t = d.t_end + SEM_LAT
                            if t > rdy:
                                rdy = t
                    if ok:
                        for d in ev.order:
                            if id(d) in inseg and d.t_end is None:
                                ok = False
                                break
                    if not ok:
                        continue
                    start = rdy if rdy > free_at[e] else free_at[e]
                    key = (start, k)
                    if best is None or key < best[0]:
                        best = (key, e, k, ev, start)
                    if start <= free_at[e]:
                        break
            if best is None:
                raise RuntimeError("scheduler deadlock")
            _, e, k, ev, start = best
            pend[e].pop(k)
            out[e].append(ev)
            free_at[e] = start + ev.cost
            ev.t_end = start + ev.cost + ev.lat
            n_left -= 1
        return out

    def finish(self):
        import contextlib
        nc = self.nc
        prog = {e: [] for e in self.ENGS}
        for seg in self.segs:
            if not seg:
                continue
            sch = self._schedule(seg)
            lastc = [sch[e][-1] for e in self.ENGS if sch[e] and not all(x.is_dma for x in sch[e])]
            lastc = []
            for e in self.ENGS:
                for ev in reversed(sch[e]):
                    if not ev.is_dma:
                        lastc.append(ev)
                        break
            dmas = {}
            for ev in seg:
                if ev.is_dma and (ev.dsem not in dmas or dmas[ev.dsem].dval < ev.dval):
                    dmas[ev.dsem] = ev
            for e in self.ENGS:
                prog[e].extend(sch[e])
                nop = Ev(e, lambda eng: eng.nop())
                nop.fence = True
                nop.waits = [d for d in lastc if d.eng != e] + list(dmas.values())
                prog[e].append(nop)
        for e in self.ENGS:
            for ev in prog[e]:
                for d in ev.waits:
                    if not d.is_dma:
                        d.need_inc = True
        self.prog = prog
        with contextlib.ExitStack() as es:
            esem = {e: es.enter_context(nc.semaphore(f"s_{e}")) for e in self.ENGS}
            dsems = [es.enter_context(nc.semaphore(f"d_{i}")) for i in range(self.n_dma_sems)]
            for e in self.ENGS:
                c = 0
                for ev in prog[e]:
                    if ev.need_inc and not ev.is_dma:
                        c += 1
                        ev.semval = c
            mk = self

            def replay(ename, eng):
                seen = {}
                for ev in prog[ename]:
                    for d in ev.waits:
                        if d.is_dma:
                            key, sem, val = ("d", d.dsem), dsems[d.dsem], d.dval
                        else:
                            key, sem, val = ("e", d.eng), esem[d.eng], d.semval
                        if seen.get(key, 0) < val:
                            eng.wait_ge(sem, val)
                            seen[key] = val
                    if ev.is_dma:
                        for (o, i) in ev.fn:
                            eng.dma_start(out=o, in_=i).then_inc(dsems[ev.dsem], 16)
                    else:
                        inst = ev.fn(eng)
                        if ev.need_inc:
                            inst.then_inc(esem[ename], 1)

            with nc.Block() as block:
                @block.tensor
                def _(eng):
                    replay("pe", eng)

                @block.scalar
                def _(eng):
                    replay("act", eng)

                @block.vector
                def _(eng):
                    replay("dve", eng)

                @block.gpsimd
                def _(eng):
                    replay("pool", eng)

                @block.sync
                def _(eng):
                    replay("sp", eng)

    def stats(self):
        return {e: len(self.prog[e]) for e in self.ENGS}, self._sb_max


def apx(buf, poff, npart, off, dims):
    t = buf.t
    shp = buf.shape
    rowlen = int(np.prod(shp[1:]))
    return bass.AP(t, poff * rowlen + off, [[rowlen, npart]] + [[int(s), int(c)] for (s, c) in dims])


D_MODEL = 1024
SEQ = 4096
HALF = 2048
NT = 16
G = 48
ALPHA = 2.0 ** 0.25
EPS = 1e-5
TWO_PI = float(2 * np.pi)
DPOW = [0, 1, 2, 3, 4, 5, 6, 7, 8, 16, 24, 32, 64, 96, 128, 256, 384, 512, 1024, 1536]
ND = len(DPOW)
SCAN_IDX = [[DPOW.index(8 * e * 4 ** l) for e in (1, 2, 3)] for l in range(4)]
DILS = (1, 4, 16)
GELU_C = float(np.sqrt(2.0 / np.pi))


class K:
    def __init__(self, dbg=None):
        self.dbg = dbg or ()
        nc = bass.Bass("TRN2", target_bir_lowering=False)
        self.nc = nc
        self.mk = MK(nc)
        self.din = {}
        self.build()

    def inp(self, name, shape, dtype=F32):
        b = self.mk.dram_in(name, shape, dtype)
        self.din[name] = b
        return b

    @staticmethod
    def _fs(ap):
        try:
            return float(ap.free_size)
        except Exception:
            return 512.0

    def mm(self, out, lhsT, rhs, start, stop, reads, writes):
        self.mk.op("pe", lambda e: e.matmul(out, lhsT, rhs, start=start, stop=stop), reads=reads, writes=writes,
                   cost=(0.035 + self._fs(out) / 2400.0) * getattr(self, 'pe_scale', 1.0))

    def act(self, out, in_, func, reads, writes, bias=None, scale=None, eng="act"):
        kw = {}
        if bias is not None:
            kw["bias"] = bias
        if scale is not None:
            kw["scale"] = scale
        self.mk.op("act", lambda e: e.activation(out, in_, func, **kw), reads=reads, writes=writes, cost=0.2 + self._fs(out) / 1100.0)

    def tt(self, eng, out, a, b, op, reads, writes):
        self.mk.op(eng, lambda e: e.tensor_tensor(out, a, b, op), reads=reads, writes=writes, cost=self._ec(eng, out))

    def ts(self, eng, out, a, s1, s2, op0, op1, reads, writes):
        if op1 is None:
            self.mk.op(eng, lambda e: e.tensor_scalar(out, a, s1, None, op0=op0), reads=reads, writes=writes, cost=self._ec(eng, out))
        else:
            self.mk.op(eng, lambda e: e.tensor_scalar(out, a, s1, s2, op0=op0, op1=op1), reads=reads, writes=writes, cost=self._ec(eng, out))

    def stt(self, out, a, s, b, op0, op1, reads, writes):
        self.mk.op("dve", lambda e: e.scalar_tensor_tensor(out, a, s, b, op0=op0, op1=op1), reads=reads, writes=writes, cost=self._ec("dve", out))

    def copy(self, eng, out, in_, reads, writes):
        if eng == "act":
            self.mk.op("act", lambda e: e.copy(out, in_), reads=reads, writes=writes, cost=0.2 + self._fs(out) / 1100.0)
        else:
            self.mk.op(eng, lambda e: e.tensor_copy(out, in_), reads=reads, writes=writes, cost=self._ec(eng, out))

    def _ec(self, eng, out):
        f = self._fs(out)
        return (0.25 + f / 550.0) if eng == "pool" else (0.12 + f / 900.0)

    def load(self, eng, dst, dst_ap, src, src_ap):
        self.mk.dma(eng, [(dst_ap, src_ap)], reads=[src], writes=[dst])

    def dump(self, name, buf, ap, shape, dtype=F32):
        if name in self.dbg:
            o = self.mk.dram_out("dbg_" + name, shape, dtype)
            self.mk.dma("sp", [(o.t.ap(), ap)], reads=[buf], writes=[o])

    def wload(self, name, src, rows, col0, ncols, eng="pool"):
        kt = rows // 128
        b = self.mk.sbuf(name, [128, kt, ncols], BF16)
        srcap = src.t.ap().rearrange("(kt p) c -> p kt c", p=128)[:, :, col0:col0 + ncols]
        self.mk.dma(eng, [(b.t[:], srcap)], reads=[src], writes=[b])
        return b

    def wload_into(self, b, src, rows, col0, ncols, eng="pool"):
        srcap = src.t.ap().rearrange("(kt p) c -> p kt c", p=128)[:, :, col0:col0 + ncols]
        self.mk.dma(eng, [(b.t[:], srcap)], reads=[src], writes=[b])

    def layernorm(self, r, gB, bB, out32, outT, tcol, tmp16, psT_bank, st):
        mk = self.mk
        ident = self.ident
        mk.op("dve", lambda e: e.bn_stats(st.t[:, 0:6], r.t[:, 0:512]), reads=[r], writes=[st], cost=0.7)
        mk.op("dve", lambda e: e.bn_stats(st.t[:, 6:12], r.t[:, 512:1024]), reads=[r], writes=[st], cost=0.6)
        mk.op("dve", lambda e: e.bn_aggr(st.t[:, 12:14], st.t[:, 0:12]), reads=[st], writes=[st], cost=0.21)
        self.act(st.t[:, 14:15], st.t[:, 13:14], AF.Sqrt, [st, self.cst], [st], bias=self.cst.t[:, 0:1], scale=1.0)
        mk.op("dve", lambda e: e.reciprocal(st.t[:, 15:16], st.t[:, 14:15]), reads=[st], writes=[st], cost=0.17)
        self.ts("dve", st.t[:, 16:17], st.t[:, 12:13], st.t[:, 15:16], -1.0, ALU.mult, ALU.mult, [st], [st])
        self.act(out32.t[:], r.t[:], AF.Identity, [st, r], [out32], bias=st.t[:, 16:17], scale=st.t[:, 15:16])
        self.tt("dve", out32.t[:], out32.t[:], gB.t[:], ALU.mult, [out32, gB], [out32])
        self.tt("pool", out32.t[:], out32.t[:], bB.t[:], ALU.add, [out32, bB], [out32])
        if outT is None:
            return
        self.copy("act", tmp16.t[:], out32.t[:], [out32], [tmp16])
        psb = psT_bank.t[:].bitcast(BF16)
        for kt in range(8):
            mk.op("pe", lambda e, kt=kt: e.transpose(psb[:, kt * 128:(kt + 1) * 128], tmp16.t[:, kt * 128:(kt + 1) * 128], ident.t[:]),
                  reads=[tmp16, ident], writes=[psT_bank], cost=0.09)
        dst = apx(outT[0], 0, 128, tcol, [(outT[1], 8), (1, 128)])
        src = psb.rearrange("p (k c) -> p k c", k=8)
        self.copy("act", dst, src, [psT_bank], [outT[0]])

    def build(self):
        mk = self.mk
        inp = self.inp
        x_own = inp("x_own", [HALF, D_MODEL]); x_prev = inp("x_prev", [HALF, D_MODEL])
        flag_d = inp("flag", [128, 1]); posb = inp("posb", [128, 512], I32)
        mem = inp("mem", [256, D_MODEL])
        w_in = inp("w_in", [1024, 5120]); w_sw = inp("w_sw", [1024, 1536])
        bin_fm = inp("bin_fm", [128, 40]); bsw_fm = inp("bsw_fm", [128, 12]); bv_row = inp("bv_row", [1, 768])
        ln_g = inp("ln_g", [4, 1024]); ln_b = inp("ln_b", [4, 1024])
        s5p = inp("s5p", [128, 3 * G])
        s5B = inp("s5B", [128, 2 * G * 16])
        s5C = inp("s5C", [128, 2 * G * 16])
        s5d = inp("s5d", [128, 6])
        w_glu = inp("w_glu", [768, 2048]); bglu_fm = inp("bglu_fm", [128, 16])
        w_au = inp("w_au", [256, 1024]); w_mo = inp("w_mo", [1024, 1024]); bmo_row = inp("bmo_row", [1, 1024])
        w_xq = inp("w_xq", [1024, 1024]); w_xkv = inp("w_xkv", [1024, 2048]); w_xo = inp("w_xo", [1024, 1024])
        w_f1 = inp("w_f1", [1024, 4096]); bf1_fm = inp("bf1_fm", [128, 32]); w_f2 = inp("w_f2", [4096, 1024])
        bf2_row = inp("bf2_row", [1, 1024])
        c_bf = inp("c_bf", [128, 128 * 3 + 256 + 64], BF16)
        c_f32 = inp("c_f32", [128, 512])
        out = mk.dram_out("out", [HALF, D_MODEL], F32)
        res_d = [mk.dram_tmp(f"res{t}", [128, D_MODEL], F32) for t in range(NT)]
        tabs = {nm: mk.dram_tmp("tab" + nm, [128, G * 8], F32) for nm in ("ZA", "ZB", "QA", "QB")}
        tabs["WR"] = mk.dram_tmp("tabWR", [128, G * 24], F32)
        tabs["Cneg"] = mk.dram_tmp("tabCneg", [128, G * 16], BF16)

        cbf = mk.sbuf("cbf", [128, 128 * 3 + 256 + 64], BF16)
        cf = mk.sbuf("cf", [128, 512], F32)
        self.cst = cf
        self.cbf = cbf
        flag = mk.sbuf("flag", [128, 1], F32)
        mk.dma("sp", [(cbf.t[:], c_bf.t.ap())], reads=[c_bf], writes=[cbf])
        mk.dma("sp", [(cf.t[:], c_f32.t.ap())], reads=[c_f32], writes=[cf])
        mk.dma("sp", [(flag.t[:], flag_d.t.ap())], reads=[flag_d], writes=[flag])
        ident = mk.view("ident", cbf.t[:, 0:128]); ident.writer = cbf.writer
        self.ident = ident
        ones = cbf.t[:, 128:256]
        maskpc = cbf.t[:, 384:640]
        AT = mk.sbuf("AT", [128, 8, HALF], BF16)
        ATp = (AT, HALF)
        st = [mk.sbuf(f"st{i}", [128, 32], F32) for i in range(8)]

        def ln_params(i, gB, bB):
            mk.dma("sp", [(gB.t[:], ln_g.t.ap()[i:i + 1, :].partition_broadcast(128).rearrange("p o c -> p (o c)"))], reads=[ln_g], writes=[gB])
            mk.dma("sp", [(bB.t[:], ln_b.t.ap()[i:i + 1, :].partition_broadcast(128).rearrange("p o c -> p (o c)"))], reads=[ln_b], writes=[bB])

        with mk.phase():
            attT = mk.sbuf("attT", [128, 2, HALF], BF16)
            HTp = mk.sbuf("HTp", [128, 8, HALF], BF16)
            COS = mk.dram_tmp("cosd", [16, SEQ], BF16); SIN = mk.dram_tmp("sind", [16, SEQ], BF16)
            with mk.phase():
                self.s5_prep(s5p, s5C, tabs)
                self.rope_tables(posb, COS, SIN)
                gB = mk.sbuf("gB", [128, 1024], F32); bB = mk.sbuf("bB", [128, 1024], F32)
                ln_params(0, gB, bB)
                NB = 4
                xt = [mk.sbuf(f"xt{i}", [128, 1024], F32) for i in range(NB)]
                o32 = [mk.sbuf(f"o32{i}", [128, 1024], F32) for i in range(NB)]
                t16 = [mk.sbuf(f"t16{i}", [128, 1024], BF16) for i in range(NB)]

                def ld(t):
                    own = t >= NT
                    tt_ = t - NT if own else t
                    src = x_own if own else x_prev
                    mk.dma("sp", [(xt[t % NB].t[:], src.t.ap()[tt_ * 128:(tt_ + 1) * 128, :])], reads=[src], writes=[xt[t % NB]])

                ld(0); ld(1)
                for t in range(2 * NT):
                    own = t >= NT
                    tt_ = t - NT if own else t
                    if t + 2 < 2 * NT:
                        ld(t + 2)
                    i = t % NB
                    self.layernorm(xt[i], gB, bB, o32[i], (AT if own else HTp, HALF), tt_ * 128, t16[i], mk.bank(t % 2), st[i])
                    if own:
                        mk.dma("sp", [(res_d[tt_].t.ap(), o32[i].t[:])], reads=[o32[i]], writes=[res_d[tt_]])
            self.dump("hT", AT, AT.t[:], [128, 8, HALF], BF16)
            self.attention(w_in, w_sw, bin_fm, bsw_fm, bv_row, COS, SIN, flag, AT, HTp, attT, ones, maskpc)
            self.dump("attT", attT, attT.t[:], [128, 2, HALF], BF16)
            gyd = [mk.dram_tmp(f"gyd{j}", [128, HALF], BF16) for j in range(6)]
            self.s5(w_in, bin_fm, s5B, s5C, s5d, tabs, flag, AT, HTp, gyd)
            gy = mk.sbuf("gy", [128, 6, HALF], BF16)
            mk.dma("sp", [(gy.t[:, j, :], gyd[j].t.ap()) for j in range(6)], reads=gyd, writes=[gy])
            MIX = mk.sbuf("MIX", [128, 8, HALF], BF16)
            self.phaseE(w_in, bin_fm, w_glu, bglu_fm, w_au, AT, gy, attT, MIX)
            self.dump("MIX", MIX, MIX.t[:], [128, 8, HALF], BF16)
            self.proj_ln(MIX, w_mo, bmo_row, 1, AT, res_d, ln_params, st, ones, NB=4)
        self.dump("h1T", AT, AT.t[:], [128, 8, HALF], BF16)
        self.xattn(mem, w_xq, w_xkv, w_xo, AT, res_d, ln_params, st, ones)
        self.dump("h2T", AT, AT.t[:], [128, 8, HALF], BF16)
        self.ffn(w_f1, bf1_fm, w_f2, bf2_row, AT, res_d, ln_params, st, ones, out)
        mk.finish()

    def phaseE(self, w_in, bin_fm, w_glu, bglu_fm, w_au, AT, gy, attT, MIX):
        mk = self.mk
        with mk.phase():
            wg1 = [mk.sbuf(f"wg1{i}", [128, 6, 128], BF16) for i in range(2)]
            wg2 = [mk.sbuf(f"wg2{i}", [128, 6, 128], BF16) for i in range(2)]
            wgs = [mk.sbuf(f"wgs{i}", [128, 8, 128], BF16) for i in range(2)]
            wga = [mk.sbuf(f"wga{i}", [128, 8, 128], BF16) for i in range(2)]
            wau = [mk.sbuf(f"wau{i}", [128, 2, 128], BF16) for i in range(2)]
            binb = mk.sbuf("binb", [128, 40], F32); bglu = mk.sbuf("bglu", [128, 16], F32)
            mk.dma("sp", [(binb.t[:], bin_fm.t.ap())], reads=[bin_fm], writes=[binb])
            mk.dma("sp", [(bglu.t[:], bglu_fm.t.ap())], reads=[bglu_fm], writes=[bglu])
            tmp = [[mk.sbuf(f"e{k}{i}", [128, 512], F32) for k in range(5)] for i in range(2)]
            it = 0
            def ldw(mt):
                self.wload_into(wg1[mt % 2], w_glu, 768, mt * 128, 128)
                self.wload_into(wg2[mt % 2], w_glu, 768, 1024 + mt * 128, 128)
                self.wload_into(wgs[mt % 2], w_in, 1024, 3072 + mt * 128, 128)
                self.wload_into(wga[mt % 2], w_in, 1024, 4096 + mt * 128, 128)
                self.wload_into(wau[mt % 2], w_au, 256, mt * 128, 128)

            ldw(0)
            for mt in range(8):
                w1, w2, w3, w4, w5 = wg1[mt % 2], wg2[mt % 2], wgs[mt % 2], wga[mt % 2], wau[mt % 2]
                if mt + 1 < 8:
                    ldw(mt + 1)
                for blk in range(4):
                    bs = slice(blk * 512, (blk + 1) * 512)
                    sg2, t1, sgs, sga, t2 = tmp[it % 2]
                    it += 1
                    bset = 4 * (it % 2)
                    pz1, pz2, pgs, pga = [mk.bank(bset + i) for i in range(4)]
                    pba = pz2
                    for kt in range(6):
                        self.mm(pz1.t[:], w1.t[:, kt, :], gy.t[:, kt, bs], kt == 0, kt == 5, [w1, gy], [pz1])
                    for kt in range(6):
                        self.mm(pz2.t[:], w2.t[:, kt, :], gy.t[:, kt, bs], kt == 0, kt == 5, [w2, gy], [pz2])
                    for kt in range(8):
                        self.mm(pgs.t[:], w3.t[:, kt, :], AT.t[:, kt, bs], kt == 0, kt == 7, [w3, AT], [pgs])
                    for kt in range(8):
                        self.mm(pga.t[:], w4.t[:, kt, :], AT.t[:, kt, bs], kt == 0, kt == 7, [w4, AT], [pga])
                    self.act(sg2.t[:], pz2.t[:], AF.Sigmoid, [pz2, bglu], [sg2], bias=bglu.t[:, 8 + mt:9 + mt])
                    for kt in range(2):
                        self.mm(pba.t[:], w5.t[:, kt, :], attT.t[:, kt, bs], kt == 0, kt == 1, [w5, attT], [pba])
                    self.stt(t1.t[:], pz1.t[:], bglu.t[:, mt:mt + 1], sg2.t[:], ALU.add, ALU.mult, [pz1, bglu, sg2], [t1])
                    self.act(sgs.t[:], pgs.t[:], AF.Sigmoid, [pgs, binb], [sgs], bias=binb.t[:, 24 + mt:25 + mt])
                    self.act(sga.t[:], pga.t[:], AF.Sigmoid, [pga, binb], [sga], bias=binb.t[:, 32 + mt:33 + mt])
                    self.tt("pool", t1.t[:], t1.t[:], sgs.t[:], ALU.mult, [t1, sgs], [t1])
                    self.tt("dve", t2.t[:], pba.t[:], sga.t[:], ALU.mult, [pba, sga], [t2])
                    self.tt("pool", MIX.t[:, mt, bs], t1.t[:], t2.t[:], ALU.add, [t1, t2], [MIX])

    def proj_ln(self, X, w, brow_d, ln_idx, AT, res_d, ln_params, st, ones, NB=3):
        mk = self.mk
        with mk.phase():
            W = self.wload("W", w, 1024, 0, 1024)
            gB = mk.sbuf("gB", [128, 1024], F32); bB = mk.sbuf("bB", [128, 1024], F32)
            ln_params(ln_idx, gB, bB)
            if brow_d is not None:
                brow = mk.sbuf("brow", [1, 1024], BF16)
                mk.dma("pool", [(brow.t[:], brow_d.t.ap())], reads=[brow_d], writes=[brow])
            rt = [mk.sbuf(f"rt{i}", [128, 1024], F32) for i in range(NB)]
            o32 = [mk.sbuf(f"o32{i}", [128, 1024], F32) for i in range(NB)]
            t16 = [mk.sbuf(f"t16{i}", [128, 1024], BF16) for i in range(NB)]

            def ld(t):
                mk.dma("sp", [(rt[t % NB].t[:], res_d[t].t.ap())], reads=[res_d[t]], writes=[rt[t % NB]])

            ld(0); ld(1)
            for t in range(NT):
                i = t % NB
                ts_ = slice(t * 128, (t + 1) * 128)
                if t + 2 < NT:
                    ld(t + 2)
                for half in range(2):
                    hs = slice(half * 512, (half + 1) * 512)
                    ps = mk.bank(2 * (t % 3) + half)
                    for kt in range(8):
                        self.mm(ps.t[:], X.t[:, kt, ts_], W.t[:, kt, hs], kt == 0, (kt == 7 and brow_d is None), [X, W], [ps])
                    if brow_d is not None:
                        self.mm(ps.t[:], ones[0:1, :], brow.t[0:1, hs], False, True, [self.cbf, brow], [ps])
                    self.stt(rt[i].t[:, hs], rt[i].t[:, hs], ALPHA, ps.t[:], ALU.mult, ALU.add, [rt[i], ps], [rt[i]])
                self.layernorm(rt[i], gB, bB, o32[i], (AT, HALF), t * 128, t16[i], mk.bank(6 + t % 2), st[t % 8])
                mk.dma("sp", [(res_d[t].t.ap(), o32[i].t[:])], reads=[o32[i]], writes=[res_d[t]])

    def xattn(self, mem, w_xq, w_xkv, w_xo, AT, res_d, ln_params, st, ones):
        mk = self.mk
        ident = self.ident
        with mk.phase():
            KmT = mk.sbuf("KmT", [128, 8, 256], BF16)
            Vm = mk.sbuf("Vm", [128, 2, 1024], BF16)
            OX = mk.sbuf("OX", [128, 8, HALF], BF16)
            with mk.phase():
                memT = mk.sbuf("memT", [128, 8, 256], BF16)
                mt32 = mk.sbuf("mt32", [128, 1024], F32); mt16 = mk.sbuf("mt16", [128, 1024], BF16)
                for mtile in range(2):
                    mk.dma("sp", [(mt32.t[:], mem.t.ap()[mtile * 128:(mtile + 1) * 128, :])], reads=[mem], writes=[mt32])
                    self.copy("dve", mt16.t[:], mt32.t[:], [mt32], [mt16])
                    pst = mk.bank(0)
                    psb = pst.t[:].bitcast(BF16)
                    for kt in range(8):
                        mk.op("pe", lambda e, kt=kt, psb=psb: e.transpose(psb[:, kt * 128:(kt + 1) * 128], mt16.t[:, kt * 128:(kt + 1) * 128], ident.t[:]),
                              reads=[mt16, ident], writes=[pst])
                    self.copy("act", apx(memT, 0, 128, mtile * 128, [(256, 8), (1, 128)]), psb.rearrange("p (k c) -> p k c", k=8), [pst], [memT])
                wk = self.wload("wk", w_xkv, 1024, 0, 1024)
                for mt in range(8):
                    ps = mk.bank(1 + mt % 2)
                    for kt in range(8):
                        self.mm(ps.t[:, 0:256], wk.t[:, kt, mt * 128:(mt + 1) * 128], memT.t[:, kt, :], kt == 0, kt == 7, [wk, memT], [ps])
                    self.copy("act" if mt % 2 else "dve", KmT.t[:, mt, :], ps.t[:, 0:256], [ps], [KmT])
                wv = self.wload("wvx", w_xkv, 1024, 1024, 1024)
                for mtile in range(2):
                    for half in range(2):
                        ps = mk.bank(3 + half)
                        for kt in range(8):
                            self.mm(ps.t[:], memT.t[:, kt, mtile * 128:(mtile + 1) * 128], wv.t[:, kt, half * 512:(half + 1) * 512], kt == 0, kt == 7, [memT, wv], [ps])
                        self.copy("act" if half else "dve", Vm.t[:, mtile, half * 512:(half + 1) * 512], ps.t[:], [ps], [Vm])
            with mk.phase():
                QX = mk.sbuf("QX", [128, 8, HALF], BF16)
                wq = self.wload("wxq", w_xq, 1024, 0, 1024)
                for blk in range(4):
                    bs = slice(blk * 512, (blk + 1) * 512)
                    for mt in range(8):
                        ps = mk.bank((blk * 8 + mt) % 4)
                        for kt in range(8):
                            self.mm(ps.t[:], wq.t[:, kt, mt * 128:(mt + 1) * 128], AT.t[:, kt, bs], kt == 0, kt == 7, [wq, AT], [ps])
                        self.copy("act" if mt % 2 else "dve", QX.t[:, mt, bs], ps.t[:], [ps], [QX])
                PTx = [[mk.sbuf(f"PTx{i}{m}", [128, 512], BF16) for m in range(2)] for i in range(2)]
                rden = [mk.sbuf(f"rden{i}", [128, 512], F32) for i in range(2)]
                it = 0
                for blk in range(4):
                    bs = slice(blk * 512, (blk + 1) * 512)
                    for h in range(4):
                        i = it % 2
                        it += 1
                        for mtile in range(2):
                            pS = mk.bank(2 * (it % 2) + mtile)
                            for j in range(2):
                                self.mm(pS.t[:], KmT.t[:, 2 * h + j, mtile * 128:(mtile + 1) * 128], QX.t[:, 2 * h + j, bs], j == 0, j == 1, [KmT, QX], [pS])
                            self.act(PTx[i][mtile].t[:], pS.t[:], AF.Exp, [pS], [PTx[i][mtile]], scale=1.0 / 16.0)
                        pD = mk.bank(4 + it % 2)
                        for mtile in range(2):
                            self.mm(pD.t[:], ones, PTx[i][mtile].t[:], mtile == 0, mtile == 1, [self.cbf, PTx[i][mtile]], [pD])
                        mk.op("dve", lambda e, i=i, pD=pD: e.reciprocal(rden[i].t[:], pD.t[:]), reads=[pD], writes=[rden[i]], cost=0.7)
                        for j in range(2):
                            pO = mk.bank(6 + j)
                            for mtile in range(2):
                                self.mm(pO.t[:], Vm.t[:, mtile, (2 * h + j) * 128:(2 * h + j + 1) * 128], PTx[i][mtile].t[:], mtile == 0, mtile == 1, [Vm, PTx[i][mtile]], [pO])
                            self.tt("dve", OX.t[:, 2 * h + j, bs], pO.t[:], rden[i].t[:], ALU.mult, [pO, rden[i]], [OX])
            self.dump("OX", OX, OX.t[:], [128, 8, HALF], BF16)
            self.proj_ln(OX, w_xo, None, 2, AT, res_d, ln_params, st, ones, NB=6)

    def ffn(self, w_f1, bf1_fm, w_f2, bf2_row, AT, res_d, ln_params, st, ones, out):
        mk = self.mk
        with mk.phase():
            acc = [mk.sbuf(f"acc{t}", [128, 1024], F32) for t in range(NT)]
            bf1 = mk.sbuf("bf1", [128, 32], F32)
            brow = mk.sbuf("brow2", [1, 1024], BF16)
            mk.dma("sp", [(bf1.t[:], bf1_fm.t.ap())], reads=[bf1_fm], writes=[bf1])
            mk.dma("pool", [(brow.t[:], bf2_row.t.ap())], reads=[bf2_row], writes=[brow])
            for t in range(NT):
                mk.dma("sp", [(acc[t].t[:], res_d[t].t.ap())], reads=[res_d[t]], writes=[acc[t]])
                mk.op("act", lambda e, t=t: e.mul(acc[t].t[:], acc[t].t[:], ALPHA), reads=[acc[t]], writes=[acc[t]])
            with mk.phase():
                W1 = [mk.sbuf(f"W1{i}", [128, 8, 512], BF16) for i in range(2)]
                W2 = [mk.sbuf(f"W2{i}", [128, 4, 1024], BF16) for i in range(2)]
                hid = [mk.sbuf(f"hid{i}", [128, 4, 512], BF16) for i in range(2)]
                tf_ = [mk.sbuf(f"tf{i}", [128, 512], F32) for i in range(2)]
                hi = 0
                oi = 0
                def ldw(c):
                    self.wload_into(W1[c % 2], w_f1, 1024, c * 512, 512)
                    mk.dma("pool", [(W2[c % 2].t[:], w_f2.t.ap()[c * 512:(c + 1) * 512, :].rearrange("(kt p) c -> p kt c", p=128))], reads=[w_f2], writes=[W2[c % 2]])

                ldw(0)
                for c in range(8):
                    w1 = W1[c % 2]; w2 = W2[c % 2]
                    if c + 1 < 8:
                        ldw(c + 1)
                    for blk in range(4):
                        bs = slice(blk * 512, (blk + 1) * 512)
                        hb = hid[hi % 2]
                        hi += 1
                        for ft in range(4):
                            pH = mk.bank(ft % 2)
                            for kt in range(8):
                                self.mm(pH.t[:], w1.t[:, kt, ft * 128:(ft + 1) * 128], AT.t[:, kt, bs], kt == 0, kt == 7, [w1, AT], [pH])
                            tb = tf_[ft % 2]
                            self.act(tb.t[:], pH.t[:], AF.Relu, [pH, bf1], [tb], bias=bf1.t[:, c * 4 + ft:c * 4 + ft + 1])
                            self.tt("pool", hb.t[:, ft, :], tb.t[:], tb.t[:], ALU.mult, [tb], [hb])
                        for tl in range(4):
                            T = blk * 4 + tl
                            for half in range(2):
                                hs = slice(half * 512, (half + 1) * 512)
                                pO = mk.bank(2 + oi % 6)
                                oi += 1
                                for ft in range(4):
                                    self.mm(pO.t[:], hb.t[:, ft, tl * 128:(tl + 1) * 128], w2.t[:, ft, hs], ft == 0, (ft == 3 and c != 0), [hb, w2], [pO])
                                if c == 0:
                                    self.mm(pO.t[:], ones[0:1, :], brow.t[0:1, hs], False, True, [self.cbf, brow], [pO])
                                self.tt("dve", acc[T].t[:, hs], acc[T].t[:, hs], pO.t[:], ALU.add, [acc[T], pO], [acc[T]])
            with mk.phase():
                gB = mk.sbuf("gB", [128, 1024], F32); bB = mk.sbuf("bB", [128, 1024], F32)
                ln_params(3, gB, bB)
                o32 = [mk.sbuf(f"o32{i}", [128, 1024], F32) for i in range(6)]
                for t in range(NT):
                    i = t % 6
                    self.layernorm(acc[t], gB, bB, o32[i], None, 0, None, None, st[t % 8])
                    mk.dma("sp", [(out.t.ap()[t * 128:(t + 1) * 128, :], o32[i].t[:])], reads=[o32[i]], writes=[out])

    def sin_of(self, out, ang, tmpi, tmpf, tmpm, eng="dve", out_ap=None):
        PI = float(np.pi)
        self.ts(eng, tmpi.t[:], ang.t[:], 1.0 / TWO_PI, None, ALU.mult, None, [ang], [tmpi])
        self.copy(eng, tmpf.t[:], tmpi.t[:], [tmpi], [tmpf])
        self.ts(eng, tmpf.t[:], tmpf.t[:], -TWO_PI, None, ALU.mult, None, [tmpf], [tmpf])
        self.tt(eng, tmpf.t[:], tmpf.t[:], ang.t[:], ALU.add, [tmpf, ang], [tmpf])
        self.ts(eng, tmpm.t[:], tmpf.t[:], PI, -TWO_PI, ALU.is_gt, ALU.mult, [tmpf], [tmpm])
        self.tt(eng, tmpf.t[:], tmpf.t[:], tmpm.t[:], ALU.add, [tmpf, tmpm], [tmpf])
        self.ts(eng, tmpm.t[:], tmpf.t[:], -PI, TWO_PI, ALU.is_lt, ALU.mult, [tmpf], [tmpm])
        self.tt(eng, tmpf.t[:], tmpf.t[:], tmpm.t[:], ALU.add, [tmpf, tmpm], [tmpf])
        self.ts(eng, tmpf.t[:], tmpf.t[:], 3.14159, -3.14159, ALU.min, ALU.max, [tmpf], [tmpf])
        self.act(out.t[:] if out_ap is None else out_ap, tmpf.t[:], AF.Sin, [tmpf], [out])

    def rope_tables(self, posb, COS, SIN):
        mk = self.mk
        cf = self.cst
        CH = 512
        pi_ = mk.sbuf("posi", [128, CH], I32)
        ang = mk.sbuf("ang", [128, CH], F32); sc = mk.sbuf("sc", [128, CH], F32)
        ti = mk.sbuf("ti", [128, CH], I32); tf = mk.sbuf("tf", [128, CH], F32); tm = mk.sbuf("tm", [128, CH], F32)
        s16 = mk.sbuf("s16", [128, CH], BF16); c16 = mk.sbuf("c16", [128, CH], BF16)
        mk.dma("sp", [(pi_.t[:], posb.t.ap())], reads=[posb], writes=[pi_])
        self.copy("dve", ang.t[:], pi_.t[:], [pi_], [ang])
        self.ts("dve", ang.t[:], ang.t[:], cf.t[:, 3:4], None, ALU.mult, None, [ang, cf], [ang])
        self.sin_of(sc, ang, ti, tf, tm)
        self.ts("dve", s16.t[:], sc.t[:], cf.t[:, 4:5], None, ALU.mult, None, [sc, cf], [s16])
        self.ts("dve", ang.t[:], ang.t[:], float(np.pi / 2), None, ALU.add, None, [ang], [ang])
        self.sin_of(c16, ang, ti, tf, tm)
        mk.dma("sp", [(SIN.t.ap().rearrange("q (c j) -> q c j", c=8)[:, c, :], s16.t[16 * c:16 * c + 16, :]) for c in range(8)], reads=[s16], writes=[SIN])
        mk.dma("sp", [(COS.t.ap().rearrange("q (c j) -> q c j", c=8)[:, c, :], c16.t[16 * c:16 * c + 16, :]) for c in range(8)], reads=[c16], writes=[COS])

    def s5_prep(self, s5p, s5C, tabs):
        mk = self.mk
        cf = self.cst
        P = mk.sbuf("P", [128, 3 * G], F32)
        CRI = mk.sbuf("CRI", [128, 2 * G * 16], F32)
        for (b_, s_) in ((P, s5p), (CRI, s5C)):
            mk.dma("sp", [(b_.t[:], s_.t.ap())], reads=[s_], writes=[b_])
        n2 = G * ND
        PR = mk.sbuf("PR", [128, G, ND], F32); PI_ = mk.sbuf("PI", [128, G, ND], F32)
        ZA = mk.sbuf("ZA", [128, G, 8], F32); ZB = mk.sbuf("ZB", [128, G, 8], F32)
        QA = mk.sbuf("QA", [128, G, 8], F32); QB = mk.sbuf("QB", [128, G, 8], F32)
        WR = mk.sbuf("WR", [128, G, 12, 2], F32)
        Cneg = mk.sbuf("Cneg", [128, G * 16], BF16)
        dt = mk.sbuf("dt", [128, G], F32); lam = mk.sbuf("lam", [128, G], F32); th = mk.sbuf("th", [128, G], F32)
        ANG = mk.sbuf("ANG", [128, n2], F32); LAM = mk.sbuf("LAMb", [128, n2], F32)
        SN = mk.sbuf("SN", [128, n2], F32); CS = mk.sbuf("CS", [128, n2], F32)
        ti = mk.sbuf("ti", [128, n2], I32); tf = mk.sbuf("tf", [128, n2], F32); tm = mk.sbuf("tm", [128, n2], F32)
        self.act(dt.t[:], P.t[:, 0:G], AF.Exp, [P], [dt])
        self.tt("dve", lam.t[:], P.t[:, G:2 * G], dt.t[:], ALU.mult, [P, dt], [lam])
        self.tt("dve", th.t[:], P.t[:, 2 * G:3 * G], dt.t[:], ALU.mult, [P, dt], [th])
        dp = apx(cf, 0, 128, 16, [(0, G), (1, ND)])
        self.tt("dve", ANG.t[:].rearrange("p (g k) -> p g k", g=G), apx(th, 0, 128, 0, [(1, G), (0, ND)]), dp, ALU.mult, [th, cf], [ANG])
        self.tt("dve", LAM.t[:].rearrange("p (g k) -> p g k", g=G), apx(lam, 0, 128, 0, [(1, G), (0, ND)]), dp, ALU.mult, [lam, cf], [LAM])
        self.act(LAM.t[:], LAM.t[:], AF.Exp, [LAM], [LAM])
        self.sin_of(SN, ANG, ti, tf, tm)
        self.ts("dve", ANG.t[:], ANG.t[:], float(np.pi / 2), None, ALU.add, None, [ANG], [ANG])
        self.sin_of(CS, ANG, ti, tf, tm)
        prf = PR.t[:].rearrange("p g k -> p (g k)"); pif = PI_.t[:].rearrange("p g k -> p (g k)")
        self.tt("dve", prf, LAM.t[:], CS.t[:], ALU.mult, [LAM, CS], [PR])
        self.tt("dve", pif, LAM.t[:], SN.t[:], ALU.mult, [LAM, SN], [PI_])
        nr = mk.sbuf("nr", [128, G], F32); den = mk.sbuf("den", [128, G], F32); t1 = mk.sbuf("t1", [128, G], F32)
        fr = mk.sbuf("fr", [128, G], F32); fi = mk.sbuf("fi", [128, G], F32)
        are = P.t[:, G:2 * G]; aim = P.t[:, 2 * G:3 * G]
        pr1 = apx(PR, 0, 128, 1, [(ND, G)]); pi1 = apx(PI_, 0, 128, 1, [(ND, G)])
        self.ts("dve", nr.t[:], pr1, -1.0, None, ALU.add, None, [PR], [nr])
        self.tt("dve", den.t[:], are, are, ALU.mult, [P], [den])
        self.tt("dve", t1.t[:], aim, aim, ALU.mult, [P], [t1])
        self.tt("dve", den.t[:], den.t[:], t1.t[:], ALU.add, [den, t1], [den])
        mk.op("dve", lambda e: e.reciprocal(den.t[:], den.t[:]), reads=[den], writes=[den])
        self.tt("dve", fr.t[:], nr.t[:], are, ALU.mult, [nr, P], [fr])
        self.tt("dve", t1.t[:], pi1, aim, ALU.mult, [PI_, P], [t1])
        self.tt("dve", fr.t[:], fr.t[:], t1.t[:], ALU.add, [fr, t1], [fr])
        self.tt("dve", fr.t[:], fr.t[:], den.t[:], ALU.mult, [fr, den], [fr])
        self.tt("dve", fi.t[:], pi1, are, ALU.mult, [PI_, P], [fi])
        self.tt("dve", t1.t[:], nr.t[:], aim, ALU.mult, [nr, P], [t1])
        self.tt("dve", fi.t[:], fi.t[:], t1.t[:], ALU.subtract, [fi, t1], [fi])
        self.tt("dve", fi.t[:], fi.t[:], den.t[:], ALU.mult, [fi, den], [fi])
        ZR = mk.sbuf("ZR", [128, G, 8], F32); ZI = mk.sbuf("ZI", [128, G, 8], F32); T8 = mk.sbuf("T8", [128, G, 8], F32)
        pr8 = apx(PR, 0, 128, 0, [(ND, G), (1, 8)]); pi8 = apx(PI_, 0, 128, 0, [(ND, G), (1, 8)])
        frb = apx(fr, 0, 128, 0, [(1, G), (0, 8)]); fib = apx(fi, 0, 128, 0, [(1, G), (0, 8)])
        self.tt("dve", ZR.t[:], pr8, frb, ALU.mult, [PR, fr], [ZR])
        self.tt("dve", T8.t[:], pi8, fib, ALU.mult, [PI_, fi], [T8])
        self.tt("dve", ZR.t[:], ZR.t[:], T8.t[:], ALU.subtract, [ZR, T8], [ZR])
        self.tt("dve", ZI.t[:], pr8, fib, ALU.mult, [PR, fi], [ZI])
        self.tt("dve", T8.t[:], pi8, frb, ALU.mult, [PI_, fr], [T8])
        self.tt("dve", ZI.t[:], ZI.t[:], T8.t[:], ALU.add, [ZI, T8], [ZI])
        U_, L_ = slice(0, 64), slice(64, 128)
        self.copy("dve", ZA.t[U_], ZR.t[U_], [ZR], [ZA]); self.copy("dve", ZA.t[L_], ZI.t[L_], [ZI], [ZA])
        self.ts("dve", ZB.t[U_], ZI.t[U_], -1.0, None, ALU.mult, None, [ZI], [ZB]); self.copy("dve", ZB.t[L_], ZR.t[L_], [ZR], [ZB])
        pr18 = lambda sl: apx(PR, sl.start, 64, 1, [(ND, G), (1, 8)])
        pi18 = lambda sl: apx(PI_, sl.start, 64, 1, [(ND, G), (1, 8)])
        self.copy("dve", QA.t[U_], pr18(U_), [PR], [QA]); self.ts("dve", QA.t[L_], pi18(L_), -1.0, None, ALU.mult, None, [PI_], [QA])
        self.ts("dve", QB.t[U_], pi18(U_), -1.0, None, ALU.mult, None, [PI_], [QB]); self.ts("dve", QB.t[L_], pr18(L_), -1.0, None, ALU.mult, None, [PR], [QB])
        wro = lambda sl, h: apx(WR, sl.start, 64, h, [(24, G), (2, 12)])
        prs = lambda sl: apx(PR, sl.start, 64, 8, [(ND, G), (1, 12)])
        pis = lambda sl: apx(PI_, sl.start, 64, 8, [(ND, G), (1, 12)])
        self.copy("dve", wro(U_, 0), prs(U_), [PR], [WR]); self.copy("dve", wro(U_, 1), pis(U_), [PI_], [WR])
        self.ts("dve", wro(L_, 0), pis(L_), -1.0, None, ALU.mult, None, [PI_], [WR]); self.copy("dve", wro(L_, 1), prs(L_), [PR], [WR])
        self.copy("dve", Cneg.t[U_], CRI.t[U_, 0:G * 16], [CRI], [Cneg])
        self.ts("dve", Cneg.t[L_], CRI.t[L_, G * 16:2 * G * 16], -1.0, None, ALU.mult, None, [CRI], [Cneg])

        for nm, b_ in (("ZA", ZA), ("ZB", ZB), ("QA", QA), ("QB", QB), ("WR", WR), ("Cneg", Cneg)):
            mk.dma("sp", [(tabs[nm].t.ap(), b_.t[:])], reads=[b_], writes=[tabs[nm]])

    def s5(self, w_in, bin_fm, s5B, s5C, s5d, tabs, flag, AT, HTp, gyd):
        self.pe_scale = 1.8
        try:
            self._s5(w_in, bin_fm, s5B, s5C, s5d, tabs, flag, AT, HTp, gyd)
        finally:
            self.pe_scale = 1.0

    def _s5(self, w_in, bin_fm, s5B, s5C, s5d, tabs, flag, AT, HTp, gyd):
        mk = self.mk
        cf = self.cst
        ident = self.ident
        with mk.phase():
            BRI = mk.sbuf("BRI", [128, 2 * G * 16], F32)
            CRI = mk.sbuf("CRI", [128, 2 * G * 16], F32)
            dd = mk.sbuf("dd", [128, 6], F32)
            binb = mk.sbuf("binb", [128, 40], F32)
            for (b_, s_) in ((BRI, s5B), (CRI, s5C), (dd, s5d), (binb, bin_fm)):
                mk.dma("sp", [(b_.t[:], s_.t.ap())], reads=[s_], writes=[b_])
            ZA = mk.sbuf("ZA", [128, G, 8], F32); ZB = mk.sbuf("ZB", [128, G, 8], F32)
            QA = mk.sbuf("QA", [128, G, 8], F32); QB = mk.sbuf("QB", [128, G, 8], F32)
            WR = mk.sbuf("WR", [128, G, 12, 2], F32)
            Cneg = mk.sbuf("Cneg", [128, G * 16], BF16)
            for nm, b_ in (("ZA", ZA), ("ZB", ZB), ("QA", QA), ("QB", QB), ("WR", WR), ("Cneg", Cneg)):
                mk.dma("sp", [(b_.t[:], tabs[nm].t.ap())], reads=[tabs[nm]], writes=[b_])

            wu = [mk.sbuf(f"wu{i}", [128, 8, 128], BF16) for i in range(2)]
            u_ = [mk.sbuf(f"u{i}", [128, SEQ], BF16) for i in range(2)]
            gys = [mk.sbuf(f"gys{i}", [128, HALF], BF16) for i in range(2)]
            Gf = mk.sbuf("Gf", [128, 1024], F32); Gf2 = mk.sbuf("Gf2", [128, 1024], F32)
            Gall = mk.sbuf("Gall", [128, 8, 128], BF16)
            Pm = mk.sbuf("Pm", [128, 8, 8, 128], BF16)
            Toep_ = [mk.sbuf(f"Toep{i}", [128, 8, 128], BF16) for i in range(2)]
            tK = mk.sbuf("tK", [128, 4, 128], F32); Dg = mk.sbuf("Dg", [128, 128], F32)
            Qp = mk.sbuf("Qp", [128, 8, 8, 64], BF16)
            qa = mk.sbuf("qa", [128, 256], F32); qb = mk.sbuf("qb", [128, 256], F32)
            NS = 4
            Rot = [mk.sbuf(f"Rot{i}", [128, 12, 128], BF16) for i in range(NS)]
            Xp_ = [mk.sbuf(f"Xp{i}", [128, 256], BF16) for i in range(NS)]
            Sp_ = [[mk.sbuf(f"Sp{k}{i}", [128, 64], BF16) for i in range(2)] for k in range(NS)]
            So_ = [[mk.sbuf(f"So{k}{i}", [128, 256], BF16) for i in range(2)] for k in range(NS)]
            Hext_ = [mk.sbuf(f"Hext{i}", [128, 8, 257], BF16) for i in range(2)]
            xs = mk.sbuf("xs", [128, 256], F32); x2 = mk.sbuf("x2", [128, 256], F32); sg = mk.sbuf("sg", [128, 256], F32)
            evq = [0]

            def evac(out, in_, reads, writes, scale=None):
                evq[0] += 1
                if scale is not None or evq[0] % 2 == 0:
                    if scale is not None:
                        self.act(out, in_, AF.Copy, reads, writes, scale=scale)
                    else:
                        self.copy("act", out, in_, reads, writes)
                else:
                    self.copy("dve", out, in_, reads, writes)

            for j in range(6):
                g0 = 8 * j
                w = wu[j % 2]
                u = u_[j % 2]; Toep = Toep_[j % 2]; Hext = Hext_[j % 2]; gyj = gys[j % 2]
                self.wload_into(w, w_in, 1024, 128 * j, 128)
                for blk in range(8):
                    src = HTp if blk < 4 else AT
                    c0 = (blk % 4) * 512
                    ps = mk.bank(2 + blk % 2)
                    for kt in range(8):
                        self.mm(ps.t[:], w.t[:, kt, :], src.t[:, kt, c0:c0 + 512], kt == 0, kt == 7, [w, src], [ps])
                    self.act(apx(u, 0, 128, (blk // 4) * HALF + (blk % 4) * 64, [(256, 8), (1, 64)]), apx(ps, 0, 128, 0, [(1, 8), (8, 64)]),
                             AF.Identity, [ps, binb], [u], bias=binb.t[:, j:j + 1])
                if j == 0:
                    self.dump("u0", u, u.t[:], [128, SEQ], BF16)
                za = apx(ZA, 0, 128, g0 * 8, [(1, 8), (8, 8), (0, 16)]); zb = apx(ZB, 0, 128, g0 * 8, [(1, 8), (8, 8), (0, 16)])
                br = apx(BRI, 0, 128, g0 * 16, [(0, 8), (16, 8), (1, 16)]); bi = apx(BRI, 0, 128, G * 16 + g0 * 16, [(0, 8), (16, 8), (1, 16)])
                gf4 = Gf.t[:].rearrange("p (d g c) -> p d g c", d=8, g=8); gf24 = Gf2.t[:].rearrange("p (d g c) -> p d g c", d=8, g=8)
                self.tt("dve", gf4, za, br, ALU.mult, [ZA, BRI], [Gf])
                self.tt("pool", gf24, zb, bi, ALU.mult, [ZB, BRI], [Gf2])
                self.tt("dve", Gall.t[:].rearrange("p d c -> p (d c)"), Gf.t[:], Gf2.t[:], ALU.add, [Gf, Gf2], [Gall])
                for hb in range(2):
                    pst = mk.bank(4)
                    pstb = pst.t[:].bitcast(BF16)
                    for dd_ in range(4):
                        d = hb * 4 + dd_
                        mk.op("pe", lambda e, d=d, dd_=dd_, pstb=pstb: e.transpose(pstb[:, dd_ * 128:(dd_ + 1) * 128], Gall.t[:, d, :], ident.t[:]),
                              reads=[Gall, ident], writes=[pst])
                    for gl in range(8):
                        o = apx(Pm, 0, 128, (hb * 4 * 8 + gl) * 128, [(8 * 128, 4), (1, 128)])
                        i_ = pstb[:, 0:512].rearrange("p (d c) -> p d c", d=4)
                        if gl % 2 == 0:
                            self.ts("dve", o, i_, cf.t[:, 8 + gl:9 + gl], None, ALU.mult, None, [pst, cf], [Pm])
                        else:
                            self.act(o, i_, AF.Copy, [pst, cf], [Pm], scale=cf.t[:, 8 + gl:9 + gl])
                    psk = mk.bank(5)
                    for dd_ in range(4):
                        d = hb * 4 + dd_
                        self.mm(psk.t[:, dd_ * 128:(dd_ + 1) * 128], Gall.t[:, d, :], Cneg.t[:, g0 * 16:g0 * 16 + 128], True, True, [Gall, Cneg], [psk])
                    self.tt("dve", tK.t[:], psk.t[:].rearrange("p (d c) -> p d c", d=4), apx(cf, 0, 128, 128, [(0, 4), (1, 128)]), ALU.mult, [psk, cf], [tK])
                    if hb == 0:
                        self.ts("dve", Dg.t[:], cf.t[:, 256:384], dd.t[:, j:j + 1], None, ALU.mult, None, [cf, dd], [Dg])
                        self.tt("dve", tK.t[:, 0, :], tK.t[:, 0, :], Dg.t[:], ALU.add, [tK, Dg], [tK])
                    self.copy("dve", Toep.t[:, hb * 4:(hb + 1) * 4, :], tK.t[:], [tK], [Toep])
                mk.op("pool", lambda e: e.memset(Qp.t[:], 0.0), reads=[], writes=[Qp])
                for q in range(4):
                    o = apx(Qp, 0, 128, q * 64 + 16 * q, [(512, 8), (256, 2), (1, 16)])
                    a0 = apx(QA, 0, 128, (g0 + q) * 8, [(1, 8), (32, 2), (0, 16)]); b0 = apx(QB, 0, 128, (g0 + q) * 8, [(1, 8), (32, 2), (0, 16)])
                    c0_ = apx(CRI, 0, 128, (g0 + q) * 16, [(0, 8), (64, 2), (1, 16)]); c1_ = apx(CRI, 0, 128, G * 16 + (g0 + q) * 16, [(0, 8), (64, 2), (1, 16)])
                    v3 = lambda b_: b_.t[:].rearrange("p (s k c) -> p s k c", s=8, k=2)
                    self.tt("dve", v3(qa), a0, c0_, ALU.mult, [QA, CRI], [qa])
                    self.tt("pool", v3(qb), b0, c1_, ALU.mult, [QB, CRI], [qb])
                    self.tt("dve", o, v3(qa), v3(qb), ALU.add, [qa, qb, Qp], [Qp])
                for gl in range(8):
                    g = g0 + gl
                    gp = gl % NS
                    R = Rot[gp]
                    Xp = Xp_[gp]; Sp = Sp_[gp]; So = So_[gp]
                    self.tt("pool", R.t[:].rearrange("p e (h c) -> p (e h) c", h=2), apx(cf, 0, 128, 384, [(0, 24), (1, 64)]),
                            apx(WR, 0, 128, g * 24, [(1, 24), (0, 64)]), ALU.mult, [cf, WR], [R])
                    ps = mk.bank(2 * gp)
                    for s in range(8):
                        self.mm(ps.t[:, 0:256], Pm.t[:, 7 - s, gl, :], apx(u, 0, 128, s * 256, [(1, 256)]), s == 0, s == 7, [Pm, u], [ps])
                    self.act(Xp.t[:], ps.t[:, 0:256], AF.Copy, [ps, flag], [Xp], scale=flag.t[:, 0:1])
                    S = Xp
                    for l in range(4):
                        N = 256 // 4 ** (l + 1)
                        ps2 = mk.bank(2 * gp + 1)
                        for e in range(4):
                            lhs = ident.t[:] if e == 0 else R.t[:, l * 3 + e - 1, :]
                            self.mm(ps2.t[:, 0:N], lhs, apx(S, 0, 128, 3 - e, [(4, N)]), e == 0, e == 3, [ident, R, S], [ps2])
                        if l < 3:
                            S2 = Sp[l % 2]
                            evac(S2.t[:, 0:N], ps2.t[:, 0:N], [ps2], [S2])
                            S = S2
                        else:
                            evac(Hext.t[:, gl, 0:1], ps2.t[:, 0:1], [ps2], [Hext])
                    ps = mk.bank(2 * gp)
                    for s in range(8):
                        self.mm(ps.t[:, 0:256], Pm.t[:, 7 - s, gl, :], apx(u, 0, 128, HALF + s * 256, [(1, 256)]), s == 0, False, [Pm, u], [ps])
                    self.mm(ps.t[:, 0:1], R.t[:, 0, :], Hext.t[:, gl, 0:1], False, True, [R, Hext], [ps])
                    S = So[0]
                    evac(S.t[:], ps.t[:, 0:256], [ps], [S])
                    for l in range(4):
                        d = 4 ** l
                        ps2 = mk.bank(2 * gp + 1)
                        self.mm(ps2.t[:, 0:256], ident.t[:], S.t[:, 0:256], True, False, [ident, S], [ps2])
                        for e in range(1, 4):
                            self.mm(ps2.t[:, e * d:256], R.t[:, l * 3 + e - 1, :], S.t[:, 0:256 - e * d], False, e == 3, [R, S], [ps2])
                        if l < 3:
                            S2 = So[(l + 1) % 2]
                            evac(S2.t[:], ps2.t[:, 0:256], [ps2], [S2])
                            S = S2
                        else:
                            evac(Hext.t[:, gl, 1:257], ps2.t[:, 0:256], [ps2], [Hext])
                if j == 0:
                    self.dump("Hext", Hext, Hext.t[:], [128, 8, 257], BF16)
                for s in range(8):
                    ps = mk.bank(2 + s % 2)
                    for gl in range(8):
                        hq = gl // 4
                        self.mm(ps.t[64 * hq:64 * hq + 64, 0:256], Qp.t[:, s, gl, :], Hext.t[:, gl, 0:256], gl % 4 == 0, False, [Qp, Hext], [ps])
                    for d in range(s + 1):
                        self.mm(ps.t[:, 0:256], Toep.t[:, d, :], apx(u, 0, 128, HALF + (s - d) * 256, [(1, 256)]), False, d == s, [Toep, u], [ps])
                    self.copy("act", xs.t[:], ps.t[:, 0:256], [ps], [xs])
                    self.tt("pool", x2.t[:], xs.t[:], xs.t[:], ALU.mult, [xs], [x2])
                    self.ts("pool", x2.t[:], x2.t[:], 2 * GELU_C * 0.044715, 2 * GELU_C, ALU.mult, ALU.add, [x2], [x2])
                    self.tt("pool", x2.t[:], x2.t[:], xs.t[:], ALU.mult, [x2, xs], [x2])
                    self.act(sg.t[:], x2.t[:], AF.Sigmoid, [x2], [sg])
                    self.tt("dve", apx(gyj, 0, 128, s, [(8, 256)]), xs.t[:], sg.t[:], ALU.mult, [xs, sg], [gyj])
                mk.dma("sp", [(gyd[j].t.ap(), gyj.t[:])], reads=[gyj], writes=[gyd[j]])

    def attention(self, w_in, w_sw, bin_fm, bsw_fm, bv_row, COS, SIN, flag, AT, HTp, attT, ones, maskpc):
        mk = self.mk
        cf = self.cst
        with mk.phase():
            binb = mk.sbuf("binb", [128, 40], F32); bswb = mk.sbuf("bswb", [128, 12], F32)
            bvb = mk.sbuf("bvb", [1, 768], BF16)
            mk.dma("sp", [(binb.t[:], bin_fm.t.ap())], reads=[bin_fm], writes=[binb])
            mk.dma("sp", [(bswb.t[:], bsw_fm.t.ap())], reads=[bsw_fm], writes=[bswb])
            mk.dma("pool", [(bvb.t[:], bv_row.t.ap())], reads=[bv_row], writes=[bvb])
            accN = mk.sbuf("accN", [128, 2, HALF], F32)
            accD = mk.sbuf("accD", [128, 2, HALF], F32)
            COSd, SINd = COS, SIN
            COS = mk.sbuf("COS", [128, SEQ], BF16); SIN = mk.sbuf("SIN", [128, SEQ], BF16)
            mk.op("pool", lambda e: e.memset(COS.t[:], 1.0), reads=[], writes=[COS], cost=5.0)
            mk.op("pool", lambda e: e.memset(SIN.t[:], 0.0), reads=[], writes=[SIN], cost=5.0)
            mk.dma("sp", [(COS.t[0:16, :], COSd.t.ap()), (COS.t[64:80, :], COSd.t.ap())], reads=[COSd], writes=[COS])
            mk.dma("sp", [(SIN.t[0:16, :], SINd.t.ap()), (SIN.t[64:80, :], SINd.t.ap())], reads=[SINd], writes=[SIN])
            wq = [mk.sbuf(f"wq{i}", [128, 8, 128], BF16) for i in range(2)]
            wqs = [mk.sbuf(f"wqs{i}", [128, 8, 128], BF16) for i in range(2)]
            wv = mk.sbuf("wv", [128, 8, 256], BF16)
            t1 = [mk.sbuf(f"rt1{i}", [128, 512], F32) for i in range(2)]
            t2 = [mk.sbuf(f"rt2{i}", [128, 512], F32) for i in range(2)]
            PT = [mk.sbuf(f"PT{i}", [128, 256], BF16) for i in range(3)]
            qraw = [mk.sbuf(f"qraw{i}", [128, 512], BF16) for i in range(2)]
            permT = self.cbf.t[:, 256:384]
            mask0 = mk.sbuf("mask0", [128, 256], BF16)
            self.copy("dve", mask0.t[:], maskpc, [self.cbf], [mask0])
            self.ts("dve", mask0.t[:, 0:128], mask0.t[:, 0:128], flag.t[:, 0:1], None, ALU.mult, None, [mask0, flag], [mask0])
            cnt = [0]
            wc = [0]

            def proj_rope(wa, wb, src, c0, bias_a, bias_b, tok0_tab, dst, oap, a0, n, dil):
                i = cnt[0] % 2
                cnt[0] += 1
                pa = mk.bank(2 * i); pb = mk.bank(2 * i + 1)
                for kt in range(8):
                    self.mm(pa.t[:], wa.t[:, kt, :], src.t[:, kt, c0:c0 + 512], kt == 0, kt == 7, [wa, src], [pa])
                self.act(qraw[i].t[:], pa.t[:], AF.Identity, [pa, binb], [qraw[i]], bias=bias_a)
                self.mm(pb.t[:], permT, qraw[i].t[:], True, True, [self.cbf, qraw[i]], [pb])
                self.tt("dve", t1[i].t[:], qraw[i].t[:], COS.t[:, tok0_tab:tok0_tab + 512], ALU.mult, [qraw[i], COS], [t1[i]])
                self.tt("dve", t2[i].t[:], pb.t[:], SIN.t[:, tok0_tab:tok0_tab + 512], ALU.mult, [pb, SIN], [t2[i]])
                if dil == 1:
                    i0 = t1[i].t[:, a0:a0 + n]; i1 = t2[i].t[:, a0:a0 + n]
                else:
                    i0 = apx(t1[i], 0, 128, a0, [(1, dil), (dil, n // dil)])
                    i1 = apx(t2[i], 0, 128, a0, [(1, dil), (dil, n // dil)])
                self.tt("pool", oap, i0, i1, ALU.add, [t1[i], t2[i]], [dst])

            def store_ap(dst, pt, L, dil, m0, n):
                if dil == 1:
                    return dst.t[:, pt, m0:m0 + n]
                return apx(dst, 0, 128, pt * dil * L + m0, [(L, dil), (1, n // dil)])

            it = 0
            vi = 0
            qbi = [0]
            for g in range(3):
                dil = DILS[g]
                Lq = HALF // dil
                Lr = 128 + HALF // dil
                nkb = 1 + 16 // dil
                nq = 16 // dil
                with mk.phase():
                    qT = mk.sbuf(f"qT{g}", [128, 2, HALF], BF16)
                    kT = mk.sbuf(f"kT{g}", [128, 2, dil * Lr], BF16)
                    V = mk.sbuf(f"V{g}", [128, dil * nkb, 256], BF16)
                    for pt in range(2):
                        mt = 2 * g + pt
                        wa = wq[wc[0] % 2]; wb = wqs[wc[0] % 2]; wc[0] += 1
                        self.wload_into(wa, w_in, 1024, 768 + 128 * mt, 128)
                        for blk in range(4):
                            oap = store_ap(qT, pt, Lq, dil, blk * 512 // dil, 512)
                            proj_rope(wa, wb, AT, blk * 512, binb.t[:, 6 + mt:7 + mt], bswb.t[:, mt:mt + 1], HALF + blk * 512,
                                      qT, oap, 0, 512, dil)
                        wa = wq[wc[0] % 2]; wb = wqs[wc[0] % 2]; wc[0] += 1
                        self.wload_into(wa, w_in, 1024, 1536 + 128 * mt, 128)
                        for blk in range(8):
                            prev = blk < 4
                            src = HTp if prev else AT
                            if prev:
                                lo = HALF - 128 * dil
                                b0 = blk * 512
                                if b0 + 512 <= lo:
                                    continue
                                a0 = max(lo, b0) - b0
                                n = 512 - a0
                                m0 = (b0 + a0 - lo) // dil
                            else:
                                a0, n = 0, 512
                                m0 = 128 + (blk - 4) * 512 // dil
                            oap = store_ap(kT, pt, Lr, dil, m0, n)
                            proj_rope(wa, wb, src, (blk % 4) * 512, binb.t[:, 12 + mt:13 + mt], bswb.t[:, 6 + mt:7 + mt], blk * 512,
                                      kT, oap, a0, n, dil)
                    self.wload_into(wv, w_in, 1024, 2304 + 256 * g, 256)
                    for r in range(dil):
                        for b in range(nkb):
                            if b == 0:
                                src = HTp; start = HALF - 128 * dil + r
                            else:
                                src = AT; start = dil * 128 * (b - 1) + r
                            ps = mk.bank(4 + vi % 2)
                            vi += 1
                            for kt in range(8):
                                lhs = apx(src, 0, 128, kt * HALF + start, [(dil, 128)])
                                self.mm(ps.t[:, 0:256], lhs, wv.t[:, kt, :], kt == 0, False, [src, wv], [ps])
                            self.mm(ps.t[:, 0:256], ones[0:1, :], bvb.t[0:1, 256 * g:256 * g + 256], False, True, [self.cbf, bvb], [ps])
                            dst = V.t[:, r * nkb + b, :]
                            if b == 0:
                                self.act(dst, ps.t[:, 0:256], AF.Copy, [ps, flag], [V], scale=flag.t[:, 0:1])
                            elif vi % 2:
                                self.copy("act", dst, ps.t[:, 0:256], [ps], [V])
                            else:
                                self.copy("dve", dst, ps.t[:, 0:256], [ps], [V])
                    if g == 1:
                        self.dump("qT1", qT, qT.t[:], [128, 2, HALF], BF16)
                        self.dump("kT1", kT, kT.t[:], [128, 2, dil * Lr], BF16)
                        self.dump("V1", V, V.t[:], [128, dil * nkb, 256], BF16)
                    for pt in range(2):
                        for r in range(dil):
                            for qb in range(nq):
                                pN = mk.bank(4 + 2 * (qbi[0] % 2)); pD = mk.bank(5 + 2 * (qbi[0] % 2)); qbi[0] += 1
                                for hp in range(2):
                                    rows = slice(64 * hp, 64 * hp + 64)
                                    pS = mk.bank(it % 3)
                                    P_ = PT[it % 3]
                                    it += 1
                                    qap = apx(qT, 64 * hp, 64, pt * HALF + r * Lq + qb * 128, [(1, 128)])
                                    for half in range(2):
                                        kap = apx(kT, 64 * hp, 64, pt * dil * Lr + r * Lr + (qb + half) * 128, [(1, 128)])
                                        self.mm(pS.t[:, half * 128:(half + 1) * 128], kap, qap, True, True, [kT, qT], [pS])
                                    self.act(P_.t[:], pS.t[:, 0:256], AF.Exp, [pS], [P_], scale=0.125)
                                    meng = "pool" if it % 2 else "dve"
                                    if qb == 0:
                                        self.tt(meng, P_.t[:], P_.t[:], mask0.t[:], ALU.mult, [P_, mask0], [P_])
                                    else:
                                        self.tt(meng, P_.t[:], P_.t[:], maskpc, ALU.mult, [P_, self.cbf], [P_])
                                    h = 2 * pt + hp
                                    for half in range(2):
                                        vb = V.t[:, r * nkb + qb + half, h * 64:(h + 1) * 64]
                                        self.mm(pN.t[rows, 0:128], vb, P_.t[:, half * 128:(half + 1) * 128], half == 0, half == 1, [V, P_], [pN])
                                    for half in range(2):
                                        self.mm(pD.t[rows, 0:128], ones[:, 0:64], P_.t[:, half * 128:(half + 1) * 128], half == 0, half == 1, [self.cbf, P_], [pD])
                                an = apx(accN, 0, 128, pt * HALF + dil * 128 * qb + r, [(dil, 128)])
                                ad = apx(accD, 0, 128, pt * HALF + dil * 128 * qb + r, [(dil, 128)])
                                if g == 0:
                                    self.copy("dve", an, pN.t[:, 0:128], [pN], [accN])
                                    self.copy("act", ad, pD.t[:, 0:128], [pD], [accD])
                                else:
                                    self.tt("dve", an, an, pN.t[:, 0:128], ALU.add, [pN, accN], [accN])
                                    self.tt("dve", ad, ad, pD.t[:, 0:128], ALU.add, [pD, accD], [accD])
            self.dump("accN", accN, accN.t[:], [128, 2, HALF])
            self.dump("accD", accD, accD.t[:], [128, 2, HALF])
            for pt in range(2):
                mk.op("dve", lambda e, pt=pt: e.reciprocal(accD.t[:, pt, :], accD.t[:, pt, :]), reads=[accD], writes=[accD], cost=2.4)
                self.tt("dve", attT.t[:, pt, :], accN.t[:, pt, :], accD.t[:, pt, :], ALU.mult, [accN, accD], [attT])


def _consts():
    bf = ml_dtypes.bfloat16
    c_bf = np.zeros((128, 128 * 3 + 256 + 64), np.float32)
    c_bf[:, 0:128] = np.eye(128)
    c_bf[:, 128:256] = 1.0
    ik = np.arange(128)[:, None]; iq = np.arange(128)[None, :]
    for k_ in range(128):
        for m_ in range(128):
            if k_ // 64 == m_ // 64:
                i_ = m_ % 64
                src_ = i_ + 8 if i_ < 8 else (i_ - 8 if i_ < 16 else i_)
                if k_ % 64 == src_:
                    c_bf[k_, 256 + m_] = 1.0
    c_bf[:, 384:512] = (ik >= iq)
    c_bf[:, 512:640] = (ik <= iq)
    c_f = np.zeros((128, 512), np.float32)
    p = np.arange(128)
    c_f[:, 0] = EPS
    c_f[:, 1] = p < 64
    c_f[:, 2] = p >= 64
    q = p % 64
    invf = (500000.0 ** (-(2.0 * (q % 8)) / 16.0)).astype(np.float32)
    q16 = p % 16
    c_f[:, 3] = (500000.0 ** (-(2.0 * (q16 % 8)) / 16.0)).astype(np.float32)
    c_f[:, 4] = np.where(q16 < 8, -1.0, 1.0)
    c_f[:, 8:16] = (p[:, None] // 16 == np.arange(8)[None, :])
    c_f[:, 16:16 + ND] = np.asarray(DPOW, np.float32)[None, :]
    c_f[:, 128:256] = (p[:, None] // 16 == (np.arange(128)[None, :] // 16))
    c_f[:, 256:384] = np.eye(128)
    c_f[:, 384:448] = (np.arange(64)[None, :] == (p[:, None] % 64))
    return c_bf.astype(bf), c_f


def _prep_shared(inp):
    f = lambda a: np.ascontiguousarray(np.asarray(a, dtype=np.float32))
    w_in = f(inp["w_in"][0]); b_in = f(inp["b_in"][0])
    perm = np.arange(64)
    perm[0:8] = np.arange(8, 16); perm[8:16] = np.arange(0, 8)
    colperm = (np.arange(12)[:, None] * 64 + perm[None, :]).reshape(-1)
    w_sw = np.concatenate([w_in[:, 768 + colperm], w_in[:, 1536 + colperm]], axis=1)
    b_sw = np.concatenate([b_in[768 + colperm], b_in[1536 + colperm]])
    fm = lambda v: np.ascontiguousarray(v.reshape(-1, 128).T)
    rep2 = lambda a: np.concatenate([a, a], axis=0)
    logdt = f(inp["ssm_log_dt"][0]); are = f(inp["ssm_a_re"][0]); aim = f(inp["ssm_a_im"][0])
    s5p = np.concatenate([np.broadcast_to(logdt[None, :], (128, G)), rep2(are.T), rep2(aim.T)], axis=1)
    br = f(inp["ssm_b_re"][0]).transpose(1, 0, 2).reshape(64, G * 16)
    bi = f(inp["ssm_b_im"][0]).transpose(1, 0, 2).reshape(64, G * 16)
    cr = f(inp["ssm_c_re"][0]).transpose(2, 0, 1).reshape(64, G * 16)
    ci = f(inp["ssm_c_im"][0]).transpose(2, 0, 1).reshape(64, G * 16)
    c_bf, c_f = _consts()
    sh = {
        "w_in": w_in, "w_sw": f(w_sw), "bin_fm": fm(b_in), "bsw_fm": fm(b_sw), "bv_row": f(b_in[None, 2304:3072]),
        "ln_g": f(np.stack([inp["ln_in_g"], inp["ln1_g"][0], inp["ln2_g"][0], inp["ln3_g"][0]])),
        "ln_b": f(np.stack([inp["ln_in_b"], inp["ln1_b"][0], inp["ln2_b"][0], inp["ln3_b"][0]])),
        "s5p": f(s5p), "s5B": f(np.concatenate([rep2(br), rep2(bi)], axis=1)), "s5C": f(np.concatenate([rep2(cr), rep2(ci)], axis=1)),
        "s5d": fm(f(inp["ssm_d"][0])),
        "w_glu": f(inp["w_glu"][0]), "bglu_fm": fm(f(inp["b_glu"][0])),
        "w_au": f(inp["w_att_up"][0]), "w_mo": f(inp["w_mix_out"][0]), "bmo_row": f(inp["b_mix_out"][0][None, :]),
        "w_xq": f(inp["w_xq"][0]), "w_xkv": f(inp["w_xkv"][0]), "w_xo": f(inp["w_xo"][0]),
        "w_f1": f(inp["w_ff1"][0]), "bf1_fm": fm(f(inp["b_ff1"][0])), "w_f2": f(inp["w_ff2"][0]), "bf2_row": f(inp["b_ff2"][0][None, :]),
        "c_bf": c_bf, "c_f32": c_f,
    }
    return sh


def _core_inputs(inp, sh, b, half):
    x = np.asarray(inp["x"], np.float32); pos = np.asarray(inp["positions"], np.int32)
    d = dict(sh)
    d["x_own"] = np.ascontiguousarray(x[b, half * HALF:(half + 1) * HALF])
    if half == 0:
        d["x_prev"] = np.zeros((HALF, D_MODEL), np.float32)
        pp = np.concatenate([np.zeros(HALF, np.int32), pos[b, :HALF]])
    else:
        d["x_prev"] = np.ascontiguousarray(x[b, :HALF])
        pp = pos[b]
    d["posb"] = np.ascontiguousarray(np.broadcast_to(pp.reshape(8, 1, 512), (8, 16, 512)).reshape(128, 512))
    d["flag"] = np.full((128, 1), float(half), np.float32)
    d["mem"] = np.ascontiguousarray(np.asarray(inp["mem"], np.float32)[b])
    return d


_PROG = {}


def kernel(**inputs):
    if "k" not in _PROG:
        _PROG["k"] = K()
    k = _PROG["k"]
    sh = _prep_shared(inputs)
    in_maps = [_core_inputs(inputs, sh, c // 2, c % 2) for c in range(8)]
    res = run_bass_kernel_spmd(k.nc, in_maps, core_ids=list(range(8)))
    out = np.zeros((4, SEQ, D_MODEL), np.float32)
    for c in range(8):
        out[c // 2, (c % 2) * HALF:(c % 2 + 1) * HALF] = res.results[c]["out"]
    return out
```

```python
import numpy as np
import ml_dtypes
import concourse.bass as bass
import concourse.mybir as mybir
from concourse.bass_utils import run_bass_kernel_spmd

F32 = mybir.dt.float32
BF16 = mybir.dt.bfloat16
I32 = mybir.dt.int32
AF = mybir.ActivationFunctionType
ALU = mybir.AluOpType
_DSZ = {F32: 4, BF16: 2, I32: 4}


class Buf:
    __slots__ = ("name", "t", "writer", "readers", "dsem", "last_dma", "shape", "dtype", "kind")

    def __init__(self, name, t, shape=None, dtype=None, kind="sbuf"):
        self.name = name
        self.kind = kind
        self.t = t
        self.writer = None
        self.readers = []
        self.dsem = None
        self.last_dma = None
        self.shape = shape
        self.dtype = dtype


class Ev:
    __slots__ = ("eng", "fn", "waits", "order", "need_inc", "semval", "is_dma", "dsem", "dval", "ndma",
                 "cost", "lat", "idx", "t_end", "done", "nsucc", "fence", "fill")

    def __init__(self, eng, fn, cost=0.3):
        self.eng = eng
        self.fn = fn
        self.waits = []
        self.order = []
        self.need_inc = False
        self.semval = None
        self.is_dma = False
        self.dsem = None
        self.dval = None
        self.ndma = 0
        self.cost = cost
        self.lat = 0.0
        self.t_end = None
        self.fence = False
        self.fill = False


class _Phase:
    def __init__(self, mk):
        self.mk = mk

    def __enter__(self):
        self.mk._phase_stack.append(self.mk._sb_off)
        return self

    def __exit__(self, *a):
        self.mk.barrier()
        self.mk._sb_off = self.mk._phase_stack.pop()
        return False


class MK:
    ENGS = ("pe", "act", "dve", "pool", "sp")

    def __init__(self, nc, n_dma_sems=72, reorder=True):
        self.nc = nc
        self.reorder = reorder
        self.segs = [[]]
        self._sb_off = 0
        self._sb_max = 0
        self._phase_stack = []
        self._uid = 0
        self.n_dma_sems = n_dma_sems
        self.n_sw = 24
        self.dsem_free = [list(range(self.n_sw, n_dma_sems)), list(range(self.n_sw))]
        self.dsem_count = [0] * n_dma_sems
        self.dsem_bufs = []
        self.psum_banks = []
        self.junk_fn = None
        self.n_junk = 0
        self.sb_base = (int(nc.sbuf_base) + 63) // 64 * 64
        self.sb_limit = int(nc.sbuf_top) - self.sb_base - 1024
        for i in range(8):
            t = nc.alloc_psum_tensor(f"psb{i}", [128, 512], F32)
            self.psum_banks.append(Buf(f"psb{i}", t, [128, 512], F32, "psum"))

    def dram_in(self, name, shape, dtype):
        return Buf(name, self.nc.dram_tensor(name, list(shape), dtype, kind="ExternalInput"), shape, dtype, "dram")

    def dram_out(self, name, shape, dtype):
        return Buf(name, self.nc.dram_tensor(name, list(shape), dtype, kind="ExternalOutput"), shape, dtype, "dram")

    def dram_tmp(self, name, shape, dtype):
        return Buf(name, self.nc.dram_tensor(name, list(shape), dtype, kind="Internal"), shape, dtype, "dram")

    def sbuf(self, name, shape, dtype):
        nbytes = int(np.prod(shape[1:])) * _DSZ[dtype]
        nbytes = (nbytes + 63) // 64 * 64
        off = self._sb_off
        if off + nbytes > self.sb_limit:
            raise RuntimeError(f"SBUF overflow allocating {name}: {off}+{nbytes}")
        self._sb_off = off + nbytes
        self._sb_max = max(self._sb_max, self._sb_off)
        self._uid += 1
        t = self.nc.alloc_sbuf_tensor_at(f"{name}_{self._uid}", list(shape), dtype, offset=self.sb_base + off)
        return Buf(name, t, shape, dtype)

    def bank(self, i):
        return self.psum_banks[i]

    def view(self, name, t):
        return Buf(name, t)

    def phase(self):
        return _Phase(self)

    def _deps(self, ev, eng, reads, writes):
        deps = []
        for b in reads:
            if b.writer is not None:
                deps.append((b.writer, True))
        for b in writes:
            if b.writer is not None:
                deps.append((b.writer, False))
            for r in b.readers:
                deps.append((r, False))
        rawset = set(id(d) for d, raw in deps if raw)
        seen = set()
        for d, _ in deps:
            if d is ev or id(d) in seen:
                continue
            seen.add(id(d))
            same = (not d.is_dma) and (not ev.is_dma) and d.eng == eng
            if same and (eng == "pe" or id(d) not in rawset):
                ev.order.append(d)
            else:
                ev.waits.append(d)
        for b in reads:
            b.readers.append(ev)
        for b in writes:
            b.writer = ev
            b.readers = []

    def op(self, eng, fn, reads=(), writes=(), cost=0.3):
        ev = Ev(eng, fn, cost)
        ev.fill = getattr(self, "fill_flag", False)
        self._deps(ev, eng, reads, writes)
        self.segs[-1].append(ev)
        return ev

    def dma(self, eng, pairs, reads=(), writes=(), sync=None, nbytes=None):
        if sync is None:
            for b in list(writes) + list(reads):
                if b.kind != "dram":
                    sync = b
                    break
        if sync is None:
            sync = (list(writes) + list(reads))[0]
        sw = 1 if eng == "pool" else 0
        if sync.dsem is None:
            sync.dsem = [None, None]
            sync.last_dma = [None, None]
            self.dsem_bufs.append(sync)
        if sync.dsem[sw] is None:
            if not self.dsem_free[sw]:
                raise RuntimeError("out of DMA semaphores")
            sync.dsem[sw] = self.dsem_free[sw].pop()
        ds = sync.dsem[sw]
        ev = Ev(eng, None, 0.6 if sw else 0.08)
        ev.is_dma = True
        ev.ndma = len(pairs)
        ev.dsem = ds
        self.dsem_count[ds] += 16 * len(pairs)
        ev.dval = self.dsem_count[ds]
        ev.fn = pairs
        if nbytes is None:
            nbytes = 0
            for (o, i) in pairs:
                try:
                    nbytes += int(o.partition_size) * int(o.free_size) * 4
                except Exception:
                    nbytes += 1 << 19
        ev.lat = 2.0 + nbytes / 150e3
        for ld in sync.last_dma:
            if ld is not None:
                ev.waits.append(ld)
        sync.last_dma[sw] = ev
        self._deps(ev, eng, reads, writes)
        self.segs[-1].append(ev)
        return ev

    def barrier(self):
        self.segs.append([])
        for b in self.dsem_bufs:
            b.dsem = None
            b.last_dma = None
        self.dsem_bufs = []
        self.dsem_free = [list(range(self.n_sw, self.n_dma_sems)), list(range(self.n_sw))]

    def _schedule(self, seg):
        ENGS = self.ENGS
        if not self.reorder:
            return {e: [ev for ev in seg if ev.eng == e] for e in ENGS}
        WIN = {"pe": 320, "act": 256, "dve": 384, "pool": 384, "sp": 64}
        SEM_LAT = 0.7
        JUNK_COST = 0.13
        pend = {e: [ev for ev in seg if ev.eng == e] for e in ENGS}
        out = {e: [] for e in ENGS}
        free_at = {e: 0.0 for e in ENGS}
        inseg = set(id(ev) for ev in seg)
        for ev in seg:
            ev.t_end = None
        n_left = len(seg)
        while n_left:
            best = None
            for e in ENGS:
                q = pend[e]
                lim = min(WIN[e], len(q))
                for k in range(lim):
                    ev = q[k]
                    rdy = 0.0
                    ok = True
                    for d in ev.waits:
                        if id(d) in inseg:
                            if d.t_end is None:
                                ok = False
                                break
                            t = d.t_end + SEM_LAT
                            if t > rdy:
                                rdy = t
                    if ok:
                        for d in ev.order:
                            if id(d) in inseg and d.t_end is None:
                                ok = False
                                break
                    if not ok:
                        continue
                    start = rdy if rdy > free_at[e] else free_at[e]
                    key = (start, k)
                    if best is None or key < best[0]:
                        best = (key, e, k, ev, start)
                    if start <= free_at[e]:
                        break
            if best is None:
                raise RuntimeError("scheduler deadlock")
            _, e, k, ev, start = best
            if e == "pe" and ev.fill and self.junk_fn is not None:
                gap = start - free_at[e]
                nj = int((gap - 0.08) / JUNK_COST) if gap > 0.3 else 0
                for _ in range(min(nj, 40)):
                    jv = Ev("pe", self.junk_fn, JUNK_COST)
                    out[e].append(jv)
                    self.n_junk += 1
            pend[e].pop(k)
            out[e].append(ev)
            free_at[e] = start + ev.cost
            ev.t_end = start + ev.cost + ev.lat
            n_left -= 1
        return out

    def finish(self):
        import contextlib
        nc = self.nc
        prog = {e: [] for e in self.ENGS}
        for seg in self.segs:
            if not seg:
                continue
            sch = self._schedule(seg)
            lastc = [sch[e][-1] for e in self.ENGS if sch[e] and not all(x.is_dma for x in sch[e])]
            lastc = []
            for e in self.ENGS:
                for ev in reversed(sch[e]):
                    if not ev.is_dma:
                        lastc.append(ev)
                        break
            dmas = {}
            for ev in seg:
                if ev.is_dma and (ev.dsem not in dmas or dmas[ev.dsem].dval < ev.dval):
                    dmas[ev.dsem] = ev
            for e in self.ENGS:
                prog[e].extend(sch[e])
                nop = Ev(e, lambda eng: eng.nop())
                nop.fence = True
                nop.waits = [d for d in lastc if d.eng != e] + list(dmas.values())
                prog[e].append(nop)
        for e in self.ENGS:
            for ev in prog[e]:
                for d in ev.waits:
                    if not d.is_dma:
                        d.need_inc = True
        self.prog = prog
        with contextlib.ExitStack() as es:
            esem = {e: es.enter_context(nc.semaphore(f"s_{e}")) for e in self.ENGS}
            dsems = [es.enter_context(nc.semaphore(f"d_{i}")) for i in range(self.n_dma_sems)]
            for e in self.ENGS:
                c = 0
                for ev in prog[e]:
                    if ev.need_inc and not ev.is_dma:
                        c += 1
                        ev.semval = c
            mk = self

            def replay(ename, eng):
                seen = {}
                for ev in prog[ename]:
                    for d in ev.waits:
                        if d.is_dma:
                            key, sem, val = ("d", d.dsem), dsems[d.dsem], d.dval
                        else:
                            key, sem, val = ("e", d.eng), esem[d.eng], d.semval
                        if seen.get(key, 0) < val:
                            eng.wait_ge(sem, val)
                            seen[key] = val
                    if ev.is_dma:
                        for (o, i) in ev.fn:
                            eng.dma_start(out=o, in_=i).then_inc(dsems[ev.dsem], 16)
                    else:
                        inst = ev.fn(eng)
                        if ev.need_inc:
                            inst.then_inc(esem[ename], 1)

            with nc.Block() as block:
                @block.tensor
                def _(eng):
                    replay("pe", eng)

                @block.scalar
                def _(eng):
                    replay("act", eng)

                @block.vector
                def _(eng):
                    replay("dve", eng)

                @block.gpsimd
                def _(eng):
                    replay("pool", eng)

                @block.sync
                def _(eng):
                    replay("sp", eng)

    def stats(self):
        return {e: len(self.prog[e]) for e in self.ENGS}, self._sb_max


def apx(buf, poff, npart, off, dims):
    t = buf.t
    shp = buf.shape
    rowlen = int(np.prod(shp[1:]))
    return bass.AP(t, poff * rowlen + off, [[rowlen, npart]] + [[int(s), int(c)] for (s, c) in dims])


D_MODEL = 1024
SEQ = 4096
HALF = 2048
NT = 16
G = 48
ALPHA = 2.0 ** 0.25
EPS = 1e-5
TWO_PI = float(2 * np.pi)
DPOW = [0, 1, 2, 3, 4, 5, 6, 7, 8, 16, 24, 32, 64, 96, 128, 256, 384, 512, 1024, 1536]
ND = len(DPOW)
SCAN_IDX = [[DPOW.index(8 * e * 4 ** l) for e in (1, 2, 3)] for l in range(4)]
DILS = (1, 4, 16)
GELU_C = float(np.sqrt(2.0 / np.pi))


class K:
    def __init__(self, dbg=None):
        self.dbg = dbg or ()
        nc = bass.Bass("TRN2", target_bir_lowering=False)
        self.nc = nc
        self.mk = MK(nc)
        self.din = {}
        self.build()

    def inp(self, name, shape, dtype=F32):
        b = self.mk.dram_in(name, shape, dtype)
        self.din[name] = b
        return b

    @staticmethod
    def _fs(ap):
        try:
            return float(ap.free_size)
        except Exception:
            return 512.0

    def mm(self, out, lhsT, rhs, start, stop, reads, writes):
        self.mk.op("pe", lambda e: e.matmul(out, lhsT, rhs, start=start, stop=stop), reads=reads, writes=writes,
                   cost=(0.035 + self._fs(out) / 2400.0) * getattr(self, 'pe_scale', 1.0))

    def act(self, out, in_, func, reads, writes, bias=None, scale=None, eng="act"):
        kw = {}
        if bias is not None:
            kw["bias"] = bias
        if scale is not None:
            kw["scale"] = scale
        self.mk.op("act", lambda e: e.activation(out, in_, func, **kw), reads=reads, writes=writes, cost=0.2 + self._fs(out) / 1100.0)

    def tt(self, eng, out, a, b, op, reads, writes):
        self.mk.op(eng, lambda e: e.tensor_tensor(out, a, b, op), reads=reads, writes=writes, cost=self._ec(eng, out))

    def ts(self, eng, out, a, s1, s2, op0, op1, reads, writes):
        if op1 is None:
            self.mk.op(eng, lambda e: e.tensor_scalar(out, a, s1, None, op0=op0), reads=reads, writes=writes, cost=self._ec(eng, out))
        else:
            self.mk.op(eng, lambda e: e.tensor_scalar(out, a, s1, s2, op0=op0, op1=op1), reads=reads, writes=writes, cost=self._ec(eng, out))

    def stt(self, out, a, s, b, op0, op1, reads, writes):
        self.mk.op("dve", lambda e: e.scalar_tensor_tensor(out, a, s, b, op0=op0, op1=op1), reads=reads, writes=writes, cost=self._ec("dve", out))

    def copy(self, eng, out, in_, reads, writes):
        if eng == "act":
            self.mk.op("act", lambda e: e.copy(out, in_), reads=reads, writes=writes, cost=0.2 + self._fs(out) / 1100.0)
        else:
            self.mk.op(eng, lambda e: e.tensor_copy(out, in_), reads=reads, writes=writes, cost=self._ec(eng, out))

    def _ec(self, eng, out):
        f = self._fs(out)
        return (0.25 + f / 550.0) if eng == "pool" else (0.12 + f / 900.0)

    def load(self, eng, dst, dst_ap, src, src_ap):
        self.mk.dma(eng, [(dst_ap, src_ap)], reads=[src], writes=[dst])

    def dump(self, name, buf, ap, shape, dtype=F32):
        if name in self.dbg:
            o = self.mk.dram_out("dbg_" + name, shape, dtype)
            self.mk.dma("sp", [(o.t.ap(), ap)], reads=[buf], writes=[o])

    def wload(self, name, src, rows, col0, ncols, eng="pool"):
        kt = rows // 128
        b = self.mk.sbuf(name, [128, kt, ncols], BF16)
        srcap = src.t.ap().rearrange("(kt p) c -> p kt c", p=128)[:, :, col0:col0 + ncols]
        self.mk.dma(eng, [(b.t[:], srcap)], reads=[src], writes=[b])
        return b

    def wload_into(self, b, src, rows, col0, ncols, eng="pool"):
        srcap = src.t.ap().rearrange("(kt p) c -> p kt c", p=128)[:, :, col0:col0 + ncols]
        self.mk.dma(eng, [(b.t[:], srcap)], reads=[src], writes=[b])

    def layernorm(self, r, gB, bB, out32, outT, tcol, tmp16, psT_bank, st):
        mk = self.mk
        ident = self.ident
        mk.op("dve", lambda e: e.bn_stats(st.t[:, 0:6], r.t[:, 0:512]), reads=[r], writes=[st], cost=0.7)
        mk.op("dve", lambda e: e.bn_stats(st.t[:, 6:12], r.t[:, 512:1024]), reads=[r], writes=[st], cost=0.6)
        mk.op("dve", lambda e: e.bn_aggr(st.t[:, 12:14], st.t[:, 0:12]), reads=[st], writes=[st], cost=0.21)
        self.act(st.t[:, 14:15], st.t[:, 13:14], AF.Sqrt, [st, self.cst], [st], bias=self.cst.t[:, 0:1], scale=1.0)
        mk.op("dve", lambda e: e.reciprocal(st.t[:, 15:16], st.t[:, 14:15]), reads=[st], writes=[st], cost=0.17)
        self.ts("dve", st.t[:, 16:17], st.t[:, 12:13], st.t[:, 15:16], -1.0, ALU.mult, ALU.mult, [st], [st])
        self.act(out32.t[:], r.t[:], AF.Identity, [st, r], [out32], bias=st.t[:, 16:17], scale=st.t[:, 15:16])
        self.tt("dve", out32.t[:], out32.t[:], gB.t[:], ALU.mult, [out32, gB], [out32])
        self.tt("pool", out32.t[:], out32.t[:], bB.t[:], ALU.add, [out32, bB], [out32])
        if outT is None:
            return
        self.copy("act", tmp16.t[:], out32.t[:], [out32], [tmp16])
        psb = psT_bank.t[:].bitcast(BF16)
        for kt in range(8):
            mk.op("pe", lambda e, kt=kt: e.transpose(psb[:, kt * 128:(kt + 1) * 128], tmp16.t[:, kt * 128:(kt + 1) * 128], ident.t[:]),
                  reads=[tmp16, ident], writes=[psT_bank], cost=0.09)
        dst = apx(outT[0], 0, 128, tcol, [(outT[1], 8), (1, 128)])
        src = psb.rearrange("p (k c) -> p k c", k=8)
        self.copy("act", dst, src, [psT_bank], [outT[0]])

    def build(self):
        mk = self.mk
        inp = self.inp
        x_own = inp("x_own", [HALF, D_MODEL]); x_prev = inp("x_prev", [HALF, D_MODEL])
        flag_d = inp("flag", [128, 1]); posb = inp("posb", [128, 512], I32)
        mem = inp("mem", [256, D_MODEL])
        w_in = inp("w_in", [1024, 5120]); w_sw = inp("w_sw", [1024, 1536])
        bin_fm = inp("bin_fm", [128, 40]); bsw_fm = inp("bsw_fm", [128, 12]); bv_row = inp("bv_row", [1, 768])
        ln_g = inp("ln_g", [4, 1024]); ln_b = inp("ln_b", [4, 1024])
        s5p = inp("s5p", [128, 3 * G])
        s5B = inp("s5B", [128, 2 * G * 16])
        s5C = inp("s5C", [128, 2 * G * 16])
        s5d = inp("s5d", [128, 6])
        w_glu = inp("w_glu", [768, 2048]); bglu_fm = inp("bglu_fm", [128, 16])
        w_au = inp("w_au", [256, 1024]); w_mo = inp("w_mo", [1024, 1024]); bmo_row = inp("bmo_row", [1, 1024])
        w_xq = inp("w_xq", [1024, 1024]); w_xkv = inp("w_xkv", [1024, 2048]); w_xo = inp("w_xo", [1024, 1024])
        w_f1 = inp("w_f1", [1024, 4096]); bf1_fm = inp("bf1_fm", [128, 32]); w_f2 = inp("w_f2", [4096, 1024])
        bf2_row = inp("bf2_row", [1, 1024])
        c_bf = inp("c_bf", [128, 128 * 3 + 256 + 64], BF16)
        c_f32 = inp("c_f32", [128, 512])
        out = mk.dram_out("out", [HALF, D_MODEL], F32)
        res_d = [mk.dram_tmp(f"res{t}", [128, D_MODEL], F32) for t in range(NT)]
        tabs = {nm: mk.dram_tmp("tab" + nm, [128, G * 8], F32) for nm in ("ZA", "ZB", "QA", "QB")}
        tabs["WR"] = mk.dram_tmp("tabWR", [128, G * 24], F32)
        tabs["Cneg"] = mk.dram_tmp("tabCneg", [128, G * 16], BF16)

        cbf = mk.sbuf("cbf", [128, 128 * 3 + 256 + 64], BF16)
        cf = mk.sbuf("cf", [128, 512], F32)
        self.cst = cf
        self.cbf = cbf
        flag = mk.sbuf("flag", [128, 1], F32)
        mk.dma("sp", [(cbf.t[:], c_bf.t.ap())], reads=[c_bf], writes=[cbf])
        mk.dma("sp", [(cf.t[:], c_f32.t.ap())], reads=[c_f32], writes=[cf])
        mk.dma("sp", [(flag.t[:], flag_d.t.ap())], reads=[flag_d], writes=[flag])
        ident = mk.view("ident", cbf.t[:, 0:128]); ident.writer = cbf.writer
        self.ident = ident
        ones = cbf.t[:, 128:256]
        maskpc = cbf.t[:, 384:640]
        AT = mk.sbuf("AT", [128, 8, HALF], BF16)
        ATp = (AT, HALF)
        st = [mk.sbuf(f"st{i}", [128, 32], F32) for i in range(8)]

        def ln_params(i, gB, bB):
            mk.dma("sp", [(gB.t[:], ln_g.t.ap()[i:i + 1, :].partition_broadcast(128).rearrange("p o c -> p (o c)"))], reads=[ln_g], writes=[gB])
            mk.dma("sp", [(bB.t[:], ln_b.t.ap()[i:i + 1, :].partition_broadcast(128).rearrange("p o c -> p (o c)"))], reads=[ln_b], writes=[bB])

        with mk.phase():
            attT = mk.sbuf("attT", [128, 2, HALF], BF16)
            HTp = mk.sbuf("HTp", [128, 8, HALF], BF16)
            COS = mk.dram_tmp("cosd", [16, SEQ], BF16); SIN = mk.dram_tmp("sind", [16, SEQ], BF16)
            with mk.phase():
                self.s5_prep(s5p, s5C, tabs)
                self.rope_tables(posb, COS, SIN)
                gB = mk.sbuf("gB", [128, 1024], F32); bB = mk.sbuf("bB", [128, 1024], F32)
                ln_params(0, gB, bB)
                NB = 4
                xt = [mk.sbuf(f"xt{i}", [128, 1024], F32) for i in range(NB)]
                o32 = [mk.sbuf(f"o32{i}", [128, 1024], F32) for i in range(NB)]
                t16 = [mk.sbuf(f"t16{i}", [128, 1024], BF16) for i in range(NB)]

                def ld(t):
                    own = t >= NT
                    tt_ = t - NT if own else t
                    src = x_own if own else x_prev
                    mk.dma("sp", [(xt[t % NB].t[:], src.t.ap()[tt_ * 128:(tt_ + 1) * 128, :])], reads=[src], writes=[xt[t % NB]])

                ld(0); ld(1)
                for t in range(2 * NT):
                    own = t >= NT
                    tt_ = t - NT if own else t
                    if t + 2 < 2 * NT:
                        ld(t + 2)
                    i = t % NB
                    self.layernorm(xt[i], gB, bB, o32[i], (AT if own else HTp, HALF), tt_ * 128, t16[i], mk.bank(t % 2), st[i])
                    if own:
                        mk.dma("sp", [(res_d[tt_].t.ap(), o32[i].t[:])], reads=[o32[i]], writes=[res_d[tt_]])
            self.dump("hT", AT, AT.t[:], [128, 8, HALF], BF16)
            self.attention(w_in, w_sw, bin_fm, bsw_fm, bv_row, COS, SIN, flag, AT, HTp, attT, ones, maskpc)
            self.dump("attT", attT, attT.t[:], [128, 2, HALF], BF16)
            gyd = [mk.dram_tmp(f"gyd{j}", [128, HALF], BF16) for j in range(6)]
            self.s5(w_in, bin_fm, s5B, s5C, s5d, tabs, flag, AT, HTp, gyd)
            gy = mk.sbuf("gy", [128, 6, HALF], BF16)
            mk.dma("sp", [(gy.t[:, j, :], gyd[j].t.ap()) for j in range(6)], reads=gyd, writes=[gy])
            MIX = mk.sbuf("MIX", [128, 8, HALF], BF16)
            self.phaseE(w_in, bin_fm, w_glu, bglu_fm, w_au, AT, gy, attT, MIX)
            self.dump("MIX", MIX, MIX.t[:], [128, 8, HALF], BF16)
            self.proj_ln(MIX, w_mo, bmo_row, 1, AT, res_d, ln_params, st, ones, NB=4)
        self.dump("h1T", AT, AT.t[:], [128, 8, HALF], BF16)
        self.xattn(mem, w_xq, w_xkv, w_xo, AT, res_d, ln_params, st, ones)
        self.dump("h2T", AT, AT.t[:], [128, 8, HALF], BF16)
        self.ffn(w_f1, bf1_fm, w_f2, bf2_row, AT, res_d, ln_params, st, ones, out)
        mk.finish()

    def phaseE(self, w_in, bin_fm, w_glu, bglu_fm, w_au, AT, gy, attT, MIX):
        mk = self.mk
        with mk.phase():
            wg1 = [mk.sbuf(f"wg1{i}", [128, 6, 128], BF16) for i in range(2)]
            wg2 = [mk.sbuf(f"wg2{i}", [128, 6, 128], BF16) for i in range(2)]
            wgs = [mk.sbuf(f"wgs{i}", [128, 8, 128], BF16) for i in range(2)]
            wga = [mk.sbuf(f"wga{i}", [128, 8, 128], BF16) for i in range(2)]
            wau = [mk.sbuf(f"wau{i}", [128, 2, 128], BF16) for i in range(2)]
            binb = mk.sbuf("binb", [128, 40], F32); bglu = mk.sbuf("bglu", [128, 16], F32)
            mk.dma("sp", [(binb.t[:], bin_fm.t.ap())], reads=[bin_fm], writes=[binb])
            mk.dma("sp", [(bglu.t[:], bglu_fm.t.ap())], reads=[bglu_fm], writes=[bglu])
            tmp = [[mk.sbuf(f"e{k}{i}", [128, 512], F32) for k in range(5)] for i in range(2)]
            it = 0
            def ldw(mt):
                self.wload_into(wg1[mt % 2], w_glu, 768, mt * 128, 128)
                self.wload_into(wg2[mt % 2], w_glu, 768, 1024 + mt * 128, 128)
                self.wload_into(wgs[mt % 2], w_in, 1024, 3072 + mt * 128, 128)
                self.wload_into(wga[mt % 2], w_in, 1024, 4096 + mt * 128, 128)
                self.wload_into(wau[mt % 2], w_au, 256, mt * 128, 128)

            ldw(0)
            for mt in range(8):
                w1, w2, w3, w4, w5 = wg1[mt % 2], wg2[mt % 2], wgs[mt % 2], wga[mt % 2], wau[mt % 2]
                if mt + 1 < 8:
                    ldw(mt + 1)
                for blk in range(4):
                    bs = slice(blk * 512, (blk + 1) * 512)
                    sg2, t1, sgs, sga, t2 = tmp[it % 2]
                    it += 1
                    bset = 4 * (it % 2)
                    pz1, pz2, pgs, pga = [mk.bank(bset + i) for i in range(4)]
                    pba = pz2
                    for kt in range(6):
                        self.mm(pz1.t[:], w1.t[:, kt, :], gy.t[:, kt, bs], kt == 0, kt == 5, [w1, gy], [pz1])
                    for kt in range(6):
                        self.mm(pz2.t[:], w2.t[:, kt, :], gy.t[:, kt, bs], kt == 0, kt == 5, [w2, gy], [pz2])
                    for kt in range(8):
                        self.mm(pgs.t[:], w3.t[:, kt, :], AT.t[:, kt, bs], kt == 0, kt == 7, [w3, AT], [pgs])
                    for kt in range(8):
                        self.mm(pga.t[:], w4.t[:, kt, :], AT.t[:, kt, bs], kt == 0, kt == 7, [w4, AT], [pga])
                    self.act(sg2.t[:], pz2.t[:], AF.Sigmoid, [pz2, bglu], [sg2], bias=bglu.t[:, 8 + mt:9 + mt])
                    for kt in range(2):
                        self.mm(pba.t[:], w5.t[:, kt, :], attT.t[:, kt, bs], kt == 0, kt == 1, [w5, attT], [pba])
                    self.stt(t1.t[:], pz1.t[:], bglu.t[:, mt:mt + 1], sg2.t[:], ALU.add, ALU.mult, [pz1, bglu, sg2], [t1])
                    self.act(sgs.t[:], pgs.t[:], AF.Sigmoid, [pgs, binb], [sgs], bias=binb.t[:, 24 + mt:25 + mt])
                    self.act(sga.t[:], pga.t[:], AF.Sigmoid, [pga, binb], [sga], bias=binb.t[:, 32 + mt:33 + mt])
                    self.tt("pool", t1.t[:], t1.t[:], sgs.t[:], ALU.mult, [t1, sgs], [t1])
                    self.tt("dve", t2.t[:], pba.t[:], sga.t[:], ALU.mult, [pba, sga], [t2])
                    self.tt("pool", MIX.t[:, mt, bs], t1.t[:], t2.t[:], ALU.add, [t1, t2], [MIX])

    def proj_ln(self, X, w, brow_d, ln_idx, AT, res_d, ln_params, st, ones, NB=3):
        mk = self.mk
        with mk.phase():
            W = self.wload("W", w, 1024, 0, 1024)
            gB = mk.sbuf("gB", [128, 1024], F32); bB = mk.sbuf("bB", [128, 1024], F32)
            ln_params(ln_idx, gB, bB)
            if brow_d is not None:
                brow = mk.sbuf("brow", [1, 1024], BF16)
                mk.dma("pool", [(brow.t[:], brow_d.t.ap())], reads=[brow_d], writes=[brow])
            rt = [mk.sbuf(f"rt{i}", [128, 1024], F32) for i in range(NB)]
            o32 = [mk.sbuf(f"o32{i}", [128, 1024], F32) for i in range(NB)]
            t16 = [mk.sbuf(f"t16{i}", [128, 1024], BF16) for i in range(NB)]

            def ld(t):
                mk.dma("sp", [(rt[t % NB].t[:], res_d[t].t.ap())], reads=[res_d[t]], writes=[rt[t % NB]])

            ld(0); ld(1)
            for t in range(NT):
                i = t % NB
                ts_ = slice(t * 128, (t + 1) * 128)
                if t + 2 < NT:
                    ld(t + 2)
                for half in range(2):
                    hs = slice(half * 512, (half + 1) * 512)
                    ps = mk.bank(2 * (t % 3) + half)
                    for kt in range(8):
                        self.mm(ps.t[:], X.t[:, kt, ts_], W.t[:, kt, hs], kt == 0, (kt == 7 and brow_d is None), [X, W], [ps])
                    if brow_d is not None:
                        self.mm(ps.t[:], ones[0:1, :], brow.t[0:1, hs], False, True, [self.cbf, brow], [ps])
                    self.stt(rt[i].t[:, hs], rt[i].t[:, hs], ALPHA, ps.t[:], ALU.mult, ALU.add, [rt[i], ps], [rt[i]])
                self.layernorm(rt[i], gB, bB, o32[i], (AT, HALF), t * 128, t16[i], mk.bank(6 + t % 2), st[t % 8])
                mk.dma("sp", [(res_d[t].t.ap(), o32[i].t[:])], reads=[o32[i]], writes=[res_d[t]])

    def xattn(self, mem, w_xq, w_xkv, w_xo, AT, res_d, ln_params, st, ones):
        mk = self.mk
        ident = self.ident
        with mk.phase():
            KmT = mk.sbuf("KmT", [128, 8, 256], BF16)
            Vm = mk.sbuf("Vm", [128, 2, 1024], BF16)
            OX = mk.sbuf("OX", [128, 8, HALF], BF16)
            with mk.phase():
                memT = mk.sbuf("memT", [128, 8, 256], BF16)
                mt32 = mk.sbuf("mt32", [128, 1024], F32); mt16 = mk.sbuf("mt16", [128, 1024], BF16)
                for mtile in range(2):
                    mk.dma("sp", [(mt32.t[:], mem.t.ap()[mtile * 128:(mtile + 1) * 128, :])], reads=[mem], writes=[mt32])
                    self.copy("dve", mt16.t[:], mt32.t[:], [mt32], [mt16])
                    pst = mk.bank(0)
                    psb = pst.t[:].bitcast(BF16)
                    for kt in range(8):
                        mk.op("pe", lambda e, kt=kt, psb=psb: e.transpose(psb[:, kt * 128:(kt + 1) * 128], mt16.t[:, kt * 128:(kt + 1) * 128], ident.t[:]),
                              reads=[mt16, ident], writes=[pst])
                    self.copy("act", apx(memT, 0, 128, mtile * 128, [(256, 8), (1, 128)]), psb.rearrange("p (k c) -> p k c", k=8), [pst], [memT])
                wk = self.wload("wk", w_xkv, 1024, 0, 1024)
                for mt in range(8):
                    ps = mk.bank(1 + mt % 2)
                    for kt in range(8):
                        self.mm(ps.t[:, 0:256], wk.t[:, kt, mt * 128:(mt + 1) * 128], memT.t[:, kt, :], kt == 0, kt == 7, [wk, memT], [ps])
                    self.copy("act" if mt % 2 else "dve", KmT.t[:, mt, :], ps.t[:, 0:256], [ps], [KmT])
                wv = self.wload("wvx", w_xkv, 1024, 1024, 1024)
                for mtile in range(2):
                    for half in range(2):
                        ps = mk.bank(3 + half)
                        for kt in range(8):
                            self.mm(ps.t[:], memT.t[:, kt, mtile * 128:(mtile + 1) * 128], wv.t[:, kt, half * 512:(half + 1) * 512], kt == 0, kt == 7, [memT, wv], [ps])
                        self.copy("act" if half else "dve", Vm.t[:, mtile, half * 512:(half + 1) * 512], ps.t[:], [ps], [Vm])
            with mk.phase():
                QX = mk.sbuf("QX", [128, 8, HALF], BF16)
                wq = self.wload("wxq", w_xq, 1024, 0, 1024)
                for blk in range(4):
                    bs = slice(blk * 512, (blk + 1) * 512)
                    for mt in range(8):
                        ps = mk.bank((blk * 8 + mt) % 4)
                        for kt in range(8):
                            self.mm(ps.t[:], wq.t[:, kt, mt * 128:(mt + 1) * 128], AT.t[:, kt, bs], kt == 0, kt == 7, [wq, AT], [ps])
                        self.copy("act" if mt % 2 else "dve", QX.t[:, mt, bs], ps.t[:], [ps], [QX])
                PTx = [[mk.sbuf(f"PTx{i}{m}", [128, 512], BF16) for m in range(2)] for i in range(2)]
                rden = [mk.sbuf(f"rden{i}", [128, 512], F32) for i in range(2)]
                it = 0
                for blk in range(4):
                    bs = slice(blk * 512, (blk + 1) * 512)
                    for h in range(4):
                        i = it % 2
                        it += 1
                        for mtile in range(2):
                            pS = mk.bank(2 * (it % 2) + mtile)
                            for j in range(2):
                                self.mm(pS.t[:], KmT.t[:, 2 * h + j, mtile * 128:(mtile + 1) * 128], QX.t[:, 2 * h + j, bs], j == 0, j == 1, [KmT, QX], [pS])
                            self.act(PTx[i][mtile].t[:], pS.t[:], AF.Exp, [pS], [PTx[i][mtile]], scale=1.0 / 16.0)
                        pD = mk.bank(4 + it % 2)
                        for mtile in range(2):
                            self.mm(pD.t[:], ones, PTx[i][mtile].t[:], mtile == 0, mtile == 1, [self.cbf, PTx[i][mtile]], [pD])
                        mk.op("dve", lambda e, i=i, pD=pD: e.reciprocal(rden[i].t[:], pD.t[:]), reads=[pD], writes=[rden[i]], cost=0.7)
                        for j in range(2):
                            pO = mk.bank(6 + j)
                            for mtile in range(2):
                                self.mm(pO.t[:], Vm.t[:, mtile, (2 * h + j) * 128:(2 * h + j + 1) * 128], PTx[i][mtile].t[:], mtile == 0, mtile == 1, [Vm, PTx[i][mtile]], [pO])
                            self.tt("dve", OX.t[:, 2 * h + j, bs], pO.t[:], rden[i].t[:], ALU.mult, [pO, rden[i]], [OX])
            self.dump("OX", OX, OX.t[:], [128, 8, HALF], BF16)
            self.proj_ln(OX, w_xo, None, 2, AT, res_d, ln_params, st, ones, NB=6)

    def ffn(self, w_f1, bf1_fm, w_f2, bf2_row, AT, res_d, ln_params, st, ones, out):
        mk = self.mk
        with mk.phase():
            acc = [mk.sbuf(f"acc{t}", [128, 1024], F32) for t in range(NT)]
            bf1 = mk.sbuf("bf1", [128, 32], F32)
            brow = mk.sbuf("brow2", [1, 1024], BF16)
            mk.dma("sp", [(bf1.t[:], bf1_fm.t.ap())], reads=[bf1_fm], writes=[bf1])
            mk.dma("pool", [(brow.t[:], bf2_row.t.ap())], reads=[bf2_row], writes=[brow])
            for t in range(NT):
                mk.dma("sp", [(acc[t].t[:], res_d[t].t.ap())], reads=[res_d[t]], writes=[acc[t]])
                mk.op("act", lambda e, t=t: e.mul(acc[t].t[:], acc[t].t[:], ALPHA), reads=[acc[t]], writes=[acc[t]])
            with mk.phase():
                W1 = [mk.sbuf(f"W1{i}", [128, 8, 512], BF16) for i in range(2)]
                W2 = [mk.sbuf(f"W2{i}", [128, 4, 1024], BF16) for i in range(2)]
                hid = [mk.sbuf(f"hid{i}", [128, 4, 512], BF16) for i in range(2)]
                tf_ = [mk.sbuf(f"tf{i}", [128, 512], F32) for i in range(2)]
                hi = 0
                oi = 0
                def ldw(c):
                    self.wload_into(W1[c % 2], w_f1, 1024, c * 512, 512)
                    mk.dma("pool", [(W2[c % 2].t[:], w_f2.t.ap()[c * 512:(c + 1) * 512, :].rearrange("(kt p) c -> p kt c", p=128))], reads=[w_f2], writes=[W2[c % 2]])

                ldw(0)
                for c in range(8):
                    w1 = W1[c % 2]; w2 = W2[c % 2]
                    if c + 1 < 8:
                        ldw(c + 1)
                    for blk in range(4):
                        bs = slice(blk * 512, (blk + 1) * 512)
                        hb = hid[hi % 2]
                        hi += 1
                        for ft in range(4):
                            pH = mk.bank(ft % 2)
                            for kt in range(8):
                                self.mm(pH.t[:], w1.t[:, kt, ft * 128:(ft + 1) * 128], AT.t[:, kt, bs], kt == 0, kt == 7, [w1, AT], [pH])
                            tb = tf_[ft % 2]
                            self.act(tb.t[:], pH.t[:], AF.Relu, [pH, bf1], [tb], bias=bf1.t[:, c * 4 + ft:c * 4 + ft + 1])
                            self.tt("pool", hb.t[:, ft, :], tb.t[:], tb.t[:], ALU.mult, [tb], [hb])
                        for tl in range(4):
                            T = blk * 4 + tl
                            for half in range(2):
                                hs = slice(half * 512, (half + 1) * 512)
                                pO = mk.bank(2 + oi % 6)
                                oi += 1
                                for ft in range(4):
                                    self.mm(pO.t[:], hb.t[:, ft, tl * 128:(tl + 1) * 128], w2.t[:, ft, hs], ft == 0, (ft == 3 and c != 0), [hb, w2], [pO])
                                if c == 0:
                                    self.mm(pO.t[:], ones[0:1, :], brow.t[0:1, hs], False, True, [self.cbf, brow], [pO])
                                self.tt("dve", acc[T].t[:, hs], acc[T].t[:, hs], pO.t[:], ALU.add, [acc[T], pO], [acc[T]])
            with mk.phase():
                gB = mk.sbuf("gB", [128, 1024], F32); bB = mk.sbuf("bB", [128, 1024], F32)
                ln_params(3, gB, bB)
                o32 = [mk.sbuf(f"o32{i}", [128, 1024], F32) for i in range(6)]
                for t in range(NT):
                    i = t % 6
                    self.layernorm(acc[t], gB, bB, o32[i], None, 0, None, None, st[t % 8])
                    mk.dma("sp", [(out.t.ap()[t * 128:(t + 1) * 128, :], o32[i].t[:])], reads=[o32[i]], writes=[out])

    def sin_of(self, out, ang, tmpi, tmpf, tmpm, eng="dve", out_ap=None):
        PI = float(np.pi)
        self.ts(eng, tmpi.t[:], ang.t[:], 1.0 / TWO_PI, None, ALU.mult, None, [ang], [tmpi])
        self.copy(eng, tmpf.t[:], tmpi.t[:], [tmpi], [tmpf])
        self.ts(eng, tmpf.t[:], tmpf.t[:], -TWO_PI, None, ALU.mult, None, [tmpf], [tmpf])
        self.tt(eng, tmpf.t[:], tmpf.t[:], ang.t[:], ALU.add, [tmpf, ang], [tmpf])
        self.ts(eng, tmpm.t[:], tmpf.t[:], PI, -TWO_PI, ALU.is_gt, ALU.mult, [tmpf], [tmpm])
        self.tt(eng, tmpf.t[:], tmpf.t[:], tmpm.t[:], ALU.add, [tmpf, tmpm], [tmpf])
        self.ts(eng, tmpm.t[:], tmpf.t[:], -PI, TWO_PI, ALU.is_lt, ALU.mult, [tmpf], [tmpm])
        self.tt(eng, tmpf.t[:], tmpf.t[:], tmpm.t[:], ALU.add, [tmpf, tmpm], [tmpf])
        self.ts(eng, tmpf.t[:], tmpf.t[:], 3.14159, -3.14159, ALU.min, ALU.max, [tmpf], [tmpf])
        self.act(out.t[:] if out_ap is None else out_ap, tmpf.t[:], AF.Sin, [tmpf], [out])

    def rope_tables(self, posb, COS, SIN):
        mk = self.mk
        cf = self.cst
        CH = 512
        pi_ = mk.sbuf("posi", [128, CH], I32)
        ang = mk.sbuf("ang", [128, CH], F32); sc = mk.sbuf("sc", [128, CH], F32)
        ti = mk.sbuf("ti", [128, CH], I32); tf = mk.sbuf("tf", [128, CH], F32); tm = mk.sbuf("tm", [128, CH], F32)
        s16 = mk.sbuf("s16", [128, CH], BF16); c16 = mk.sbuf("c16", [128, CH], BF16)
        mk.dma("sp", [(pi_.t[:], posb.t.ap())], reads=[posb], writes=[pi_])
        self.copy("dve", ang.t[:], pi_.t[:], [pi_], [ang])
        self.ts("dve", ang.t[:], ang.t[:], cf.t[:, 3:4], None, ALU.mult, None, [ang, cf], [ang])
        self.sin_of(sc, ang, ti, tf, tm)
        self.ts("dve", s16.t[:], sc.t[:], cf.t[:, 4:5], None, ALU.mult, None, [sc, cf], [s16])
        self.ts("dve", ang.t[:], ang.t[:], float(np.pi / 2), None, ALU.add, None, [ang], [ang])
        self.sin_of(c16, ang, ti, tf, tm)
        mk.dma("sp", [(SIN.t.ap().rearrange("q (c j) -> q c j", c=8)[:, c, :], s16.t[16 * c:16 * c + 16, :]) for c in range(8)], reads=[s16], writes=[SIN])
        mk.dma("sp", [(COS.t.ap().rearrange("q (c j) -> q c j", c=8)[:, c, :], c16.t[16 * c:16 * c + 16, :]) for c in range(8)], reads=[c16], writes=[COS])

    def s5_prep(self, s5p, s5C, tabs):
        mk = self.mk
        cf = self.cst
        P = mk.sbuf("P", [128, 3 * G], F32)
        CRI = mk.sbuf("CRI", [128, 2 * G * 16], F32)
        for (b_, s_) in ((P, s5p), (CRI, s5C)):
            mk.dma("sp", [(b_.t[:], s_.t.ap())], reads=[s_], writes=[b_])
        n2 = G * ND
        PR = mk.sbuf("PR", [128, G, ND], F32); PI_ = mk.sbuf("PI", [128, G, ND], F32)
        ZA = mk.sbuf("ZA", [128, G, 8], F32); ZB = mk.sbuf("ZB", [128, G, 8], F32)
        QA = mk.sbuf("QA", [128, G, 8], F32); QB = mk.sbuf("QB", [128, G, 8], F32)
        WR = mk.sbuf("WR", [128, G, 12, 2], F32)
        Cneg = mk.sbuf("Cneg", [128, G * 16], BF16)
        dt = mk.sbuf("dt", [128, G], F32); lam = mk.sbuf("lam", [128, G], F32); th = mk.sbuf("th", [128, G], F32)
        ANG = mk.sbuf("ANG", [128, n2], F32); LAM = mk.sbuf("LAMb", [128, n2], F32)
        SN = mk.sbuf("SN", [128, n2], F32); CS = mk.sbuf("CS", [128, n2], F32)
        ti = mk.sbuf("ti", [128, n2], I32); tf = mk.sbuf("tf", [128, n2], F32); tm = mk.sbuf("tm", [128, n2], F32)
        self.act(dt.t[:], P.t[:, 0:G], AF.Exp, [P], [dt])
        self.tt("dve", lam.t[:], P.t[:, G:2 * G], dt.t[:], ALU.mult, [P, dt], [lam])
        self.tt("dve", th.t[:], P.t[:, 2 * G:3 * G], dt.t[:], ALU.mult, [P, dt], [th])
        dp = apx(cf, 0, 128, 16, [(0, G), (1, ND)])
        self.tt("dve", ANG.t[:].rearrange("p (g k) -> p g k", g=G), apx(th, 0, 128, 0, [(1, G), (0, ND)]), dp, ALU.mult, [th, cf], [ANG])
        self.tt("dve", LAM.t[:].rearrange("p (g k) -> p g k", g=G), apx(lam, 0, 128, 0, [(1, G), (0, ND)]), dp, ALU.mult, [lam, cf], [LAM])
        self.act(LAM.t[:], LAM.t[:], AF.Exp, [LAM], [LAM])
        self.sin_of(SN, ANG, ti, tf, tm)
        self.ts("dve", ANG.t[:], ANG.t[:], float(np.pi / 2), None, ALU.add, None, [ANG], [ANG])
        self.sin_of(CS, ANG, ti, tf, tm)
        prf = PR.t[:].rearrange("p g k -> p (g k)"); pif = PI_.t[:].rearrange("p g k -> p (g k)")
        self.tt("dve", prf, LAM.t[:], CS.t[:], ALU.mult, [LAM, CS], [PR])
        self.tt("dve", pif, LAM.t[:], SN.t[:], ALU.mult, [LAM, SN], [PI_])
        nr = mk.sbuf("nr", [128, G], F32); den = mk.sbuf("den", [128, G], F32); t1 = mk.sbuf("t1", [128, G], F32)
        fr = mk.sbuf("fr", [128, G], F32); fi = mk.sbuf("fi", [128, G], F32)
        are = P.t[:, G:2 * G]; aim = P.t[:, 2 * G:3 * G]
        pr1 = apx(PR, 0, 128, 1, [(ND, G)]); pi1 = apx(PI_, 0, 128, 1, [(ND, G)])
        self.ts("dve", nr.t[:], pr1, -1.0, None, ALU.add, None, [PR], [nr])
        self.tt("dve", den.t[:], are, are, ALU.mult, [P], [den])
        self.tt("dve", t1.t[:], aim, aim, ALU.mult, [P], [t1])
        self.tt("dve", den.t[:], den.t[:], t1.t[:], ALU.add, [den, t1], [den])
        mk.op("dve", lambda e: e.reciprocal(den.t[:], den.t[:]), reads=[den], writes=[den])
        self.tt("dve", fr.t[:], nr.t[:], are, ALU.mult, [nr, P], [fr])
        self.tt("dve", t1.t[:], pi1, aim, ALU.mult, [PI_, P], [t1])
        self.tt("dve", fr.t[:], fr.t[:], t1.t[:], ALU.add, [fr, t1], [fr])
        self.tt("dve", fr.t[:], fr.t[:], den.t[:], ALU.mult, [fr, den], [fr])
        self.tt("dve", fi.t[:], pi1, are, ALU.mult, [PI_, P], [fi])
        self.tt("dve", t1.t[:], nr.t[:], aim, ALU.mult, [nr, P], [t1])
        self.tt("dve", fi.t[:], fi.t[:], t1.t[:], ALU.subtract, [fi, t1], [fi])
        self.tt("dve", fi.t[:], fi.t[:], den.t[:], ALU.mult, [fi, den], [fi])
        ZR = mk.sbuf("ZR", [128, G, 8], F32); ZI = mk.sbuf("ZI", [128, G, 8], F32); T8 = mk.sbuf("T8", [128, G, 8], F32)
        pr8 = apx(PR, 0, 128, 0, [(ND, G), (1, 8)]); pi8 = apx(PI_, 0, 128, 0, [(ND, G), (1, 8)])
        frb = apx(fr, 0, 128, 0, [(1, G), (0, 8)]); fib = apx(fi, 0, 128, 0, [(1, G), (0, 8)])
        self.tt("dve", ZR.t[:], pr8, frb, ALU.mult, [PR, fr], [ZR])
        self.tt("dve", T8.t[:], pi8, fib, ALU.mult, [PI_, fi], [T8])
        self.tt("dve", ZR.t[:], ZR.t[:], T8.t[:], ALU.subtract, [ZR, T8], [ZR])
        self.tt("dve", ZI.t[:], pr8, fib, ALU.mult, [PR, fi], [ZI])
        self.tt("dve", T8.t[:], pi8, frb, ALU.mult, [PI_, fr], [T8])
        self.tt("dve", ZI.t[:], ZI.t[:], T8.t[:], ALU.add, [ZI, T8], [ZI])
        U_, L_ = slice(0, 64), slice(64, 128)
        self.copy("dve", ZA.t[U_], ZR.t[U_], [ZR], [ZA]); self.copy("dve", ZA.t[L_], ZI.t[L_], [ZI], [ZA])
        self.ts("dve", ZB.t[U_], ZI.t[U_], -1.0, None, ALU.mult, None, [ZI], [ZB]); self.copy("dve", ZB.t[L_], ZR.t[L_], [ZR], [ZB])
        pr18 = lambda sl: apx(PR, sl.start, 64, 1, [(ND, G), (1, 8)])
        pi18 = lambda sl: apx(PI_, sl.start, 64, 1, [(ND, G), (1, 8)])
        self.copy("dve", QA.t[U_], pr18(U_), [PR], [QA]); self.ts("dve", QA.t[L_], pi18(L_), -1.0, None, ALU.mult, None, [PI_], [QA])
        self.ts("dve", QB.t[U_], pi18(U_), -1.0, None, ALU.mult, None, [PI_], [QB]); self.ts("dve", QB.t[L_], pr18(L_), -1.0, None, ALU.mult, None, [PR], [QB])
        wro = lambda sl, h: apx(WR, sl.start, 64, h, [(24, G), (2, 12)])
        prs = lambda sl: apx(PR, sl.start, 64, 8, [(ND, G), (1, 12)])
        pis = lambda sl: apx(PI_, sl.start, 64, 8, [(ND, G), (1, 12)])
        self.copy("dve", wro(U_, 0), prs(U_), [PR], [WR]); self.copy("dve", wro(U_, 1), pis(U_), [PI_], [WR])
        self.ts("dve", wro(L_, 0), pis(L_), -1.0, None, ALU.mult, None, [PI_], [WR]); self.copy("dve", wro(L_, 1), prs(L_), [PR], [WR])
        self.copy("dve", Cneg.t[U_], CRI.t[U_, 0:G * 16], [CRI], [Cneg])
        self.ts("dve", Cneg.t[L_], CRI.t[L_, G * 16:2 * G * 16], -1.0, None, ALU.mult, None, [CRI], [Cneg])

        for nm, b_ in (("ZA", ZA), ("ZB", ZB), ("QA", QA), ("QB", QB), ("WR", WR), ("Cneg", Cneg)):
            mk.dma("sp", [(tabs[nm].t.ap(), b_.t[:])], reads=[b_], writes=[tabs[nm]])

    def s5(self, w_in, bin_fm, s5B, s5C, s5d, tabs, flag, AT, HTp, gyd):
        self.pe_scale = 1.8
        jb = self.mk.bank(7)
        idt = self.ident
        cb = self.cbf
        self.mk.junk_fn = lambda e: e.matmul(jb.t[:, 0:256], idt.t[:], cb.t[:, 0:256], start=True, stop=True)
        self.mk.fill_flag = True
        try:
            self._s5(w_in, bin_fm, s5B, s5C, s5d, tabs, flag, AT, HTp, gyd)
        finally:
            self.pe_scale = 1.0
            self.mk.fill_flag = False

    def _s5(self, w_in, bin_fm, s5B, s5C, s5d, tabs, flag, AT, HTp, gyd):
        mk = self.mk
        cf = self.cst
        ident = self.ident
        with mk.phase():
            BRI = mk.sbuf("BRI", [128, 2 * G * 16], F32)
            CRI = mk.sbuf("CRI", [128, 2 * G * 16], F32)
            dd = mk.sbuf("dd", [128, 6], F32)
            binb = mk.sbuf("binb", [128, 40], F32)
            for (b_, s_) in ((BRI, s5B), (CRI, s5C), (dd, s5d), (binb, bin_fm)):
                mk.dma("sp", [(b_.t[:], s_.t.ap())], reads=[s_], writes=[b_])
            ZA = mk.sbuf("ZA", [128, G, 8], F32); ZB = mk.sbuf("ZB", [128, G, 8], F32)
            QA = mk.sbuf("QA", [128, G, 8], F32); QB = mk.sbuf("QB", [128, G, 8], F32)
            WR = mk.sbuf("WR", [128, G, 12, 2], F32)
            Cneg = mk.sbuf("Cneg", [128, G * 16], BF16)
            for nm, b_ in (("ZA", ZA), ("ZB", ZB), ("QA", QA), ("QB", QB), ("WR", WR), ("Cneg", Cneg)):
                mk.dma("sp", [(b_.t[:], tabs[nm].t.ap())], reads=[tabs[nm]], writes=[b_])

            wu = [mk.sbuf(f"wu{i}", [128, 8, 128], BF16) for i in range(2)]
            u_ = [mk.sbuf(f"u{i}", [128, SEQ], BF16) for i in range(2)]
            gys = [mk.sbuf(f"gys{i}", [128, HALF], BF16) for i in range(2)]
            Gf = mk.sbuf("Gf", [128, 1024], F32); Gf2 = mk.sbuf("Gf2", [128, 1024], F32)
            Gall = mk.sbuf("Gall", [128, 8, 128], BF16)
            Pm = mk.sbuf("Pm", [128, 8, 8, 128], BF16)
            Toep_ = [mk.sbuf(f"Toep{i}", [128, 8, 128], BF16) for i in range(2)]
            tK = mk.sbuf("tK", [128, 4, 128], F32); Dg = mk.sbuf("Dg", [128, 128], F32)
            Qp = mk.sbuf("Qp", [128, 8, 8, 64], BF16)
            qa = mk.sbuf("qa", [128, 256], F32); qb = mk.sbuf("qb", [128, 256], F32)
            NS = 3
            Rot = [mk.sbuf(f"Rot{i}", [128, 12, 128], BF16) for i in range(NS)]
            Xp_ = [mk.sbuf(f"Xp{i}", [128, 256], BF16) for i in range(NS)]
            Sp_ = [[mk.sbuf(f"Sp{k}{i}", [128, 64], BF16) for i in range(2)] for k in range(NS)]
            So_ = [[mk.sbuf(f"So{k}{i}", [128, 256], BF16) for i in range(2)] for k in range(NS)]
            Hext_ = [mk.sbuf(f"Hext{i}", [128, 8, 257], BF16) for i in range(2)]
            xs = mk.sbuf("xs", [128, 256], F32); x2 = mk.sbuf("x2", [128, 256], F32); sg = mk.sbuf("sg", [128, 256], F32)
            evq = [0]

            def evac(out, in_, reads, writes, scale=None):
                evq[0] += 1
                if scale is not None or evq[0] % 2 == 0:
                    if scale is not None:
                        self.act(out, in_, AF.Copy, reads, writes, scale=scale)
                    else:
                        self.copy("act", out, in_, reads, writes)
                else:
                    self.copy("dve", out, in_, reads, writes)

            for j in range(6):
                g0 = 8 * j
                w = wu[j % 2]
                u = u_[j % 2]; Toep = Toep_[j % 2]; Hext = Hext_[j % 2]; gyj = gys[j % 2]
                self.wload_into(w, w_in, 1024, 128 * j, 128)
                for blk in range(8):
                    src = HTp if blk < 4 else AT
                    c0 = (blk % 4) * 512
                    ps = mk.bank(2 + blk % 2)
                    for kt in range(8):
                        self.mm(ps.t[:], w.t[:, kt, :], src.t[:, kt, c0:c0 + 512], kt == 0, kt == 7, [w, src], [ps])
                    self.act(apx(u, 0, 128, (blk // 4) * HALF + (blk % 4) * 64, [(256, 8), (1, 64)]), apx(ps, 0, 128, 0, [(1, 8), (8, 64)]),
                             AF.Identity, [ps, binb], [u], bias=binb.t[:, j:j + 1])
                if j == 0:
                    self.dump("u0", u, u.t[:], [128, SEQ], BF16)
                za = apx(ZA, 0, 128, g0 * 8, [(1, 8), (8, 8), (0, 16)]); zb = apx(ZB, 0, 128, g0 * 8, [(1, 8), (8, 8), (0, 16)])
                br = apx(BRI, 0, 128, g0 * 16, [(0, 8), (16, 8), (1, 16)]); bi = apx(BRI, 0, 128, G * 16 + g0 * 16, [(0, 8), (16, 8), (1, 16)])
                gf4 = Gf.t[:].rearrange("p (d g c) -> p d g c", d=8, g=8); gf24 = Gf2.t[:].rearrange("p (d g c) -> p d g c", d=8, g=8)
                self.tt("dve", gf4, za, br, ALU.mult, [ZA, BRI], [Gf])
                self.tt("pool", gf24, zb, bi, ALU.mult, [ZB, BRI], [Gf2])
                self.tt("dve", Gall.t[:].rearrange("p d c -> p (d c)"), Gf.t[:], Gf2.t[:], ALU.add, [Gf, Gf2], [Gall])
                for hb in range(2):
                    pst = mk.bank(4)
                    pstb = pst.t[:].bitcast(BF16)
                    for dd_ in range(4):
                        d = hb * 4 + dd_
                        mk.op("pe", lambda e, d=d, dd_=dd_, pstb=pstb: e.transpose(pstb[:, dd_ * 128:(dd_ + 1) * 128], Gall.t[:, d, :], ident.t[:]),
                              reads=[Gall, ident], writes=[pst])
                    for gl in range(8):
                        o = apx(Pm, 0, 128, (hb * 4 * 8 + gl) * 128, [(8 * 128, 4), (1, 128)])
                        i_ = pstb[:, 0:512].rearrange("p (d c) -> p d c", d=4)
                        if gl % 2 == 0:
                            self.ts("dve", o, i_, cf.t[:, 8 + gl:9 + gl], None, ALU.mult, None, [pst, cf], [Pm])
                        else:
                            self.act(o, i_, AF.Copy, [pst, cf], [Pm], scale=cf.t[:, 8 + gl:9 + gl])
                    psk = mk.bank(5)
                    for dd_ in range(4):
                        d = hb * 4 + dd_
                        self.mm(psk.t[:, dd_ * 128:(dd_ + 1) * 128], Gall.t[:, d, :], Cneg.t[:, g0 * 16:g0 * 16 + 128], True, True, [Gall, Cneg], [psk])
                    self.tt("dve", tK.t[:], psk.t[:].rearrange("p (d c) -> p d c", d=4), apx(cf, 0, 128, 128, [(0, 4), (1, 128)]), ALU.mult, [psk, cf], [tK])
                    if hb == 0:
                        self.ts("dve", Dg.t[:], cf.t[:, 256:384], dd.t[:, j:j + 1], None, ALU.mult, None, [cf, dd], [Dg])
                        self.tt("dve", tK.t[:, 0, :], tK.t[:, 0, :], Dg.t[:], ALU.add, [tK, Dg], [tK])
                    self.copy("dve", Toep.t[:, hb * 4:(hb + 1) * 4, :], tK.t[:], [tK], [Toep])
                mk.op("pool", lambda e: e.memset(Qp.t[:], 0.0), reads=[], writes=[Qp])
                for q in range(4):
                    o = apx(Qp, 0, 128, q * 64 + 16 * q, [(512, 8), (256, 2), (1, 16)])
                    a0 = apx(QA, 0, 128, (g0 + q) * 8, [(1, 8), (32, 2), (0, 16)]); b0 = apx(QB, 0, 128, (g0 + q) * 8, [(1, 8), (32, 2), (0, 16)])
                    c0_ = apx(CRI, 0, 128, (g0 + q) * 16, [(0, 8), (64, 2), (1, 16)]); c1_ = apx(CRI, 0, 128, G * 16 + (g0 + q) * 16, [(0, 8), (64, 2), (1, 16)])
                    v3 = lambda b_: b_.t[:].rearrange("p (s k c) -> p s k c", s=8, k=2)
                    self.tt("dve", v3(qa), a0, c0_, ALU.mult, [QA, CRI], [qa])
                    self.tt("pool", v3(qb), b0, c1_, ALU.mult, [QB, CRI], [qb])
                    self.tt("dve", o, v3(qa), v3(qb), ALU.add, [qa, qb, Qp], [Qp])
                for gl in range(8):
                    g = g0 + gl
                    gp = gl % NS
                    R = Rot[gp]
                    Xp = Xp_[gp]; Sp = Sp_[gp]; So = So_[gp]
                    self.tt("pool", R.t[:].rearrange("p e (h c) -> p (e h) c", h=2), apx(cf, 0, 128, 384, [(0, 24), (1, 64)]),
                            apx(WR, 0, 128, g * 24, [(1, 24), (0, 64)]), ALU.mult, [cf, WR], [R])
                    ps = mk.bank(2 * gp)
                    for s in range(8):
                        self.mm(ps.t[:, 0:256], Pm.t[:, 7 - s, gl, :], apx(u, 0, 128, s * 256, [(1, 256)]), s == 0, s == 7, [Pm, u], [ps])
                    self.act(Xp.t[:], ps.t[:, 0:256], AF.Copy, [ps, flag], [Xp], scale=flag.t[:, 0:1])
                    S = Xp
                    for l in range(4):
                        N = 256 // 4 ** (l + 1)
                        ps2 = mk.bank(2 * gp + 1)
                        for e in range(4):
                            lhs = ident.t[:] if e == 0 else R.t[:, l * 3 + e - 1, :]
                            self.mm(ps2.t[:, 0:N], lhs, apx(S, 0, 128, 3 - e, [(4, N)]), e == 0, e == 3, [ident, R, S], [ps2])
                        if l < 3:
                            S2 = Sp[l % 2]
                            evac(S2.t[:, 0:N], ps2.t[:, 0:N], [ps2], [S2])
                            S = S2
                        else:
                            evac(Hext.t[:, gl, 0:1], ps2.t[:, 0:1], [ps2], [Hext])
                    ps = mk.bank(2 * gp)
                    for s in range(8):
                        self.mm(ps.t[:, 0:256], Pm.t[:, 7 - s, gl, :], apx(u, 0, 128, HALF + s * 256, [(1, 256)]), s == 0, False, [Pm, u], [ps])
                    self.mm(ps.t[:, 0:1], R.t[:, 0, :], Hext.t[:, gl, 0:1], False, True, [R, Hext], [ps])
                    S = So[0]
                    evac(S.t[:], ps.t[:, 0:256], [ps], [S])
                    for l in range(4):
                        d = 4 ** l
                        ps2 = mk.bank(2 * gp + 1)
                        self.mm(ps2.t[:, 0:256], ident.t[:], S.t[:, 0:256], True, False, [ident, S], [ps2])
                        for e in range(1, 4):
                            self.mm(ps2.t[:, e * d:256], R.t[:, l * 3 + e - 1, :], S.t[:, 0:256 - e * d], False, e == 3, [R, S], [ps2])
                        if l < 3:
                            S2 = So[(l + 1) % 2]
                            evac(S2.t[:], ps2.t[:, 0:256], [ps2], [S2])
                            S = S2
                        else:
                            evac(Hext.t[:, gl, 1:257], ps2.t[:, 0:256], [ps2], [Hext])
                if j == 0:
                    self.dump("Hext", Hext, Hext.t[:], [128, 8, 257], BF16)
                for s in range(8):
                    ps = mk.bank(2 + s % 2)
                    for gl in range(8):
                        hq = gl // 4
                        self.mm(ps.t[64 * hq:64 * hq + 64, 0:256], Qp.t[:, s, gl, :], Hext.t[:, gl, 0:256], gl % 4 == 0, False, [Qp, Hext], [ps])
                    for d in range(s + 1):
                        self.mm(ps.t[:, 0:256], Toep.t[:, d, :], apx(u, 0, 128, HALF + (s - d) * 256, [(1, 256)]), False, d == s, [Toep, u], [ps])
                    self.copy("act", xs.t[:], ps.t[:, 0:256], [ps], [xs])
                    self.tt("pool", x2.t[:], xs.t[:], xs.t[:], ALU.mult, [xs], [x2])
                    self.ts("pool", x2.t[:], x2.t[:], 2 * GELU_C * 0.044715, 2 * GELU_C, ALU.mult, ALU.add, [x2], [x2])
                    self.tt("pool", x2.t[:], x2.t[:], xs.t[:], ALU.mult, [x2, xs], [x2])
                    self.act(sg.t[:], x2.t[:], AF.Sigmoid, [x2], [sg])
                    self.tt("dve", apx(gyj, 0, 128, s, [(8, 256)]), xs.t[:], sg.t[:], ALU.mult, [xs, sg], [gyj])
                mk.dma("sp", [(gyd[j].t.ap(), gyj.t[:])], reads=[gyj], writes=[gyd[j]])

    def attention(self, w_in, w_sw, bin_fm, bsw_fm, bv_row, COS, SIN, flag, AT, HTp, attT, ones, maskpc):
        mk = self.mk
        cf = self.cst
        with mk.phase():
            binb = mk.sbuf("binb", [128, 40], F32); bswb = mk.sbuf("bswb", [128, 12], F32)
            bvb = mk.sbuf("bvb", [1, 768], BF16)
            mk.dma("sp", [(binb.t[:], bin_fm.t.ap())], reads=[bin_fm], writes=[binb])
            mk.dma("sp", [(bswb.t[:], bsw_fm.t.ap())], reads=[bsw_fm], writes=[bswb])
            mk.dma("pool", [(bvb.t[:], bv_row.t.ap())], reads=[bv_row], writes=[bvb])
            accN = mk.sbuf("accN", [128, 2, HALF], F32)
            accD = mk.sbuf("accD", [128, 2, HALF], F32)
            COSd, SINd = COS, SIN
            COS = mk.sbuf("COS", [128, SEQ], BF16); SIN = mk.sbuf("SIN", [128, SEQ], BF16)
            mk.op("pool", lambda e: e.memset(COS.t[:], 1.0), reads=[], writes=[COS], cost=5.0)
            mk.op("pool", lambda e: e.memset(SIN.t[:], 0.0), reads=[], writes=[SIN], cost=5.0)
            mk.dma("sp", [(COS.t[0:16, :], COSd.t.ap()), (COS.t[64:80, :], COSd.t.ap())], reads=[COSd], writes=[COS])
            mk.dma("sp", [(SIN.t[0:16, :], SINd.t.ap()), (SIN.t[64:80, :], SINd.t.ap())], reads=[SINd], writes=[SIN])
            wq = [mk.sbuf(f"wq{i}", [128, 8, 128], BF16) for i in range(2)]
            wqs = [mk.sbuf(f"wqs{i}", [128, 8, 128], BF16) for i in range(2)]
            wv = mk.sbuf("wv", [128, 8, 256], BF16)
            t1 = [mk.sbuf(f"rt1{i}", [128, 512], F32) for i in range(2)]
            t2 = [mk.sbuf(f"rt2{i}", [128, 512], F32) for i in range(2)]
            PT = [mk.sbuf(f"PT{i}", [128, 256], BF16) for i in range(3)]
            qraw = [mk.sbuf(f"qraw{i}", [128, 512], BF16) for i in range(2)]
            permT = self.cbf.t[:, 256:384]
            mask0 = mk.sbuf("mask0", [128, 256], BF16)
            self.copy("dve", mask0.t[:], maskpc, [self.cbf], [mask0])
            self.ts("dve", mask0.t[:, 0:128], mask0.t[:, 0:128], flag.t[:, 0:1], None, ALU.mult, None, [mask0, flag], [mask0])
            cnt = [0]
            wc = [0]

            def proj_rope(wa, wb, src, c0, bias_a, bias_b, tok0_tab, dst, oap, a0, n, dil):
                i = cnt[0] % 2
                cnt[0] += 1
                pa = mk.bank(2 * i); pb = mk.bank(2 * i + 1)
                for kt in range(8):
                    self.mm(pa.t[:], wa.t[:, kt, :], src.t[:, kt, c0:c0 + 512], kt == 0, kt == 7, [wa, src], [pa])
                self.act(qraw[i].t[:], pa.t[:], AF.Identity, [pa, binb], [qraw[i]], bias=bias_a)
                self.mm(pb.t[:], permT, qraw[i].t[:], True, True, [self.cbf, qraw[i]], [pb])
                self.tt("dve", t1[i].t[:], qraw[i].t[:], COS.t[:, tok0_tab:tok0_tab + 512], ALU.mult, [qraw[i], COS], [t1[i]])
                self.tt("dve", t2[i].t[:], pb.t[:], SIN.t[:, tok0_tab:tok0_tab + 512], ALU.mult, [pb, SIN], [t2[i]])
                if dil == 1:
                    i0 = t1[i].t[:, a0:a0 + n]; i1 = t2[i].t[:, a0:a0 + n]
                else:
                    i0 = apx(t1[i], 0, 128, a0, [(1, dil), (dil, n // dil)])
                    i1 = apx(t2[i], 0, 128, a0, [(1, dil), (dil, n // dil)])
                self.tt("pool", oap, i0, i1, ALU.add, [t1[i], t2[i]], [dst])

            def store_ap(dst, pt, L, dil, m0, n):
                if dil == 1:
                    return dst.t[:, pt, m0:m0 + n]
                return apx(dst, 0, 128, pt * dil * L + m0, [(L, dil), (1, n // dil)])

            it = 0
            vi = 0
            qbi = [0]
            for g in range(3):
                dil = DILS[g]
                Lq = HALF // dil
                Lr = 128 + HALF // dil
                nkb = 1 + 16 // dil
                nq = 16 // dil
                with mk.phase():
                    qT = mk.sbuf(f"qT{g}", [128, 2, HALF], BF16)
                    kT = mk.sbuf(f"kT{g}", [128, 2, dil * Lr], BF16)
                    V = mk.sbuf(f"V{g}", [128, dil * nkb, 256], BF16)
                    for pt in range(2):
                        mt = 2 * g + pt
                        wa = wq[wc[0] % 2]; wb = wqs[wc[0] % 2]; wc[0] += 1
                        self.wload_into(wa, w_in, 1024, 768 + 128 * mt, 128)
                        for blk in range(4):
                            oap = store_ap(qT, pt, Lq, dil, blk * 512 // dil, 512)
                            proj_rope(wa, wb, AT, blk * 512, binb.t[:, 6 + mt:7 + mt], bswb.t[:, mt:mt + 1], HALF + blk * 512,
                                      qT, oap, 0, 512, dil)
                        wa = wq[wc[0] % 2]; wb = wqs[wc[0] % 2]; wc[0] += 1
                        self.wload_into(wa, w_in, 1024, 1536 + 128 * mt, 128)
                        for blk in range(8):
                            prev = blk < 4
                            src = HTp if prev else AT
                            if prev:
                                lo = HALF - 128 * dil
                                b0 = blk * 512
                                if b0 + 512 <= lo:
                                    continue
                                a0 = max(lo, b0) - b0
                                n = 512 - a0
                                m0 = (b0 + a0 - lo) // dil
                            else:
                                a0, n = 0, 512
                                m0 = 128 + (blk - 4) * 512 // dil
                            oap = store_ap(kT, pt, Lr, dil, m0, n)
                            proj_rope(wa, wb, src, (blk % 4) * 512, binb.t[:, 12 + mt:13 + mt], bswb.t[:, 6 + mt:7 + mt], blk * 512,
                                      kT, oap, a0, n, dil)
                    self.wload_into(wv, w_in, 1024, 2304 + 256 * g, 256)
                    for r in range(dil):
                        for b in range(nkb):
                            if b == 0:
                                src = HTp; start = HALF - 128 * dil + r
                            else:
                                src = AT; start = dil * 128 * (b - 1) + r
                            ps = mk.bank(4 + vi % 2)
                            vi += 1
                            for kt in range(8):
                                lhs = apx(src, 0, 128, kt * HALF + start, [(dil, 128)])
                                self.mm(ps.t[:, 0:256], lhs, wv.t[:, kt, :], kt == 0, False, [src, wv], [ps])
                            self.mm(ps.t[:, 0:256], ones[0:1, :], bvb.t[0:1, 256 * g:256 * g + 256], False, True, [self.cbf, bvb], [ps])
                            dst = V.t[:, r * nkb + b, :]
                            if b == 0:
                                self.act(dst, ps.t[:, 0:256], AF.Copy, [ps, flag], [V], scale=flag.t[:, 0:1])
                            elif vi % 2:
                                self.copy("act", dst, ps.t[:, 0:256], [ps], [V])
                            else:
                                self.copy("dve", dst, ps.t[:, 0:256], [ps], [V])
                    if g == 1:
                        self.dump("qT1", qT, qT.t[:], [128, 2, HALF], BF16)
                        self.dump("kT1", kT, kT.t[:], [128, 2, dil * Lr], BF16)
                        self.dump("V1", V, V.t[:], [128, dil * nkb, 256], BF16)
                    for pt in range(2):
                        for r in range(dil):
                            for qb in range(nq):
                                pN = mk.bank(4 + 2 * (qbi[0] % 2)); pD = mk.bank(5 + 2 * (qbi[0] % 2)); qbi[0] += 1
                                for hp in range(2):
                                    rows = slice(64 * hp, 64 * hp + 64)
                                    pS = mk.bank(it % 3)
                                    P_ = PT[it % 3]
                                    it += 1
                                    qap = apx(qT, 64 * hp, 64, pt * HALF + r * Lq + qb * 128, [(1, 128)])
                                    for half in range(2):
                                        kap = apx(kT, 64 * hp, 64, pt * dil * Lr + r * Lr + (qb + half) * 128, [(1, 128)])
                                        self.mm(pS.t[:, half * 128:(half + 1) * 128], kap, qap, True, True, [kT, qT], [pS])
                                    self.act(P_.t[:], pS.t[:, 0:256], AF.Exp, [pS], [P_], scale=0.125)
                                    meng = "pool" if it % 2 else "dve"
                                    if qb == 0:
                                        self.tt(meng, P_.t[:], P_.t[:], mask0.t[:], ALU.mult, [P_, mask0], [P_])
                                    else:
                                        self.tt(meng, P_.t[:], P_.t[:], maskpc, ALU.mult, [P_, self.cbf], [P_])
                                    h = 2 * pt + hp
                                    for half in range(2):
                                        vb = V.t[:, r * nkb + qb + half, h * 64:(h + 1) * 64]
                                        self.mm(pN.t[rows, 0:128], vb, P_.t[:, half * 128:(half + 1) * 128], half == 0, half == 1, [V, P_], [pN])
                                    for half in range(2):
                                        self.mm(pD.t[rows, 0:128], ones[:, 0:64], P_.t[:, half * 128:(half + 1) * 128], half == 0, half == 1, [self.cbf, P_], [pD])
                                an = apx(accN, 0, 128, pt * HALF + dil * 128 * qb + r, [(dil, 128)])
                                ad = apx(accD, 0, 128, pt * HALF + dil * 128 * qb + r, [(dil, 128)])
                                if g == 0:
                                    self.copy("dve", an, pN.t[:, 0:128], [pN], [accN])
                                    self.copy("act", ad, pD.t[:, 0:128], [pD], [accD])
                                else:
                                    self.tt("dve", an, an, pN.t[:, 0:128], ALU.add, [pN, accN], [accN])
                                    self.tt("dve", ad, ad, pD.t[:, 0:128], ALU.add, [pD, accD], [accD])
            self.dump("accN", accN, accN.t[:], [128, 2, HALF])
            self.dump("accD", accD, accD.t[:], [128, 2, HALF])
            for pt in range(2):
                mk.op("dve", lambda e, pt=pt: e.reciprocal(accD.t[:, pt, :], accD.t[:, pt, :]), reads=[accD], writes=[accD], cost=2.4)
                self.tt("dve", attT.t[:, pt, :], accN.t[:, pt, :], accD.t[:, pt, :], ALU.mult, [accN, accD], [attT])


def _consts():
    bf = ml_dtypes.bfloat16
    c_bf = np.zeros((128, 128 * 3 + 256 + 64), np.float32)
    c_bf[:, 0:128] = np.eye(128)
    c_bf[:, 128:256] = 1.0
    ik = np.arange(128)[:, None]; iq = np.arange(128)[None, :]
    for k_ in range(128):
        for m_ in range(128):
            if k_ // 64 == m_ // 64:
                i_ = m_ % 64
                src_ = i_ + 8 if i_ < 8 else (i_ - 8 if i_ < 16 else i_)
                if k_ % 64 == src_:
                    c_bf[k_, 256 + m_] = 1.0
    c_bf[:, 384:512] = (ik >= iq)
    c_bf[:, 512:640] = (ik <= iq)
    c_f = np.zeros((128, 512), np.float32)
    p = np.arange(128)
    c_f[:, 0] = EPS
    c_f[:, 1] = p < 64
    c_f[:, 2] = p >= 64
    q = p % 64
    invf = (500000.0 ** (-(2.0 * (q % 8)) / 16.0)).astype(np.float32)
    q16 = p % 16
    c_f[:, 3] = (500000.0 ** (-(2.0 * (q16 % 8)) / 16.0)).astype(np.float32)
    c_f[:, 4] = np.where(q16 < 8, -1.0, 1.0)
    c_f[:, 8:16] = (p[:, None] // 16 == np.arange(8)[None, :])
    c_f[:, 16:16 + ND] = np.asarray(DPOW, np.float32)[None, :]
    c_f[:, 128:256] = (p[:, None] // 16 == (np.arange(128)[None, :] // 16))
    c_f[:, 256:384] = np.eye(128)
    c_f[:, 384:448] = (np.arange(64)[None, :] == (p[:, None] % 64))
    return c_bf.astype(bf), c_f


def _prep_shared(inp):
    f = lambda a: np.ascontiguousarray(np.asarray(a, dtype=np.float32))
    w_in = f(inp["w_in"][0]); b_in = f(inp["b_in"][0])
    perm = np.arange(64)
    perm[0:8] = np.arange(8, 16); perm[8:16] = np.arange(0, 8)
    colperm = (np.arange(12)[:, None] * 64 + perm[None, :]).reshape(-1)
    w_sw = np.concatenate([w_in[:, 768 + colperm], w_in[:, 1536 + colperm]], axis=1)
    b_sw = np.concatenate([b_in[768 + colperm], b_in[1536 + colperm]])
    fm = lambda v: np.ascontiguousarray(v.reshape(-1, 128).T)
    rep2 = lambda a: np.concatenate([a, a], axis=0)
    logdt = f(inp["ssm_log_dt"][0]); are = f(inp["ssm_a_re"][0]); aim = f(inp["ssm_a_im"][0])
    s5p = np.concatenate([np.broadcast_to(logdt[None, :], (128, G)), rep2(are.T), rep2(aim.T)], axis=1)
    br = f(inp["ssm_b_re"][0]).transpose(1, 0, 2).reshape(64, G * 16)
    bi = f(inp["ssm_b_im"][0]).transpose(1, 0, 2).reshape(64, G * 16)
    cr = f(inp["ssm_c_re"][0]).transpose(2, 0, 1).reshape(64, G * 16)
    ci = f(inp["ssm_c_im"][0]).transpose(2, 0, 1).reshape(64, G * 16)
    c_bf, c_f = _consts()
    sh = {
        "w_in": w_in, "w_sw": f(w_sw), "bin_fm": fm(b_in), "bsw_fm": fm(b_sw), "bv_row": f(b_in[None, 2304:3072]),
        "ln_g": f(np.stack([inp["ln_in_g"], inp["ln1_g"][0], inp["ln2_g"][0], inp["ln3_g"][0]])),
        "ln_b": f(np.stack([inp["ln_in_b"], inp["ln1_b"][0], inp["ln2_b"][0], inp["ln3_b"][0]])),
        "s5p": f(s5p), "s5B": f(np.concatenate([rep2(br), rep2(bi)], axis=1)), "s5C": f(np.concatenate([rep2(cr), rep2(ci)], axis=1)),
        "s5d": fm(f(inp["ssm_d"][0])),
        "w_glu": f(inp["w_glu"][0]), "bglu_fm": fm(f(inp["b_glu"][0])),
        "w_au": f(inp["w_att_up"][0]), "w_mo": f(inp["w_mix_out"][0]), "bmo_row": f(inp["b_mix_out"][0][None, :]),
        "w_xq": f(inp["w_xq"][0]), "w_xkv": f(inp["w_xkv"][0]), "w_xo": f(inp["w_xo"][0]),
        "w_f1": f(inp["w_ff1"][0]), "bf1_fm": fm(f(inp["b_ff1"][0])), "w_f2": f(inp["w_ff2"][0]), "bf2_row": f(inp["b_ff2"][0][None, :]),
        "c_bf": c_bf, "c_f32": c_f,
    }
    return sh


def _core_inputs(inp, sh, b, half):
    x = np.asarray(inp["x"], np.float32); pos = np.asarray(inp["positions"], np.int32)
    d = dict(sh)
    d["x_own"] = np.ascontiguousarray(x[b, half * HALF:(half + 1) * HALF])
    if half == 0:
        d["x_prev"] = np.zeros((HALF, D_MODEL), np.float32)
        pp = np.concatenate([np.zeros(HALF, np.int32), pos[b, :HALF]])
    else:
        d["x_prev"] = np.ascontiguousarray(x[b, :HALF])
        pp = pos[b]
    d["posb"] = np.ascontiguousarray(np.broadcast_to(pp.reshape(8, 1, 512), (8, 16, 512)).reshape(128, 512))
    d["flag"] = np.full((128, 1), float(half), np.float32)
    d["mem"] = np.ascontiguousarray(np.asarray(inp["mem"], np.float32)[b])
    return d


_PROG = {}


def kernel(**inputs):
    if "k" not in _PROG:
        _PROG["k"] = K()
    k = _PROG["k"]
    sh = _prep_shared(inputs)
    in_maps = [_core_inputs(inputs, sh, c // 2, c % 2) for c in range(8)]
    res = run_bass_kernel_spmd(k.nc, in_maps, core_ids=list(range(8)))
    out = np.zeros((4, SEQ, D_MODEL), np.float32)
    for c in range(8):
        out[c // 2, (c % 2) * HALF:(c % 2 + 1) * HALF] = res.results[c]["out"]
    return out
```

```python
import numpy as np
import ml_dtypes
import concourse.bass as bass
import concourse.mybir as mybir
from concourse.bass_utils import run_bass_kernel_spmd

F32 = mybir.dt.float32
BF16 = mybir.dt.bfloat16
I32 = mybir.dt.int32
AF = mybir.ActivationFunctionType
ALU = mybir.AluOpType
_DSZ = {F32: 4, BF16: 2, I32: 4}


class Buf:
    __slots__ = ("name", "t", "writer", "readers", "dsem", "last_dma", "shape", "dtype", "kind")

    def __init__(self, name, t, shape=None, dtype=None, kind="sbuf"):
        self.name = name
        self.kind = kind
        self.t = t
        self.writer = None
        self.readers = []
        self.dsem = None
        self.last_dma = None
        self.shape = shape
        self.dtype = dtype


class Ev:
    __slots__ = ("eng", "fn", "waits", "order", "need_inc", "semval", "is_dma", "dsem", "dval", "ndma",
                 "cost", "lat", "idx", "t_end", "done", "nsucc", "fence", "fill")

    def __init__(self, eng, fn, cost=0.3):
        self.eng = eng
        self.fn = fn
        self.waits = []
        self.order = []
        self.need_inc = False
        self.semval = None
        self.is_dma = False
        self.dsem = None
        self.dval = None
        self.ndma = 0
        self.cost = cost
        self.lat = 0.0
        self.t_end = None
        self.fence = False
        self.fill = False


class _Phase:
    def __init__(self, mk):
        self.mk = mk

    def __enter__(self):
        self.mk._phase_stack.append(self.mk._sb_off)
        return self

    def __exit__(self, *a):
        self.mk.barrier()
        self.mk._sb_off = self.mk._phase_stack.pop()
        return False


class MK:
    ENGS = ("pe", "act", "dve", "pool", "sp")

    def __init__(self, nc, n_dma_sems=72, reorder=True):
        self.nc = nc
        self.reorder = reorder
        self.segs = [[]]
        self._sb_off = 0
        self._sb_max = 0
        self._phase_stack = []
        self._uid = 0
        self.n_dma_sems = n_dma_sems
        self.n_sw = 24
        self.dsem_free = [list(range(self.n_sw, n_dma_sems)), list(range(self.n_sw))]
        self.dsem_count = [0] * n_dma_sems
        self.dsem_bufs = []
        self.psum_banks = []
        self.junk_fn = None
        self.n_junk = 0
        self.sb_base = (int(nc.sbuf_base) + 63) // 64 * 64
        self.sb_limit = int(nc.sbuf_top) - self.sb_base - 1024
        for i in range(8):
            t = nc.alloc_psum_tensor(f"psb{i}", [128, 512], F32)
            self.psum_banks.append(Buf(f"psb{i}", t, [128, 512], F32, "psum"))

    def dram_in(self, name, shape, dtype):
        return Buf(name, self.nc.dram_tensor(name, list(shape), dtype, kind="ExternalInput"), shape, dtype, "dram")

    def dram_out(self, name, shape, dtype):
        return Buf(name, self.nc.dram_tensor(name, list(shape), dtype, kind="ExternalOutput"), shape, dtype, "dram")

    def dram_tmp(self, name, shape, dtype):
        return Buf(name, self.nc.dram_tensor(name, list(shape), dtype, kind="Internal"), shape, dtype, "dram")

    def sbuf(self, name, shape, dtype):
        nbytes = int(np.prod(shape[1:])) * _DSZ[dtype]
        nbytes = (nbytes + 63) // 64 * 64
        off = self._sb_off
        if off + nbytes > self.sb_limit:
            raise RuntimeError(f"SBUF overflow allocating {name}: {off}+{nbytes}")
        self._sb_off = off + nbytes
        self._sb_max = max(self._sb_max, self._sb_off)
        self._uid += 1
        t = self.nc.alloc_sbuf_tensor_at(f"{name}_{self._uid}", list(shape), dtype, offset=self.sb_base + off)
        return Buf(name, t, shape, dtype)

    def bank(self, i):
        return self.psum_banks[i]

    def view(self, name, t):
        return Buf(name, t)

    def phase(self):
        return _Phase(self)

    def _deps(self, ev, eng, reads, writes):
        deps = []
        for b in reads:
            if b.writer is not None:
                deps.append((b.writer, True))
        for b in writes:
            if b.writer is not None:
                deps.append((b.writer, False))
            for r in b.readers:
                deps.append((r, False))
        rawset = set(id(d) for d, raw in deps if raw)
        seen = set()
        for d, _ in deps:
            if d is ev or id(d) in seen:
                continue
            seen.add(id(d))
            same = (not d.is_dma) and (not ev.is_dma) and d.eng == eng
            if same and (eng == "pe" or id(d) not in rawset):
                ev.order.append(d)
            else:
                ev.waits.append(d)
        for b in reads:
            b.readers.append(ev)
        for b in writes:
            b.writer = ev
            b.readers = []

    def op(self, eng, fn, reads=(), writes=(), cost=0.3):
        ev = Ev(eng, fn, cost)
        ev.fill = getattr(self, "fill_flag", False)
        self._deps(ev, eng, reads, writes)
        self.segs[-1].append(ev)
        return ev

    def dma(self, eng, pairs, reads=(), writes=(), sync=None, nbytes=None):
        if sync is None:
            for b in list(writes) + list(reads):
                if b.kind != "dram":
                    sync = b
                    break
        if sync is None:
            sync = (list(writes) + list(reads))[0]
        sw = 1 if eng == "pool" else 0
        if sync.dsem is None:
            sync.dsem = [None, None]
            sync.last_dma = [None, None]
            self.dsem_bufs.append(sync)
        if sync.dsem[sw] is None:
            if not self.dsem_free[sw]:
                raise RuntimeError("out of DMA semaphores")
            sync.dsem[sw] = self.dsem_free[sw].pop()
        ds = sync.dsem[sw]
        ev = Ev(eng, None, 0.6 if sw else 0.08)
        ev.is_dma = True
        ev.ndma = len(pairs)
        ev.dsem = ds
        self.dsem_count[ds] += 16 * len(pairs)
        ev.dval = self.dsem_count[ds]
        ev.fn = pairs
        if nbytes is None:
            nbytes = 0
            for (o, i) in pairs:
                try:
                    nbytes += int(o.partition_size) * int(o.free_size) * 4
                except Exception:
                    nbytes += 1 << 19
        ev.lat = 2.0 + nbytes / 150e3
        for ld in sync.last_dma:
            if ld is not None:
                ev.waits.append(ld)
        sync.last_dma[sw] = ev
        self._deps(ev, eng, reads, writes)
        self.segs[-1].append(ev)
        return ev

    def barrier(self):
        self.segs.append([])
        for b in self.dsem_bufs:
            b.dsem = None
            b.last_dma = None
        self.dsem_bufs = []
        self.dsem_free = [list(range(self.n_sw, self.n_dma_sems)), list(range(self.n_sw))]

    def _schedule(self, seg):
        ENGS = self.ENGS
        if not self.reorder:
            return {e: [ev for ev in seg if ev.eng == e] for e in ENGS}
        WIN = {"pe": 320, "act": 256, "dve": 384, "pool": 384, "sp": 64}
        SEM_LAT = 0.7
        JUNK_COST = 0.13
        pend = {e: [ev for ev in seg if ev.eng == e] for e in ENGS}
        out = {e: [] for e in ENGS}
        free_at = {e: 0.0 for e in ENGS}
        inseg = set(id(ev) for ev in seg)
        for ev in seg:
            ev.t_end = None
        n_left = len(seg)
        while n_left:
            best = None
            for e in ENGS:
                q = pend[e]
                lim = min(WIN[e], len(q))
                for k in range(lim):
                    ev = q[k]
                    rdy = 0.0
                    ok = True
                    for d in ev.waits:
                        if id(d) in inseg:
                            if d.t_end is None:
                                ok = False
                                break
                            t = d.t_end + SEM_LAT
                            if t > rdy:
                                rdy = t
                    if ok:
                        for d in ev.order:
                            if id(d) in inseg and d.t_end is None:
                                ok = False
                                break
                    if not ok:
                        continue
                    start = rdy if rdy > free_at[e] else free_at[e]
                    key = (start, k)
                    if best is None or key < best[0]:
                        best = (key, e, k, ev, start)
                    if start <= free_at[e]:
                        break
            if best is None:
                raise RuntimeError("scheduler deadlock")
            _, e, k, ev, start = best
            if e == "pe" and ev.fill and self.junk_fn is not None:
                gap = start - free_at[e]
                nj = int((gap - 0.08) / JUNK_COST) if gap > 0.3 else 0
                for _ in range(min(nj, 40)):
                    jv = Ev("pe", self.junk_fn, JUNK_COST)
                    out[e].append(jv)
                    self.n_junk += 1
            pend[e].pop(k)
            out[e].append(ev)
            free_at[e] = start + ev.cost
            ev.t_end = start + ev.cost + ev.lat
            n_left -= 1
        return out

    def finish(self):
        import contextlib
        nc = self.nc
        prog = {e: [] for e in self.ENGS}
        for seg in self.segs:
            if not seg:
                continue
            sch = self._schedule(seg)
            lastc = [sch[e][-1] for e in self.ENGS if sch[e] and not all(x.is_dma for x in sch[e])]
            lastc = []
            for e in self.ENGS:
                for ev in reversed(sch[e]):
                    if not ev.is_dma:
                        lastc.append(ev)
                        break
            dmas = {}
            for ev in seg:
                if ev.is_dma and (ev.dsem not in dmas or dmas[ev.dsem].dval < ev.dval):
                    dmas[ev.dsem] = ev
            for e in self.ENGS:
                prog[e].extend(sch[e])
                nop = Ev(e, lambda eng: eng.nop())
                nop.fence = True
                nop.waits = [d for d in lastc if d.eng != e] + list(dmas.values())
                prog[e].append(nop)
        for e in self.ENGS:
            for ev in prog[e]:
                for d in ev.waits:
                    if not d.is_dma:
                        d.need_inc = True
        self.prog = prog
        with contextlib.ExitStack() as es:
            esem = {e: es.enter_context(nc.semaphore(f"s_{e}")) for e in self.ENGS}
            dsems = [es.enter_context(nc.semaphore(f"d_{i}")) for i in range(self.n_dma_sems)]
            for e in self.ENGS:
                c = 0
                for ev in prog[e]:
                    if ev.need_inc and not ev.is_dma:
                        c += 1
                        ev.semval = c
            mk = self

            def replay(ename, eng):
                seen = {}
                for ev in prog[ename]:
                    for d in ev.waits:
                        if d.is_dma:
                            key, sem, val = ("d", d.dsem), dsems[d.dsem], d.dval
                        else:
                            key, sem, val = ("e", d.eng), esem[d.eng], d.semval
                        if seen.get(key, 0) < val:
                            eng.wait_ge(sem, val)
                            seen[key] = val
                    if ev.is_dma:
                        for (o, i) in ev.fn:
                            eng.dma_start(out=o, in_=i).then_inc(dsems[ev.dsem], 16)
                    else:
                        inst = ev.fn(eng)
                        if ev.need_inc:
                            inst.then_inc(esem[ename], 1)

            with nc.Block() as block:
                @block.tensor
                def _(eng):
                    replay("pe", eng)

                @block.scalar
                def _(eng):
                    replay("act", eng)

                @block.vector
                def _(eng):
                    replay("dve", eng)

                @block.gpsimd
                def _(eng):
                    replay("pool", eng)

                @block.sync
                def _(eng):
                    replay("sp", eng)

    def stats(self):
        return {e: len(self.prog[e]) for e in self.ENGS}, self._sb_max


def apx(buf, poff, npart, off, dims):
    t = buf.t
    shp = buf.shape
    rowlen = int(np.prod(shp[1:]))
    return bass.AP(t, poff * rowlen + off, [[rowlen, npart]] + [[int(s), int(c)] for (s, c) in dims])


D_MODEL = 1024
SEQ = 4096
HALF = 2048
NT = 16
G = 48
ALPHA = 2.0 ** 0.25
EPS = 1e-5
TWO_PI = float(2 * np.pi)
DPOW = [0, 1, 2, 3, 4, 5, 6, 7, 8, 16, 24, 32, 64, 96, 128, 256, 384, 512, 1024, 1536]
ND = len(DPOW)
SCAN_IDX = [[DPOW.index(8 * e * 4 ** l) for e in (1, 2, 3)] for l in range(4)]
DILS = (1, 4, 16)
GELU_C = float(np.sqrt(2.0 / np.pi))


class K:
    def __init__(self, dbg=None):
        self.dbg = dbg or ()
        nc = bass.Bass("TRN2", target_bir_lowering=False)
        self.nc = nc
        self.mk = MK(nc)
        self.din = {}
        self.build()

    def inp(self, name, shape, dtype=F32):
        b = self.mk.dram_in(name, shape, dtype)
        self.din[name] = b
        return b

    @staticmethod
    def _fs(ap):
        try:
            return float(ap.free_size)
        except Exception:
            return 512.0

    def mm(self, out, lhsT, rhs, start, stop, reads, writes):
        self.mk.op("pe", lambda e: e.matmul(out, lhsT, rhs, start=start, stop=stop), reads=reads, writes=writes,
                   cost=(0.035 + self._fs(out) / 2400.0) * getattr(self, 'pe_scale', 1.0))

    def act(self, out, in_, func, reads, writes, bias=None, scale=None, eng="act"):
        kw = {}
        if bias is not None:
            kw["bias"] = bias
        if scale is not None:
            kw["scale"] = scale
        self.mk.op("act", lambda e: e.activation(out, in_, func, **kw), reads=reads, writes=writes, cost=0.2 + self._fs(out) / 1100.0)

    def tt(self, eng, out, a, b, op, reads, writes):
        self.mk.op(eng, lambda e: e.tensor_tensor(out, a, b, op), reads=reads, writes=writes, cost=self._ec(eng, out))

    def ts(self, eng, out, a, s1, s2, op0, op1, reads, writes):
        if op1 is None:
            self.mk.op(eng, lambda e: e.tensor_scalar(out, a, s1, None, op0=op0), reads=reads, writes=writes, cost=self._ec(eng, out))
        else:
            self.mk.op(eng, lambda e: e.tensor_scalar(out, a, s1, s2, op0=op0, op1=op1), reads=reads, writes=writes, cost=self._ec(eng, out))

    def stt(self, out, a, s, b, op0, op1, reads, writes):
        self.mk.op("dve", lambda e: e.scalar_tensor_tensor(out, a, s, b, op0=op0, op1=op1), reads=reads, writes=writes, cost=self._ec("dve", out))

    def copy(self, eng, out, in_, reads, writes):
        if eng == "act":
            self.mk.op("act", lambda e: e.copy(out, in_), reads=reads, writes=writes, cost=0.2 + self._fs(out) / 1100.0)
        else:
            self.mk.op(eng, lambda e: e.tensor_copy(out, in_), reads=reads, writes=writes, cost=self._ec(eng, out))

    def _ec(self, eng, out):
        f = self._fs(out)
        return (0.25 + f / 550.0) if eng == "pool" else (0.12 + f / 900.0)

    def load(self, eng, dst, dst_ap, src, src_ap):
        self.mk.dma(eng, [(dst_ap, src_ap)], reads=[src], writes=[dst])

    def dump(self, name, buf, ap, shape, dtype=F32):
        if name in self.dbg:
            o = self.mk.dram_out("dbg_" + name, shape, dtype)
            self.mk.dma("sp", [(o.t.ap(), ap)], reads=[buf], writes=[o])

    def wload(self, name, src, rows, col0, ncols, eng="pool"):
        kt = rows // 128
        b = self.mk.sbuf(name, [128, kt, ncols], BF16)
        srcap = src.t.ap().rearrange("(kt p) c -> p kt c", p=128)[:, :, col0:col0 + ncols]
        self.mk.dma(eng, [(b.t[:], srcap)], reads=[src], writes=[b])
        return b

    def wload_into(self, b, src, rows, col0, ncols, eng="pool"):
        srcap = src.t.ap().rearrange("(kt p) c -> p kt c", p=128)[:, :, col0:col0 + ncols]
        self.mk.dma(eng, [(b.t[:], srcap)], reads=[src], writes=[b])

    def layernorm(self, r, gB, bB, out32, outT, tcol, tmp16, psT_bank, st):
        mk = self.mk
        ident = self.ident
        mk.op("dve", lambda e: e.bn_stats(st.t[:, 0:6], r.t[:, 0:512]), reads=[r], writes=[st], cost=0.7)
        mk.op("dve", lambda e: e.bn_stats(st.t[:, 6:12], r.t[:, 512:1024]), reads=[r], writes=[st], cost=0.6)
        mk.op("dve", lambda e: e.bn_aggr(st.t[:, 12:14], st.t[:, 0:12]), reads=[st], writes=[st], cost=0.21)
        self.act(st.t[:, 14:15], st.t[:, 13:14], AF.Sqrt, [st, self.cst], [st], bias=self.cst.t[:, 0:1], scale=1.0)
        mk.op("dve", lambda e: e.reciprocal(st.t[:, 15:16], st.t[:, 14:15]), reads=[st], writes=[st], cost=0.17)
        self.ts("dve", st.t[:, 16:17], st.t[:, 12:13], st.t[:, 15:16], -1.0, ALU.mult, ALU.mult, [st], [st])
        self.act(out32.t[:], r.t[:], AF.Identity, [st, r], [out32], bias=st.t[:, 16:17], scale=st.t[:, 15:16])
        self.tt("dve", out32.t[:], out32.t[:], gB.t[:], ALU.mult, [out32, gB], [out32])
        self.tt("pool", out32.t[:], out32.t[:], bB.t[:], ALU.add, [out32, bB], [out32])
        if outT is None:
            return
        self.copy("act", tmp16.t[:], out32.t[:], [out32], [tmp16])
        psb = psT_bank.t[:].bitcast(BF16)
        for kt in range(8):
            mk.op("pe", lambda e, kt=kt: e.transpose(psb[:, kt * 128:(kt + 1) * 128], tmp16.t[:, kt * 128:(kt + 1) * 128], ident.t[:]),
                  reads=[tmp16, ident], writes=[psT_bank], cost=0.09)
        dst = apx(outT[0], 0, 128, tcol, [(outT[1], 8), (1, 128)])
        src = psb.rearrange("p (k c) -> p k c", k=8)
        self.copy("act", dst, src, [psT_bank], [outT[0]])

    def build(self):
        mk = self.mk
        inp = self.inp
        x_own = inp("x_own", [HALF, D_MODEL]); x_prev = inp("x_prev", [HALF, D_MODEL])
        flag_d = inp("flag", [128, 1]); posb = inp("posb", [128, 512], I32)
        mem = inp("mem", [256, D_MODEL])
        w_in = inp("w_in", [1024, 5120]); w_sw = inp("w_sw", [1024, 1536])
        bin_fm = inp("bin_fm", [128, 40]); bsw_fm = inp("bsw_fm", [128, 12]); bv_row = inp("bv_row", [1, 768])
        ln_g = inp("ln_g", [4, 1024]); ln_b = inp("ln_b", [4, 1024])
        s5p = inp("s5p", [128, 3 * G])
        s5B = inp("s5B", [128, 2 * G * 16])
        s5C = inp("s5C", [128, 2 * G * 16])
        s5d = inp("s5d", [128, 6])
        w_glu = inp("w_glu", [768, 2048]); bglu_fm = inp("bglu_fm", [128, 16])
        w_au = inp("w_au", [256, 1024]); w_mo = inp("w_mo", [1024, 1024]); bmo_row = inp("bmo_row", [1, 1024])
        w_xq = inp("w_xq", [1024, 1024]); w_xkv = inp("w_xkv", [1024, 2048]); w_xo = inp("w_xo", [1024, 1024])
        w_f1 = inp("w_f1", [1024, 4096]); bf1_fm = inp("bf1_fm", [128, 32]); w_f2 = inp("w_f2", [4096, 1024])
        bf2_row = inp("bf2_row", [1, 1024])
        c_bf = inp("c_bf", [128, 128 * 3 + 256 + 64], BF16)
        c_f32 = inp("c_f32", [128, 512])
        out = mk.dram_out("out", [HALF, D_MODEL], F32)
        res_d = [mk.dram_tmp(f"res{t}", [128, D_MODEL], F32) for t in range(NT)]
        tabs = {nm: mk.dram_tmp("tab" + nm, [128, G * 8], F32) for nm in ("ZA", "ZB", "QA", "QB")}
        tabs["WR"] = mk.dram_tmp("tabWR", [128, G * 24], F32)
        tabs["Cneg"] = mk.dram_tmp("tabCneg", [128, G * 16], BF16)

        cbf = mk.sbuf("cbf", [128, 128 * 3 + 256 + 64], BF16)
        cf = mk.sbuf("cf", [128, 512], F32)
        self.cst = cf
        self.cbf = cbf
        flag = mk.sbuf("flag", [128, 1], F32)
        mk.dma("sp", [(cbf.t[:], c_bf.t.ap())], reads=[c_bf], writes=[cbf])
        mk.dma("sp", [(cf.t[:], c_f32.t.ap())], reads=[c_f32], writes=[cf])
        mk.dma("sp", [(flag.t[:], flag_d.t.ap())], reads=[flag_d], writes=[flag])
        ident = mk.view("ident", cbf.t[:, 0:128]); ident.writer = cbf.writer
        self.ident = ident
        ones = cbf.t[:, 128:256]
        maskpc = cbf.t[:, 384:640]
        AT = mk.sbuf("AT", [128, 8, HALF], BF16)
        ATp = (AT, HALF)
        st = [mk.sbuf(f"st{i}", [128, 32], F32) for i in range(8)]

        def ln_params(i, gB, bB):
            mk.dma("sp", [(gB.t[:], ln_g.t.ap()[i:i + 1, :].partition_broadcast(128).rearrange("p o c -> p (o c)"))], reads=[ln_g], writes=[gB])
            mk.dma("sp", [(bB.t[:], ln_b.t.ap()[i:i + 1, :].partition_broadcast(128).rearrange("p o c -> p (o c)"))], reads=[ln_b], writes=[bB])

        with mk.phase():
            attT = mk.sbuf("attT", [128, 2, HALF], BF16)
            HTp = mk.sbuf("HTp", [128, 8, HALF], BF16)
            COS = mk.dram_tmp("cosd", [16, SEQ], BF16); SIN = mk.dram_tmp("sind", [16, SEQ], BF16)
            with mk.phase():
                self.s5_prep(s5p, s5C, tabs)
                self.rope_tables(posb, COS, SIN)
                gB = mk.sbuf("gB", [128, 1024], F32); bB = mk.sbuf("bB", [128, 1024], F32)
                ln_params(0, gB, bB)
                NB = 4
                xt = [mk.sbuf(f"xt{i}", [128, 1024], F32) for i in range(NB)]
                o32 = [mk.sbuf(f"o32{i}", [128, 1024], F32) for i in range(NB)]
                t16 = [mk.sbuf(f"t16{i}", [128, 1024], BF16) for i in range(NB)]

                def ld(t):
                    own = t >= NT
                    tt_ = t - NT if own else t
                    src = x_own if own else x_prev
                    mk.dma("sp", [(xt[t % NB].t[:], src.t.ap()[tt_ * 128:(tt_ + 1) * 128, :])], reads=[src], writes=[xt[t % NB]])

                ld(0); ld(1)
                for t in range(2 * NT):
                    own = t >= NT
                    tt_ = t - NT if own else t
                    if t + 2 < 2 * NT:
                        ld(t + 2)
                    i = t % NB
                    self.layernorm(xt[i], gB, bB, o32[i], (AT if own else HTp, HALF), tt_ * 128, t16[i], mk.bank(t % 2), st[i])
                    if own:
                        mk.dma("sp", [(res_d[tt_].t.ap(), o32[i].t[:])], reads=[o32[i]], writes=[res_d[tt_]])
            self.dump("hT", AT, AT.t[:], [128, 8, HALF], BF16)
            self.attention(w_in, w_sw, bin_fm, bsw_fm, bv_row, COS, SIN, flag, AT, HTp, attT, ones, maskpc)
            self.dump("attT", attT, attT.t[:], [128, 2, HALF], BF16)
            gyd = [mk.dram_tmp(f"gyd{j}", [128, HALF], BF16) for j in range(6)]
            self.s5(w_in, bin_fm, s5B, s5C, s5d, tabs, flag, AT, HTp, gyd)
            gy = mk.sbuf("gy", [128, 6, HALF], BF16)
            mk.dma("sp", [(gy.t[:, j, :], gyd[j].t.ap()) for j in range(6)], reads=gyd, writes=[gy])
            MIX = mk.sbuf("MIX", [128, 8, HALF], BF16)
            self.phaseE(w_in, bin_fm, w_glu, bglu_fm, w_au, AT, gy, attT, MIX)
            self.dump("MIX", MIX, MIX.t[:], [128, 8, HALF], BF16)
            self.proj_ln(MIX, w_mo, bmo_row, 1, AT, res_d, ln_params, st, ones, NB=4)
        self.dump("h1T", AT, AT.t[:], [128, 8, HALF], BF16)
        self.xattn(mem, w_xq, w_xkv, w_xo, AT, res_d, ln_params, st, ones)
        self.dump("h2T", AT, AT.t[:], [128, 8, HALF], BF16)
        self.ffn(w_f1, bf1_fm, w_f2, bf2_row, AT, res_d, ln_params, st, ones, out)
        mk.finish()

    def phaseE(self, w_in, bin_fm, w_glu, bglu_fm, w_au, AT, gy, attT, MIX):
        mk = self.mk
        with mk.phase():
            wg1 = [mk.sbuf(f"wg1{i}", [128, 6, 128], BF16) for i in range(2)]
            wg2 = [mk.sbuf(f"wg2{i}", [128, 6, 128], BF16) for i in range(2)]
            wgs = [mk.sbuf(f"wgs{i}", [128, 8, 128], BF16) for i in range(2)]
            wga = [mk.sbuf(f"wga{i}", [128, 8, 128], BF16) for i in range(2)]
            wau = [mk.sbuf(f"wau{i}", [128, 2, 128], BF16) for i in range(2)]
            binb = mk.sbuf("binb", [128, 40], F32); bglu = mk.sbuf("bglu", [128, 16], F32)
            mk.dma("sp", [(binb.t[:], bin_fm.t.ap())], reads=[bin_fm], writes=[binb])
            mk.dma("sp", [(bglu.t[:], bglu_fm.t.ap())], reads=[bglu_fm], writes=[bglu])
            tmp = [[mk.sbuf(f"e{k}{i}", [128, 512], F32) for k in range(5)] for i in range(2)]
            it = 0
            def ldw(mt):
                self.wload_into(wg1[mt % 2], w_glu, 768, mt * 128, 128)
                self.wload_into(wg2[mt % 2], w_glu, 768, 1024 + mt * 128, 128)
                self.wload_into(wgs[mt % 2], w_in, 1024, 3072 + mt * 128, 128)
                self.wload_into(wga[mt % 2], w_in, 1024, 4096 + mt * 128, 128)
                self.wload_into(wau[mt % 2], w_au, 256, mt * 128, 128)

            ldw(0)
            for mt in range(8):
                w1, w2, w3, w4, w5 = wg1[mt % 2], wg2[mt % 2], wgs[mt % 2], wga[mt % 2], wau[mt % 2]
                if mt + 1 < 8:
                    ldw(mt + 1)
                for blk in range(4):
                    bs = slice(blk * 512, (blk + 1) * 512)
                    sg2, t1, sgs, sga, t2 = tmp[it % 2]
                    it += 1
                    bset = 4 * (it % 2)
                    pz1, pz2, pgs, pga = [mk.bank(bset + i) for i in range(4)]
                    pba = pz2
                    for kt in range(6):
                        self.mm(pz1.t[:], w1.t[:, kt, :], gy.t[:, kt, bs], kt == 0, kt == 5, [w1, gy], [pz1])
                    for kt in range(6):
                        self.mm(pz2.t[:], w2.t[:, kt, :], gy.t[:, kt, bs], kt == 0, kt == 5, [w2, gy], [pz2])
                    for kt in range(8):
                        self.mm(pgs.t[:], w3.t[:, kt, :], AT.t[:, kt, bs], kt == 0, kt == 7, [w3, AT], [pgs])
                    for kt in range(8):
                        self.mm(pga.t[:], w4.t[:, kt, :], AT.t[:, kt, bs], kt == 0, kt == 7, [w4, AT], [pga])
                    self.act(sg2.t[:], pz2.t[:], AF.Sigmoid, [pz2, bglu], [sg2], bias=bglu.t[:, 8 + mt:9 + mt])
                    for kt in range(2):
                        self.mm(pba.t[:], w5.t[:, kt, :], attT.t[:, kt, bs], kt == 0, kt == 1, [w5, attT], [pba])
                    self.stt(t1.t[:], pz1.t[:], bglu.t[:, mt:mt + 1], sg2.t[:], ALU.add, ALU.mult, [pz1, bglu, sg2], [t1])
                    self.act(sgs.t[:], pgs.t[:], AF.Sigmoid, [pgs, binb], [sgs], bias=binb.t[:, 24 + mt:25 + mt])
                    self.act(sga.t[:], pga.t[:], AF.Sigmoid, [pga, binb], [sga], bias=binb.t[:, 32 + mt:33 + mt])
                    self.tt("pool", t1.t[:], t1.t[:], sgs.t[:], ALU.mult, [t1, sgs], [t1])
                    self.tt("dve", t2.t[:], pba.t[:], sga.t[:], ALU.mult, [pba, sga], [t2])
                    self.tt("pool", MIX.t[:, mt, bs], t1.t[:], t2.t[:], ALU.add, [t1, t2], [MIX])

    def proj_ln(self, X, w, brow_d, ln_idx, AT, res_d, ln_params, st, ones, NB=3):
        mk = self.mk
        with mk.phase():
            W = self.wload("W", w, 1024, 0, 1024)
            gB = mk.sbuf("gB", [128, 1024], F32); bB = mk.sbuf("bB", [128, 1024], F32)
            ln_params(ln_idx, gB, bB)
            if brow_d is not None:
                brow = mk.sbuf("brow", [1, 1024], BF16)
                mk.dma("pool", [(brow.t[:], brow_d.t.ap())], reads=[brow_d], writes=[brow])
            rt = [mk.sbuf(f"rt{i}", [128, 1024], F32) for i in range(NB)]
            o32 = [mk.sbuf(f"o32{i}", [128, 1024], F32) for i in range(NB)]
            t16 = [mk.sbuf(f"t16{i}", [128, 1024], BF16) for i in range(NB)]

            def ld(t):
                mk.dma("sp", [(rt[t % NB].t[:], res_d[t].t.ap())], reads=[res_d[t]], writes=[rt[t % NB]])

            ld(0); ld(1)
            for t in range(NT):
                i = t % NB
                ts_ = slice(t * 128, (t + 1) * 128)
                if t + 2 < NT:
                    ld(t + 2)
                for half in range(2):
                    hs = slice(half * 512, (half + 1) * 512)
                    ps = mk.bank(2 * (t % 3) + half)
                    for kt in range(8):
                        self.mm(ps.t[:], X.t[:, kt, ts_], W.t[:, kt, hs], kt == 0, (kt == 7 and brow_d is None), [X, W], [ps])
                    if brow_d is not None:
                        self.mm(ps.t[:], ones[0:1, :], brow.t[0:1, hs], False, True, [self.cbf, brow], [ps])
                    self.stt(rt[i].t[:, hs], rt[i].t[:, hs], ALPHA, ps.t[:], ALU.mult, ALU.add, [rt[i], ps], [rt[i]])
                self.layernorm(rt[i], gB, bB, o32[i], (AT, HALF), t * 128, t16[i], mk.bank(6 + t % 2), st[t % 8])
                mk.dma("sp", [(res_d[t].t.ap(), o32[i].t[:])], reads=[o32[i]], writes=[res_d[t]])

    def xattn(self, mem, w_xq, w_xkv, w_xo, AT, res_d, ln_params, st, ones):
        mk = self.mk
        ident = self.ident
        with mk.phase():
            KmT = mk.sbuf("KmT", [128, 8, 256], BF16)
            Vm = mk.sbuf("Vm", [128, 2, 1024], BF16)
            OX = mk.sbuf("OX", [128, 8, HALF], BF16)
            with mk.phase():
                memT = mk.sbuf("memT", [128, 8, 256], BF16)
                mt32 = mk.sbuf("mt32", [128, 1024], F32); mt16 = mk.sbuf("mt16", [128, 1024], BF16)
                for mtile in range(2):
                    mk.dma("sp", [(mt32.t[:], mem.t.ap()[mtile * 128:(mtile + 1) * 128, :])], reads=[mem], writes=[mt32])
                    self.copy("dve", mt16.t[:], mt32.t[:], [mt32], [mt16])
                    pst = mk.bank(0)
                    psb = pst.t[:].bitcast(BF16)
                    for kt in range(8):
                        mk.op("pe", lambda e, kt=kt, psb=psb: e.transpose(psb[:, kt * 128:(kt + 1) * 128], mt16.t[:, kt * 128:(kt + 1) * 128], ident.t[:]),
                              reads=[mt16, ident], writes=[pst])
                    self.copy("act", apx(memT, 0, 128, mtile * 128, [(256, 8), (1, 128)]), psb.rearrange("p (k c) -> p k c", k=8), [pst], [memT])
                wk = self.wload("wk", w_xkv, 1024, 0, 1024)
                for mt in range(8):
                    ps = mk.bank(1 + mt % 2)
                    for kt in range(8):
                        self.mm(ps.t[:, 0:256], wk.t[:, kt, mt * 128:(mt + 1) * 128], memT.t[:, kt, :], kt == 0, kt == 7, [wk, memT], [ps])
                    self.copy("act" if mt % 2 else "dve", KmT.t[:, mt, :], ps.t[:, 0:256], [ps], [KmT])
                wv = self.wload("wvx", w_xkv, 1024, 1024, 1024)
                for mtile in range(2):
                    for half in range(2):
                        ps = mk.bank(3 + half)
                        for kt in range(8):
                            self.mm(ps.t[:], memT.t[:, kt, mtile * 128:(mtile + 1) * 128], wv.t[:, kt, half * 512:(half + 1) * 512], kt == 0, kt == 7, [memT, wv], [ps])
                        self.copy("act" if half else "dve", Vm.t[:, mtile, half * 512:(half + 1) * 512], ps.t[:], [ps], [Vm])
            with mk.phase():
                QX = mk.sbuf("QX", [128, 8, HALF], BF16)
                wq = self.wload("wxq", w_xq, 1024, 0, 1024)
                for blk in range(4):
                    bs = slice(blk * 512, (blk + 1) * 512)
                    for mt in range(8):
                        ps = mk.bank((blk * 8 + mt) % 4)
                        for kt in range(8):
                            self.mm(ps.t[:], wq.t[:, kt, mt * 128:(mt + 1) * 128], AT.t[:, kt, bs], kt == 0, kt == 7, [wq, AT], [ps])
                        self.copy("act" if mt % 2 else "dve", QX.t[:, mt, bs], ps.t[:], [ps], [QX])
                PTx = [[mk.sbuf(f"PTx{i}{m}", [128, 512], BF16) for m in range(2)] for i in range(2)]
                rden = [mk.sbuf(f"rden{i}", [128, 512], F32) for i in range(2)]
                it = 0
                for blk in range(4):
                    bs = slice(blk * 512, (blk + 1) * 512)
                    for h in range(4):
                        i = it % 2
                        it += 1
                        for mtile in range(2):
                            pS = mk.bank(2 * (it % 2) + mtile)
                            for j in range(2):
                                self.mm(pS.t[:], KmT.t[:, 2 * h + j, mtile * 128:(mtile + 1) * 128], QX.t[:, 2 * h + j, bs], j == 0, j == 1, [KmT, QX], [pS])
                            self.act(PTx[i][mtile].t[:], pS.t[:], AF.Exp, [pS], [PTx[i][mtile]], scale=1.0 / 16.0)
                        pD = mk.bank(4 + it % 2)
                        for mtile in range(2):
                            self.mm(pD.t[:], ones, PTx[i][mtile].t[:], mtile == 0, mtile == 1, [self.cbf, PTx[i][mtile]], [pD])
                        mk.op("dve", lambda e, i=i, pD=pD: e.reciprocal(rden[i].t[:], pD.t[:]), reads=[pD], writes=[rden[i]], cost=0.7)
                        for j in range(2):
                            pO = mk.bank(6 + j)
                            for mtile in range(2):
                                self.mm(pO.t[:], Vm.t[:, mtile, (2 * h + j) * 128:(2 * h + j + 1) * 128], PTx[i][mtile].t[:], mtile == 0, mtile == 1, [Vm, PTx[i][mtile]], [pO])
                            self.tt("dve", OX.t[:, 2 * h + j, bs], pO.t[:], rden[i].t[:], ALU.mult, [pO, rden[i]], [OX])
            self.dump("OX", OX, OX.t[:], [128, 8, HALF], BF16)
            self.proj_ln(OX, w_xo, None, 2, AT, res_d, ln_params, st, ones, NB=6)

    def ffn(self, w_f1, bf1_fm, w_f2, bf2_row, AT, res_d, ln_params, st, ones, out):
        mk = self.mk
        with mk.phase():
            acc = [mk.sbuf(f"acc{t}", [128, 1024], F32) for t in range(NT)]
            bf1 = mk.sbuf("bf1", [128, 32], F32)
            brow = mk.sbuf("brow2", [1, 1024], BF16)
            mk.dma("sp", [(bf1.t[:], bf1_fm.t.ap())], reads=[bf1_fm], writes=[bf1])
            mk.dma("pool", [(brow.t[:], bf2_row.t.ap())], reads=[bf2_row], writes=[brow])
            for t in range(NT):
                mk.dma("sp", [(acc[t].t[:], res_d[t].t.ap())], reads=[res_d[t]], writes=[acc[t]])
                mk.op("act", lambda e, t=t: e.mul(acc[t].t[:], acc[t].t[:], ALPHA), reads=[acc[t]], writes=[acc[t]])
            with mk.phase():
                W1 = [mk.sbuf(f"W1{i}", [128, 8, 512], BF16) for i in range(2)]
                W2 = [mk.sbuf(f"W2{i}", [128, 4, 1024], BF16) for i in range(2)]
                hid = [mk.sbuf(f"hid{i}", [128, 4, 512], BF16) for i in range(2)]
                tf_ = [mk.sbuf(f"tf{i}", [128, 512], F32) for i in range(2)]
                hi = 0
                oi = 0
                def ldw(c):
                    self.wload_into(W1[c % 2], w_f1, 1024, c * 512, 512)
                    mk.dma("pool", [(W2[c % 2].t[:], w_f2.t.ap()[c * 512:(c + 1) * 512, :].rearrange("(kt p) c -> p kt c", p=128))], reads=[w_f2], writes=[W2[c % 2]])

                ldw(0)
                for c in range(8):
                    w1 = W1[c % 2]; w2 = W2[c % 2]
                    if c + 1 < 8:
                        ldw(c + 1)
                    for blk in range(4):
                        bs = slice(blk * 512, (blk + 1) * 512)
                        hb = hid[hi % 2]
                        hi += 1
                        for ft in range(4):
                            pH = mk.bank(ft % 2)
                            for kt in range(8):
                                self.mm(pH.t[:], w1.t[:, kt, ft * 128:(ft + 1) * 128], AT.t[:, kt, bs], kt == 0, kt == 7, [w1, AT], [pH])
                            tb = tf_[ft % 2]
                            self.act(tb.t[:], pH.t[:], AF.Relu, [pH, bf1], [tb], bias=bf1.t[:, c * 4 + ft:c * 4 + ft + 1])
                            self.tt("pool", hb.t[:, ft, :], tb.t[:], tb.t[:], ALU.mult, [tb], [hb])
                        for tl in range(4):
                            T = blk * 4 + tl
                            for half in range(2):
                                hs = slice(half * 512, (half + 1) * 512)
                                pO = mk.bank(2 + oi % 6)
                                oi += 1
                                for ft in range(4):
                                    self.mm(pO.t[:], hb.t[:, ft, tl * 128:(tl + 1) * 128], w2.t[:, ft, hs], ft == 0, (ft == 3 and c != 0), [hb, w2], [pO])
                                if c == 0:
                                    self.mm(pO.t[:], ones[0:1, :], brow.t[0:1, hs], False, True, [self.cbf, brow], [pO])
                                self.tt("dve", acc[T].t[:, hs], acc[T].t[:, hs], pO.t[:], ALU.add, [acc[T], pO], [acc[T]])
            with mk.phase():
                gB = mk.sbuf("gB", [128, 1024], F32); bB = mk.sbuf("bB", [128, 1024], F32)
                ln_params(3, gB, bB)
                o32 = [mk.sbuf(f"o32{i}", [128, 1024], F32) for i in range(6)]
                for t in range(NT):
                    i = t % 6
                    self.layernorm(acc[t], gB, bB, o32[i], None, 0, None, None, st[t % 8])
                    mk.dma("sp", [(out.t.ap()[t * 128:(t + 1) * 128, :], o32[i].t[:])], reads=[o32[i]], writes=[out])

    def sin_of(self, out, ang, tmpi, tmpf, tmpm, eng="dve", out_ap=None):
        PI = float(np.pi)
        self.ts(eng, tmpi.t[:], ang.t[:], 1.0 / TWO_PI, None, ALU.mult, None, [ang], [tmpi])
        self.copy(eng, tmpf.t[:], tmpi.t[:], [tmpi], [tmpf])
        self.ts(eng, tmpf.t[:], tmpf.t[:], -TWO_PI, None, ALU.mult, None, [tmpf], [tmpf])
        self.tt(eng, tmpf.t[:], tmpf.t[:], ang.t[:], ALU.add, [tmpf, ang], [tmpf])
        self.ts(eng, tmpm.t[:], tmpf.t[:], PI, -TWO_PI, ALU.is_gt, ALU.mult, [tmpf], [tmpm])
        self.tt(eng, tmpf.t[:], tmpf.t[:], tmpm.t[:], ALU.add, [tmpf, tmpm], [tmpf])
        self.ts(eng, tmpm.t[:], tmpf.t[:], -PI, TWO_PI, ALU.is_lt, ALU.mult, [tmpf], [tmpm])
        self.tt(eng, tmpf.t[:], tmpf.t[:], tmpm.t[:], ALU.add, [tmpf, tmpm], [tmpf])
        self.ts(eng, tmpf.t[:], tmpf.t[:], 3.14159, -3.14159, ALU.min, ALU.max, [tmpf], [tmpf])
        self.act(out.t[:] if out_ap is None else out_ap, tmpf.t[:], AF.Sin, [tmpf], [out])

    def rope_tables(self, posb, COS, SIN):
        mk = self.mk
        cf = self.cst
        CH = 512
        pi_ = mk.sbuf("posi", [128, CH], I32)
        ang = mk.sbuf("ang", [128, CH], F32); sc = mk.sbuf("sc", [128, CH], F32)
        ti = mk.sbuf("ti", [128, CH], I32); tf = mk.sbuf("tf", [128, CH], F32); tm = mk.sbuf("tm", [128, CH], F32)
        s16 = mk.sbuf("s16", [128, CH], BF16); c16 = mk.sbuf("c16", [128, CH], BF16)
        mk.dma("sp", [(pi_.t[:], posb.t.ap())], reads=[posb], writes=[pi_])
        self.copy("dve", ang.t[:], pi_.t[:], [pi_], [ang])
        self.ts("dve", ang.t[:], ang.t[:], cf.t[:, 3:4], None, ALU.mult, None, [ang, cf], [ang])
        self.sin_of(sc, ang, ti, tf, tm)
        self.ts("dve", s16.t[:], sc.t[:], cf.t[:, 4:5], None, ALU.mult, None, [sc, cf], [s16])
        self.ts("dve", ang.t[:], ang.t[:], float(np.pi / 2), None, ALU.add, None, [ang], [ang])
        self.sin_of(c16, ang, ti, tf, tm)
        mk.dma("sp", [(SIN.t.ap().rearrange("q (c j) -> q c j", c=8)[:, c, :], s16.t[16 * c:16 * c + 16, :]) for c in range(8)], reads=[s16], writes=[SIN])
        mk.dma("sp", [(COS.t.ap().rearrange("q (c j) -> q c j", c=8)[:, c, :], c16.t[16 * c:16 * c + 16, :]) for c in range(8)], reads=[c16], writes=[COS])

    def s5_prep(self, s5p, s5C, tabs):
        mk = self.mk
        cf = self.cst
        P = mk.sbuf("P", [128, 3 * G], F32)
        CRI = mk.sbuf("CRI", [128, 2 * G * 16], F32)
        for (b_, s_) in ((P, s5p), (CRI, s5C)):
            mk.dma("sp", [(b_.t[:], s_.t.ap())], reads=[s_], writes=[b_])
        n2 = G * ND
        PR = mk.sbuf("PR", [128, G, ND], F32); PI_ = mk.sbuf("PI", [128, G, ND], F32)
        ZA = mk.sbuf("ZA", [128, G, 8], F32); ZB = mk.sbuf("ZB", [128, G, 8], F32)
        QA = mk.sbuf("QA", [128, G, 8], F32); QB = mk.sbuf("QB", [128, G, 8], F32)
        WR = mk.sbuf("WR", [128, G, 12, 2], F32)
        Cneg = mk.sbuf("Cneg", [128, G * 16], BF16)
        dt = mk.sbuf("dt", [128, G], F32); lam = mk.sbuf("lam", [128, G], F32); th = mk.sbuf("th", [128, G], F32)
        ANG = mk.sbuf("ANG", [128, n2], F32); LAM = mk.sbuf("LAMb", [128, n2], F32)
        SN = mk.sbuf("SN", [128, n2], F32); CS = mk.sbuf("CS", [128, n2], F32)
        ti = mk.sbuf("ti", [128, n2], I32); tf = mk.sbuf("tf", [128, n2], F32); tm = mk.sbuf("tm", [128, n2], F32)
        self.act(dt.t[:], P.t[:, 0:G], AF.Exp, [P], [dt])
        self.tt("dve", lam.t[:], P.t[:, G:2 * G], dt.t[:], ALU.mult, [P, dt], [lam])
        self.tt("dve", th.t[:], P.t[:, 2 * G:3 * G], dt.t[:], ALU.mult, [P, dt], [th])
        dp = apx(cf, 0, 128, 16, [(0, G), (1, ND)])
        self.tt("dve", ANG.t[:].rearrange("p (g k) -> p g k", g=G), apx(th, 0, 128, 0, [(1, G), (0, ND)]), dp, ALU.mult, [th, cf], [ANG])
        self.tt("dve", LAM.t[:].rearrange("p (g k) -> p g k", g=G), apx(lam, 0, 128, 0, [(1, G), (0, ND)]), dp, ALU.mult, [lam, cf], [LAM])
        self.act(LAM.t[:], LAM.t[:], AF.Exp, [LAM], [LAM])
        self.sin_of(SN, ANG, ti, tf, tm)
        self.ts("dve", ANG.t[:], ANG.t[:], float(np.pi / 2), None, ALU.add, None, [ANG], [ANG])
        self.sin_of(CS, ANG, ti, tf, tm)
        prf = PR.t[:].rearrange("p g k -> p (g k)"); pif = PI_.t[:].rearrange("p g k -> p (g k)")
        self.tt("dve", prf, LAM.t[:], CS.t[:], ALU.mult, [LAM, CS], [PR])
        self.tt("dve", pif, LAM.t[:], SN.t[:], ALU.mult, [LAM, SN], [PI_])
        nr = mk.sbuf("nr", [128, G], F32); den = mk.sbuf("den", [128, G], F32); t1 = mk.sbuf("t1", [128, G], F32)
        fr = mk.sbuf("fr", [128, G], F32); fi = mk.sbuf("fi", [128, G], F32)
        are = P.t[:, G:2 * G]; aim = P.t[:, 2 * G:3 * G]
        pr1 = apx(PR, 0, 128, 1, [(ND, G)]); pi1 = apx(PI_, 0, 128, 1, [(ND, G)])
        self.ts("dve", nr.t[:], pr1, -1.0, None, ALU.add, None, [PR], [nr])
        self.tt("dve", den.t[:], are, are, ALU.mult, [P], [den])
        self.tt("dve", t1.t[:], aim, aim, ALU.mult, [P], [t1])
        self.tt("dve", den.t[:], den.t[:], t1.t[:], ALU.add, [den, t1], [den])
        mk.op("dve", lambda e: e.reciprocal(den.t[:], den.t[:]), reads=[den], writes=[den])
        self.tt("dve", fr.t[:], nr.t[:], are, ALU.mult, [nr, P], [fr])
        self.tt("dve", t1.t[:], pi1, aim, ALU.mult, [PI_, P], [t1])
        self.tt("dve", fr.t[:], fr.t[:], t1.t[:], ALU.add, [fr, t1], [fr])
        self.tt("dve", fr.t[:], fr.t[:], den.t[:], ALU.mult, [fr, den], [fr])
        self.tt("dve", fi.t[:], pi1, are, ALU.mult, [PI_, P], [fi])
        self.tt("dve", t1.t[:], nr.t[:], aim, ALU.mult, [nr, P], [t1])
        self.tt("dve", fi.t[:], fi.t[:], t1.t[:], ALU.subtract, [fi, t1], [fi])
        self.tt("dve", fi.t[:], fi.t[:], den.t[:], ALU.mult, [fi, den], [fi])
        ZR = mk.sbuf("ZR", [128, G, 8], F32); ZI = mk.sbuf("ZI", [128, G, 8], F32); T8 = mk.sbuf("T8", [128, G, 8], F32)
        pr8 = apx(PR, 0, 128, 0, [(ND, G), (1, 8)]); pi8 = apx(PI_, 0, 128, 0, [(ND, G), (1, 8)])
        frb = apx(fr, 0, 128, 0, [(1, G), (0, 8)]); fib = apx(fi, 0, 128, 0, [(1, G), (0, 8)])
        self.tt("dve", ZR.t[:], pr8, frb, ALU.mult, [PR, fr], [ZR])
        self.tt("dve", T8.t[:], pi8, fib, ALU.mult, [PI_, fi], [T8])
        self.tt("dve", ZR.t[:], ZR.t[:], T8.t[:], ALU.subtract, [ZR, T8], [ZR])
        self.tt("dve", ZI.t[:], pr8, fib, ALU.mult, [PR, fi], [ZI])
        self.tt("dve", T8.t[:], pi8, frb, ALU.mult, [PI_, fr], [T8])
        self.tt("dve", ZI.t[:], ZI.t[:], T8.t[:], ALU.add, [ZI, T8], [ZI])
        U_, L_ = slice(0, 64), slice(64, 128)
        self.copy("dve", ZA.t[U_], ZR.t[U_], [ZR], [ZA]); self.copy("dve", ZA.t[L_], ZI.t[L_], [ZI], [ZA])
        self.ts("dve", ZB.t[U_], ZI.t[U_], -1.0, None, ALU.mult, None, [ZI], [ZB]); self.copy("dve", ZB.t[L_], ZR.t[L_], [ZR], [ZB])
        pr18 = lambda sl: apx(PR, sl.start, 64, 1, [(ND, G), (1, 8)])
        pi18 = lambda sl: apx(PI_, sl.start, 64, 1, [(ND, G), (1, 8)])
        self.copy("dve", QA.t[U_], pr18(U_), [PR], [QA]); self.ts("dve", QA.t[L_], pi18(L_), -1.0, None, ALU.mult, None, [PI_], [QA])
        self.ts("dve", QB.t[U_], pi18(U_), -1.0, None, ALU.mult, None, [PI_], [QB]); self.ts("dve", QB.t[L_], pr18(L_), -1.0, None, ALU.mult, None, [PR], [QB])
        wro = lambda sl, h: apx(WR, sl.start, 64, h, [(24, G), (2, 12)])
        prs = lambda sl: apx(PR, sl.start, 64, 8, [(ND, G), (1, 12)])
        pis = lambda sl: apx(PI_, sl.start, 64, 8, [(ND, G), (1, 12)])
        self.copy("dve", wro(U_, 0), prs(U_), [PR], [WR]); self.copy("dve", wro(U_, 1), pis(U_), [PI_], [WR])
        self.ts("dve", wro(L_, 0), pis(L_), -1.0, None, ALU.mult, None, [PI_], [WR]); self.copy("dve", wro(L_, 1), prs(L_), [PR], [WR])
        self.copy("dve", Cneg.t[U_], CRI.t[U_, 0:G * 16], [CRI], [Cneg])
        self.ts("dve", Cneg.t[L_], CRI.t[L_, G * 16:2 * G * 16], -1.0, None, ALU.mult, None, [CRI], [Cneg])

        for nm, b_ in (("ZA", ZA), ("ZB", ZB), ("QA", QA), ("QB", QB), ("WR", WR), ("Cneg", Cneg)):
            mk.dma("sp", [(tabs[nm].t.ap(), b_.t[:])], reads=[b_], writes=[tabs[nm]])

    def s5(self, w_in, bin_fm, s5B, s5C, s5d, tabs, flag, AT, HTp, gyd):
        self.pe_scale = 1.8
        jb = self.mk.bank(7)
        idt = self.ident
        cb = self.cbf
        self.mk.junk_fn = lambda e: e.matmul(jb.t[:, 0:256], idt.t[:], cb.t[:, 0:256], start=True, stop=True)
        self.mk.fill_flag = True
        try:
            self._s5(w_in, bin_fm, s5B, s5C, s5d, tabs, flag, AT, HTp, gyd)
        finally:
            self.pe_scale = 1.0
            self.mk.fill_flag = False

    def _s5(self, w_in, bin_fm, s5B, s5C, s5d, tabs, flag, AT, HTp, gyd):
        mk = self.mk
        cf = self.cst
        ident = self.ident
        with mk.phase():
            BRI = mk.sbuf("BRI", [128, 2 * G * 16], F32)
            CRI = mk.sbuf("CRI", [128, 2 * G * 16], F32)
            dd = mk.sbuf("dd", [128, 6], F32)
            binb = mk.sbuf("binb", [128, 40], F32)
            for (b_, s_) in ((BRI, s5B), (CRI, s5C), (dd, s5d), (binb, bin_fm)):
                mk.dma("sp", [(b_.t[:], s_.t.ap())], reads=[s_], writes=[b_])
            ZA = mk.sbuf("ZA", [128, G, 8], F32); ZB = mk.sbuf("ZB", [128, G, 8], F32)
            QA = mk.sbuf("QA", [128, G, 8], F32); QB = mk.sbuf("QB", [128, G, 8], F32)
            WR = mk.sbuf("WR", [128, G, 12, 2], F32)
            Cneg = mk.sbuf("Cneg", [128, G * 16], BF16)
            for nm, b_ in (("ZA", ZA), ("ZB", ZB), ("QA", QA), ("QB", QB), ("WR", WR), ("Cneg", Cneg)):
                mk.dma("sp", [(b_.t[:], tabs[nm].t.ap())], reads=[tabs[nm]], writes=[b_])

            wu = [mk.sbuf(f"wu{i}", [128, 8, 128], BF16) for i in range(2)]
            u_ = [mk.sbuf(f"u{i}", [128, SEQ], BF16) for i in range(2)]
            gys = [mk.sbuf(f"gys{i}", [128, HALF], BF16) for i in range(2)]
            Gf = mk.sbuf("Gf", [128, 1024], F32); Gf2 = mk.sbuf("Gf2", [128, 1024], F32)
            Gall = mk.sbuf("Gall", [128, 8, 128], BF16)
            Pm = mk.sbuf("Pm", [128, 8, 8, 128], BF16)
            Toep_ = [mk.sbuf(f"Toep{i}", [128, 8, 128], BF16) for i in range(2)]
            tK = mk.sbuf("tK", [128, 4, 128], F32); Dg = mk.sbuf("Dg", [128, 128], F32)
            Qp = mk.sbuf("Qp", [128, 8, 8, 64], BF16)
            qa = mk.sbuf("qa", [128, 256], F32); qb = mk.sbuf("qb", [128, 256], F32)
            NS = 3
            Rot = [mk.sbuf(f"Rot{i}", [128, 12, 128], BF16) for i in range(NS)]
            Xp_ = [mk.sbuf(f"Xp{i}", [128, 256], BF16) for i in range(NS)]
            Sp_ = [[mk.sbuf(f"Sp{k}{i}", [128, 64], BF16) for i in range(2)] for k in range(NS)]
            So_ = [[mk.sbuf(f"So{k}{i}", [128, 256], BF16) for i in range(2)] for k in range(NS)]
            Hext_ = [mk.sbuf(f"Hext{i}", [128, 8, 257], BF16) for i in range(2)]
            xs_ = [mk.sbuf(f"xs{i}", [128, 256], F32) for i in range(3)]
            x2_ = [mk.sbuf(f"x2{i}", [128, 256], F32) for i in range(3)]
            sg_ = [mk.sbuf(f"sg{i}", [128, 256], F32) for i in range(3)]
            evq = [0]

            def evac(out, in_, reads, writes, scale=None):
                evq[0] += 1
                if scale is not None or evq[0] % 2 == 0:
                    if scale is not None:
                        self.act(out, in_, AF.Copy, reads, writes, scale=scale)
                    else:
                        self.copy("act", out, in_, reads, writes)
                else:
                    self.copy("dve", out, in_, reads, writes)

            for j in range(6):
                g0 = 8 * j
                w = wu[j % 2]
                u = u_[j % 2]; Toep = Toep_[j % 2]; Hext = Hext_[j % 2]; gyj = gys[j % 2]
                self.wload_into(w, w_in, 1024, 128 * j, 128)
                for blk in range(8):
                    src = HTp if blk < 4 else AT
                    c0 = (blk % 4) * 512
                    ps = mk.bank(2 + blk % 2)
                    for kt in range(8):
                        self.mm(ps.t[:], w.t[:, kt, :], src.t[:, kt, c0:c0 + 512], kt == 0, kt == 7, [w, src], [ps])
                    self.act(apx(u, 0, 128, (blk // 4) * HALF + (blk % 4) * 64, [(256, 8), (1, 64)]), apx(ps, 0, 128, 0, [(1, 8), (8, 64)]),
                             AF.Identity, [ps, binb], [u], bias=binb.t[:, j:j + 1])
                if j == 0:
                    self.dump("u0", u, u.t[:], [128, SEQ], BF16)
                za = apx(ZA, 0, 128, g0 * 8, [(1, 8), (8, 8), (0, 16)]); zb = apx(ZB, 0, 128, g0 * 8, [(1, 8), (8, 8), (0, 16)])
                br = apx(BRI, 0, 128, g0 * 16, [(0, 8), (16, 8), (1, 16)]); bi = apx(BRI, 0, 128, G * 16 + g0 * 16, [(0, 8), (16, 8), (1, 16)])
                gf4 = Gf.t[:].rearrange("p (d g c) -> p d g c", d=8, g=8); gf24 = Gf2.t[:].rearrange("p (d g c) -> p d g c", d=8, g=8)
                self.tt("dve", gf4, za, br, ALU.mult, [ZA, BRI], [Gf])
                self.tt("pool", gf24, zb, bi, ALU.mult, [ZB, BRI], [Gf2])
                self.tt("dve", Gall.t[:].rearrange("p d c -> p (d c)"), Gf.t[:], Gf2.t[:], ALU.add, [Gf, Gf2], [Gall])
                for hb in range(2):
                    pst = mk.bank(4)
                    pstb = pst.t[:].bitcast(BF16)
                    for dd_ in range(4):
                        d = hb * 4 + dd_
                        mk.op("pe", lambda e, d=d, dd_=dd_, pstb=pstb: e.transpose(pstb[:, dd_ * 128:(dd_ + 1) * 128], Gall.t[:, d, :], ident.t[:]),
                              reads=[Gall, ident], writes=[pst])
                    for gl in range(8):
                        o = apx(Pm, 0, 128, (hb * 4 * 8 + gl) * 128, [(8 * 128, 4), (1, 128)])
                        i_ = pstb[:, 0:512].rearrange("p (d c) -> p d c", d=4)
                        if gl % 2 == 0:
                            self.ts("dve", o, i_, cf.t[:, 8 + gl:9 + gl], None, ALU.mult, None, [pst, cf], [Pm])
                        else:
                            self.act(o, i_, AF.Copy, [pst, cf], [Pm], scale=cf.t[:, 8 + gl:9 + gl])
                    psk = mk.bank(5)
                    for dd_ in range(4):
                        d = hb * 4 + dd_
                        self.mm(psk.t[:, dd_ * 128:(dd_ + 1) * 128], Gall.t[:, d, :], Cneg.t[:, g0 * 16:g0 * 16 + 128], True, True, [Gall, Cneg], [psk])
                    self.tt("dve", tK.t[:], psk.t[:].rearrange("p (d c) -> p d c", d=4), apx(cf, 0, 128, 128, [(0, 4), (1, 128)]), ALU.mult, [psk, cf], [tK])
                    if hb == 0:
                        self.ts("dve", Dg.t[:], cf.t[:, 256:384], dd.t[:, j:j + 1], None, ALU.mult, None, [cf, dd], [Dg])
                        self.tt("dve", tK.t[:, 0, :], tK.t[:, 0, :], Dg.t[:], ALU.add, [tK, Dg], [tK])
                    self.copy("dve", Toep.t[:, hb * 4:(hb + 1) * 4, :], tK.t[:], [tK], [Toep])
                mk.op("pool", lambda e: e.memset(Qp.t[:], 0.0), reads=[], writes=[Qp])
                for q in range(4):
                    o = apx(Qp, 0, 128, q * 64 + 16 * q, [(512, 8), (256, 2), (1, 16)])
                    a0 = apx(QA, 0, 128, (g0 + q) * 8, [(1, 8), (32, 2), (0, 16)]); b0 = apx(QB, 0, 128, (g0 + q) * 8, [(1, 8), (32, 2), (0, 16)])
                    c0_ = apx(CRI, 0, 128, (g0 + q) * 16, [(0, 8), (64, 2), (1, 16)]); c1_ = apx(CRI, 0, 128, G * 16 + (g0 + q) * 16, [(0, 8), (64, 2), (1, 16)])
                    v3 = lambda b_: b_.t[:].rearrange("p (s k c) -> p s k c", s=8, k=2)
                    self.tt("dve", v3(qa), a0, c0_, ALU.mult, [QA, CRI], [qa])
                    self.tt("pool", v3(qb), b0, c1_, ALU.mult, [QB, CRI], [qb])
                    self.tt("dve", o, v3(qa), v3(qb), ALU.add, [qa, qb, Qp], [Qp])
                for gl in range(8):
                    g = g0 + gl
                    gp = gl % NS
                    R = Rot[gp]
                    Xp = Xp_[gp]; Sp = Sp_[gp]; So = So_[gp]
                    self.tt("pool", R.t[:].rearrange("p e (h c) -> p (e h) c", h=2), apx(cf, 0, 128, 384, [(0, 24), (1, 64)]),
                            apx(WR, 0, 128, g * 24, [(1, 24), (0, 64)]), ALU.mult, [cf, WR], [R])
                    ps = mk.bank(2 * gp)
                    for s in range(8):
                        self.mm(ps.t[:, 0:256], Pm.t[:, 7 - s, gl, :], apx(u, 0, 128, s * 256, [(1, 256)]), s == 0, s == 7, [Pm, u], [ps])
                    self.act(Xp.t[:], ps.t[:, 0:256], AF.Copy, [ps, flag], [Xp], scale=flag.t[:, 0:1])
                    S = Xp
                    for l in range(4):
                        N = 256 // 4 ** (l + 1)
                        ps2 = mk.bank(2 * gp + 1)
                        for e in range(4):
                            lhs = ident.t[:] if e == 0 else R.t[:, l * 3 + e - 1, :]
                            self.mm(ps2.t[:, 0:N], lhs, apx(S, 0, 128, 3 - e, [(4, N)]), e == 0, e == 3, [ident, R, S], [ps2])
                        if l < 3:
                            S2 = Sp[l % 2]
                            evac(S2.t[:, 0:N], ps2.t[:, 0:N], [ps2], [S2])
                            S = S2
                        else:
                            evac(Hext.t[:, gl, 0:1], ps2.t[:, 0:1], [ps2], [Hext])
                    ps = mk.bank(2 * gp)
                    for s in range(8):
                        self.mm(ps.t[:, 0:256], Pm.t[:, 7 - s, gl, :], apx(u, 0, 128, HALF + s * 256, [(1, 256)]), s == 0, False, [Pm, u], [ps])
                    self.mm(ps.t[:, 0:1], R.t[:, 0, :], Hext.t[:, gl, 0:1], False, True, [R, Hext], [ps])
                    S = So[0]
                    evac(S.t[:], ps.t[:, 0:256], [ps], [S])
                    for l in range(4):
                        d = 4 ** l
                        ps2 = mk.bank(2 * gp + 1)
                        self.mm(ps2.t[:, 0:256], ident.t[:], S.t[:, 0:256], True, False, [ident, S], [ps2])
                        for e in range(1, 4):
                            self.mm(ps2.t[:, e * d:256], R.t[:, l * 3 + e - 1, :], S.t[:, 0:256 - e * d], False, e == 3, [R, S], [ps2])
                        if l < 3:
                            S2 = So[(l + 1) % 2]
                            evac(S2.t[:], ps2.t[:, 0:256], [ps2], [S2])
                            S = S2
                        else:
                            evac(Hext.t[:, gl, 1:257], ps2.t[:, 0:256], [ps2], [Hext])
                if j == 0:
                    self.dump("Hext", Hext, Hext.t[:], [128, 8, 257], BF16)
                for s in range(8):
                    ps = mk.bank(2 + s % 2)
                    for gl in range(8):
                        hq = gl // 4
                        self.mm(ps.t[64 * hq:64 * hq + 64, 0:256], Qp.t[:, s, gl, :], Hext.t[:, gl, 0:256], gl % 4 == 0, False, [Qp, Hext], [ps])
                    for d in range(s + 1):
                        self.mm(ps.t[:, 0:256], Toep.t[:, d, :], apx(u, 0, 128, HALF + (s - d) * 256, [(1, 256)]), False, d == s, [Toep, u], [ps])
                    xs = xs_[s % 3]; x2 = x2_[s % 3]; sg = sg_[s % 3]
                    self.copy("act", xs.t[:], ps.t[:, 0:256], [ps], [xs])
                    self.tt("pool", x2.t[:], xs.t[:], xs.t[:], ALU.mult, [xs], [x2])
                    self.ts("dve", x2.t[:], x2.t[:], 2 * GELU_C * 0.044715, 2 * GELU_C, ALU.mult, ALU.add, [x2], [x2])
                    self.tt("dve", x2.t[:], x2.t[:], xs.t[:], ALU.mult, [x2, xs], [x2])
                    self.act(sg.t[:], x2.t[:], AF.Sigmoid, [x2], [sg])
                    self.tt("dve", apx(gyj, 0, 128, s, [(8, 256)]), xs.t[:], sg.t[:], ALU.mult, [xs, sg], [gyj])
                mk.dma("sp", [(gyd[j].t.ap(), gyj.t[:])], reads=[gyj], writes=[gyd[j]])

    def attention(self, w_in, w_sw, bin_fm, bsw_fm, bv_row, COS, SIN, flag, AT, HTp, attT, ones, maskpc):
        mk = self.mk
        cf = self.cst
        with mk.phase():
            binb = mk.sbuf("binb", [128, 40], F32); bswb = mk.sbuf("bswb", [128, 12], F32)
            bvb = mk.sbuf("bvb", [1, 768], BF16)
            mk.dma("sp", [(binb.t[:], bin_fm.t.ap())], reads=[bin_fm], writes=[binb])
            mk.dma("sp", [(bswb.t[:], bsw_fm.t.ap())], reads=[bsw_fm], writes=[bswb])
            mk.dma("pool", [(bvb.t[:], bv_row.t.ap())], reads=[bv_row], writes=[bvb])
            accN = mk.sbuf("accN", [128, 2, HALF], F32)
            accD = mk.sbuf("accD", [128, 2, HALF], F32)
            COSd, SINd = COS, SIN
            COS = mk.sbuf("COS", [128, SEQ], BF16); SIN = mk.sbuf("SIN", [128, SEQ], BF16)
            mk.op("pool", lambda e: e.memset(COS.t[:], 1.0), reads=[], writes=[COS], cost=5.0)
            mk.op("pool", lambda e: e.memset(SIN.t[:], 0.0), reads=[], writes=[SIN], cost=5.0)
            mk.dma("sp", [(COS.t[0:16, :], COSd.t.ap()), (COS.t[64:80, :], COSd.t.ap())], reads=[COSd], writes=[COS])
            mk.dma("sp", [(SIN.t[0:16, :], SINd.t.ap()), (SIN.t[64:80, :], SINd.t.ap())], reads=[SINd], writes=[SIN])
            wq = [mk.sbuf(f"wq{i}", [128, 8, 128], BF16) for i in range(2)]
            wqs = [mk.sbuf(f"wqs{i}", [128, 8, 128], BF16) for i in range(2)]
            wv = mk.sbuf("wv", [128, 8, 256], BF16)
            t1 = [mk.sbuf(f"rt1{i}", [128, 512], F32) for i in range(2)]
            t2 = [mk.sbuf(f"rt2{i}", [128, 512], F32) for i in range(2)]
            PT = [mk.sbuf(f"PT{i}", [128, 256], BF16) for i in range(3)]
            qraw = [mk.sbuf(f"qraw{i}", [128, 512], BF16) for i in range(2)]
            permT = self.cbf.t[:, 256:384]
            mask0 = mk.sbuf("mask0", [128, 256], BF16)
            self.copy("dve", mask0.t[:], maskpc, [self.cbf], [mask0])
            self.ts("dve", mask0.t[:, 0:128], mask0.t[:, 0:128], flag.t[:, 0:1], None, ALU.mult, None, [mask0, flag], [mask0])
            cnt = [0]
            wc = [0]

            def proj_rope(wa, wb, src, c0, bias_a, bias_b, tok0_tab, dst, oap, a0, n, dil):
                i = cnt[0] % 2
                cnt[0] += 1
                pa = mk.bank(2 * i); pb = mk.bank(2 * i + 1)
                for kt in range(8):
                    self.mm(pa.t[:], wa.t[:, kt, :], src.t[:, kt, c0:c0 + 512], kt == 0, kt == 7, [wa, src], [pa])
                self.act(qraw[i].t[:], pa.t[:], AF.Identity, [pa, binb], [qraw[i]], bias=bias_a)
                self.mm(pb.t[:], permT, qraw[i].t[:], True, True, [self.cbf, qraw[i]], [pb])
                self.tt("dve", t1[i].t[:], qraw[i].t[:], COS.t[:, tok0_tab:tok0_tab + 512], ALU.mult, [qraw[i], COS], [t1[i]])
                self.tt("dve", t2[i].t[:], pb.t[:], SIN.t[:, tok0_tab:tok0_tab + 512], ALU.mult, [pb, SIN], [t2[i]])
                if dil == 1:
                    i0 = t1[i].t[:, a0:a0 + n]; i1 = t2[i].t[:, a0:a0 + n]
                else:
                    i0 = apx(t1[i], 0, 128, a0, [(1, dil), (dil, n // dil)])
                    i1 = apx(t2[i], 0, 128, a0, [(1, dil), (dil, n // dil)])
                self.tt("pool", oap, i0, i1, ALU.add, [t1[i], t2[i]], [dst])

            def store_ap(dst, pt, L, dil, m0, n):
                if dil == 1:
                    return dst.t[:, pt, m0:m0 + n]
                return apx(dst, 0, 128, pt * dil * L + m0, [(L, dil), (1, n // dil)])

            it = 0
            vi = 0
            qbi = [0]
            for g in range(3):
                dil = DILS[g]
                Lq = HALF // dil
                Lr = 128 + HALF // dil
                nkb = 1 + 16 // dil
                nq = 16 // dil
                with mk.phase():
                    qT = mk.sbuf(f"qT{g}", [128, 2, HALF], BF16)
                    kT = mk.sbuf(f"kT{g}", [128, 2, dil * Lr], BF16)
                    V = mk.sbuf(f"V{g}", [128, dil * nkb, 256], BF16)
                    for pt in range(2):
                        mt = 2 * g + pt
                        wa = wq[wc[0] % 2]; wb = wqs[wc[0] % 2]; wc[0] += 1
                        self.wload_into(wa, w_in, 1024, 768 + 128 * mt, 128)
                        for blk in range(4):
                            oap = store_ap(qT, pt, Lq, dil, blk * 512 // dil, 512)
                            proj_rope(wa, wb, AT, blk * 512, binb.t[:, 6 + mt:7 + mt], bswb.t[:, mt:mt + 1], HALF + blk * 512,
                                      qT, oap, 0, 512, dil)
                        wa = wq[wc[0] % 2]; wb = wqs[wc[0] % 2]; wc[0] += 1
                        self.wload_into(wa, w_in, 1024, 1536 + 128 * mt, 128)
                        for blk in range(8):
                            prev = blk < 4
                            src = HTp if prev else AT
                            if prev:
                                lo = HALF - 128 * dil
                                b0 = blk * 512
                                if b0 + 512 <= lo:
                                    continue
                                a0 = max(lo, b0) - b0
                                n = 512 - a0
                                m0 = (b0 + a0 - lo) // dil
                            else:
                                a0, n = 0, 512
                                m0 = 128 + (blk - 4) * 512 // dil
                            oap = store_ap(kT, pt, Lr, dil, m0, n)
                            proj_rope(wa, wb, src, (blk % 4) * 512, binb.t[:, 12 + mt:13 + mt], bswb.t[:, 6 + mt:7 + mt], blk * 512,
                                      kT, oap, a0, n, dil)
                    self.wload_into(wv, w_in, 1024, 2304 + 256 * g, 256)
                    for r in range(dil):
                        for b in range(nkb):
                            if b == 0:
                                src = HTp; start = HALF - 128 * dil + r
                            else:
                                src = AT; start = dil * 128 * (b - 1) + r
                            ps = mk.bank(4 + vi % 2)
                            vi += 1
                            for kt in range(8):
                                lhs = apx(src, 0, 128, kt * HALF + start, [(dil, 128)])
                                self.mm(ps.t[:, 0:256], lhs, wv.t[:, kt, :], kt == 0, False, [src, wv], [ps])
                            self.mm(ps.t[:, 0:256], ones[0:1, :], bvb.t[0:1, 256 * g:256 * g + 256], False, True, [self.cbf, bvb], [ps])
                            dst = V.t[:, r * nkb + b, :]
                            if b == 0:
                                self.act(dst, ps.t[:, 0:256], AF.Copy, [ps, flag], [V], scale=flag.t[:, 0:1])
                            elif vi % 2:
                                self.copy("act", dst, ps.t[:, 0:256], [ps], [V])
                            else:
                                self.copy("dve", dst, ps.t[:, 0:256], [ps], [V])
                    if g == 1:
                        self.dump("qT1", qT, qT.t[:], [128, 2, HALF], BF16)
                        self.dump("kT1", kT, kT.t[:], [128, 2, dil * Lr], BF16)
                        self.dump("V1", V, V.t[:], [128, dil * nkb, 256], BF16)
                    for pt in range(2):
                        for r in range(dil):
                            for qb in range(nq):
                                pN = mk.bank(4 + 2 * (qbi[0] % 2)); pD = mk.bank(5 + 2 * (qbi[0] % 2)); qbi[0] += 1
                                for hp in range(2):
                                    rows = slice(64 * hp, 64 * hp + 64)
                                    pS = mk.bank(it % 3)
                                    P_ = PT[it % 3]
                                    it += 1
                                    qap = apx(qT, 64 * hp, 64, pt * HALF + r * Lq + qb * 128, [(1, 128)])
                                    for half in range(2):
                                        kap = apx(kT, 64 * hp, 64, pt * dil * Lr + r * Lr + (qb + half) * 128, [(1, 128)])
                                        self.mm(pS.t[:, half * 128:(half + 1) * 128], kap, qap, True, True, [kT, qT], [pS])
                                    self.act(P_.t[:], pS.t[:, 0:256], AF.Exp, [pS], [P_], scale=0.125)
                                    meng = "pool" if it % 2 else "dve"
                                    if qb == 0:
                                        self.tt(meng, P_.t[:], P_.t[:], mask0.t[:], ALU.mult, [P_, mask0], [P_])
                                    else:
                                        self.tt(meng, P_.t[:], P_.t[:], maskpc, ALU.mult, [P_, self.cbf], [P_])
                                    h = 2 * pt + hp
                                    for half in range(2):
                                        vb = V.t[:, r * nkb + qb + half, h * 64:(h + 1) * 64]
                                        self.mm(pN.t[rows, 0:128], vb, P_.t[:, half * 128:(half + 1) * 128], half == 0, half == 1, [V, P_], [pN])
                                    for half in range(2):
                                        self.mm(pD.t[rows, 0:128], ones[:, 0:64], P_.t[:, half * 128:(half + 1) * 128], half == 0, half == 1, [self.cbf, P_], [pD])
                                an = apx(accN, 0, 128, pt * HALF + dil * 128 * qb + r, [(dil, 128)])
                                ad = apx(accD, 0, 128, pt * HALF + dil * 128 * qb + r, [(dil, 128)])
                                if g == 0:
                                    self.copy("dve", an, pN.t[:, 0:128], [pN], [accN])
                                    self.copy("act", ad, pD.t[:, 0:128], [pD], [accD])
                                else:
                                    self.tt("dve", an, an, pN.t[:, 0:128], ALU.add, [pN, accN], [accN])
                                    self.tt("dve", ad, ad, pD.t[:, 0:128], ALU.add, [pD, accD], [accD])
            self.dump("accN", accN, accN.t[:], [128, 2, HALF])
            self.dump("accD", accD, accD.t[:], [128, 2, HALF])
            for pt in range(2):
                mk.op("dve", lambda e, pt=pt: e.reciprocal(accD.t[:, pt, :], accD.t[:, pt, :]), reads=[accD], writes=[accD], cost=2.4)
                self.tt("dve", attT.t[:, pt, :], accN.t[:, pt, :], accD.t[:, pt, :], ALU.mult, [accN, accD], [attT])


def _consts():
    bf = ml_dtypes.bfloat16
    c_bf = np.zeros((128, 128 * 3 + 256 + 64), np.float32)
    c_bf[:, 0:128] = np.eye(128)
    c_bf[:, 128:256] = 1.0
    ik = np.arange(128)[:, None]; iq = np.arange(128)[None, :]
    for k_ in range(128):
        for m_ in range(128):
            if k_ // 64 == m_ // 64:
                i_ = m_ % 64
                src_ = i_ + 8 if i_ < 8 else (i_ - 8 if i_ < 16 else i_)
                if k_ % 64 == src_:
                    c_bf[k_, 256 + m_] = 1.0
    c_bf[:, 384:512] = (ik >= iq)
    c_bf[:, 512:640] = (ik <= iq)
    c_f = np.zeros((128, 512), np.float32)
    p = np.arange(128)
    c_f[:, 0] = EPS
    c_f[:, 1] = p < 64
    c_f[:, 2] = p >= 64
    q = p % 64
    invf = (500000.0 ** (-(2.0 * (q % 8)) / 16.0)).astype(np.float32)
    q16 = p % 16
    c_f[:, 3] = (500000.0 ** (-(2.0 * (q16 % 8)) / 16.0)).astype(np.float32)
    c_f[:, 4] = np.where(q16 < 8, -1.0, 1.0)
    c_f[:, 8:16] = (p[:, None] // 16 == np.arange(8)[None, :])
    c_f[:, 16:16 + ND] = np.asarray(DPOW, np.float32)[None, :]
    c_f[:, 128:256] = (p[:, None] // 16 == (np.arange(128)[None, :] // 16))
    c_f[:, 256:384] = np.eye(128)
    c_f[:, 384:448] = (np.arange(64)[None, :] == (p[:, None] % 64))
    return c_bf.astype(bf), c_f


def _prep_shared(inp):
    f = lambda a: np.ascontiguousarray(np.asarray(a, dtype=np.float32))
    w_in = f(inp["w_in"][0]); b_in = f(inp["b_in"][0])
    perm = np.arange(64)
    perm[0:8] = np.arange(8, 16); perm[8:16] = np.arange(0, 8)
    colperm = (np.arange(12)[:, None] * 64 + perm[None, :]).reshape(-1)
    w_sw = np.concatenate([w_in[:, 768 + colperm], w_in[:, 1536 + colperm]], axis=1)
    b_sw = np.concatenate([b_in[768 + colperm], b_in[1536 + colperm]])
    fm = lambda v: np.ascontiguousarray(v.reshape(-1, 128).T)
    rep2 = lambda a: np.concatenate([a, a], axis=0)
    logdt = f(inp["ssm_log_dt"][0]); are = f(inp["ssm_a_re"][0]); aim = f(inp["ssm_a_im"][0])
    s5p = np.concatenate([np.broadcast_to(logdt[None, :], (128, G)), rep2(are.T), rep2(aim.T)], axis=1)
    br = f(inp["ssm_b_re"][0]).transpose(1, 0, 2).reshape(64, G * 16)
    bi = f(inp["ssm_b_im"][0]).transpose(1, 0, 2).reshape(64, G * 16)
    cr = f(inp["ssm_c_re"][0]).transpose(2, 0, 1).reshape(64, G * 16)
    ci = f(inp["ssm_c_im"][0]).transpose(2, 0, 1).reshape(64, G * 16)
    c_bf, c_f = _consts()
    sh = {
        "w_in": w_in, "w_sw": f(w_sw), "bin_fm": fm(b_in), "bsw_fm": fm(b_sw), "bv_row": f(b_in[None, 2304:3072]),
        "ln_g": f(np.stack([inp["ln_in_g"], inp["ln1_g"][0], inp["ln2_g"][0], inp["ln3_g"][0]])),
        "ln_b": f(np.stack([inp["ln_in_b"], inp["ln1_b"][0], inp["ln2_b"][0], inp["ln3_b"][0]])),
        "s5p": f(s5p), "s5B": f(np.concatenate([rep2(br), rep2(bi)], axis=1)), "s5C": f(np.concatenate([rep2(cr), rep2(ci)], axis=1)),
        "s5d": fm(f(inp["ssm_d"][0])),
        "w_glu": f(inp["w_glu"][0]), "bglu_fm": fm(f(inp["b_glu"][0])),
        "w_au": f(inp["w_att_up"][0]), "w_mo": f(inp["w_mix_out"][0]), "bmo_row": f(inp["b_mix_out"][0][None, :]),
        "w_xq": f(inp["w_xq"][0]), "w_xkv": f(inp["w_xkv"][0]), "w_xo": f(inp["w_xo"][0]),
        "w_f1": f(inp["w_ff1"][0]), "bf1_fm": fm(f(inp["b_ff1"][0])), "w_f2": f(inp["w_ff2"][0]), "bf2_row": f(inp["b_ff2"][0][None, :]),
        "c_bf": c_bf, "c_f32": c_f,
    }
    return sh


def _core_inputs(inp, sh, b, half):
    x = np.asarray(inp["x"], np.float32); pos = np.asarray(inp["positions"], np.int32)
    d = dict(sh)
    d["x_own"] = np.ascontiguousarray(x[b, half * HALF:(half + 1) * HALF])
    if half == 0:
        d["x_prev"] = np.zeros((HALF, D_MODEL), np.float32)
        pp = np.concatenate([np.zeros(HALF, np.int32), pos[b, :HALF]])
    else:
        d["x_prev"] = np.ascontiguousarray(x[b, :HALF])
        pp = pos[b]
    d["posb"] = np.ascontiguousarray(np.broadcast_to(pp.reshape(8, 1, 512), (8, 16, 512)).reshape(128, 512))
    d["flag"] = np.full((128, 1), float(half), np.float32)
    d["mem"] = np.ascontiguousarray(np.asarray(inp["mem"], np.float32)[b])
    return d


_PROG = {}


def kernel(**inputs):
    if "k" not in _PROG:
        _PROG["k"] = K()
    k = _PROG["k"]
    sh = _prep_shared(inputs)
    in_maps = [_core_inputs(inputs, sh, c // 2, c % 2) for c in range(8)]
    res = run_bass_kernel_spmd(k.nc, in_maps, core_ids=list(range(8)))
    out = np.zeros((4, SEQ, D_MODEL), np.float32)
    for c in range(8):
        out[c // 2, (c % 2) * HALF:(c % 2 + 1) * HALF] = res.results[c]["out"]
    return out
```
